# Optimizing a Trainium2 kernel written in Bass

```python
import jax, jax.numpy as jnp
from jax import lax
import numpy as np

D_MODEL = 1024
BATCH = 2
SEQ = 8192
DEPTH = 2

H_A = 8
NOPE = 64
ROPE_D = 32
V_A = 64
Q_LORA = 256
KV_LORA = 128
W_A = H_A * V_A
H_B = 8
HD_B = 64
W_B = H_B * HD_B
H_C = 8
HD_C = 64
W_C = H_C * HD_C
DECAY_LORA = 64
AAA_LORA = 64
MV_LORA = 32
SHIFT_COLS = 3 * W_C + DECAY_LORA + AAA_LORA
ROPE_THETA = 10000.0
Q_BLOCK = 128
RWKV_DECAY_SCALE = 0.606531
GN_EPS = 64e-5
EPS = 1e-6
NEG_INF = -1e30

IN_SIZES = (Q_LORA, KV_LORA, ROPE_D, W_A,
            W_B, W_B, W_B, H_B, W_B,
            SHIFT_COLS, W_C,
            3 * D_MODEL)
IN_COLS = sum(IN_SIZES)
RWKV_SIZES = (W_C, W_C, W_C, DECAY_LORA, AAA_LORA)

kernel_name = "hybrid_mla_fox_rwkv7_gated_merge"


def rmsnorm(x, g, eps=EPS):
    xf = x.astype(jnp.float32)
    y = xf * lax.rsqrt(jnp.mean(xf * xf, axis=-1, keepdims=True) + eps)
    return (y * g.astype(jnp.float32)).astype(x.dtype)


def split_cols(p, sizes):
    idx = np.cumsum(np.array(sizes))[:-1].tolist()
    return jnp.split(p, idx, axis=-1)


def rope_tables(seq):
    inv = ROPE_THETA ** (-jnp.arange(0, ROPE_D, 2, dtype=jnp.float32) / ROPE_D)
    ang = jnp.arange(seq, dtype=jnp.float32)[:, None] * inv[None, :]
    return jnp.cos(ang)[:, None, :], jnp.sin(ang)[:, None, :]


def apply_rope(x, cos, sin):
    half = ROPE_D // 2
    x1 = x[..., :half].astype(jnp.float32)
    x2 = x[..., half:].astype(jnp.float32)
    out = jnp.concatenate([x1 * cos - x2 * sin, x2 * cos + x1 * sin], axis=-1)
    return out.astype(x.dtype)


def causal_block_attention(q, k, v, cum=None):
    B, H, S, Dk = q.shape
    Dv = v.shape[-1]
    nb = S // Q_BLOCK
    scale = Dk ** -0.5
    kpos = jnp.arange(S)
    q_blocks = q.reshape(B, H, nb, Q_BLOCK, Dk).transpose(2, 0, 1, 3, 4)
    use_decay = cum is not None
    xs = (jnp.arange(nb), q_blocks)
    if use_decay:
        xs = xs + (cum.reshape(B, H, nb, Q_BLOCK).transpose(2, 0, 1, 3),)

    def block(args):
        i, qb = args[0], args[1]
        s = jnp.einsum('bhqd,bhkd->bhqk', qb, k).astype(jnp.float32) * scale
        if use_decay:
            s = s + args[2][..., :, None] - cum[:, :, None, :]
        qpos = i * Q_BLOCK + jnp.arange(Q_BLOCK)
        s = jnp.where(kpos[None, :] <= qpos[:, None], s, NEG_INF)
        p = jax.nn.softmax(s, axis=-1)
        return jnp.einsum('bhqk,bhkd->bhqd', p.astype(v.dtype), v)

    o = lax.map(block, xs)
    return o.transpose(1, 0, 3, 2, 4).reshape(B, S, H * Dv)


def mla_branch(c_q, c_kv, k_r, gate, qa_g, w_uq, kva_g, w_ukv, q_g, knope_g, krope_g, cos, sin):
    B, S, _ = c_q.shape
    q = (rmsnorm(c_q, qa_g) @ w_uq).reshape(B, S, H_A, NOPE + ROPE_D)
    q = rmsnorm(q, q_g)
    q = jnp.concatenate([q[..., :NOPE], apply_rope(q[..., NOPE:], cos, sin)], axis=-1)
    kv = (rmsnorm(c_kv, kva_g) @ w_ukv).reshape(B, S, H_A, NOPE + V_A)
    k_nope = rmsnorm(kv[..., :NOPE], knope_g)
    v = kv[..., NOPE:]
    k_rope = apply_rope(rmsnorm(k_r, krope_g)[:, :, None, :], cos, sin)
    k = jnp.concatenate([k_nope, jnp.broadcast_to(k_rope, (B, S, H_A, ROPE_D))], axis=-1)
    o = causal_block_attention(q.transpose(0, 2, 1, 3), k.transpose(0, 2, 1, 3), v.transpose(0, 2, 1, 3))
    return o * jax.nn.silu(gate)


def fox_branch(fq, fk, fv, ff, gate, b_f, q_g, k_g):
    B, S, _ = fq.shape
    q = rmsnorm(fq.reshape(B, S, H_B, HD_B), q_g)
    k = rmsnorm(fk.reshape(B, S, H_B, HD_B), k_g)
    v = fv.reshape(B, S, H_B, HD_B)
    log_f = jax.nn.log_sigmoid((ff + b_f).astype(jnp.float32))
    cum = jnp.cumsum(log_f, axis=1).transpose(0, 2, 1)
    o = causal_block_attention(q.transpose(0, 2, 1, 3), k.transpose(0, 2, 1, 3), v.transpose(0, 2, 1, 3), cum)
    return o * jax.nn.silu(gate)


def rwkv7_scan(r, w, k, v, kk, a):
    B, S, H, N = r.shape

    def step(state, inp):
        r_t, w_t, k_t, v_t, kk_t, a_t = inp
        sa = jnp.einsum('bhvk,bhk->bhv', state, -kk_t)
        state = (state * w_t[:, :, None, :] + sa[..., None] * (kk_t * a_t)[:, :, None, :]
                 + v_t[..., None] * k_t[:, :, None, :])
        return state, jnp.einsum('bhvk,bhk->bhv', state, r_t)

    xs = tuple(t.transpose(1, 0, 2, 3) for t in (r, w, k, v, kk, a))
    _, y = lax.scan(step, jnp.zeros((B, H, N, N), jnp.float32), xs)
    return y.transpose(1, 0, 2, 3)


def rwkv_branch(r, k, v, wl, al, gate, w0, w_up, a0, a_up, k_k, k_a, r_k, lnx_g, lnx_b):
    B, S, _ = r.shape
    f32 = jnp.float32
    heads = lambda t: t.astype(f32).reshape(B, S, H_C, HD_C)
    w = jnp.exp(-RWKV_DECAY_SCALE * jax.nn.sigmoid((w0 + jnp.tanh(wl) @ w_up).astype(f32)))
    a = jax.nn.sigmoid((a0 + al @ a_up).astype(f32))
    kk = heads(k * k_k)
    kk = kk / jnp.maximum(jnp.sqrt(jnp.sum(kk * kk, axis=-1, keepdims=True)), 1e-12)
    k_mod = k.astype(f32) * (1.0 + (a - 1.0) * k_a.astype(f32))
    r_h, w_h, k_h, v_h, a_h = heads(r), heads(w), heads(k_mod), heads(v), heads(a)
    y = rwkv7_scan(r_h, w_h, k_h, v_h, kk, a_h)
    mu = jnp.mean(y, axis=-1, keepdims=True)
    var = jnp.mean(jnp.square(y - mu), axis=-1, keepdims=True)
    y = ((y - mu) * lax.rsqrt(var + GN_EPS)).reshape(B, S, W_C) * lnx_g.astype(f32) + lnx_b.astype(f32)
    bonus = jnp.sum(r_h * k_h * r_k.astype(f32), axis=-1, keepdims=True) * v_h
    out = (y + bonus.reshape(B, S, W_C)) * jax.nn.silu(gate.astype(f32))
    return out.astype(gate.dtype)


def setup_inputs(seed: int = 0) -> dict:
    key = jax.random.key(seed)
    ks = iter(jax.random.split(key, 48))
    f32 = jnp.float32
    L, Lv = DEPTH, DEPTH - 1

    def nrm(shape, scale):
        return scale * jax.random.normal(next(ks), shape, f32)

    def gain(shape):
        return 1.0 + nrm(shape, 0.02)

    x = jax.random.normal(next(ks), (BATCH, SEQ, D_MODEL), f32)
    return {
        "x": x,
        "norm_g": gain((L, D_MODEL)),
        "w_in": nrm((L, D_MODEL, IN_COLS), D_MODEL ** -0.5),
        "mla_qa_g": gain((L, Q_LORA)),
        "mla_w_uq": nrm((L, Q_LORA, H_A * (NOPE + ROPE_D)), Q_LORA ** -0.5),
        "mla_kva_g": gain((L, KV_LORA)),
        "mla_w_ukv": nrm((L, KV_LORA, H_A * (NOPE + V_A)), KV_LORA ** -0.5),
        "mla_q_g": gain((L, NOPE + ROPE_D)),
        "mla_knope_g": gain((L, NOPE)),
        "mla_krope_g": gain((L, ROPE_D)),
        "fox_b_f": jnp.broadcast_to(jnp.linspace(1.0, 6.0, H_B, dtype=f32), (L, H_B)) + nrm((L, H_B), 0.1),
        "fox_q_g": gain((L, HD_B)),
        "fox_k_g": gain((L, HD_B)),
        "rwkv_mu": jax.random.uniform(next(ks), (L, SHIFT_COLS), f32),
        "rwkv_w0": nrm((L, W_C), 1.0),
        "rwkv_w_up": nrm((L, DECAY_LORA, W_C), 0.5 * DECAY_LORA ** -0.5),
        "rwkv_a0": nrm((L, W_C), 0.5),
        "rwkv_a_up": nrm((L, AAA_LORA, W_C), 0.5 * AAA_LORA ** -0.5),
        "rwkv_k_k": 0.85 + nrm((L, W_C), 0.1),
        "rwkv_k_a": 1.0 + nrm((L, W_C), 0.1),
        "rwkv_r_k": nrm((L, H_C, HD_C), 0.1),
        "rwkv_lnx_g": gain((L, W_C)),
        "rwkv_lnx_b": nrm((L, W_C), 0.02),
        "rwkv_v0": nrm((Lv, W_C), 0.5),
        "rwkv_v_down": nrm((Lv, W_C, MV_LORA), W_C ** -0.5),
        "rwkv_v_up": nrm((Lv, MV_LORA, W_C), 0.5 * MV_LORA ** -0.5),
        "w_pa": nrm((L, W_A, D_MODEL), W_A ** -0.5),
        "w_pb": nrm((L, W_B, D_MODEL), W_B ** -0.5),
        "w_pc": nrm((L, W_C, D_MODEL), W_C ** -0.5),
        "w_out": nrm((L, D_MODEL, D_MODEL), D_MODEL ** -0.5),
    }


def reference(x, norm_g, w_in, mla_qa_g, mla_w_uq, mla_kva_g, mla_w_ukv, mla_q_g, mla_knope_g,
              mla_krope_g, fox_b_f, fox_q_g, fox_k_g, rwkv_mu, rwkv_w0, rwkv_w_up, rwkv_a0, rwkv_a_up,
              rwkv_k_k, rwkv_k_a, rwkv_r_k, rwkv_lnx_g, rwkv_lnx_b, rwkv_v0, rwkv_v_down, rwkv_v_up,
              w_pa, w_pb, w_pc, w_out):
    S = x.shape[1]
    cos, sin = rope_tables(S)
    v_first = None
    for l in range(DEPTH):
        h = rmsnorm(x, norm_g[l])
        p = h @ w_in[l]
        (c_q, c_kv, k_r, gate_a, fq, fk, fv, ff, gate_b,
         shift_cols, gate_c, merge_cols) = split_cols(p, IN_SIZES)

        o_a = mla_branch(c_q, c_kv, k_r, gate_a, mla_qa_g[l], mla_w_uq[l], mla_kva_g[l], mla_w_ukv[l],
                         mla_q_g[l], mla_knope_g[l], mla_krope_g[l], cos, sin)
        o_b = fox_branch(fq, fk, fv, ff, gate_b, fox_b_f[l], fox_q_g[l], fox_k_g[l])
        prev = jnp.pad(shift_cols, ((0, 0), (1, 0), (0, 0)))[:, :-1]
        shifted = shift_cols + rwkv_mu[l] * (prev - shift_cols)
        r, k, v, wl, al = split_cols(shifted, RWKV_SIZES)
        if l == 0:
            v_first = v
        else:
            nu = jax.nn.sigmoid(rwkv_v0[l - 1] + (v @ rwkv_v_down[l - 1]) @ rwkv_v_up[l - 1])
            v = v + (v_first - v) * nu
        o_c = rwkv_branch(r, k, v, wl, al, gate_c, rwkv_w0[l], rwkv_w_up[l], rwkv_a0[l], rwkv_a_up[l],
                          rwkv_k_k[l], rwkv_k_a[l], rwkv_r_k[l], rwkv_lnx_g[l], rwkv_lnx_b[l])

        g_a, g_b, g_c = jnp.split(merge_cols, 3, axis=-1)
        merged = (jax.nn.sigmoid(g_a) * (o_a @ w_pa[l])
                  + jax.nn.sigmoid(g_b) * (o_b @ w_pb[l])
                  + jax.nn.sigmoid(g_c) * (o_c @ w_pc[l]))
        x = x + (merged @ w_out[l]).astype(x.dtype)
    return x
```

```python
import contextlib
import numpy as np
import concourse.bass as bass
import concourse.mybir as mybir
from concourse.bass_utils import run_bass_kernel_spmd

F32 = mybir.dt.float32
BF16 = mybir.dt.bfloat16
AF = mybir.ActivationFunctionType
ALU = mybir.AluOpType
AX = mybir.AxisListType
SEM_LIMIT = 30000

D = 1024
EPS = 1e-6
GN_EPS = 64e-5
DECAY = 0.606531
NA = 1186
V_QG2, V_KNG2, V_KRG, V_FQG2, V_FKG2, V_BF, V_KK, V_KA, V_RK, V_LG, V_LB, V_MU = (
    0, 192, 320, 352, 480, 608, 610, 738, 866, 994, 1122, 1250)
NV = 1762


class T:
    __slots__ = ("name", "w", "r", "nowaw", "stream")

    def __init__(self, name, nowaw=False):
        self.name = name
        self.w = {}
        self.r = {}
        self.nowaw = nowaw
        self.stream = None


class B:
    def __init__(self, a, name, nowaw=False, psum=False):
        self.a = a
        self.T = T(name, nowaw)
        self.psum = psum


class Rot:
    def __init__(self, items):
        self.items = items
        self.i = 0

    def next(self):
        x = self.items[self.i % len(self.items)]
        self.i += 1
        return x


class FW:
    ENG = ("pe", "dve", "act", "pool", "sp")

    def __init__(self, nc, es):
        self.nc = nc
        self.es = es
        self.e = {"pe": nc.tensor, "dve": nc.vector, "act": nc.scalar, "pool": nc.gpsimd, "sp": nc.sync}
        self.sem = {}
        self.cnt = {}
        self.cur = {}
        self.nsem = 0
        self.seen = {k: {} for k in self.ENG}
        self.free_dma = []
        self.used_dma = []
        for k in self.ENG:
            self._new_key(k)

    def _new_key(self, stream):
        key = "%s#%d" % (stream, self.nsem)
        self.sem[key] = self.es.enter_context(self.nc.semaphore("s%d" % self.nsem))
        self.nsem += 1
        self.cnt[key] = 0
        self.cur[stream] = key
        return key

    def _wait(self, eng, key, seq):
        if self.seen[eng].get(key, 0) >= seq:
            return
        self.seen[eng][key] = seq
        self.e[eng].wait_ge(self.sem[key], seq)

    def deps(self, eng, reads, writes):
        own = eng + "#"
        for b in reads:
            for k, s in b.T.w.items():
                if eng == "pe" and k.startswith(own):
                    continue
                self._wait(eng, k, s)
        for b in writes:
            t = b.T
            if not t.nowaw:
                for k, s in t.w.items():
                    if eng == "pe" and k.startswith(own):
                        continue
                    self._wait(eng, k, s)
            for k, s in t.r.items():
                if k.startswith(own):
                    continue
                self._wait(eng, k, s)

    def done(self, ins, stream, inc, reads, writes):
        key = self.cur[stream]
        if self.cnt[key] + inc > SEM_LIMIT:
            key = self._new_key(stream)
        self.cnt[key] += inc
        seq = self.cnt[key]
        ins.then_inc(self.sem[key], inc)
        for b in reads:
            b.T.r[key] = seq
        for b in writes:
            t = b.T
            if t.nowaw:
                t.w[key] = seq
            else:
                t.w = {key: seq}
                t.r = {}

    def op(self, eng, fn, R, W):
        pr = [b for b in R if b.psum]
        if pr:
            R = [b for b in R if not b.psum]
            W = list(W) + [b for b in pr if b not in W]
        self.deps(eng, R, W)
        ins = fn(self.e[eng])
        self.done(ins, eng, 1, R, W)

    def dma(self, issuer, out, in_, R, W, slot):
        t = slot.T
        if t.stream is None:
            self.nstream = getattr(self, "nstream", 0) + 1
            t.stream = "dma%d" % self.nstream
            if self.free_dma:
                self.cur[t.stream] = self.free_dma.pop()
            else:
                self._new_key(t.stream)
            self.used_dma.append(t.stream)
        self.deps(issuer, R, W)
        ins = self.e[issuer].dma_start(out=out, in_=in_)
        self.done(ins, t.stream, 16, R, W)

    def barrier(self):
        snap = {k: c for k, c in self.cnt.items() if c > 0}
        for eng in self.ENG:
            for k, c in snap.items():
                if k.startswith(eng + "#"):
                    continue
                self._wait(eng, k, c)
        keep = getattr(self, "keep", set())
        for st in self.used_dma:
            if st not in keep:
                self.free_dma.append(self.cur[st])
        self.used_dma = [st for st in self.used_dma if st in keep]


class StopBuild(Exception):
    pass


class Kern:
    stage = 99
    dve_only = False
    debug = False
    dbg_tile = 0

    def dump(self, idx, src_b, ap, n):
        if not self.debug:
            return
        self.fw.dma("sp", self.dbg[idx, :, 0:n], ap, [src_b], [self.Bdbg], self.Bdbg)
        self.fw.keep = {self.Bdbg.T.stream}

    def chk(self, st):
        if self.stage <= st:
            raise StopBuild()

    def __init__(self, S, n_layers, layers=None):
        self.S = S
        self.NT = S // 128
        self.NCH = S // 512
        self.layers = list(range(n_layers)) if layers is None else list(layers)
        self.n_layers = max(self.layers) + 1
        self.uid = 0

    def sb(self, es, name, shape, dt, nowaw=False):
        self.uid += 1
        a = es.enter_context(self.nc.sbuf_tensor("%s_%d" % (name, self.uid), shape, dt))
        return B(a, name, nowaw)

    def ps(self, es, name, shape, dt):
        self.uid += 1
        a = es.enter_context(self.nc.psum_tensor("%s_%d" % (name, self.uid), shape, dt))
        return B(a, name, psum=True)

    def dve(self, fn, R, W):
        self.fw.op("dve", fn, R, W)

    def act(self, fn, R, W):
        self.fw.op("act", fn, R, W)

    def pool(self, fn, R, W):
        self.fw.op("pool", fn, R, W)

    def pe(self, fn, R, W):
        self.fw.op("pe", fn, R, W)

    def evac(self, out, in_, R, W):
        self._ev = getattr(self, "_ev", 0) + 1
        if self._ev % 2 or self.dve_only:
            self.dve(lambda e: e.tensor_copy(out=out, in_=in_), R, W)
        else:
            self.act(lambda e: e.activation(out=out, in_=in_, func=AF.Copy), R, W)

    def transpose(self, out_ap, in_ap, np_in, nf_in, dt, R, W, evac_eng=None, scale=None):
        if dt == BF16:
            slot = self.mib.next()
            idn = self.ident_b
        else:
            slot = self.mif.next()
            idn = self.ident_f
        pv = slot.a[0:nf_in, 0:np_in]
        self.pe(lambda e: e.transpose(out=pv, in_=in_ap, identity=idn.a[0:np_in, 0:np_in]), R + [idn], [slot])
        self.evac(out_ap, pv, [slot], W)

    def build(self):
        S, NT = self.S, self.NT
        nc = bass.Bass("TRN2", target_bir_lowering=False)
        self.nc = nc
        L = self.n_layers
        dr = {}

        def din(name, shape):
            dr[name] = nc.dram_tensor(name, shape, F32, kind="ExternalInput").ap()

        din("x", [S, D])
        din("xq", [S // 4, D])
        din("cs", [S, 64])
        for l in self.layers:
            din("wA%d" % l, [D, NA])
            din("wS%d" % l, [D, 512])
            din("wG%d" % l, [D, 3072])
            din("wuq%d" % l, [256, 192])
            din("wukv%d" % l, [128, 256])
            din("wup%d" % l, [65, 128])
            din("aup%d" % l, [65, 128])
            din("vec%d" % l, [1, NV])
            din("gT%d" % l, [128, 8])
            din("qagT%d" % l, [128, 2])
            din("kvagT%d" % l, [128, 1])
            din("wp%d" % l, [3, 128, D])
            din("wo%d" % l, [D, D])
        if 1 in self.layers:
            din("wvT", [512, D])
            din("muVT", [128, 4])
            din("vdown", [512, 32])
            din("vup", [33, 128])
        out = nc.dram_tensor("out", [S // 4, D], F32, kind="ExternalOutput").ap()
        if self.debug:
            self.dbg = nc.dram_tensor("dbg", [24, 128, 512], F32, kind="ExternalOutput").ap()
            self.Bdbg = B(None, "dbg", nowaw=True)
            self.dbgb = nc.dram_tensor("dbgb", [4, 128, 512], BF16, kind="ExternalOutput").ap()
        self.dr = dr
        part = [nc.dram_tensor("part%d" % l, [S, D], F32).ap() for l in range(L)]
        red = [nc.dram_tensor("red%d" % l, [S, D], F32).ap() for l in range(L)]
        rs = [nc.dram_tensor("rs%d" % l, [S // 4, D], F32).ap() for l in range(L)]
        self.rs = rs
        self.Brs = [B(None, "rs%d" % l) for l in range(L)]
        oT_scr = nc.dram_tensor("oT_scr", [3, 128, S], BF16).ap()
        if 1 in self.layers and 0 not in self.layers:
            vf_scr = nc.dram_tensor("vf", [S, 128], F32, kind="ExternalInput").ap()
        elif self.layers == [0]:
            vf_scr = nc.dram_tensor("vf", [S, 128], F32, kind="ExternalOutput").ap()
        else:
            vf_scr = nc.dram_tensor("vf_scr", [S, 128], F32).ap()
        self.part, self.red, self.oT_scr, self.vf_scr = part, red, oT_scr, vf_scr
        self.Bpart = [B(None, "part%d" % l, nowaw=True) for l in range(L)]
        self.Bred = [B(None, "red%d" % l) for l in range(L)]
        self.BoT = B(None, "oTscr", nowaw=True)
        self.Bvf = B(None, "vfscr", nowaw=True)
        self.Bout = B(None, "out", nowaw=True)

        with contextlib.ExitStack() as es:
            self.fw = FW(nc, es)
            self.consts(es)
            self.pj = Rot([self.ps(es, "pj", [128, 512], F32) for _ in range(2)])
            self.sc = Rot([self.ps(es, "sc", [128, 512], F32) for _ in range(2)])
            self.oa = self.ps(es, "oa", [128, 4, 128], F32)
            self.mib = Rot([self.ps(es, "mib", [128, 1024], BF16)])
            self.mif = Rot([self.ps(es, "mif", [128, 512], F32) for _ in range(2)])
            for l in self.layers:
                with contextlib.ExitStack() as esA:
                    self.passA(esA, l)
                if self.stage <= 50:
                    self.fw.barrier()
                    self.layers = []
                    break
                self.fw.barrier()
                with contextlib.ExitStack() as esB:
                    self.passB(esB, l)
                self.fw.barrier()
                fw = self.fw
                groups = [[0, 1, 2, 3], [4, 5, 6, 7]]
                fw.deps("pool", [self.Bpart[l]], [self.Brs[l]])
                ins = nc.gpsimd.collective_compute("ReduceScatter", ALU.add, replica_groups=groups,
                                                   ins=[part[l]], outs=[rs[l]])
                fw.done(ins, "pool", 1, [self.Bpart[l]], [self.Brs[l]])
                if l < self.layers[-1]:
                    nchk = 8
                    rows = S // nchk
                    for c in range(nchk):
                        fw.deps("pool", [self.Bpart[l]], [self.Bred[l]])
                        ins = nc.gpsimd.collective_compute("AllReduce", ALU.add, replica_groups=groups,
                                                           ins=[part[l][c * rows:(c + 1) * rows, :]],
                                                           outs=[red[l][c * rows:(c + 1) * rows, :]])
                        fw.done(ins, "pool", 1, [self.Bpart[l]], [self.Bred[l]])
                self.fw.barrier()
            with contextlib.ExitStack() as esF:
                self.final(esF, out)
            self.fw.barrier()
        return nc

    def consts(self, es):
        k = self
        self.ident_f = k.sb(es, "identf", [128, 128], F32)
        self.ident_b = k.sb(es, "identb", [128, 128], BF16)
        self.triU = k.sb(es, "triU", [128, 128], F32)
        self.triBD = k.sb(es, "triBD", [128, 128], F32)
        self.sel127 = k.sb(es, "sel127", [128, 128], F32)
        self.ones_f = k.sb(es, "onesf", [128, 128], F32)
        self.mask2 = k.sb(es, "mask2", [128, 256], F32)
        self.masksl = k.sb(es, "masksl", [128, 128], F32)
        idf, idb = self.ident_f, self.ident_b
        k.pool(lambda e: e.memset(idf.a[:], 1.0), [], [idf])
        k.pool(lambda e: e.affine_select(out=idf.a[:], in_=idf.a[:], pattern=[[-1, 128]], compare_op=ALU.is_equal,
                                         fill=0.0, base=0, channel_multiplier=1), [idf], [idf])
        k.pool(lambda e: e.tensor_copy(out=idb.a[:], in_=idf.a[:]), [idf], [idb])
        tu = self.triU
        k.pool(lambda e: e.memset(tu.a[:], 1.0), [], [tu])
        k.pool(lambda e: e.affine_select(out=tu.a[:], in_=tu.a[:], pattern=[[1, 128]], compare_op=ALU.is_ge,
                                         fill=0.0, base=0, channel_multiplier=-1), [tu], [tu])
        tb = self.triBD
        k.pool(lambda e: e.tensor_copy(out=tb.a[:], in_=tu.a[:]), [tu], [tb])
        k.pool(lambda e: e.memset(tb.a[0:64, 64:128], 0.0), [tb], [tb])
        s1 = self.sel127
        k.pool(lambda e: e.memset(s1.a[:], 1.0), [], [s1])
        k.pool(lambda e: e.affine_select(out=s1.a[:], in_=s1.a[:], pattern=[[0, 128]], compare_op=ALU.is_ge,
                                         fill=0.0, base=-127, channel_multiplier=1), [s1], [s1])
        k.pool(lambda e: e.memset(self.ones_f.a[:], 1.0), [], [self.ones_f])
        m2 = self.mask2
        k.pool(lambda e: e.memset(m2.a[:, 0:128], 1.0), [], [m2])
        k.pool(lambda e: e.affine_select(out=m2.a[:, 0:128], in_=m2.a[:, 0:128], pattern=[[1, 128]], compare_op=ALU.is_gt,
                                         fill=0.0, base=0, channel_multiplier=-1), [m2], [m2])
        k.pool(lambda e: e.memset(m2.a[0:64, 64:128], 0.0), [m2], [m2])
        k.pool(lambda e: e.tensor_copy(out=m2.a[:, 128:256], in_=tb.a[:]), [tb, m2], [m2])
        ml = self.masksl
        k.pool(lambda e: e.memset(ml.a[:], 1.0), [], [ml])
        k.pool(lambda e: e.affine_select(out=ml.a[:], in_=ml.a[:], pattern=[[-1, 128]], compare_op=ALU.is_gt,
                                         fill=0.0, base=0, channel_multiplier=1), [ml], [ml])
        k.pool(lambda e: e.memset(ml.a[64:128, 0:64], 0.0), [ml], [ml])

    def load_cast(self, dst, dcol0, src_ap, ncols, gT, stg_rot, nk=8):
        k = self
        c0 = 0
        while c0 < ncols:
            cw = min(256, ncols - c0)
            stg = stg_rot.next()
            for kk in range(nk):
                k.fw.dma("sp", stg.a[:, kk, 0:cw], src_ap[kk * 128:(kk + 1) * 128, c0:c0 + cw], [], [stg], stg)
            for kk in range(nk):
                o = dst.a[:, kk, dcol0 + c0:dcol0 + c0 + cw]
                i = stg.a[:, kk, 0:cw]
                if gT is None:
                    if kk % 2:
                        k.pool(lambda e: e.tensor_copy(out=o, in_=i), [stg], [dst])
                    else:
                        k.dve(lambda e: e.tensor_copy(out=o, in_=i), [stg], [dst])
                else:
                    g = gT.a[:, kk:kk + 1]
                    if kk % 2:
                        k.pool(lambda e: e.tensor_scalar(out=o, in0=i, scalar1=g, scalar2=None, op0=ALU.mult), [stg, gT], [dst])
                    else:
                        k.dve(lambda e: e.tensor_scalar(out=o, in0=i, scalar1=g, scalar2=None, op0=ALU.mult), [stg, gT], [dst])
            c0 += cw

    def small_load(self, es, name, shape, src_ap):
        b = self.sb(es, name, shape, F32)
        self.fw.dma("sp", b.a[:], src_ap, [], [b], b)
        return b

    def load_h(self, l, ti, first, want_hb=False):
        k = self
        fw = k.fw
        xt = k.xrot.next()
        rows = slice(ti * 128, (ti + 1) * 128)
        fw.dma("sp", xt.a[:], k.dr["x"][rows, :], [], [xt], xt)
        for ll in [q for q in self.layers if q < l]:
            for half in range(2):
                rt = k.rrot.next()
                hc = slice(half * 512, (half + 1) * 512)
                fw.dma("sp", rt.a[:, 0:512], k.red[ll][rows, hc], [k.Bred[ll]], [rt], rt)
                k.dve(lambda e: e.tensor_add(out=xt.a[:, hc], in0=xt.a[:, hc], in1=rt.a[:, 0:512]), [xt, rt], [xt])
        if k.stage <= 1.1:
            return None
        hb = k.hbrot.next()
        junk, st = hb, k.xstat.next()
        k.act(lambda e: e.activation(out=junk.a[:], in_=xt.a[:], func=AF.Square, scale=float(D ** -0.5),
                                     accum_out=st.a[:, 0:1]), [xt], [junk, st])
        if k.stage <= 1.2:
            return None
        k.dve(lambda e: e.tensor_scalar_add(out=st.a[:, 1:2], in0=st.a[:, 0:1], scalar1=EPS), [st], [st])
        k.act(lambda e: e.sqrt(out=st.a[:, 1:2], in_=st.a[:, 1:2]), [st], [st])
        k.dve(lambda e: e.reciprocal(out=st.a[:, 2:3], in_=st.a[:, 1:2]), [st], [st])
        if k.stage <= 1.3:
            return None
        k.act(lambda e: e.activation(out=hb.a[:], in_=xt.a[:], func=AF.Copy, scale=st.a[:, 2:3]), [xt, st], [hb])
        if k.stage <= 1.4:
            return None
        hT = k.hTrot.next()
        for half in range(2):
            slot = k.mib.next()
            for q in range(4):
                kk = half * 4 + q
                k.pe(lambda e: e.transpose(out=slot.a[:, q * 128:(q + 1) * 128], in_=hb.a[:, kk * 128:(kk + 1) * 128],
                                           identity=k.ident_b.a[:]), [hb, k.ident_b], [slot])
            if k.stage <= 1.45:
                continue
            for q in range(4):
                k.evac(hT.a[:, half * 4 + q, 1:129], slot.a[:, q * 128:(q + 1) * 128], [slot], [hT])
        if k.stage <= 1.5:
            return None
        if first:
            k.dve(lambda e: e.memset(hT.a[:, :, 0:1], 0.0), [], [hT])
        else:
            prev = k.hT_prev
            k.dve(lambda e: e.tensor_copy(out=hT.a[:, :, 0:1], in_=prev.a[:, :, 128:129]), [prev], [hT])
        k.hT_prev = hT
        return hT

    def rstd_cols(self, st, n, kk_cols=()):
        k = self
        k.act(lambda e: e.sqrt(out=st.a[:, 0:n], in_=st.a[:, 0:n]), [st], [st])
        for c in kk_cols:
            k.dve(lambda e: e.tensor_scalar_max(out=st.a[:, c:c + 1], in0=st.a[:, c:c + 1], scalar1=1e-12), [st], [st])
        k.dve(lambda e: e.reciprocal(out=st.a[:, 0:n], in_=st.a[:, 0:n]), [st], [st])

    def passA(self, es, l):
        k = self
        fw = k.fw
        S, NT, NCH = k.S, k.NT, k.NCH
        dr = k.dr
        L1 = l > 0
        NS = 544 if L1 else 512
        vec = k.sb(es, "vec", [128, V_MU], F32)
        fw.dma("sp", vec.a[:], dr["vec%d" % l][:, 0:V_MU].partition_broadcast(128), [], [vec], vec)
        gT = k.small_load(es, "gT", [128, 8], dr["gT%d" % l])
        qagT = k.small_load(es, "qagT", [128, 2], dr["qagT%d" % l])
        kvagT = k.small_load(es, "kvagT", [128, 1], dr["kvagT%d" % l])
        wup = k.small_load(es, "wup", [65, 128], dr["wup%d" % l])
        aup = k.small_load(es, "aup", [65, 128], dr["aup%d" % l])
        wA = k.sb(es, "wA", [128, 8, NA], BF16)
        wS1 = k.sb(es, "wS1", [128, 8, NS], BF16)
        wS2 = k.sb(es, "wS2", [128, 8, NS], BF16)
        wuq = k.sb(es, "wuq", [128, 2, 192], BF16)
        wukv = k.sb(es, "wukv", [128, 1, 256], BF16)
        with contextlib.ExitStack() as est:
            stg_rot = Rot([k.sb(est, "stg", [128, 8, 256], F32) for _ in range(2)])
            k.load_cast(wA, 0, dr["wA%d" % l], NA, gT, stg_rot)
            k.load_cast(wuq, 0, dr["wuq%d" % l], 192, qagT, stg_rot, nk=2)
            k.load_cast(wukv, 0, dr["wukv%d" % l], 256, kvagT, stg_rot, nk=1)
            muv = k.sb(est, "muv", [128, 512], F32)
            fw.dma("sp", muv.a[:], dr["vec%d" % l][:, V_MU:V_MU + 512].partition_broadcast(128), [], [muv], muv)
            tmp = k.sb(est, "wtmp", [128, 256], F32)
            tmp2 = k.sb(est, "wtmp2", [128, 256], F32)
            for c0 in (0, 256):
                stg = stg_rot.next()
                for kk in range(8):
                    fw.dma("sp", stg.a[:, kk, :], dr["wS%d" % l][kk * 128:(kk + 1) * 128, c0:c0 + 256], [], [stg], stg)
                mu = muv.a[:, c0:c0 + 256]
                for kk in range(8):
                    g = gT.a[:, kk:kk + 1]
                    k.dve(lambda e: e.tensor_mul(out=tmp.a[:], in0=stg.a[:, kk, :], in1=mu), [stg, muv], [tmp])
                    k.dve(lambda e: e.tensor_scalar(out=wS2.a[:, kk, c0:c0 + 256], in0=tmp.a[:], scalar1=g, scalar2=None,
                                                    op0=ALU.mult), [tmp, gT], [wS2])
                    k.dve(lambda e: e.tensor_sub(out=tmp2.a[:], in0=stg.a[:, kk, :], in1=tmp.a[:]), [stg, tmp], [tmp2])
                    k.dve(lambda e: e.tensor_scalar(out=wS1.a[:, kk, c0:c0 + 256], in0=tmp2.a[:], scalar1=g, scalar2=None,
                                                    op0=ALU.mult), [tmp2, gT], [wS1])
            if L1:
                muVT = k.small_load(est, "muVT", [128, 4], dr["muVT"])
                vdn = k.sb(est, "vdn", [128, 4, 32], F32)
                for kc in range(4):
                    fw.dma("sp", vdn.a[:, kc, :], dr["vdown"][kc * 128:(kc + 1) * 128, :], [], [vdn], vdn)
                wv = k.sb(est, "wv", [128, 4, 128], F32)
                wv1 = k.sb(est, "wv1", [128, 4, 128], F32)
                wv2 = k.sb(est, "wv2", [128, 4, 128], F32)
                for dc in range(8):
                    for kc in range(4):
                        fw.dma("sp", wv.a[:, kc, :], dr["wvT"][kc * 128:(kc + 1) * 128, dc * 128:(dc + 1) * 128], [], [wv], wv)
                    for kc in range(4):
                        k.dve(lambda e: e.tensor_scalar(out=wv2.a[:, kc, :], in0=wv.a[:, kc, :], scalar1=muVT.a[:, kc:kc + 1],
                                                        scalar2=None, op0=ALU.mult), [wv, muVT], [wv2])
                    k.dve(lambda e: e.tensor_sub(out=wv1.a[:], in0=wv.a[:], in1=wv2.a[:]), [wv, wv2], [wv1])
                    for (wsrc, wdst) in ((wv1, wS1), (wv2, wS2)):
                        slot = k.mif.next()
                        for kc in range(4):
                            k.pe(lambda e: e.matmul(slot.a[:, 0:32], lhsT=wsrc.a[:, kc, :], rhs=vdn.a[:, kc, :],
                                                    start=(kc == 0), stop=(kc == 3)), [wsrc, vdn], [slot])
                        k.dve(lambda e: e.tensor_scalar(out=wdst.a[:, dc, 512:544], in0=slot.a[:, 0:32], scalar1=gT.a[:, dc:dc + 1],
                                                        scalar2=None, op0=ALU.mult), [slot, gT], [wdst])
        fw.barrier()
        if k.stage <= 1:
            return
        if L1:
            vup = k.small_load(es, "vup", [33, 128], dr["vup"])
        kTm = [k.sb(es, "kTm%d" % h, [96, S], BF16) for h in range(2)]
        kTf = k.sb(es, "kTf", [128, S], BF16)
        Vm = k.sb(es, "Vm", [128, NT, 2, 65], BF16)
        Vf = k.sb(es, "Vf", [128, NT, 2, 65], BF16)
        ncum = k.sb(es, "ncum", [128, NT, 2], F32)
        k.xrot = Rot([k.sb(es, "xt", [128, D], F32) for _ in range(2)])
        k.xstat = Rot([k.sb(es, "xst", [128, 4], F32) for _ in range(2)])
        k.hbrot = Rot([k.sb(es, "hb", [128, D], BF16) for _ in range(1)])
        k.hTrot = Rot([k.sb(es, "hT", [128, 8, 129], BF16) for _ in range(2)])
        csr = Rot([k.sb(es, "cs", [128, 64], F32) for _ in range(2)])
        s1r = Rot([k.sb(es, "s1", [128, 418], F32) for _ in range(1)])
        s2r = Rot([k.sb(es, "s2", [128, 384], F32) for _ in range(1)])
        s4r = Rot([k.sb(es, "s4", [128, 512], F32) for _ in range(1)])
        s5r = Rot([k.sb(es, "s5", [128, 32], F32) for _ in range(2)])
        sga = k.sb(es, "sga", [128, 4, 128], BF16)
        sgb = k.sb(es, "sgb", [128, 4, 128], BF16)
        sgc = Rot([k.sb(es, "sgc", [128, 128], F32) for _ in range(1)])
        sq = k.sb(es, "sq", [128, 512], F32)
        k.rrot = Rot([sq])
        stA = Rot([k.sb(es, "stA", [128, 16], F32) for _ in range(2)])
        stB = Rot([k.sb(es, "stB", [128, 8], F32) for _ in range(2)])
        cqn = k.sb(es, "cqn", [128, 384], BF16)
        cT = k.sb(es, "cT", [128, 3, 128], BF16)
        qk = k.sb(es, "qk", [128, 448], F32)
        qn = k.sb(es, "qn", [128, 2, 96], F32)
        kn = k.sb(es, "kn", [128, 2, 96], F32)
        rp = k.sb(es, "rp", [128, 4, 32], F32)
        qb = k.sb(es, "qb", [128, 2, 96], BF16)
        kb = k.sb(es, "kb", [128, 2, 96], BF16)
        fqb = k.sb(es, "fqb", [128, 128], BF16)
        fkb = k.sb(es, "fkb", [128, 128], BF16)
        qTm = [k.sb(es, "qTm%d" % h, [96, 512], BF16) for h in range(2)]
        qTf = k.sb(es, "qTf", [128, 512], BF16)
        lf = Rot([k.sb(es, "lf", [128, 4], F32) for _ in range(2)])
        cumr = Rot([k.sb(es, "cum", [128, 2], F32) for _ in range(2)])
        dcum = k.sb(es, "dcum", [128, 2, 128], F32)
        cumbc = k.sb(es, "cumbc", [128, 2, 512], F32)
        pTr = Rot([k.sb(es, "pT", [128, 512], BF16) for _ in range(2)])
        ftmp = Rot([k.sb(es, "ftmp", [128, 512], F32) for _ in range(1)])
        rinv = k.sb(es, "rinv", [128, 4], F32)
        ogm = k.sb(es, "ogm", [128, 4, 128], BF16)
        ogf = k.sb(es, "ogf", [128, 4, 128], BF16)
        oTr = Rot([k.sb(es, "oT", [128, 512], BF16) for _ in range(1)])
        ocb = k.sb(es, "ocb", [128, 128], BF16)
        oTc = Rot([k.sb(es, "oTc", [128, 128], BF16) for _ in range(2)])
        twT = k.sb(es, "twT", [65, 128], F32)
        alT = k.sb(es, "alT", [65, 128], F32)
        k.pool(lambda e: e.memset(twT.a[64:65, :], 1.0), [], [twT])
        k.pool(lambda e: e.memset(alT.a[64:65, :], 1.0), [], [alT])
        tw = k.sb(es, "tw", [128, 64], F32)
        lw = k.sb(es, "lw", [128, 128], F32)
        av = k.sb(es, "av", [128, 128], F32)
        kkn = k.sb(es, "kkn", [128, 128], F32)
        kmod = k.sb(es, "kmod", [128, 128], F32)
        vv = k.sb(es, "vv", [128, 128], F32)
        rt1 = k.sb(es, "rt1", [128, 128], F32)
        rt2 = k.sb(es, "rt2", [128, 128], F32)
        bon = k.sb(es, "bon", [128, 2], F32)
        E1 = k.sb(es, "E1", [128, 128], F32)
        E2 = k.sb(es, "E2", [128, 128], F32)
        E3 = k.sb(es, "E3", [128, 128], F32)
        qa_ = k.sb(es, "qalpha", [128, 128], F32)
        qk_ = k.sb(es, "qkt", [128, 128], F32)
        qb_ = k.sb(es, "qbeta", [128, 128], F32)
        qr_ = k.sb(es, "qr", [128, 128], F32)
        qa_hi = k.sb(es, "qahi", [128, 128], F32)
        qk_hi = k.sb(es, "qkhi", [128, 128], F32)
        k.pool(lambda e: e.memset(qa_hi.a[:], 0.0), [], [qa_hi])
        k.pool(lambda e: e.memset(qk_hi.a[:], 0.0), [], [qk_hi])
        qT4 = [k.sb(es, "qT4_%d" % h, [64, 4, 128], F32) for h in range(2)]
        rlo = [k.sb(es, "rlo%d" % h, [64, 128], F32) for h in range(2)]
        rhi = [k.sb(es, "rhi%d" % h, [64, 128], F32) for h in range(2)]
        for h in range(2):
            k.pool(lambda e: e.memset(rlo[h].a[:], 0.0), [], [rlo[h]])
            k.pool(lambda e: e.memset(rhi[h].a[:], 0.0), [], [rhi[h]])
        GA = [k.sb(es, "GA%d" % h, [128, 256], F32) for h in range(2)]
        GK = [k.sb(es, "GK%d" % h, [128, 256], F32) for h in range(2)]
        XT = [Rot([k.sb(es, "XT%d_%d" % (h, i), [128, 128], F32) for i in range(2)]) for h in range(2)]
        XX = [Rot([k.sb(es, "XX%d_%d" % (h, i), [128, 128], F32) for i in range(2)]) for h in range(2)]
        PP = [Rot([k.sb(es, "PP%d_%d" % (h, i), [128, 128], F32) for i in range(2)]) for h in range(2)]
        Wb = [k.sb(es, "Wb%d" % h, [128, 64], F32) for h in range(2)]
        Ub = [k.sb(es, "Ub%d" % h, [128, 64], F32) for h in range(2)]
        Mst = [Rot([k.sb(es, "M%d_%d" % (h, i), [64, 64], F32) for i in range(3)]) for h in range(2)]
        pC = [k.sb(es, "pC%d" % h, [64, 2], F32) for h in range(2)]
        mtmp = [k.sb(es, "mtmp%d" % h, [64, 64], F32) for h in range(2)]
        yv = k.sb(es, "yv", [128, 2, 64], F32)
        ysq = k.sb(es, "ysq", [128, 2, 64], F32)
        yst = k.sb(es, "yst", [128, 8], F32)
        vdT = k.sb(es, "vdT", [33, 128], F32)
        k.pool(lambda e: e.memset(vdT.a[32:33, :], 1.0), [], [vdT])
        vfl = k.sb(es, "vfl", [128, 128], F32)
        nu = rt1
        Mcur = []
        for h in range(2):
            m0 = Mst[h].next()
            k.pool(lambda e: e.memset(m0.a[:], 0.0), [], [m0])
            Mcur.append(m0)
        cum_prev = None
        qscale_m = float(96 ** -0.5)
        qscale_f = float(64 ** -0.5)

        for ch in range(NCH):
            for jq in range(4):
                ti = ch * 4 + jq
                rows = slice(ti * 128, (ti + 1) * 128)
                cols = slice(ti * 128, (ti + 1) * 128)
                ccols = slice(jq * 128, (jq + 1) * 128)
                hT = k.load_h(l, ti, ti == 0)
                if k.stage <= 2:
                    continue
                cs = csr.next()
                fw.dma("sp", cs.a[:], dr["cs"][rows, :], [], [cs], cs)
                s1, s2, s4, s5 = s1r.next(), s2r.next(), s4r.next(), s5r.next()
                sg_c = sgc.next()
                p = k.pj.next()
                for kk in range(8):
                    k.pe(lambda e: e.matmul(p.a[:, 0:418], lhsT=hT.a[:, kk, 1:129], rhs=wA.a[:, kk, 0:418],
                                            start=(kk == 0), stop=(kk == 7)), [hT, wA], [p])
                k.evac(s1.a[:], p.a[:, 0:418], [p], [s1])
                p = k.pj.next()
                for kk in range(8):
                    k.pe(lambda e: e.matmul(p.a[:, 0:512], lhsT=hT.a[:, kk, 1:129], rhs=wA.a[:, kk, 418:930],
                                            start=(kk == 0), stop=(kk == 7)), [hT, wA], [p])
                k.dve(lambda e: e.tensor_copy(out=s2.a[:], in_=p.a[:, 0:384]), [p], [s2])
                k.act(lambda e: e.activation(out=sgb.a[:, jq, :], in_=p.a[:, 384:512], func=AF.Silu), [p], [sgb])
                p = k.pj.next()
                for kk in range(8):
                    k.pe(lambda e: e.matmul(p.a[:, 0:256], lhsT=hT.a[:, kk, 1:129], rhs=wA.a[:, kk, 930:1186],
                                            start=(kk == 0), stop=(kk == 7)), [hT, wA], [p])
                k.act(lambda e: e.activation(out=sga.a[:, jq, :], in_=p.a[:, 0:128], func=AF.Silu), [p], [sga])
                k.act(lambda e: e.activation(out=sg_c.a[:], in_=p.a[:, 128:256], func=AF.Silu), [p], [sg_c])
                p = k.pj.next()
                for kk in range(8):
                    k.pe(lambda e: e.matmul(p.a[:, 0:512], lhsT=hT.a[:, kk, 1:129], rhs=wS1.a[:, kk, 0:512],
                                            start=(kk == 0), stop=False), [hT, wS1], [p])
                for kk in range(8):
                    k.pe(lambda e: e.matmul(p.a[:, 0:512], lhsT=hT.a[:, kk, 0:128], rhs=wS2.a[:, kk, 0:512],
                                            start=False, stop=(kk == 7)), [hT, wS2], [p])
                k.evac(s4.a[:], p.a[:, 0:512], [p], [s4])
                if L1:
                    p = k.mif.next()
                    for kk in range(8):
                        k.pe(lambda e: e.matmul(p.a[:, 0:32], lhsT=hT.a[:, kk, 1:129], rhs=wS1.a[:, kk, 512:544],
                                                start=(kk == 0), stop=False), [hT, wS1], [p])
                    for kk in range(8):
                        k.pe(lambda e: e.matmul(p.a[:, 0:32], lhsT=hT.a[:, kk, 0:128], rhs=wS2.a[:, kk, 512:544],
                                                start=False, stop=(kk == 7)), [hT, wS2], [p])
                    k.evac(s5.a[:], p.a[:, 0:32], [p], [s5])
                if ti == k.dbg_tile:
                    k.dump(0, s1, s1.a[:], 418)
                    k.dump(1, s2, s2.a[:], 384)
                    k.dump(2, s4, s4.a[:], 512)
                    k.dump(15, sga, sga.a[:, jq, :], 128)
                if k.stage <= 3:
                    continue
                st = stA.next()
                k.dve(lambda e: e.tensor_mul(out=sq.a[:, 0:416], in0=s1.a[:, 0:416], in1=s1.a[:, 0:416]), [s1], [sq])
                k.dve(lambda e: e.tensor_reduce(out=st.a[:, 0:1], in_=sq.a[:, 0:256], axis=AX.X, op=ALU.add), [sq], [st])
                k.dve(lambda e: e.tensor_reduce(out=st.a[:, 1:2], in_=sq.a[:, 256:384], axis=AX.X, op=ALU.add), [sq], [st])
                k.dve(lambda e: e.tensor_reduce(out=st.a[:, 2:3], in_=sq.a[:, 384:416], axis=AX.X, op=ALU.add), [sq], [st])
                k.dve(lambda e: e.tensor_mul(out=sq.a[:, 0:256], in0=s2.a[:, 0:256], in1=s2.a[:, 0:256]), [s2, st], [sq])
                k.dve(lambda e: e.tensor_reduce(out=st.a[:, 3:7], in_=sq.a[:, 0:256].rearrange("p (h d) -> p h d", h=4),
                                                axis=AX.X, op=ALU.add), [sq], [st])
                k.dve(lambda e: e.tensor_mul(out=kkn.a[:], in0=s4.a[:, 128:256], in1=vec.a[:, V_KK:V_KK + 128]), [s4, vec], [kkn])
                k.dve(lambda e: e.tensor_mul(out=sq.a[:, 256:384], in0=kkn.a[:], in1=kkn.a[:]), [kkn], [sq])
                k.dve(lambda e: e.tensor_reduce(out=st.a[:, 7:9], in_=sq.a[:, 256:384].rearrange("p (h d) -> p h d", h=2),
                                                axis=AX.X, op=ALU.add), [sq], [st])
                k.dve(lambda e: e.tensor_scalar(out=st.a[:, 0:1], in0=st.a[:, 0:1], scalar1=1.0 / 256, scalar2=EPS, op0=ALU.mult, op1=ALU.add), [st], [st])
                k.dve(lambda e: e.tensor_scalar(out=st.a[:, 1:2], in0=st.a[:, 1:2], scalar1=1.0 / 128, scalar2=EPS, op0=ALU.mult, op1=ALU.add), [st], [st])
                k.dve(lambda e: e.tensor_scalar(out=st.a[:, 2:3], in0=st.a[:, 2:3], scalar1=1.0 / 32, scalar2=EPS, op0=ALU.mult, op1=ALU.add), [st], [st])
                k.dve(lambda e: e.tensor_scalar(out=st.a[:, 3:7], in0=st.a[:, 3:7], scalar1=1.0 / 64, scalar2=EPS, op0=ALU.mult, op1=ALU.add), [st], [st])
                k.rstd_cols(st, 9, kk_cols=(7, 8))
                k.act(lambda e: e.activation(out=cqn.a[:, 0:256], in_=s1.a[:, 0:256], func=AF.Copy, scale=st.a[:, 0:1]), [s1, st], [cqn])
                k.act(lambda e: e.activation(out=cqn.a[:, 256:384], in_=s1.a[:, 256:384], func=AF.Copy, scale=st.a[:, 1:2]), [s1, st], [cqn])
                for c in range(3):
                    k.transpose(cT.a[:, c, :], cqn.a[:, c * 128:(c + 1) * 128], 128, 128, BF16, [cqn], [cT])
                p = k.pj.next()
                for c in range(2):
                    k.pe(lambda e: e.matmul(p.a[:, 0:192], lhsT=cT.a[:, c, :], rhs=wuq.a[:, c, :], start=(c == 0), stop=(c == 1)),
                         [cT, wuq], [p])
                k.pe(lambda e: e.matmul(p.a[:, 192:448], lhsT=cT.a[:, 2, :], rhs=wukv.a[:, 0, :], start=True, stop=True),
                     [cT, wukv], [p])
                k.evac(qk.a[:], p.a[:, 0:448], [p], [qk])
                sb_ = stB.next()
                k.dve(lambda e: e.tensor_mul(out=sq.a[:, 0:320], in0=qk.a[:, 0:320], in1=qk.a[:, 0:320]), [qk], [sq])
                k.dve(lambda e: e.tensor_reduce(out=sb_.a[:, 0:2], in_=sq.a[:, 0:192].rearrange("p (h d) -> p h d", h=2),
                                                axis=AX.X, op=ALU.add), [sq], [sb_])
                k.dve(lambda e: e.tensor_reduce(out=sb_.a[:, 2:4], in_=sq.a[:, 192:320].rearrange("p (h d) -> p h d", h=2),
                                                axis=AX.X, op=ALU.add), [sq], [sb_])
                k.dve(lambda e: e.tensor_scalar(out=sb_.a[:, 0:2], in0=sb_.a[:, 0:2], scalar1=1.0 / 96, scalar2=EPS, op0=ALU.mult, op1=ALU.add), [sb_], [sb_])
                k.dve(lambda e: e.tensor_scalar(out=sb_.a[:, 2:4], in0=sb_.a[:, 2:4], scalar1=1.0 / 64, scalar2=EPS, op0=ALU.mult, op1=ALU.add), [sb_], [sb_])
                k.rstd_cols(sb_, 4)
                for h in range(2):
                    k.dve(lambda e: e.scalar_tensor_tensor(out=qn.a[:, h, :], in0=qk.a[:, h * 96:(h + 1) * 96], scalar=sb_.a[:, h:h + 1],
                                                           in1=vec.a[:, V_QG2 + h * 96:V_QG2 + (h + 1) * 96], op0=ALU.mult, op1=ALU.mult),
                          [qk, sb_, vec], [qn])
                    k.dve(lambda e: e.scalar_tensor_tensor(out=kn.a[:, h, 0:64], in0=qk.a[:, 192 + h * 64:192 + (h + 1) * 64],
                                                           scalar=sb_.a[:, 2 + h:3 + h],
                                                           in1=vec.a[:, V_KNG2 + h * 64:V_KNG2 + (h + 1) * 64], op0=ALU.mult, op1=ALU.mult),
                          [qk, sb_, vec], [kn])
                    k.dve(lambda e: e.scalar_tensor_tensor(out=kn.a[:, h, 64:96], in0=s1.a[:, 384:416], scalar=st.a[:, 2:3],
                                                           in1=vec.a[:, V_KRG:V_KRG + 32], op0=ALU.mult, op1=ALU.mult),
                          [s1, st, vec], [kn])
                cosv = cs.a[:, 0:32].rearrange("p (h d) -> p h d", h=2)
                sinv = cs.a[:, 32:64].rearrange("p (h d) -> p h d", h=2)
                for (src, dstb, scl) in ((qn, qb, qscale_m), (kn, kb, 1.0)):
                    x1 = src.a[:, :, 64:80]
                    x2 = src.a[:, :, 80:96]
                    r1 = rp.a[:, 0:1, :].rearrange("p a (h d) -> p (a h) d", h=2)
                    r2 = rp.a[:, 1:2, :].rearrange("p a (h d) -> p (a h) d", h=2)
                    r3 = rp.a[:, 2:3, :].rearrange("p a (h d) -> p (a h) d", h=2)
                    r4 = rp.a[:, 3:4, :].rearrange("p a (h d) -> p (a h) d", h=2)
                    k.dve(lambda e: e.tensor_mul(out=r1, in0=x1, in1=cosv), [src, cs], [rp])
                    k.dve(lambda e: e.tensor_mul(out=r2, in0=x2, in1=sinv), [src, cs, rp], [rp])
                    k.dve(lambda e: e.tensor_mul(out=r3, in0=x2, in1=cosv), [src, cs, rp], [rp])
                    k.dve(lambda e: e.tensor_mul(out=r4, in0=x1, in1=sinv), [src, cs, rp], [rp])
                    k.dve(lambda e: e.tensor_sub(out=src.a[:, :, 64:80], in0=r1, in1=r2), [rp, src], [src])
                    k.dve(lambda e: e.tensor_add(out=src.a[:, :, 80:96], in0=r3, in1=r4), [rp, src], [src])
                    k.act(lambda e: e.activation(out=dstb.a[:], in_=src.a[:], func=AF.Copy, scale=scl), [src], [dstb])
                for h in range(2):
                    k.transpose(qTm[h].a[:, ccols], qb.a[:, h, :], 128, 96, BF16, [qb], [qTm[h]])
                    k.transpose(kTm[h].a[:, cols], kb.a[:, h, :], 128, 96, BF16, [kb], [kTm[h]])
                k.dve(lambda e: e.tensor_copy(out=Vm.a[:, ti, :, 0:64], in_=qk.a[:, 320:448].rearrange("p (h d) -> p h d", h=2)), [qk], [Vm])
                k.dve(lambda e: e.tensor_copy(out=Vm.a[:, ti, :, 64:65], in_=k.ones_f.a[:, 0:2].rearrange("p (h o) -> p h o", o=1)), [k.ones_f], [Vm])
                if ti == k.dbg_tile:
                    k.dump(3, qk, qk.a[:], 448)
                    k.dump(4, qn, qn.a[:].rearrange("p h d -> p (h d)"), 192)
                    k.dump(5, kn, kn.a[:].rearrange("p h d -> p (h d)"), 192)
                    k.dump(16, st, st.a[:], 16)
                if k.stage <= 4:
                    continue
                for h in range(2):
                    k.dve(lambda e: e.scalar_tensor_tensor(out=sq.a[:, h * 64:(h + 1) * 64], in0=s2.a[:, h * 64:(h + 1) * 64],
                                                           scalar=st.a[:, 3 + h:4 + h], in1=vec.a[:, V_FQG2 + h * 64:V_FQG2 + (h + 1) * 64],
                                                           op0=ALU.mult, op1=ALU.mult), [s2, st, vec, sq], [sq])
                    k.dve(lambda e: e.scalar_tensor_tensor(out=sq.a[:, 128 + h * 64:128 + (h + 1) * 64], in0=s2.a[:, 128 + h * 64:128 + (h + 1) * 64],
                                                           scalar=st.a[:, 5 + h:6 + h], in1=vec.a[:, V_FKG2 + h * 64:V_FKG2 + (h + 1) * 64],
                                                           op0=ALU.mult, op1=ALU.mult), [s2, st, vec, sq], [sq])
                k.act(lambda e: e.activation(out=fqb.a[:], in_=sq.a[:, 0:128], func=AF.Copy, scale=qscale_f), [sq], [fqb])
                k.act(lambda e: e.activation(out=fkb.a[:], in_=sq.a[:, 128:256], func=AF.Copy), [sq], [fkb])
                k.transpose(qTf.a[:, ccols], fqb.a[:], 128, 128, BF16, [fqb], [qTf])
                k.transpose(kTf.a[:, cols], fkb.a[:], 128, 128, BF16, [fkb], [kTf])
                k.dve(lambda e: e.tensor_copy(out=Vf.a[:, ti, :, 0:64], in_=s2.a[:, 256:384].rearrange("p (h d) -> p h d", h=2)), [s2], [Vf])
                k.dve(lambda e: e.tensor_copy(out=Vf.a[:, ti, :, 64:65], in_=k.ones_f.a[:, 0:2].rearrange("p (h o) -> p h o", o=1)), [k.ones_f], [Vf])
                lf_ = lf.next()
                k.dve(lambda e: e.tensor_add(out=lf_.a[:, 0:2], in0=s1.a[:, 416:418], in1=vec.a[:, V_BF:V_BF + 2]), [s1, vec], [lf_])
                k.act(lambda e: e.activation(out=lf_.a[:, 0:2], in_=lf_.a[:, 0:2], func=AF.Exp, scale=-1.0), [lf_], [lf_])
                k.act(lambda e: e.activation(out=lf_.a[:, 0:2], in_=lf_.a[:, 0:2], func=AF.Ln, bias=1.0), [lf_], [lf_])
                k.dve(lambda e: e.tensor_scalar_mul(out=lf_.a[:, 2:4], in0=lf_.a[:, 0:2], scalar1=-1.0), [lf_], [lf_])
                cum = cumr.next()
                p = k.mif.next()
                k.pe(lambda e: e.matmul(p.a[:, 0:2], lhsT=k.triU.a[:], rhs=lf_.a[:, 2:4], start=True, stop=(cum_prev is None)),
                     [k.triU, lf_], [p])
                if cum_prev is not None:
                    cp = cum_prev
                    k.pe(lambda e: e.matmul(p.a[:, 0:2], lhsT=k.sel127.a[:], rhs=cp.a[:], start=False, stop=True),
                         [k.sel127, cp], [p])
                k.dve(lambda e: e.tensor_copy(out=cum.a[:], in_=p.a[:, 0:2]), [p], [cum])
                cum_prev = cum
                k.dve(lambda e: e.tensor_scalar_mul(out=ncum.a[:, ti, :], in0=cum.a[:], scalar1=-1.0), [cum], [ncum])
                for h in range(2):
                    k.dve(lambda e: e.tensor_scalar(out=dcum.a[:, h, :], in0=k.ident_f.a[:], scalar1=cum.a[:, h:h + 1], scalar2=None,
                                                    op0=ALU.mult), [k.ident_f, cum], [dcum])
                p = k.mif.next()
                for h in range(2):
                    k.pe(lambda e: e.matmul(p.a[:, h * 128:(h + 1) * 128], lhsT=k.ones_f.a[:], rhs=dcum.a[:, h, :], start=True, stop=True),
                         [k.ones_f, dcum], [p])
                k.evac(cumbc.a[:, :, ccols], p.a[:, 0:256].rearrange("p (h c) -> p h c", h=2), [p], [cumbc])

                if ti == k.dbg_tile:
                    k.dump(6, cum, cum.a[:], 2)
                    k.dump(7, cumbc, cumbc.a[:, 0, ccols], 128)
                    k.dump(17, sq, sq.a[:, 0:256], 256)
                if k.stage <= 5:
                    continue
                k.act(lambda e: e.activation(out=tw.a[:], in_=s4.a[:, 384:448], func=AF.Tanh), [s4], [tw])
                k.transpose(twT.a[0:64, :], tw.a[:], 128, 64, F32, [tw], [twT])
                k.transpose(alT.a[0:64, :], s4.a[:, 448:512], 128, 64, F32, [s4], [alT])
                p = k.mif.next()
                k.pe(lambda e: e.matmul(p.a[:, 0:128], lhsT=twT.a[:], rhs=wup.a[:], start=True, stop=True), [twT, wup], [p])
                k.pe(lambda e: e.matmul(p.a[:, 128:256], lhsT=alT.a[:], rhs=aup.a[:], start=True, stop=True), [alT, aup], [p])
                k.act(lambda e: e.activation(out=lw.a[:], in_=p.a[:, 0:128], func=AF.Sigmoid), [p], [lw])
                k.act(lambda e: e.activation(out=av.a[:], in_=p.a[:, 128:256], func=AF.Sigmoid), [p], [av])
                k.dve(lambda e: e.tensor_scalar_mul(out=lw.a[:], in0=lw.a[:], scalar1=-DECAY), [lw], [lw])
                for h in range(2):
                    k.dve(lambda e: e.tensor_scalar(out=kkn.a[:, h * 64:(h + 1) * 64], in0=kkn.a[:, h * 64:(h + 1) * 64],
                                                    scalar1=st.a[:, 7 + h:8 + h], scalar2=None, op0=ALU.mult), [kkn, st], [kkn])
                k.dve(lambda e: e.scalar_tensor_tensor(out=rt1.a[:], in0=av.a[:], scalar=-1.0, in1=vec.a[:, V_KA:V_KA + 128],
                                                       op0=ALU.add, op1=ALU.mult), [av, vec], [rt1])
                k.dve(lambda e: e.scalar_tensor_tensor(out=kmod.a[:], in0=rt1.a[:], scalar=1.0, in1=s4.a[:, 128:256],
                                                       op0=ALU.add, op1=ALU.mult), [rt1, s4], [kmod])
                if not L1:
                    k.dve(lambda e: e.tensor_copy(out=vv.a[:], in_=s4.a[:, 256:384]), [s4], [vv])
                    fw.dma("pool", k.vf_scr[rows, :], vv.a[:], [vv], [k.Bvf], vv)
                else:
                    fw.dma("sp", vfl.a[:], k.vf_scr[rows, :], [k.Bvf], [vfl], vfl)
                    k.transpose(vdT.a[0:32, :], s5.a[:], 128, 32, F32, [s5], [vdT])
                    p = k.mif.next()
                    k.pe(lambda e: e.matmul(p.a[:, 0:128], lhsT=vdT.a[:], rhs=vup.a[:], start=True, stop=True), [vdT, vup], [p])
                    k.act(lambda e: e.activation(out=nu.a[:], in_=p.a[:, 0:128], func=AF.Sigmoid), [p], [nu])
                    k.dve(lambda e: e.tensor_sub(out=rt2.a[:], in0=vfl.a[:], in1=s4.a[:, 256:384]), [vfl, s4], [rt2])
                    k.dve(lambda e: e.tensor_mul(out=rt2.a[:], in0=rt2.a[:], in1=nu.a[:]), [rt2, nu], [rt2])
                    k.dve(lambda e: e.tensor_add(out=vv.a[:], in0=rt2.a[:], in1=s4.a[:, 256:384]), [rt2, s4], [vv])
                k.dve(lambda e: e.tensor_mul(out=rt1.a[:], in0=s4.a[:, 0:128], in1=kmod.a[:]), [s4, kmod], [rt1])
                k.dve(lambda e: e.tensor_mul(out=rt1.a[:], in0=rt1.a[:], in1=vec.a[:, V_RK:V_RK + 128]), [rt1, vec], [rt1])
                k.dve(lambda e: e.tensor_reduce(out=bon.a[:], in_=rt1.a[:].rearrange("p (h d) -> p h d", h=2), axis=AX.X, op=ALU.add),
                      [rt1], [bon])
                if ti == k.dbg_tile:
                    k.dump(8, lw, lw.a[:], 128)
                    k.dump(9, av, av.a[:], 128)
                    k.dump(10, kkn, kkn.a[:], 128)
                    k.dump(11, kmod, kmod.a[:], 128)
                    k.dump(12, vv, vv.a[:], 128)
                if k.stage <= 6:
                    continue
                p = k.mif.next()
                k.pe(lambda e: e.matmul(p.a[:, 0:128], lhsT=k.triBD.a[:], rhs=lw.a[:], start=True, stop=True), [k.triBD, lw], [p])
                k.act(lambda e: e.activation(out=E1.a[:], in_=p.a[:, 0:128], func=AF.Exp), [p], [E1])
                k.act(lambda e: e.activation(out=E2.a[:], in_=p.a[:, 0:128], func=AF.Exp, scale=-1.0), [p], [E2])
                k.dve(lambda e: e.tensor_sub(out=E3.a[:], in0=p.a[:, 0:128], in1=lw.a[:]), [p, lw], [E3])
                k.act(lambda e: e.activation(out=E3.a[:], in_=E3.a[:], func=AF.Exp), [E3], [E3])
                for h in range(2):
                    pp = k.mif.next()
                    k.pe(lambda e: e.matmul(pp.a[0:64, 0:1], lhsT=lw.a[:, h * 64:(h + 1) * 64], rhs=k.triBD.a[:, 63:64], start=True, stop=True),
                         [lw, k.triBD], [pp])
                    k.pe(lambda e: e.matmul(pp.a[0:64, 1:2], lhsT=lw.a[:, h * 64:(h + 1) * 64], rhs=k.triBD.a[:, 127:128], start=True, stop=True),
                         [lw, k.triBD], [pp])
                    k.act(lambda e: e.activation(out=pC[h].a[:], in_=pp.a[0:64, 0:2], func=AF.Exp), [pp], [pC[h]])
                k.dve(lambda e: e.tensor_mul(out=qr_.a[:], in0=s4.a[:, 0:128], in1=E1.a[:]), [s4, E1], [qr_])
                k.dve(lambda e: e.tensor_mul(out=qk_.a[:], in0=kmod.a[:], in1=E2.a[:]), [kmod, E2], [qk_])
                k.dve(lambda e: e.tensor_mul(out=qa_.a[:], in0=kkn.a[:], in1=av.a[:]), [kkn, av], [qa_])
                k.dve(lambda e: e.tensor_mul(out=qa_.a[:], in0=qa_.a[:], in1=E2.a[:]), [qa_, E2], [qa_])
                k.dve(lambda e: e.scalar_tensor_tensor(out=qb_.a[:], in0=kkn.a[:], scalar=-1.0, in1=E3.a[:], op0=ALU.mult, op1=ALU.mult),
                      [kkn, E3], [qb_])
                k.dve(lambda e: e.tensor_copy(out=qa_hi.a[64:128, :], in_=qa_.a[64:128, :]), [qa_], [qa_hi])
                k.dve(lambda e: e.tensor_copy(out=qk_hi.a[64:128, :], in_=qk_.a[64:128, :]), [qk_], [qk_hi])
                for h in range(2):
                    hc = slice(h * 64, (h + 1) * 64)
                    q4 = qT4[h]
                    for pair in range(2):
                        slot = k.mif.next()
                        for qi in range(2):
                            src = (qa_, qk_, qb_, qr_)[pair * 2 + qi]
                            k.pe(lambda e: e.transpose(out=slot.a[0:64, qi * 128:(qi + 1) * 128], in_=src.a[:, hc], identity=k.ident_f.a[:]),
                                 [src, k.ident_f], [slot])
                        k.evac(q4.a[:, pair * 2:pair * 2 + 2, :], slot.a[0:64, 0:256].rearrange("p (q c) -> p q c", q=2), [slot], [q4])
                    k.dve(lambda e: e.tensor_copy(out=rlo[h].a[:, 0:64], in_=q4.a[:, 3, 0:64]), [q4], [rlo[h]])
                    k.dve(lambda e: e.tensor_copy(out=rhi[h].a[:, 64:128], in_=q4.a[:, 3, 64:128]), [q4], [rhi[h]])
                    aT, kT_, bT, rT = q4.a[:, 0, :], q4.a[:, 1, :], q4.a[:, 2, :], q4.a[:, 3, :]
                    brT = q4.a[:, 2:4, :].rearrange("p q c -> p (q c)")
                    slot = k.mif.next()
                    k.pe(lambda e: e.matmul(slot.a[:, 0:256], lhsT=aT, rhs=brT, start=True, stop=True), [q4], [slot])
                    k.dve(lambda e: e.tensor_mul(out=GA[h].a[:], in0=slot.a[:, 0:256], in1=k.mask2.a[:]), [slot, k.mask2], [GA[h]])
                    slot = k.mif.next()
                    k.pe(lambda e: e.matmul(slot.a[:, 0:256], lhsT=kT_, rhs=brT, start=True, stop=True), [q4], [slot])
                    k.dve(lambda e: e.tensor_mul(out=GK[h].a[:], in0=slot.a[:, 0:256], in1=k.mask2.a[:]), [slot, k.mask2], [GK[h]])
                    slot = k.mif.next()
                    k.pe(lambda e: e.matmul(slot.a[:, 0:128], lhsT=bT, rhs=aT, start=True, stop=True), [q4], [slot])
                    xt_ = XT[h].next()
                    k.dve(lambda e: e.tensor_mul(out=xt_.a[:], in0=slot.a[:, 0:128], in1=k.masksl.a[:]), [slot, k.masksl], [xt_])
                    if k.stage <= 7:
                        continue
                    Pc = PP[h].next()
                    k.dve(lambda e: e.tensor_add(out=Pc.a[:], in0=GA[h].a[:, 0:128], in1=k.ident_f.a[:]), [GA[h], k.ident_f], [Pc])
                    Xc_ap, Xc_b = GA[h].a[:, 0:128], GA[h]
                    XTc = xt_
                    for lev in range(5):
                        last = lev == 4
                        slot = k.mif.next()
                        k.pe(lambda e: e.matmul(slot.a[:, 0:128], lhsT=Xc_ap, rhs=XTc.a[:], start=True, stop=True), [Xc_b, XTc], [slot])
                        XTn = XT[h].next()
                        if not last:
                            slot2 = k.mif.next()
                            k.pe(lambda e: e.matmul(slot2.a[:, 0:128], lhsT=XTc.a[:], rhs=Xc_ap, start=True, stop=True), [Xc_b, XTc], [slot2])
                        k.evac(XTn.a[:], slot.a[:, 0:128], [slot], [XTn])
                        if not last:
                            Xn = XX[h].next()
                            k.evac(Xn.a[:], slot2.a[:, 0:128], [slot2], [Xn])
                        slot3 = k.mif.next()
                        k.pe(lambda e: e.matmul(slot3.a[:, 0:128], lhsT=XTn.a[:], rhs=Pc.a[:], start=True, stop=True), [XTn, Pc], [slot3])
                        Pn = PP[h].next()
                        k.dve(lambda e: e.tensor_add(out=Pn.a[:], in0=slot3.a[:, 0:128], in1=Pc.a[:]), [slot3, Pc], [Pn])
                        Pc = Pn
                        XTc = XTn
                        if not last:
                            Xc_ap, Xc_b = Xn.a[:], Xn
                    TT = Pc
                    if k.stage <= 8:
                        continue
                    M0 = Mcur[h]
                    W_, U_ = Wb[h], Ub[h]
                    slot = k.mif.next()
                    k.pe(lambda e: e.matmul(slot.a[0:64, 0:64], lhsT=q4.a[:, 2, 0:64], rhs=M0.a[:], start=True, stop=False), [q4, M0], [slot])
                    k.pe(lambda e: e.matmul(slot.a[0:64, 0:64], lhsT=GK[h].a[0:64, 0:64], rhs=vv.a[0:64, hc], start=False, stop=True),
                         [GK[h], vv], [slot])
                    k.evac(W_.a[0:64, :], slot.a[0:64, 0:64], [slot], [W_])
                    slot = k.mif.next()
                    k.pe(lambda e: e.matmul(slot.a[0:64, 0:64], lhsT=TT.a[0:64, 0:64], rhs=W_.a[0:64, :], start=True, stop=True), [TT, W_], [slot])
                    k.evac(U_.a[0:64, :], slot.a[0:64, 0:64], [slot], [U_])
                    slot = k.mif.next()
                    k.pe(lambda e: e.matmul(slot.a[0:64, 0:64], lhsT=qa_.a[0:64, hc], rhs=U_.a[0:64, :], start=True, stop=False), [qa_, U_], [slot])
                    k.pe(lambda e: e.matmul(slot.a[0:64, 0:64], lhsT=qk_.a[0:64, hc], rhs=vv.a[0:64, hc], start=False, stop=True), [qk_, vv], [slot])
                    M1 = Mst[h].next()
                    k.dve(lambda e: e.tensor_add(out=mtmp[h].a[:], in0=slot.a[0:64, 0:64], in1=M0.a[:]), [slot, M0], [mtmp[h]])
                    k.dve(lambda e: e.tensor_scalar(out=M1.a[:], in0=mtmp[h].a[:], scalar1=pC[h].a[:, 0:1], scalar2=None, op0=ALU.mult),
                          [mtmp[h], pC[h]], [M1])
                    slot = k.mif.next()
                    k.pe(lambda e: e.matmul(slot.a[:, 0:64], lhsT=q4.a[:, 2, :], rhs=M1.a[:], start=True, stop=False), [q4, M1], [slot])
                    k.pe(lambda e: e.matmul(slot.a[:, 0:64], lhsT=GK[h].a[:, 0:128], rhs=vv.a[:, hc], start=False, stop=True),
                         [GK[h], vv], [slot])
                    k.evac(W_.a[64:128, :], slot.a[64:128, 0:64], [slot], [W_])
                    slot = k.mif.next()
                    k.pe(lambda e: e.matmul(slot.a[:, 0:64], lhsT=TT.a[:, 0:128], rhs=W_.a[:], start=True, stop=True), [TT, W_], [slot])
                    k.evac(U_.a[64:128, :], slot.a[64:128, 0:64], [slot], [U_])
                    slot = k.mif.next()
                    k.pe(lambda e: e.matmul(slot.a[0:64, 0:64], lhsT=qa_hi.a[:, hc], rhs=U_.a[:], start=True, stop=False), [qa_hi, U_], [slot])
                    k.pe(lambda e: e.matmul(slot.a[0:64, 0:64], lhsT=qk_hi.a[:, hc], rhs=vv.a[:, hc], start=False, stop=True), [qk_hi, vv], [slot])
                    M2 = Mst[h].next()
                    k.dve(lambda e: e.tensor_add(out=mtmp[h].a[:], in0=slot.a[0:64, 0:64], in1=M1.a[:]), [slot, M1], [mtmp[h]])
                    k.dve(lambda e: e.tensor_scalar(out=M2.a[:], in0=mtmp[h].a[:], scalar1=pC[h].a[:, 1:2], scalar2=None, op0=ALU.mult),
                          [mtmp[h], pC[h]], [M2])
                    slot = k.mif.next()
                    k.pe(lambda e: e.matmul(slot.a[:, 0:64], lhsT=rlo[h].a[:], rhs=M0.a[:], start=True, stop=False), [rlo[h], M0], [slot])
                    k.pe(lambda e: e.matmul(slot.a[:, 0:64], lhsT=rhi[h].a[:], rhs=M1.a[:], start=False, stop=False), [rhi[h], M1], [slot])
                    k.pe(lambda e: e.matmul(slot.a[:, 0:64], lhsT=GA[h].a[:, 128:256], rhs=U_.a[:], start=False, stop=False), [GA[h], U_], [slot])
                    k.pe(lambda e: e.matmul(slot.a[:, 0:64], lhsT=GK[h].a[:, 128:256], rhs=vv.a[:, hc], start=False, stop=True), [GK[h], vv], [slot])
                    k.evac(yv.a[:, h, :], slot.a[:, 0:64], [slot], [yv])
                    Mcur[h] = M2
                if ti == k.dbg_tile:
                    k.dump(13, yv, yv.a[:].rearrange("p h d -> p (h d)"), 128)
                    k.dump(18, GA[0], GA[0].a[:], 256)
                    k.dump(19, GK[0], GK[0].a[:], 256)
                    k.dump(20, TT, TT.a[:], 128)
                if k.stage <= 9:
                    continue
                k.dve(lambda e: e.tensor_reduce(out=yst.a[:, 0:2], in_=yv.a[:], axis=AX.X, op=ALU.add), [yv], [yst])
                k.dve(lambda e: e.tensor_scalar_mul(out=yst.a[:, 0:2], in0=yst.a[:, 0:2], scalar1=1.0 / 64), [yst], [yst])
                for h in range(2):
                    k.dve(lambda e: e.tensor_scalar(out=yv.a[:, h, :], in0=yv.a[:, h, :], scalar1=yst.a[:, h:h + 1], scalar2=None, op0=ALU.subtract),
                          [yv, yst], [yv])
                k.dve(lambda e: e.tensor_mul(out=ysq.a[:], in0=yv.a[:], in1=yv.a[:]), [yv], [ysq])
                k.dve(lambda e: e.tensor_reduce(out=yst.a[:, 2:4], in_=ysq.a[:], axis=AX.X, op=ALU.add), [ysq], [yst])
                k.dve(lambda e: e.tensor_scalar(out=yst.a[:, 2:4], in0=yst.a[:, 2:4], scalar1=1.0 / 64, scalar2=GN_EPS, op0=ALU.mult, op1=ALU.add), [yst], [yst])
                k.act(lambda e: e.sqrt(out=yst.a[:, 2:4], in_=yst.a[:, 2:4]), [yst], [yst])
                k.dve(lambda e: e.reciprocal(out=yst.a[:, 2:4], in_=yst.a[:, 2:4]), [yst], [yst])
                for h in range(2):
                    hc = slice(h * 64, (h + 1) * 64)
                    k.dve(lambda e: e.scalar_tensor_tensor(out=yv.a[:, h, :], in0=yv.a[:, h, :], scalar=yst.a[:, 2 + h:3 + h],
                                                           in1=vec.a[:, V_LG + h * 64:V_LG + (h + 1) * 64], op0=ALU.mult, op1=ALU.mult),
                          [yv, yst, vec], [yv])
                    k.dve(lambda e: e.tensor_add(out=yv.a[:, h, :], in0=yv.a[:, h, :], in1=vec.a[:, V_LB + h * 64:V_LB + (h + 1) * 64]), [yv, vec], [yv])
                    k.dve(lambda e: e.scalar_tensor_tensor(out=yv.a[:, h, :], in0=vv.a[:, hc], scalar=bon.a[:, h:h + 1], in1=yv.a[:, h, :],
                                                           op0=ALU.mult, op1=ALU.add), [vv, bon, yv], [yv])
                if ti == k.dbg_tile:
                    k.dump(14, yv, yv.a[:].rearrange("p h d -> p (h d)"), 128)
                k.dve(lambda e: e.tensor_mul(out=ocb.a[:], in0=yv.a[:].rearrange("p h d -> p (h d)"), in1=sg_c.a[:]), [yv, sg_c], [ocb])
                oc = oTc.next()
                k.transpose(oc.a[:], ocb.a[:], 128, 128, BF16, [ocb], [oc])
                fw.dma("pool", k.oT_scr[2, :, cols], oc.a[:], [oc], [k.BoT], oc)

            for mixer in range(2 if k.stage > 10 else 0):
                og = ogm if mixer == 0 else ogf
                gate = sga if mixer == 0 else sgb
                Vr = Vm if mixer == 0 else Vf
                for h in range(2):
                    nkt = 4 * (ch + 1)
                    oa = k.oa
                    k.dve(lambda e: e.memset(oa.a[:], 0.0), [], [oa])
                    for kt in range(nkt):
                        dj = kt - 4 * ch
                        j0 = max(0, dj)
                        c0 = j0 * 128
                        kc = slice(kt * 128, (kt + 1) * 128)
                        s = k.sc.next()
                        if mixer == 0:
                            k.pe(lambda e: e.matmul(s.a[:, c0:512], lhsT=kTm[h].a[:, kc], rhs=qTm[h].a[:, c0:512], start=True, stop=True),
                                 [kTm[h], qTm[h]], [s])
                        else:
                            hp = slice(h * 64, (h + 1) * 64)
                            k.pe(lambda e: e.matmul(s.a[:, c0:512], lhsT=kTf.a[hp, kc], rhs=qTf.a[hp, c0:512], start=True, stop=True),
                                 [kTf, qTf], [s])
                        pT = pTr.next()
                        if mixer == 0:
                            k.act(lambda e: e.activation(out=pT.a[:, c0:512], in_=s.a[:, c0:512], func=AF.Exp), [s], [pT])
                        else:
                            ft = ftmp.next()
                            k.dve(lambda e: e.tensor_add(out=ft.a[:, c0:512], in0=s.a[:, c0:512], in1=cumbc.a[:, h, c0:512]), [s, cumbc], [ft])
                            k.act(lambda e: e.activation(out=pT.a[:, c0:512], in_=ft.a[:, c0:512], func=AF.Exp, bias=ncum.a[:, kt, h:h + 1]),
                                  [ft, ncum], [pT])
                        if dj >= 0:
                            k.pool(lambda e: e.affine_select(out=pT.a[:, c0:c0 + 128], in_=pT.a[:, c0:c0 + 128], pattern=[[1, 128]],
                                                             compare_op=ALU.is_ge, fill=0.0, base=0, channel_multiplier=-1), [pT], [pT])
                        for j in range(j0, 4):
                            k.pe(lambda e: e.matmul(oa.a[:, j, 0:65], lhsT=pT.a[:, j * 128:(j + 1) * 128], rhs=Vr.a[:, kt, h, :],
                                                    start=False, stop=(kt == 4 * ch + j), skip_group_check=True), [pT, Vr], [oa])
                    k.dve(lambda e: e.reciprocal(out=rinv.a[:], in_=oa.a[:, :, 64:65].rearrange("p j o -> p (j o)")), [oa], [rinv])
                    for j in range(4):
                        k.dve(lambda e: e.scalar_tensor_tensor(out=og.a[:, j, h * 64:(h + 1) * 64], in0=oa.a[:, j, 0:64], scalar=rinv.a[:, j:j + 1],
                                                               in1=gate.a[:, j, h * 64:(h + 1) * 64], op0=ALU.mult, op1=ALU.mult),
                              [oa, rinv, gate], [og])
                oT = oTr.next()
                for j in range(4):
                    k.transpose(oT.a[:, j * 128:(j + 1) * 128], og.a[:, j, :], 128, 128, BF16, [og], [oT])
                fw.dma("pool", k.oT_scr[mixer, :, ch * 512:(ch + 1) * 512], oT.a[:], [oT], [k.BoT], oT)

    def passB(self, es, l):
        k = self
        fw = k.fw
        dr = k.dr
        NT = k.NT
        gT = k.small_load(es, "gTb", [128, 8], dr["gT%d" % l])
        wG = k.sb(es, "wG", [128, 8, 3072], BF16)
        wp = k.sb(es, "wp", [128, 3, D], BF16)
        wo = k.sb(es, "wo", [128, 8, D], BF16)
        with contextlib.ExitStack() as est:
            stg_rot = Rot([k.sb(est, "stgb", [128, 8, 256], F32) for _ in range(2)])
            k.load_cast(wG, 0, dr["wG%d" % l], 3072, gT, stg_rot)
            k.load_cast(wo, 0, dr["wo%d" % l], D, None, stg_rot)
            for br in range(3):
                for c0 in range(0, D, 256):
                    stg = stg_rot.next()
                    fw.dma("sp", stg.a[:, 0, :], dr["wp%d" % l][br, :, c0:c0 + 256], [], [stg], stg)
                    k.dve(lambda e: e.tensor_copy(out=wp.a[:, br, c0:c0 + 256], in_=stg.a[:, 0, :]), [stg], [wp])
        fw.barrier()
        k.xrot = Rot([k.sb(es, "xtb", [128, D], F32) for _ in range(2)])
        k.rrot = Rot([k.sb(es, "rtb", [128, 512], F32) for _ in range(2)])
        k.xstat = Rot([k.sb(es, "xstb", [128, 4], F32) for _ in range(2)])
        k.hbrot = Rot([k.sb(es, "hbb", [128, D], BF16) for _ in range(2)])
        k.hTrot = Rot([k.sb(es, "hTb", [128, 8, 129], BF16) for _ in range(2)])
        oTl = Rot([k.sb(es, "oTl", [128, 3, 128], BF16) for _ in range(2)])
        gs = Rot([k.sb(es, "gs", [128, 512], F32) for _ in range(2)])
        mg = k.sb(es, "mg", [128, D], F32)
        mgt = Rot([k.sb(es, "mgt", [128, 512], F32) for _ in range(2)])
        mb = k.sb(es, "mb", [128, D], BF16)
        mT = k.sb(es, "mT", [128, 8, 128], BF16)
        po = Rot([k.sb(es, "po", [128, D], F32) for _ in range(2)])
        for ti in range(NT):
            rows = slice(ti * 128, (ti + 1) * 128)
            hT = k.load_h(l, ti, True)
            ot = oTl.next()
            for br in range(3):
                fw.dma("sp", ot.a[:, br, :], k.oT_scr[br, :, rows], [k.BoT], [ot], ot)
            for br in range(3):
                for half in range(2):
                    hc = slice(half * 512, (half + 1) * 512)
                    pg = k.pj.next()
                    for kk in range(8):
                        k.pe(lambda e: e.matmul(pg.a[:], lhsT=hT.a[:, kk, 1:129], rhs=wG.a[:, kk, br * D + half * 512:br * D + (half + 1) * 512],
                                                start=(kk == 0), stop=(kk == 7)), [hT, wG], [pg])
                    g = gs.next()
                    k.act(lambda e: e.activation(out=g.a[:], in_=pg.a[:], func=AF.Sigmoid), [pg], [g])
                    pb = k.sc.next()
                    k.pe(lambda e: e.matmul(pb.a[:], lhsT=ot.a[:, br, :], rhs=wp.a[:, br, hc], start=True, stop=True), [ot, wp], [pb])
                    if br == 0:
                        k.dve(lambda e: e.tensor_mul(out=mg.a[:, hc], in0=pb.a[:], in1=g.a[:]), [pb, g], [mg])
                    else:
                        t_ = mgt.next()
                        k.dve(lambda e: e.tensor_mul(out=t_.a[:], in0=pb.a[:], in1=g.a[:]), [pb, g], [t_])
                        k.pool(lambda e: e.tensor_add(out=mg.a[:, hc], in0=mg.a[:, hc], in1=t_.a[:]), [mg, t_], [mg])
            k.act(lambda e: e.activation(out=mb.a[:], in_=mg.a[:], func=AF.Copy), [mg], [mb])
            for half in range(2):
                slot = k.mib.next()
                for q in range(4):
                    kk = half * 4 + q
                    k.pe(lambda e: e.transpose(out=slot.a[:, q * 128:(q + 1) * 128], in_=mb.a[:, kk * 128:(kk + 1) * 128],
                                               identity=k.ident_b.a[:]), [mb, k.ident_b], [slot])
                k.evac(mT.a[:, half * 4:half * 4 + 4, :], slot.a[:, 0:512].rearrange("p (q c) -> p q c", q=4), [slot], [mT])
            pout = po.next()
            for half in range(2):
                hc = slice(half * 512, (half + 1) * 512)
                pp = k.pj.next()
                for kk in range(8):
                    k.pe(lambda e: e.matmul(pp.a[:], lhsT=mT.a[:, kk, :], rhs=wo.a[:, kk, hc], start=(kk == 0), stop=(kk == 7)), [mT, wo], [pp])
                k.evac(pout.a[:, hc], pp.a[:], [pp], [pout])
            if ti == k.dbg_tile and k.debug:
                for br in range(3):
                    fw.dma("sp", k.dbgb[br, :, 0:128], ot.a[:, br, :], [ot], [k.Bdbg], k.Bdbg)
                fw.dma("sp", k.dbgb[3, :, 0:512], wp.a[:, 0, 0:512], [wp], [k.Bdbg], k.Bdbg)
            if ti == k.dbg_tile:
                k.dump(22, pout, pout.a[:, 0:512], 512)
                k.dump(23, mg, mg.a[:, 0:512], 512)
            fw.dma("pool", k.part[l][rows, :], pout.a[:], [pout], [k.Bpart[l]], pout)

    def final(self, es, out):
        k = self
        fw = k.fw
        L = k.n_layers
        Sq = k.S // 4
        xr = Rot([k.sb(es, "xf", [128, D], F32) for _ in range(2)])
        rr = Rot([k.sb(es, "rf", [128, D], F32) for _ in range(2)])
        for i in range(Sq // 128):
            xt = xr.next()
            fw.dma("sp", xt.a[:], k.dr["xq"][i * 128:(i + 1) * 128, :], [], [xt], xt)
            for l in k.layers:
                rt = rr.next()
                fw.dma("sp", rt.a[:], k.rs[l][i * 128:(i + 1) * 128, :], [k.Brs[l]], [rt], rt)
                k.dve(lambda e: e.tensor_add(out=xt.a[:], in0=xt.a[:], in1=rt.a[:]), [xt, rt], [xt])
            fw.dma("pool", out[i * 128:(i + 1) * 128, :], xt.a[:], [xt], [k.Bout], xt)


IN_SIZES = (256, 128, 32, 512, 512, 512, 512, 8, 512, 1664, 512, 3072)
OFF = np.concatenate([[0], np.cumsum(IN_SIZES)]).tolist()
(O_CQ, O_CKV, O_KR, O_GA, O_FQ, O_FK, O_FV, O_FF, O_GB, O_SH, O_GC, O_MG) = OFF[:12]


def rope_cs(S):
    inv = (np.float32(10000.0) ** (-np.arange(0, 32, 2, dtype=np.float32) / np.float32(32))).astype(np.float32)
    ang = (np.arange(S, dtype=np.float32)[:, None] * inv[None, :]).astype(np.float32)
    c, s_ = np.cos(ang).astype(np.float32), np.sin(ang).astype(np.float32)
    return np.ascontiguousarray(np.concatenate([c, c, s_, s_], axis=1))


def core_inputs(inp, c, S, L, layers=None):
    f = lambda a: np.ascontiguousarray(a, dtype=np.float32)
    b, j = c // 4, c % 4
    hs = slice(128 * j, 128 * (j + 1))
    m = {"x": f(inp["x"][b, :S]), "xq": f(inp["x"][b, j * (S // 4):(j + 1) * (S // 4)]), "cs": rope_cs(S)}
    layers = list(range(L)) if layers is None else layers
    for l in layers:
        w = inp["w_in"][l]
        colsA = np.concatenate([
            np.arange(O_CQ, O_CQ + 416),
            np.arange(O_FF + 2 * j, O_FF + 2 * j + 2),
            np.arange(O_FQ + 128 * j, O_FQ + 128 * j + 128),
            np.arange(O_FK + 128 * j, O_FK + 128 * j + 128),
            np.arange(O_FV + 128 * j, O_FV + 128 * j + 128),
            np.arange(O_GB + 128 * j, O_GB + 128 * j + 128),
            np.arange(O_GA + 128 * j, O_GA + 128 * j + 128),
            np.arange(O_GC + 128 * j, O_GC + 128 * j + 128)])
        shl = np.concatenate([np.arange(128 * j, 128 * j + 128), np.arange(512 + 128 * j, 512 + 128 * j + 128),
                              np.arange(1024 + 128 * j, 1024 + 128 * j + 128), np.arange(1536, 1664)])
        m["wA%d" % l] = f(w[:, colsA])
        m["wS%d" % l] = f(w[:, O_SH + shl])
        m["wG%d" % l] = f(w[:, O_MG:O_MG + 3072])
        uq = inp["mla_w_uq"][l].reshape(256, 8, 96)[:, 2 * j:2 * j + 2].reshape(256, 192)
        m["wuq%d" % l] = f(uq)
        ukv = inp["mla_w_ukv"][l].reshape(128, 8, 128)[:, 2 * j:2 * j + 2]
        m["wukv%d" % l] = f(np.concatenate([ukv[:, 0, :64], ukv[:, 1, :64], ukv[:, 0, 64:], ukv[:, 1, 64:]], axis=1))
        m["wup%d" % l] = f(np.concatenate([inp["rwkv_w_up"][l][:, hs], inp["rwkv_w0"][l][None, hs]], axis=0))
        m["aup%d" % l] = f(np.concatenate([inp["rwkv_a_up"][l][:, hs], inp["rwkv_a0"][l][None, hs]], axis=0))
        qg = inp["mla_q_g"][l]
        vec = np.concatenate([
            qg, qg, inp["mla_knope_g"][l], inp["mla_knope_g"][l], inp["mla_krope_g"][l],
            inp["fox_q_g"][l], inp["fox_q_g"][l], inp["fox_k_g"][l], inp["fox_k_g"][l],
            inp["fox_b_f"][l][2 * j:2 * j + 2], inp["rwkv_k_k"][l][hs], inp["rwkv_k_a"][l][hs],
            inp["rwkv_r_k"][l].reshape(-1)[hs], inp["rwkv_lnx_g"][l][hs], inp["rwkv_lnx_b"][l][hs],
            inp["rwkv_mu"][l][shl]])
        assert vec.shape[0] == NV
        m["vec%d" % l] = f(vec[None, :])
        m["gT%d" % l] = f(inp["norm_g"][l].reshape(8, 128).T)
        m["qagT%d" % l] = f(inp["mla_qa_g"][l].reshape(2, 128).T)
        m["kvagT%d" % l] = f(inp["mla_kva_g"][l].reshape(1, 128).T)
        m["wp%d" % l] = f(np.stack([inp["w_pa"][l][hs], inp["w_pb"][l][hs], inp["w_pc"][l][hs]]))
        m["wo%d" % l] = f(inp["w_out"][l])
    if 1 in layers:
        w = inp["w_in"][1]
        m["wvT"] = f(w[:, O_SH + 1024:O_SH + 1536].T)
        m["muVT"] = f(inp["rwkv_mu"][1][1024:1536].reshape(4, 128).T)
        m["vdown"] = f(inp["rwkv_v_down"][0])
        m["vup"] = f(np.concatenate([inp["rwkv_v_up"][0][:, hs], inp["rwkv_v0"][0][None, hs]], axis=0))
    return m


_CACHE = {}


def run(inp, S, L):
    key = (S, L)
    if key not in _CACHE:
        _CACHE[key] = Kern(S, L).build()
    nc = _CACHE[key]
    in_maps = [core_inputs(inp, c, S, L) for c in range(8)]
    res = run_bass_kernel_spmd(nc, in_maps, core_ids=list(range(8)))
    global LAST
    LAST = res
    out = np.zeros((2, S, D), np.float32)
    q = S // 4
    for c in range(8):
        b, j = c // 4, c % 4
        out[b, j * q:(j + 1) * q] = res.results[c]["out"]
    return out


def run_split(inp, S):
    q = S // 4
    key = (S, "l0")
    if key not in _CACHE:
        _CACHE[key] = Kern(S, 1, layers=[0]).build()
    res0 = run_bass_kernel_spmd(_CACHE[key], [core_inputs(inp, c, S, 2, layers=[0]) for c in range(8)], core_ids=list(range(8)))
    x1 = np.zeros((2, S, D), np.float32)
    for c in range(8):
        x1[c // 4, (c % 4) * q:(c % 4 + 1) * q] = res0.results[c]["out"]
    key = (S, "l1")
    if key not in _CACHE:
        _CACHE[key] = Kern(S, 2, layers=[1]).build()
    inp1 = dict(inp)
    inp1["x"] = x1
    maps = []
    for c in range(8):
        m = core_inputs(inp1, c, S, 2, layers=[1])
        m["vf"] = np.ascontiguousarray(res0.results[c]["vf"])
        maps.append(m)
    res1 = run_bass_kernel_spmd(_CACHE[key], maps, core_ids=list(range(8)))
    out = np.zeros((2, S, D), np.float32)
    for c in range(8):
        out[c // 4, (c % 4) * q:(c % 4 + 1) * q] = res1.results[c]["out"]
    return out


def kernel(**inputs):
    inp = {k: np.asarray(v) for k, v in inputs.items()}
    return run(inp, inp["x"].shape[1], 2)
```

```python
import contextlib
import numpy as np
import concourse.bass as bass
import concourse.mybir as mybir
from concourse.bass_utils import run_bass_kernel_spmd

F32 = mybir.dt.float32
BF16 = mybir.dt.bfloat16
AF = mybir.ActivationFunctionType
ALU = mybir.AluOpType
AX = mybir.AxisListType
SEM_LIMIT = 30000

D = 1024
EPS = 1e-6
GN_EPS = 64e-5
DECAY = 0.606531
NA = 1186
V_QG2, V_KNG2, V_KRG, V_FQG2, V_FKG2, V_BF, V_KK, V_KA, V_RK, V_LG, V_LB, V_MU = (
    0, 192, 320, 352, 480, 608, 610, 738, 866, 994, 1122, 1250)
NV = 1762


class T:
    __slots__ = ("name", "w", "r", "nowaw", "stream")

    def __init__(self, name, nowaw=False):
        self.name = name
        self.w = {}
        self.r = {}
        self.nowaw = nowaw
        self.stream = None


class B:
    def __init__(self, a, name, nowaw=False, psum=False):
        self.a = a
        self.T = T(name, nowaw)
        self.psum = psum


class Rot:
    def __init__(self, items):
        self.items = items
        self.i = 0

    def next(self):
        x = self.items[self.i % len(self.items)]
        self.i += 1
        return x


class FW:
    ENG = ("pe", "dve", "act", "pool", "sp")

    def __init__(self, nc, es):
        self.nc = nc
        self.es = es
        self.e = {"pe": nc.tensor, "dve": nc.vector, "act": nc.scalar, "pool": nc.gpsimd, "sp": nc.sync}
        self.sem = {}
        self.cnt = {}
        self.cur = {}
        self.nsem = 0
        self.seen = {k: {} for k in self.ENG}
        self.free_dma = []
        self.used_dma = []
        for k in self.ENG:
            self._new_key(k)

    def _new_key(self, stream):
        key = "%s#%d" % (stream, self.nsem)
        self.sem[key] = self.es.enter_context(self.nc.semaphore("s%d" % self.nsem))
        self.nsem += 1
        self.cnt[key] = 0
        self.cur[stream] = key
        return key

    def _wait(self, eng, key, seq):
        if self.seen[eng].get(key, 0) >= seq:
            return
        self.seen[eng][key] = seq
        self.e[eng].wait_ge(self.sem[key], seq)

    def deps(self, eng, reads, writes):
        own = eng + "#"
        for b in reads:
            for k, s in b.T.w.items():
                if eng == "pe" and k.startswith(own):
                    continue
                self._wait(eng, k, s)
        for b in writes:
            t = b.T
            if not t.nowaw:
                for k, s in t.w.items():
                    if eng == "pe" and k.startswith(own):
                        continue
                    self._wait(eng, k, s)
            for k, s in t.r.items():
                if k.startswith(own):
                    continue
                self._wait(eng, k, s)

    def done(self, ins, stream, inc, reads, writes):
        key = self.cur[stream]
        if self.cnt[key] + inc > SEM_LIMIT:
            key = self._new_key(stream)
        self.cnt[key] += inc
        seq = self.cnt[key]
        ins.then_inc(self.sem[key], inc)
        for b in reads:
            b.T.r[key] = seq
        for b in writes:
            t = b.T
            if t.nowaw:
                t.w[key] = seq
            else:
                t.w = {key: seq}
                t.r = {}

    def op(self, eng, fn, R, W):
        pr = [b for b in R if b.psum]
        if pr:
            R = [b for b in R if not b.psum]
            W = list(W) + [b for b in pr if b not in W]
        self.deps(eng, R, W)
        ins = fn(self.e[eng])
        self.done(ins, eng, 1, R, W)

    def dma(self, issuer, out, in_, R, W, slot):
        t = slot.T
        if t.stream is None:
            self.nstream = getattr(self, "nstream", 0) + 1
            t.stream = "dma%d" % self.nstream
            if self.free_dma:
                self.cur[t.stream] = self.free_dma.pop()
            else:
                self._new_key(t.stream)
            self.used_dma.append(t.stream)
        self.deps(issuer, R, W)
        ins = self.e[issuer].dma_start(out=out, in_=in_)
        self.done(ins, t.stream, 16, R, W)

    def barrier(self):
        snap = {k: c for k, c in self.cnt.items() if c > 0}
        for eng in self.ENG:
            for k, c in snap.items():
                if k.startswith(eng + "#"):
                    continue
                self._wait(eng, k, c)
        keep = getattr(self, "keep", set())
        for st in self.used_dma:
            if st not in keep:
                self.free_dma.append(self.cur[st])
        self.used_dma = [st for st in self.used_dma if st in keep]


class StopBuild(Exception):
    pass


class Kern:
    stage = 99
    dve_only = False
    debug = False
    dbg_tile = 0

    def dump(self, idx, src_b, ap, n):
        if not self.debug:
            return
        self.fw.dma("sp", self.dbg[idx, :, 0:n], ap, [src_b], [self.Bdbg], self.Bdbg)
        self.fw.keep = {self.Bdbg.T.stream}

    def chk(self, st):
        if self.stage <= st:
            raise StopBuild()

    def __init__(self, S, n_layers, layers=None):
        self.S = S
        self.NT = S // 128
        self.NCH = S // 512
        self.layers = list(range(n_layers)) if layers is None else list(layers)
        self.n_layers = max(self.layers) + 1
        self.uid = 0

    def sb(self, es, name, shape, dt, nowaw=False):
        self.uid += 1
        a = es.enter_context(self.nc.sbuf_tensor("%s_%d" % (name, self.uid), shape, dt))
        return B(a, name, nowaw)

    def ps(self, es, name, shape, dt):
        self.uid += 1
        a = es.enter_context(self.nc.psum_tensor("%s_%d" % (name, self.uid), shape, dt))
        return B(a, name, psum=True)

    def dve(self, fn, R, W):
        self.fw.op("dve", fn, R, W)

    def act(self, fn, R, W):
        self.fw.op("act", fn, R, W)

    def pool(self, fn, R, W):
        self.fw.op("pool", fn, R, W)

    def pe(self, fn, R, W):
        self.fw.op("pe", fn, R, W)

    def evac(self, out, in_, R, W):
        self._ev = getattr(self, "_ev", 0) + 1
        if self._ev % 2 or self.dve_only:
            self.dve(lambda e: e.tensor_copy(out=out, in_=in_), R, W)
        else:
            self.act(lambda e: e.activation(out=out, in_=in_, func=AF.Copy), R, W)

    def transpose(self, out_ap, in_ap, np_in, nf_in, dt, R, W, evac_eng=None, scale=None):
        if dt == BF16:
            slot = self.mib.next()
            idn = self.ident_b
        else:
            slot = self.mif.next()
            idn = self.ident_f
        pv = slot.a[0:nf_in, 0:np_in]
        self.pe(lambda e: e.transpose(out=pv, in_=in_ap, identity=idn.a[0:np_in, 0:np_in]), R + [idn], [slot])
        self.evac(out_ap, pv, [slot], W)

    def build(self):
        S, NT = self.S, self.NT
        nc = bass.Bass("TRN2", target_bir_lowering=False)
        self.nc = nc
        L = self.n_layers
        dr = {}

        def din(name, shape):
            dr[name] = nc.dram_tensor(name, shape, F32, kind="ExternalInput").ap()

        din("x", [S, D])
        din("xq", [S // 4, D])
        din("cs", [S, 64])
        for l in self.layers:
            din("wA%d" % l, [D, NA])
            din("wS%d" % l, [D, 512])
            din("wG%d" % l, [D, 3072])
            din("wuq%d" % l, [256, 192])
            din("wukv%d" % l, [128, 256])
            din("wup%d" % l, [65, 128])
            din("aup%d" % l, [65, 128])
            din("vec%d" % l, [1, NV])
            din("gT%d" % l, [128, 8])
            din("qagT%d" % l, [128, 2])
            din("kvagT%d" % l, [128, 1])
            din("wp%d" % l, [3, 128, D])
            din("wo%d" % l, [D, D])
        if 1 in self.layers:
            din("wvT", [512, D])
            din("muVT", [128, 4])
            din("vdown", [512, 32])
            din("vup", [33, 128])
        out = nc.dram_tensor("out", [S // 4, D], F32, kind="ExternalOutput").ap()
        if self.debug:
            self.dbg = nc.dram_tensor("dbg", [24, 128, 512], F32, kind="ExternalOutput").ap()
            self.Bdbg = B(None, "dbg", nowaw=True)
            self.dbgb = nc.dram_tensor("dbgb", [4, 128, 512], BF16, kind="ExternalOutput").ap()
        self.dr = dr
        part = [nc.dram_tensor("part%d" % l, [S, D], F32).ap() for l in range(L)]
        red = [nc.dram_tensor("red%d" % l, [S, D], F32).ap() for l in range(L)]
        rs = [nc.dram_tensor("rs%d" % l, [S // 4, D], F32).ap() for l in range(L)]
        self.rs = rs
        self.Brs = [B(None, "rs%d" % l) for l in range(L)]
        oT_scr = nc.dram_tensor("oT_scr", [3, 128, S], BF16).ap()
        if 1 in self.layers and 0 not in self.layers:
            vf_scr = nc.dram_tensor("vf", [S, 128], F32, kind="ExternalInput").ap()
        elif self.layers == [0]:
            vf_scr = nc.dram_tensor("vf", [S, 128], F32, kind="ExternalOutput").ap()
        else:
            vf_scr = nc.dram_tensor("vf_scr", [S, 128], F32).ap()
        self.part, self.red, self.oT_scr, self.vf_scr = part, red, oT_scr, vf_scr
        self.Bpart = [B(None, "part%d" % l, nowaw=True) for l in range(L)]
        self.Bred = [B(None, "red%d" % l) for l in range(L)]
        self.BoT = B(None, "oTscr", nowaw=True)
        self.Bvf = B(None, "vfscr", nowaw=True)
        self.Bout = B(None, "out", nowaw=True)

        with contextlib.ExitStack() as es:
            self.fw = FW(nc, es)
            self.consts(es)
            self.pj = Rot([self.ps(es, "pj", [128, 512], F32) for _ in range(2)])
            self.sc = Rot([self.ps(es, "sc", [128, 512], F32) for _ in range(2)])
            self.oa = self.ps(es, "oa", [128, 4, 128], F32)
            self.mib = Rot([self.ps(es, "mib", [128, 1024], BF16)])
            self.mif = Rot([self.ps(es, "mif", [128, 512], F32) for _ in range(2)])
            for l in self.layers:
                with contextlib.ExitStack() as esA:
                    self.passA(esA, l)
                if self.stage <= 50:
                    self.fw.barrier()
                    self.layers = []
                    break
                self.fw.barrier()
                with contextlib.ExitStack() as esB:
                    self.passB(esB, l)
                self.fw.barrier()
                fw = self.fw
                groups = [[0, 1, 2, 3], [4, 5, 6, 7]]
                fw.deps("pool", [self.Bpart[l]], [self.Brs[l]])
                ins = nc.gpsimd.collective_compute("ReduceScatter", ALU.add, replica_groups=groups,
                                                   ins=[part[l]], outs=[rs[l]])
                fw.done(ins, "pool", 1, [self.Bpart[l]], [self.Brs[l]])
                if l < self.layers[-1]:
                    nchk = 8
                    rows = S // nchk
                    for c in range(nchk):
                        fw.deps("pool", [self.Bpart[l]], [self.Bred[l]])
                        ins = nc.gpsimd.collective_compute("AllReduce", ALU.add, replica_groups=groups,
                                                           ins=[part[l][c * rows:(c + 1) * rows, :]],
                                                           outs=[red[l][c * rows:(c + 1) * rows, :]])
                        fw.done(ins, "pool", 1, [self.Bpart[l]], [self.Bred[l]])
                self.fw.barrier()
            with contextlib.ExitStack() as esF:
                self.final(esF, out)
            self.fw.barrier()
        return nc

    def consts(self, es):
        k = self
        self.ident_f = k.sb(es, "identf", [128, 128], F32)
        self.ident_b = k.sb(es, "identb", [128, 128], BF16)
        self.triU = k.sb(es, "triU", [128, 128], F32)
        self.triBD = k.sb(es, "triBD", [128, 128], F32)
        self.sel127 = k.sb(es, "sel127", [128, 128], F32)
        self.ones_f = k.sb(es, "onesf", [128, 128], F32)
        self.mask2 = k.sb(es, "mask2", [128, 256], F32)
        self.masksl = k.sb(es, "masksl", [128, 128], F32)
        idf, idb = self.ident_f, self.ident_b
        k.pool(lambda e: e.memset(idf.a[:], 1.0), [], [idf])
        k.pool(lambda e: e.affine_select(out=idf.a[:], in_=idf.a[:], pattern=[[-1, 128]], compare_op=ALU.is_equal,
                                         fill=0.0, base=0, channel_multiplier=1), [idf], [idf])
        k.pool(lambda e: e.tensor_copy(out=idb.a[:], in_=idf.a[:]), [idf], [idb])
        tu = self.triU
        k.pool(lambda e: e.memset(tu.a[:], 1.0), [], [tu])
        k.pool(lambda e: e.affine_select(out=tu.a[:], in_=tu.a[:], pattern=[[1, 128]], compare_op=ALU.is_ge,
                                         fill=0.0, base=0, channel_multiplier=-1), [tu], [tu])
        tb = self.triBD
        k.pool(lambda e: e.tensor_copy(out=tb.a[:], in_=tu.a[:]), [tu], [tb])
        k.pool(lambda e: e.memset(tb.a[0:64, 64:128], 0.0), [tb], [tb])
        s1 = self.sel127
        k.pool(lambda e: e.memset(s1.a[:], 1.0), [], [s1])
        k.pool(lambda e: e.affine_select(out=s1.a[:], in_=s1.a[:], pattern=[[0, 128]], compare_op=ALU.is_ge,
                                         fill=0.0, base=-127, channel_multiplier=1), [s1], [s1])
        k.pool(lambda e: e.memset(self.ones_f.a[:], 1.0), [], [self.ones_f])
        m2 = self.mask2
        k.pool(lambda e: e.memset(m2.a[:, 0:128], 1.0), [], [m2])
        k.pool(lambda e: e.affine_select(out=m2.a[:, 0:128], in_=m2.a[:, 0:128], pattern=[[1, 128]], compare_op=ALU.is_gt,
                                         fill=0.0, base=0, channel_multiplier=-1), [m2], [m2])
        k.pool(lambda e: e.memset(m2.a[0:64, 64:128], 0.0), [m2], [m2])
        k.pool(lambda e: e.tensor_copy(out=m2.a[:, 128:256], in_=tb.a[:]), [tb, m2], [m2])
        ml = self.masksl
        k.pool(lambda e: e.memset(ml.a[:], 1.0), [], [ml])
        k.pool(lambda e: e.affine_select(out=ml.a[:], in_=ml.a[:], pattern=[[-1, 128]], compare_op=ALU.is_gt,
                                         fill=0.0, base=0, channel_multiplier=1), [ml], [ml])
        k.pool(lambda e: e.memset(ml.a[64:128, 0:64], 0.0), [ml], [ml])

    def load_cast(self, dst, dcol0, src_ap, ncols, gT, stg_rot, nk=8):
        k = self
        c0 = 0
        while c0 < ncols:
            cw = min(256, ncols - c0)
            stg = stg_rot.next()
            for kk in range(nk):
                k.fw.dma("sp", stg.a[:, kk, 0:cw], src_ap[kk * 128:(kk + 1) * 128, c0:c0 + cw], [], [stg], stg)
            for kk in range(nk):
                o = dst.a[:, kk, dcol0 + c0:dcol0 + c0 + cw]
                i = stg.a[:, kk, 0:cw]
                if gT is None:
                    if kk % 2:
                        k.pool(lambda e: e.tensor_copy(out=o, in_=i), [stg], [dst])
                    else:
                        k.dve(lambda e: e.tensor_copy(out=o, in_=i), [stg], [dst])
                else:
                    g = gT.a[:, kk:kk + 1]
                    if kk % 2:
                        k.pool(lambda e: e.tensor_scalar(out=o, in0=i, scalar1=g, scalar2=None, op0=ALU.mult), [stg, gT], [dst])
                    else:
                        k.dve(lambda e: e.tensor_scalar(out=o, in0=i, scalar1=g, scalar2=None, op0=ALU.mult), [stg, gT], [dst])
            c0 += cw

    def small_load(self, es, name, shape, src_ap):
        b = self.sb(es, name, shape, F32)
        self.fw.dma("sp", b.a[:], src_ap, [], [b], b)
        return b

    def load_h(self, l, ti, first, want_hb=False):
        k = self
        fw = k.fw
        xt = k.xrot.next()
        rows = slice(ti * 128, (ti + 1) * 128)
        fw.dma("sp", xt.a[:], k.dr["x"][rows, :], [], [xt], xt)
        for ll in [q for q in self.layers if q < l]:
            for half in range(2):
                rt = k.rrot.next()
                hc = slice(half * 512, (half + 1) * 512)
                fw.dma("sp", rt.a[:, 0:512], k.red[ll][rows, hc], [k.Bred[ll]], [rt], rt)
                k.dve(lambda e: e.tensor_add(out=xt.a[:, hc], in0=xt.a[:, hc], in1=rt.a[:, 0:512]), [xt, rt], [xt])
        if k.stage <= 1.1:
            return None
        hb = k.hbrot.next()
        junk, st = hb, k.xstat.next()
        k.act(lambda e: e.activation(out=junk.a[:], in_=xt.a[:], func=AF.Square, scale=float(D ** -0.5),
                                     accum_out=st.a[:, 0:1]), [xt], [junk, st])
        if k.stage <= 1.2:
            return None
        k.dve(lambda e: e.tensor_scalar_add(out=st.a[:, 1:2], in0=st.a[:, 0:1], scalar1=EPS), [st], [st])
        k.act(lambda e: e.sqrt(out=st.a[:, 1:2], in_=st.a[:, 1:2]), [st], [st])
        k.dve(lambda e: e.reciprocal(out=st.a[:, 2:3], in_=st.a[:, 1:2]), [st], [st])
        if k.stage <= 1.3:
            return None
        k.act(lambda e: e.activation(out=hb.a[:], in_=xt.a[:], func=AF.Copy, scale=st.a[:, 2:3]), [xt, st], [hb])
        if k.stage <= 1.4:
            return None
        hT = k.hTrot.next()
        for half in range(2):
            slot = k.mib.next()
            for q in range(4):
                kk = half * 4 + q
                k.pe(lambda e: e.transpose(out=slot.a[:, q * 128:(q + 1) * 128], in_=hb.a[:, kk * 128:(kk + 1) * 128],
                                           identity=k.ident_b.a[:]), [hb, k.ident_b], [slot])
            if k.stage <= 1.45:
                continue
            for q in range(4):
                k.evac(hT.a[:, half * 4 + q, 1:129], slot.a[:, q * 128:(q + 1) * 128], [slot], [hT])
        if k.stage <= 1.5:
            return None
        if first:
            k.dve(lambda e: e.memset(hT.a[:, :, 0:1], 0.0), [], [hT])
        else:
            prev = k.hT_prev
            k.dve(lambda e: e.tensor_copy(out=hT.a[:, :, 0:1], in_=prev.a[:, :, 128:129]), [prev], [hT])
        k.hT_prev = hT
        return hT

    def rstd_cols(self, st, n, kk_cols=()):
        k = self
        k.act(lambda e: e.sqrt(out=st.a[:, 0:n], in_=st.a[:, 0:n]), [st], [st])
        for c in kk_cols:
            k.dve(lambda e: e.tensor_scalar_max(out=st.a[:, c:c + 1], in0=st.a[:, c:c + 1], scalar1=1e-12), [st], [st])
        k.dve(lambda e: e.reciprocal(out=st.a[:, 0:n], in_=st.a[:, 0:n]), [st], [st])

    def passA(self, es, l):
        k = self
        fw = k.fw
        S, NT, NCH = k.S, k.NT, k.NCH
        dr = k.dr
        L1 = l > 0
        NS = 544 if L1 else 512
        vec = k.sb(es, "vec", [128, V_MU], F32)
        fw.dma("sp", vec.a[:], dr["vec%d" % l][:, 0:V_MU].partition_broadcast(128), [], [vec], vec)
        gT = k.small_load(es, "gT", [128, 8], dr["gT%d" % l])
        qagT = k.small_load(es, "qagT", [128, 2], dr["qagT%d" % l])
        kvagT = k.small_load(es, "kvagT", [128, 1], dr["kvagT%d" % l])
        wup = k.small_load(es, "wup", [65, 128], dr["wup%d" % l])
        aup = k.small_load(es, "aup", [65, 128], dr["aup%d" % l])
        wA = k.sb(es, "wA", [128, 8, NA], BF16)
        wS1 = k.sb(es, "wS1", [128, 8, NS], BF16)
        wS2 = k.sb(es, "wS2", [128, 8, NS], BF16)
        wuq = k.sb(es, "wuq", [128, 2, 192], BF16)
        wukv = k.sb(es, "wukv", [128, 1, 256], BF16)
        with contextlib.ExitStack() as est:
            stg_rot = Rot([k.sb(est, "stg", [128, 8, 256], F32) for _ in range(2)])
            k.load_cast(wA, 0, dr["wA%d" % l], NA, gT, stg_rot)
            k.load_cast(wuq, 0, dr["wuq%d" % l], 192, qagT, stg_rot, nk=2)
            k.load_cast(wukv, 0, dr["wukv%d" % l], 256, kvagT, stg_rot, nk=1)
            muv = k.sb(est, "muv", [128, 512], F32)
            fw.dma("sp", muv.a[:], dr["vec%d" % l][:, V_MU:V_MU + 512].partition_broadcast(128), [], [muv], muv)
            tmp = k.sb(est, "wtmp", [128, 256], F32)
            tmp2 = k.sb(est, "wtmp2", [128, 256], F32)
            for c0 in (0, 256):
                stg = stg_rot.next()
                for kk in range(8):
                    fw.dma("sp", stg.a[:, kk, :], dr["wS%d" % l][kk * 128:(kk + 1) * 128, c0:c0 + 256], [], [stg], stg)
                mu = muv.a[:, c0:c0 + 256]
                for kk in range(8):
                    g = gT.a[:, kk:kk + 1]
                    k.dve(lambda e: e.tensor_mul(out=tmp.a[:], in0=stg.a[:, kk, :], in1=mu), [stg, muv], [tmp])
                    k.dve(lambda e: e.tensor_scalar(out=wS2.a[:, kk, c0:c0 + 256], in0=tmp.a[:], scalar1=g, scalar2=None,
                                                    op0=ALU.mult), [tmp, gT], [wS2])
                    k.dve(lambda e: e.tensor_sub(out=tmp2.a[:], in0=stg.a[:, kk, :], in1=tmp.a[:]), [stg, tmp], [tmp2])
                    k.dve(lambda e: e.tensor_scalar(out=wS1.a[:, kk, c0:c0 + 256], in0=tmp2.a[:], scalar1=g, scalar2=None,
                                                    op0=ALU.mult), [tmp2, gT], [wS1])
            if L1:
                muVT = k.small_load(est, "muVT", [128, 4], dr["muVT"])
                vdn = k.sb(est, "vdn", [128, 4, 32], F32)
                for kc in range(4):
                    fw.dma("sp", vdn.a[:, kc, :], dr["vdown"][kc * 128:(kc + 1) * 128, :], [], [vdn], vdn)
                wv = k.sb(est, "wv", [128, 4, 128], F32)
                wv1 = k.sb(est, "wv1", [128, 4, 128], F32)
                wv2 = k.sb(est, "wv2", [128, 4, 128], F32)
                for dc in range(8):
                    for kc in range(4):
                        fw.dma("sp", wv.a[:, kc, :], dr["wvT"][kc * 128:(kc + 1) * 128, dc * 128:(dc + 1) * 128], [], [wv], wv)
                    for kc in range(4):
                        k.dve(lambda e: e.tensor_scalar(out=wv2.a[:, kc, :], in0=wv.a[:, kc, :], scalar1=muVT.a[:, kc:kc + 1],
                                                        scalar2=None, op0=ALU.mult), [wv, muVT], [wv2])
                    k.dve(lambda e: e.tensor_sub(out=wv1.a[:], in0=wv.a[:], in1=wv2.a[:]), [wv, wv2], [wv1])
                    for (wsrc, wdst) in ((wv1, wS1), (wv2, wS2)):
                        slot = k.mif.next()
                        for kc in range(4):
                            k.pe(lambda e: e.matmul(slot.a[:, 0:32], lhsT=wsrc.a[:, kc, :], rhs=vdn.a[:, kc, :],
                                                    start=(kc == 0), stop=(kc == 3)), [wsrc, vdn], [slot])
                        k.dve(lambda e: e.tensor_scalar(out=wdst.a[:, dc, 512:544], in0=slot.a[:, 0:32], scalar1=gT.a[:, dc:dc + 1],
                                                        scalar2=None, op0=ALU.mult), [slot, gT], [wdst])
        fw.barrier()
        if k.stage <= 1:
            return
        if L1:
            vup = k.small_load(es, "vup", [33, 128], dr["vup"])
        kTm = [k.sb(es, "kTm%d" % h, [96, S], BF16) for h in range(2)]
        kTf = k.sb(es, "kTf", [128, S], BF16)
        Vm = k.sb(es, "Vm", [128, NT, 2, 65], BF16)
        Vf = k.sb(es, "Vf", [128, NT, 2, 65], BF16)
        ncum = k.sb(es, "ncum", [128, NT, 2], F32)
        k.xrot = Rot([k.sb(es, "xt", [128, D], F32) for _ in range(2)])
        k.xstat = Rot([k.sb(es, "xst", [128, 4], F32) for _ in range(2)])
        k.hbrot = Rot([k.sb(es, "hb", [128, D], BF16) for _ in range(1)])
        k.hTrot = Rot([k.sb(es, "hT", [128, 8, 129], BF16) for _ in range(2)])
        csr = Rot([k.sb(es, "cs", [128, 64], F32) for _ in range(2)])
        s1r = Rot([k.sb(es, "s1", [128, 418], F32) for _ in range(1)])
        s2r = Rot([k.sb(es, "s2", [128, 384], F32) for _ in range(1)])
        s4r = Rot([k.sb(es, "s4", [128, 512], F32) for _ in range(1)])
        s5r = Rot([k.sb(es, "s5", [128, 32], F32) for _ in range(2)])
        sga = k.sb(es, "sga", [128, 4, 128], BF16)
        sgb = k.sb(es, "sgb", [128, 4, 128], BF16)
        sgc = Rot([k.sb(es, "sgc", [128, 128], F32) for _ in range(1)])
        sq = k.sb(es, "sq", [128, 512], F32)
        k.rrot = Rot([sq])
        stA = Rot([k.sb(es, "stA", [128, 16], F32) for _ in range(2)])
        stB = Rot([k.sb(es, "stB", [128, 8], F32) for _ in range(2)])
        cqn = k.sb(es, "cqn", [128, 384], BF16)
        cT = k.sb(es, "cT", [128, 3, 128], BF16)
        qk = k.sb(es, "qk", [128, 448], F32)
        qn = k.sb(es, "qn", [128, 2, 96], F32)
        kn = k.sb(es, "kn", [128, 2, 96], F32)
        rp = k.sb(es, "rp", [128, 4, 32], F32)
        qb = k.sb(es, "qb", [128, 2, 96], BF16)
        kb = k.sb(es, "kb", [128, 2, 96], BF16)
        fqb = k.sb(es, "fqb", [128, 128], BF16)
        fkb = k.sb(es, "fkb", [128, 128], BF16)
        qTm = [k.sb(es, "qTm%d" % h, [96, 512], BF16) for h in range(2)]
        qTf = k.sb(es, "qTf", [128, 512], BF16)
        lf = Rot([k.sb(es, "lf", [128, 4], F32) for _ in range(2)])
        cumr = Rot([k.sb(es, "cum", [128, 2], F32) for _ in range(2)])
        dcum = k.sb(es, "dcum", [128, 2, 128], F32)
        cumbc = k.sb(es, "cumbc", [128, 2, 512], F32)
        pTr = Rot([k.sb(es, "pT", [128, 512], BF16) for _ in range(2)])
        ftmp = Rot([k.sb(es, "ftmp", [128, 512], F32) for _ in range(1)])
        rinv = k.sb(es, "rinv", [128, 4], F32)
        ogm = k.sb(es, "ogm", [128, 4, 128], BF16)
        ogf = k.sb(es, "ogf", [128, 4, 128], BF16)
        oTr = Rot([k.sb(es, "oT", [128, 512], BF16) for _ in range(1)])
        ocb = k.sb(es, "ocb", [128, 128], BF16)
        oTc = Rot([k.sb(es, "oTc", [128, 128], BF16) for _ in range(2)])
        twT = k.sb(es, "twT", [65, 128], F32)
        alT = k.sb(es, "alT", [65, 128], F32)
        k.pool(lambda e: e.memset(twT.a[64:65, :], 1.0), [], [twT])
        k.pool(lambda e: e.memset(alT.a[64:65, :], 1.0), [], [alT])
        tw = k.sb(es, "tw", [128, 64], F32)
        lw = k.sb(es, "lw", [128, 128], F32)
        av = k.sb(es, "av", [128, 128], F32)
        kkn = k.sb(es, "kkn", [128, 128], F32)
        kmod = k.sb(es, "kmod", [128, 128], F32)
        vv = k.sb(es, "vv", [128, 128], F32)
        rt1 = k.sb(es, "rt1", [128, 128], F32)
        rt2 = k.sb(es, "rt2", [128, 128], F32)
        bon = k.sb(es, "bon", [128, 2], F32)
        E1 = k.sb(es, "E1", [128, 128], F32)
        E2 = k.sb(es, "E2", [128, 128], F32)
        E3 = k.sb(es, "E3", [128, 128], F32)
        qa_ = k.sb(es, "qalpha", [128, 128], F32)
        qk_ = k.sb(es, "qkt", [128, 128], F32)
        qb_ = k.sb(es, "qbeta", [128, 128], F32)
        qr_ = k.sb(es, "qr", [128, 128], F32)
        qa_hi = k.sb(es, "qahi", [128, 128], F32)
        qk_hi = k.sb(es, "qkhi", [128, 128], F32)
        k.pool(lambda e: e.memset(qa_hi.a[:], 0.0), [], [qa_hi])
        k.pool(lambda e: e.memset(qk_hi.a[:], 0.0), [], [qk_hi])
        qT4 = [k.sb(es, "qT4_%d" % h, [64, 4, 128], F32) for h in range(2)]
        rlo = [k.sb(es, "rlo%d" % h, [64, 128], F32) for h in range(2)]
        rhi = [k.sb(es, "rhi%d" % h, [64, 128], F32) for h in range(2)]
        for h in range(2):
            k.pool(lambda e: e.memset(rlo[h].a[:], 0.0), [], [rlo[h]])
            k.pool(lambda e: e.memset(rhi[h].a[:], 0.0), [], [rhi[h]])
        GA = [k.sb(es, "GA%d" % h, [128, 256], F32) for h in range(2)]
        GK = [k.sb(es, "GK%d" % h, [128, 256], F32) for h in range(2)]
        XT = [Rot([k.sb(es, "XT%d_%d" % (h, i), [128, 128], F32) for i in range(2)]) for h in range(2)]
        XX = [Rot([k.sb(es, "XX%d_%d" % (h, i), [128, 128], F32) for i in range(2)]) for h in range(2)]
        PP = [Rot([k.sb(es, "PP%d_%d" % (h, i), [128, 128], F32) for i in range(2)]) for h in range(2)]
        Wb = [k.sb(es, "Wb%d" % h, [128, 64], F32) for h in range(2)]
        Ub = [k.sb(es, "Ub%d" % h, [128, 64], F32) for h in range(2)]
        Mst = [Rot([k.sb(es, "M%d_%d" % (h, i), [64, 64], F32) for i in range(3)]) for h in range(2)]
        pC = [k.sb(es, "pC%d" % h, [64, 2], F32) for h in range(2)]
        mtmp = [k.sb(es, "mtmp%d" % h, [64, 64], F32) for h in range(2)]
        yv = k.sb(es, "yv", [128, 2, 64], F32)
        ysq = k.sb(es, "ysq", [128, 2, 64], F32)
        yst = k.sb(es, "yst", [128, 8], F32)
        vdT = k.sb(es, "vdT", [33, 128], F32)
        k.pool(lambda e: e.memset(vdT.a[32:33, :], 1.0), [], [vdT])
        vfl = k.sb(es, "vfl", [128, 128], F32)
        nu = rt1
        Mcur = []
        for h in range(2):
            m0 = Mst[h].next()
            k.pool(lambda e: e.memset(m0.a[:], 0.0), [], [m0])
            Mcur.append(m0)
        cum_prev = None
        qscale_m = float(96 ** -0.5)
        qscale_f = float(64 ** -0.5)

        for ch in range(NCH):
            for jq in range(4):
                ti = ch * 4 + jq
                rows = slice(ti * 128, (ti + 1) * 128)
                cols = slice(ti * 128, (ti + 1) * 128)
                ccols = slice(jq * 128, (jq + 1) * 128)
                hT = k.load_h(l, ti, ti == 0)
                if k.stage <= 2:
                    continue
                cs = csr.next()
                fw.dma("sp", cs.a[:], dr["cs"][rows, :], [], [cs], cs)
                s1, s2, s4, s5 = s1r.next(), s2r.next(), s4r.next(), s5r.next()
                sg_c = sgc.next()
                p = k.pj.next()
                for kk in range(8):
                    k.pe(lambda e: e.matmul(p.a[:, 0:418], lhsT=hT.a[:, kk, 1:129], rhs=wA.a[:, kk, 0:418],
                                            start=(kk == 0), stop=(kk == 7)), [hT, wA], [p])
                k.evac(s1.a[:], p.a[:, 0:418], [p], [s1])
                p = k.pj.next()
                for kk in range(8):
                    k.pe(lambda e: e.matmul(p.a[:, 0:512], lhsT=hT.a[:, kk, 1:129], rhs=wA.a[:, kk, 418:930],
                                            start=(kk == 0), stop=(kk == 7)), [hT, wA], [p])
                k.dve(lambda e: e.tensor_copy(out=s2.a[:], in_=p.a[:, 0:384]), [p], [s2])
                k.act(lambda e: e.activation(out=sgb.a[:, jq, :], in_=p.a[:, 384:512], func=AF.Silu), [p], [sgb])
                p = k.pj.next()
                for kk in range(8):
                    k.pe(lambda e: e.matmul(p.a[:, 0:256], lhsT=hT.a[:, kk, 1:129], rhs=wA.a[:, kk, 930:1186],
                                            start=(kk == 0), stop=(kk == 7)), [hT, wA], [p])
                k.act(lambda e: e.activation(out=sga.a[:, jq, :], in_=p.a[:, 0:128], func=AF.Silu), [p], [sga])
                k.act(lambda e: e.activation(out=sg_c.a[:], in_=p.a[:, 128:256], func=AF.Silu), [p], [sg_c])
                p = k.pj.next()
                for kk in range(8):
                    k.pe(lambda e: e.matmul(p.a[:, 0:512], lhsT=hT.a[:, kk, 1:129], rhs=wS1.a[:, kk, 0:512],
                                            start=(kk == 0), stop=False), [hT, wS1], [p])
                for kk in range(8):
                    k.pe(lambda e: e.matmul(p.a[:, 0:512], lhsT=hT.a[:, kk, 0:128], rhs=wS2.a[:, kk, 0:512],
                                            start=False, stop=(kk == 7)), [hT, wS2], [p])
                k.evac(s4.a[:], p.a[:, 0:512], [p], [s4])
                if L1:
                    p = k.mif.next()
                    for kk in range(8):
                        k.pe(lambda e: e.matmul(p.a[:, 0:32], lhsT=hT.a[:, kk, 1:129], rhs=wS1.a[:, kk, 512:544],
                                                start=(kk == 0), stop=False), [hT, wS1], [p])
                    for kk in range(8):
                        k.pe(lambda e: e.matmul(p.a[:, 0:32], lhsT=hT.a[:, kk, 0:128], rhs=wS2.a[:, kk, 512:544],
                                                start=False, stop=(kk == 7)), [hT, wS2], [p])
                    k.evac(s5.a[:], p.a[:, 0:32], [p], [s5])
                if ti == k.dbg_tile:
                    k.dump(0, s1, s1.a[:], 418)
                    k.dump(1, s2, s2.a[:], 384)
                    k.dump(2, s4, s4.a[:], 512)
                    k.dump(15, sga, sga.a[:, jq, :], 128)
                if k.stage <= 3:
                    continue
                st = stA.next()
                k.dve(lambda e: e.tensor_mul(out=sq.a[:, 0:416], in0=s1.a[:, 0:416], in1=s1.a[:, 0:416]), [s1], [sq])
                k.dve(lambda e: e.tensor_reduce(out=st.a[:, 0:1], in_=sq.a[:, 0:256], axis=AX.X, op=ALU.add), [sq], [st])
                k.dve(lambda e: e.tensor_reduce(out=st.a[:, 1:2], in_=sq.a[:, 256:384], axis=AX.X, op=ALU.add), [sq], [st])
                k.dve(lambda e: e.tensor_reduce(out=st.a[:, 2:3], in_=sq.a[:, 384:416], axis=AX.X, op=ALU.add), [sq], [st])
                k.dve(lambda e: e.tensor_mul(out=sq.a[:, 0:256], in0=s2.a[:, 0:256], in1=s2.a[:, 0:256]), [s2, st], [sq])
                k.dve(lambda e: e.tensor_reduce(out=st.a[:, 3:7], in_=sq.a[:, 0:256].rearrange("p (h d) -> p h d", h=4),
                                                axis=AX.X, op=ALU.add), [sq], [st])
                k.dve(lambda e: e.tensor_mul(out=kkn.a[:], in0=s4.a[:, 128:256], in1=vec.a[:, V_KK:V_KK + 128]), [s4, vec], [kkn])
                k.dve(lambda e: e.tensor_mul(out=sq.a[:, 256:384], in0=kkn.a[:], in1=kkn.a[:]), [kkn], [sq])
                k.dve(lambda e: e.tensor_reduce(out=st.a[:, 7:9], in_=sq.a[:, 256:384].rearrange("p (h d) -> p h d", h=2),
                                                axis=AX.X, op=ALU.add), [sq], [st])
                k.dve(lambda e: e.tensor_scalar(out=st.a[:, 0:1], in0=st.a[:, 0:1], scalar1=1.0 / 256, scalar2=EPS, op0=ALU.mult, op1=ALU.add), [st], [st])
                k.dve(lambda e: e.tensor_scalar(out=st.a[:, 1:2], in0=st.a[:, 1:2], scalar1=1.0 / 128, scalar2=EPS, op0=ALU.mult, op1=ALU.add), [st], [st])
                k.dve(lambda e: e.tensor_scalar(out=st.a[:, 2:3], in0=st.a[:, 2:3], scalar1=1.0 / 32, scalar2=EPS, op0=ALU.mult, op1=ALU.add), [st], [st])
                k.dve(lambda e: e.tensor_scalar(out=st.a[:, 3:7], in0=st.a[:, 3:7], scalar1=1.0 / 64, scalar2=EPS, op0=ALU.mult, op1=ALU.add), [st], [st])
                k.rstd_cols(st, 9, kk_cols=(7, 8))
                k.act(lambda e: e.activation(out=cqn.a[:, 0:256], in_=s1.a[:, 0:256], func=AF.Copy, scale=st.a[:, 0:1]), [s1, st], [cqn])
                k.act(lambda e: e.activation(out=cqn.a[:, 256:384], in_=s1.a[:, 256:384], func=AF.Copy, scale=st.a[:, 1:2]), [s1, st], [cqn])
                for c in range(3):
                    k.transpose(cT.a[:, c, :], cqn.a[:, c * 128:(c + 1) * 128], 128, 128, BF16, [cqn], [cT])
                p = k.pj.next()
                for c in range(2):
                    k.pe(lambda e: e.matmul(p.a[:, 0:192], lhsT=cT.a[:, c, :], rhs=wuq.a[:, c, :], start=(c == 0), stop=(c == 1)),
                         [cT, wuq], [p])
                k.pe(lambda e: e.matmul(p.a[:, 192:448], lhsT=cT.a[:, 2, :], rhs=wukv.a[:, 0, :], start=True, stop=True),
                     [cT, wukv], [p])
                k.evac(qk.a[:], p.a[:, 0:448], [p], [qk])
                sb_ = stB.next()
                k.dve(lambda e: e.tensor_mul(out=sq.a[:, 0:320], in0=qk.a[:, 0:320], in1=qk.a[:, 0:320]), [qk], [sq])
                k.dve(lambda e: e.tensor_reduce(out=sb_.a[:, 0:2], in_=sq.a[:, 0:192].rearrange("p (h d) -> p h d", h=2),
                                                axis=AX.X, op=ALU.add), [sq], [sb_])
                k.dve(lambda e: e.tensor_reduce(out=sb_.a[:, 2:4], in_=sq.a[:, 192:320].rearrange("p (h d) -> p h d", h=2),
                                                axis=AX.X, op=ALU.add), [sq], [sb_])
                k.dve(lambda e: e.tensor_scalar(out=sb_.a[:, 0:2], in0=sb_.a[:, 0:2], scalar1=1.0 / 96, scalar2=EPS, op0=ALU.mult, op1=ALU.add), [sb_], [sb_])
                k.dve(lambda e: e.tensor_scalar(out=sb_.a[:, 2:4], in0=sb_.a[:, 2:4], scalar1=1.0 / 64, scalar2=EPS, op0=ALU.mult, op1=ALU.add), [sb_], [sb_])
                k.rstd_cols(sb_, 4)
                for h in range(2):
                    k.dve(lambda e: e.scalar_tensor_tensor(out=qn.a[:, h, :], in0=qk.a[:, h * 96:(h + 1) * 96], scalar=sb_.a[:, h:h + 1],
                                                           in1=vec.a[:, V_QG2 + h * 96:V_QG2 + (h + 1) * 96], op0=ALU.mult, op1=ALU.mult),
                          [qk, sb_, vec], [qn])
                    k.dve(lambda e: e.scalar_tensor_tensor(out=kn.a[:, h, 0:64], in0=qk.a[:, 192 + h * 64:192 + (h + 1) * 64],
                                                           scalar=sb_.a[:, 2 + h:3 + h],
                                                           in1=vec.a[:, V_KNG2 + h * 64:V_KNG2 + (h + 1) * 64], op0=ALU.mult, op1=ALU.mult),
                          [qk, sb_, vec], [kn])
                    k.dve(lambda e: e.scalar_tensor_tensor(out=kn.a[:, h, 64:96], in0=s1.a[:, 384:416], scalar=st.a[:, 2:3],
                                                           in1=vec.a[:, V_KRG:V_KRG + 32], op0=ALU.mult, op1=ALU.mult),
                          [s1, st, vec], [kn])
                cosv = cs.a[:, 0:32].rearrange("p (h d) -> p h d", h=2)
                sinv = cs.a[:, 32:64].rearrange("p (h d) -> p h d", h=2)
                for (src, dstb, scl) in ((qn, qb, qscale_m), (kn, kb, 1.0)):
                    x1 = src.a[:, :, 64:80]
                    x2 = src.a[:, :, 80:96]
                    r1 = rp.a[:, 0:1, :].rearrange("p a (h d) -> p (a h) d", h=2)
                    r2 = rp.a[:, 1:2, :].rearrange("p a (h d) -> p (a h) d", h=2)
                    r3 = rp.a[:, 2:3, :].rearrange("p a (h d) -> p (a h) d", h=2)
                    r4 = rp.a[:, 3:4, :].rearrange("p a (h d) -> p (a h) d", h=2)
                    k.dve(lambda e: e.tensor_mul(out=r1, in0=x1, in1=cosv), [src, cs], [rp])
                    k.dve(lambda e: e.tensor_mul(out=r2, in0=x2, in1=sinv), [src, cs, rp], [rp])
                    k.dve(lambda e: e.tensor_mul(out=r3, in0=x2, in1=cosv), [src, cs, rp], [rp])
                    k.dve(lambda e: e.tensor_mul(out=r4, in0=x1, in1=sinv), [src, cs, rp], [rp])
                    k.dve(lambda e: e.tensor_sub(out=src.a[:, :, 64:80], in0=r1, in1=r2), [rp, src], [src])
                    k.dve(lambda e: e.tensor_add(out=src.a[:, :, 80:96], in0=r3, in1=r4), [rp, src], [src])
                    k.act(lambda e: e.activation(out=dstb.a[:], in_=src.a[:], func=AF.Copy, scale=scl), [src], [dstb])
                for h in range(2):
                    k.transpose(qTm[h].a[:, ccols], qb.a[:, h, :], 128, 96, BF16, [qb], [qTm[h]])
                    k.transpose(kTm[h].a[:, cols], kb.a[:, h, :], 128, 96, BF16, [kb], [kTm[h]])
                k.dve(lambda e: e.tensor_copy(out=Vm.a[:, ti, :, 0:64], in_=qk.a[:, 320:448].rearrange("p (h d) -> p h d", h=2)), [qk], [Vm])
                k.dve(lambda e: e.tensor_copy(out=Vm.a[:, ti, :, 64:65], in_=k.ones_f.a[:, 0:2].rearrange("p (h o) -> p h o", o=1)), [k.ones_f], [Vm])
                if ti == k.dbg_tile:
                    k.dump(3, qk, qk.a[:], 448)
                    k.dump(4, qn, qn.a[:].rearrange("p h d -> p (h d)"), 192)
                    k.dump(5, kn, kn.a[:].rearrange("p h d -> p (h d)"), 192)
                    k.dump(16, st, st.a[:], 16)
                if k.stage <= 4:
                    continue
                for h in range(2):
                    k.dve(lambda e: e.scalar_tensor_tensor(out=sq.a[:, h * 64:(h + 1) * 64], in0=s2.a[:, h * 64:(h + 1) * 64],
                                                           scalar=st.a[:, 3 + h:4 + h], in1=vec.a[:, V_FQG2 + h * 64:V_FQG2 + (h + 1) * 64],
                                                           op0=ALU.mult, op1=ALU.mult), [s2, st, vec, sq], [sq])
                    k.dve(lambda e: e.scalar_tensor_tensor(out=sq.a[:, 128 + h * 64:128 + (h + 1) * 64], in0=s2.a[:, 128 + h * 64:128 + (h + 1) * 64],
                                                           scalar=st.a[:, 5 + h:6 + h], in1=vec.a[:, V_FKG2 + h * 64:V_FKG2 + (h + 1) * 64],
                                                           op0=ALU.mult, op1=ALU.mult), [s2, st, vec, sq], [sq])
                k.act(lambda e: e.activation(out=fqb.a[:], in_=sq.a[:, 0:128], func=AF.Copy, scale=qscale_f), [sq], [fqb])
                k.act(lambda e: e.activation(out=fkb.a[:], in_=sq.a[:, 128:256], func=AF.Copy), [sq], [fkb])
                k.transpose(qTf.a[:, ccols], fqb.a[:], 128, 128, BF16, [fqb], [qTf])
                k.transpose(kTf.a[:, cols], fkb.a[:], 128, 128, BF16, [fkb], [kTf])
                k.dve(lambda e: e.tensor_copy(out=Vf.a[:, ti, :, 0:64], in_=s2.a[:, 256:384].rearrange("p (h d) -> p h d", h=2)), [s2], [Vf])
                k.dve(lambda e: e.tensor_copy(out=Vf.a[:, ti, :, 64:65], in_=k.ones_f.a[:, 0:2].rearrange("p (h o) -> p h o", o=1)), [k.ones_f], [Vf])
                lf_ = lf.next()
                k.dve(lambda e: e.tensor_add(out=lf_.a[:, 0:2], in0=s1.a[:, 416:418], in1=vec.a[:, V_BF:V_BF + 2]), [s1, vec], [lf_])
                k.act(lambda e: e.activation(out=lf_.a[:, 0:2], in_=lf_.a[:, 0:2], func=AF.Exp, scale=-1.0), [lf_], [lf_])
                k.act(lambda e: e.activation(out=lf_.a[:, 0:2], in_=lf_.a[:, 0:2], func=AF.Ln, bias=1.0), [lf_], [lf_])
                k.dve(lambda e: e.tensor_scalar_mul(out=lf_.a[:, 2:4], in0=lf_.a[:, 0:2], scalar1=-1.0), [lf_], [lf_])
                cum = cumr.next()
                p = k.mif.next()
                k.pe(lambda e: e.matmul(p.a[:, 0:2], lhsT=k.triU.a[:], rhs=lf_.a[:, 2:4], start=True, stop=(cum_prev is None)),
                     [k.triU, lf_], [p])
                if cum_prev is not None:
                    cp = cum_prev
                    k.pe(lambda e: e.matmul(p.a[:, 0:2], lhsT=k.sel127.a[:], rhs=cp.a[:], start=False, stop=True),
                         [k.sel127, cp], [p])
                k.dve(lambda e: e.tensor_copy(out=cum.a[:], in_=p.a[:, 0:2]), [p], [cum])
                cum_prev = cum
                k.dve(lambda e: e.tensor_scalar_mul(out=ncum.a[:, ti, :], in0=cum.a[:], scalar1=-1.0), [cum], [ncum])
                for h in range(2):
                    k.dve(lambda e: e.tensor_scalar(out=dcum.a[:, h, :], in0=k.ident_f.a[:], scalar1=cum.a[:, h:h + 1], scalar2=None,
                                                    op0=ALU.mult), [k.ident_f, cum], [dcum])
                p = k.mif.next()
                for h in range(2):
                    k.pe(lambda e: e.matmul(p.a[:, h * 128:(h + 1) * 128], lhsT=k.ones_f.a[:], rhs=dcum.a[:, h, :], start=True, stop=True),
                         [k.ones_f, dcum], [p])
                k.evac(cumbc.a[:, :, ccols], p.a[:, 0:256].rearrange("p (h c) -> p h c", h=2), [p], [cumbc])

                if ti == k.dbg_tile:
                    k.dump(6, cum, cum.a[:], 2)
                    k.dump(7, cumbc, cumbc.a[:, 0, ccols], 128)
                    k.dump(17, sq, sq.a[:, 0:256], 256)
                if k.stage <= 5:
                    continue
                k.act(lambda e: e.activation(out=tw.a[:], in_=s4.a[:, 384:448], func=AF.Tanh), [s4], [tw])
                k.transpose(twT.a[0:64, :], tw.a[:], 128, 64, F32, [tw], [twT])
                k.transpose(alT.a[0:64, :], s4.a[:, 448:512], 128, 64, F32, [s4], [alT])
                p = k.mif.next()
                k.pe(lambda e: e.matmul(p.a[:, 0:128], lhsT=twT.a[:], rhs=wup.a[:], start=True, stop=True), [twT, wup], [p])
                k.pe(lambda e: e.matmul(p.a[:, 128:256], lhsT=alT.a[:], rhs=aup.a[:], start=True, stop=True), [alT, aup], [p])
                k.act(lambda e: e.activation(out=lw.a[:], in_=p.a[:, 0:128], func=AF.Sigmoid), [p], [lw])
                k.act(lambda e: e.activation(out=av.a[:], in_=p.a[:, 128:256], func=AF.Sigmoid), [p], [av])
                k.dve(lambda e: e.tensor_scalar_mul(out=lw.a[:], in0=lw.a[:], scalar1=-DECAY), [lw], [lw])
                for h in range(2):
                    k.dve(lambda e: e.tensor_scalar(out=kkn.a[:, h * 64:(h + 1) * 64], in0=kkn.a[:, h * 64:(h + 1) * 64],
                                                    scalar1=st.a[:, 7 + h:8 + h], scalar2=None, op0=ALU.mult), [kkn, st], [kkn])
                k.dve(lambda e: e.scalar_tensor_tensor(out=rt1.a[:], in0=av.a[:], scalar=-1.0, in1=vec.a[:, V_KA:V_KA + 128],
                                                       op0=ALU.add, op1=ALU.mult), [av, vec], [rt1])
                k.dve(lambda e: e.scalar_tensor_tensor(out=kmod.a[:], in0=rt1.a[:], scalar=1.0, in1=s4.a[:, 128:256],
                                                       op0=ALU.add, op1=ALU.mult), [rt1, s4], [kmod])
                if not L1:
                    k.dve(lambda e: e.tensor_copy(out=vv.a[:], in_=s4.a[:, 256:384]), [s4], [vv])
                    fw.dma("pool", k.vf_scr[rows, :], vv.a[:], [vv], [k.Bvf], vv)
                else:
                    fw.dma("sp", vfl.a[:], k.vf_scr[rows, :], [k.Bvf], [vfl], vfl)
                    k.transpose(vdT.a[0:32, :], s5.a[:], 128, 32, F32, [s5], [vdT])
                    p = k.mif.next()
                    k.pe(lambda e: e.matmul(p.a[:, 0:128], lhsT=vdT.a[:], rhs=vup.a[:], start=True, stop=True), [vdT, vup], [p])
                    k.act(lambda e: e.activation(out=nu.a[:], in_=p.a[:, 0:128], func=AF.Sigmoid), [p], [nu])
                    k.dve(lambda e: e.tensor_sub(out=rt2.a[:], in0=vfl.a[:], in1=s4.a[:, 256:384]), [vfl, s4], [rt2])
                    k.dve(lambda e: e.tensor_mul(out=rt2.a[:], in0=rt2.a[:], in1=nu.a[:]), [rt2, nu], [rt2])
                    k.dve(lambda e: e.tensor_add(out=vv.a[:], in0=rt2.a[:], in1=s4.a[:, 256:384]), [rt2, s4], [vv])
                k.dve(lambda e: e.tensor_mul(out=rt1.a[:], in0=s4.a[:, 0:128], in1=kmod.a[:]), [s4, kmod], [rt1])
                k.dve(lambda e: e.tensor_mul(out=rt1.a[:], in0=rt1.a[:], in1=vec.a[:, V_RK:V_RK + 128]), [rt1, vec], [rt1])
                k.dve(lambda e: e.tensor_reduce(out=bon.a[:], in_=rt1.a[:].rearrange("p (h d) -> p h d", h=2), axis=AX.X, op=ALU.add),
                      [rt1], [bon])
                if ti == k.dbg_tile:
                    k.dump(8, lw, lw.a[:], 128)
                    k.dump(9, av, av.a[:], 128)
                    k.dump(10, kkn, kkn.a[:], 128)
                    k.dump(11, kmod, kmod.a[:], 128)
                    k.dump(12, vv, vv.a[:], 128)
                if k.stage <= 6:
                    continue
                p = k.mif.next()
                k.pe(lambda e: e.matmul(p.a[:, 0:128], lhsT=k.triBD.a[:], rhs=lw.a[:], start=True, stop=True), [k.triBD, lw], [p])
                k.act(lambda e: e.activation(out=E1.a[:], in_=p.a[:, 0:128], func=AF.Exp), [p], [E1])
                k.act(lambda e: e.activation(out=E2.a[:], in_=p.a[:, 0:128], func=AF.Exp, scale=-1.0), [p], [E2])
                k.dve(lambda e: e.tensor_sub(out=E3.a[:], in0=p.a[:, 0:128], in1=lw.a[:]), [p, lw], [E3])
                k.act(lambda e: e.activation(out=E3.a[:], in_=E3.a[:], func=AF.Exp), [E3], [E3])
                for h in range(2):
                    pp = k.mif.next()
                    k.pe(lambda e: e.matmul(pp.a[0:64, 0:1], lhsT=lw.a[:, h * 64:(h + 1) * 64], rhs=k.triBD.a[:, 63:64], start=True, stop=True),
                         [lw, k.triBD], [pp])
                    k.pe(lambda e: e.matmul(pp.a[0:64, 1:2], lhsT=lw.a[:, h * 64:(h + 1) * 64], rhs=k.triBD.a[:, 127:128], start=True, stop=True),
                         [lw, k.triBD], [pp])
                    k.act(lambda e: e.activation(out=pC[h].a[:], in_=pp.a[0:64, 0:2], func=AF.Exp), [pp], [pC[h]])
                k.dve(lambda e: e.tensor_mul(out=qr_.a[:], in0=s4.a[:, 0:128], in1=E1.a[:]), [s4, E1], [qr_])
                k.dve(lambda e: e.tensor_mul(out=qk_.a[:], in0=kmod.a[:], in1=E2.a[:]), [kmod, E2], [qk_])
                k.dve(lambda e: e.tensor_mul(out=qa_.a[:], in0=kkn.a[:], in1=av.a[:]), [kkn, av], [qa_])
                k.dve(lambda e: e.tensor_mul(out=qa_.a[:], in0=qa_.a[:], in1=E2.a[:]), [qa_, E2], [qa_])
                k.dve(lambda e: e.scalar_tensor_tensor(out=qb_.a[:], in0=kkn.a[:], scalar=-1.0, in1=E3.a[:], op0=ALU.mult, op1=ALU.mult),
                      [kkn, E3], [qb_])
                k.dve(lambda e: e.tensor_copy(out=qa_hi.a[64:128, :], in_=qa_.a[64:128, :]), [qa_], [qa_hi])
                k.dve(lambda e: e.tensor_copy(out=qk_hi.a[64:128, :], in_=qk_.a[64:128, :]), [qk_], [qk_hi])
                def head_gen(h):
                    hc = slice(h * 64, (h + 1) * 64)
                    q4 = qT4[h]
                    for pair in range(2):
                        slot = k.mif.next()
                        for qi in range(2):
                            src = (qa_, qk_, qb_, qr_)[pair * 2 + qi]
                            k.pe(lambda e: e.transpose(out=slot.a[0:64, qi * 128:(qi + 1) * 128], in_=src.a[:, hc], identity=k.ident_f.a[:]),
                                 [src, k.ident_f], [slot])
                        k.evac(q4.a[:, pair * 2:pair * 2 + 2, :], slot.a[0:64, 0:256].rearrange("p (q c) -> p q c", q=2), [slot], [q4])
                        yield
                    k.dve(lambda e: e.tensor_copy(out=rlo[h].a[:, 0:64], in_=q4.a[:, 3, 0:64]), [q4], [rlo[h]])
                    k.dve(lambda e: e.tensor_copy(out=rhi[h].a[:, 64:128], in_=q4.a[:, 3, 64:128]), [q4], [rhi[h]])
                    aT, kT_, bT, rT = q4.a[:, 0, :], q4.a[:, 1, :], q4.a[:, 2, :], q4.a[:, 3, :]
                    brT = q4.a[:, 2:4, :].rearrange("p q c -> p (q c)")
                    slot = k.mif.next()
                    k.pe(lambda e: e.matmul(slot.a[:, 0:256], lhsT=aT, rhs=brT, start=True, stop=True), [q4], [slot])
                    k.dve(lambda e: e.tensor_mul(out=GA[h].a[:], in0=slot.a[:, 0:256], in1=k.mask2.a[:]), [slot, k.mask2], [GA[h]])
                    yield
                    slot = k.mif.next()
                    k.pe(lambda e: e.matmul(slot.a[:, 0:256], lhsT=kT_, rhs=brT, start=True, stop=True), [q4], [slot])
                    k.dve(lambda e: e.tensor_mul(out=GK[h].a[:], in0=slot.a[:, 0:256], in1=k.mask2.a[:]), [slot, k.mask2], [GK[h]])
                    yield
                    slot = k.mif.next()
                    k.pe(lambda e: e.matmul(slot.a[:, 0:128], lhsT=bT, rhs=aT, start=True, stop=True), [q4], [slot])
                    xt_ = XT[h].next()
                    k.dve(lambda e: e.tensor_mul(out=xt_.a[:], in0=slot.a[:, 0:128], in1=k.masksl.a[:]), [slot, k.masksl], [xt_])
                    yield
                    if k.stage <= 7:
                        return
                    Pc = PP[h].next()
                    k.dve(lambda e: e.tensor_add(out=Pc.a[:], in0=GA[h].a[:, 0:128], in1=k.ident_f.a[:]), [GA[h], k.ident_f], [Pc])
                    Xc_ap, Xc_b = GA[h].a[:, 0:128], GA[h]
                    XTc = xt_
                    for lev in range(5):
                        last = lev == 4
                        slot = k.mif.next()
                        k.pe(lambda e: e.matmul(slot.a[:, 0:128], lhsT=Xc_ap, rhs=XTc.a[:], start=True, stop=True), [Xc_b, XTc], [slot])
                        XTn = XT[h].next()
                        if not last:
                            slot2 = k.mif.next()
                            k.pe(lambda e: e.matmul(slot2.a[:, 0:128], lhsT=XTc.a[:], rhs=Xc_ap, start=True, stop=True), [Xc_b, XTc], [slot2])
                        k.evac(XTn.a[:], slot.a[:, 0:128], [slot], [XTn])
                        yield
                        if not last:
                            Xn = XX[h].next()
                            k.evac(Xn.a[:], slot2.a[:, 0:128], [slot2], [Xn])
                            yield
                        slot3 = k.mif.next()
                        k.pe(lambda e: e.matmul(slot3.a[:, 0:128], lhsT=XTn.a[:], rhs=Pc.a[:], start=True, stop=True), [XTn, Pc], [slot3])
                        Pn = PP[h].next()
                        k.dve(lambda e: e.tensor_add(out=Pn.a[:], in0=slot3.a[:, 0:128], in1=Pc.a[:]), [slot3, Pc], [Pn])
                        yield
                        Pc = Pn
                        XTc = XTn
                        if not last:
                            Xc_ap, Xc_b = Xn.a[:], Xn
                    TT = Pc
                    if k.stage <= 8:
                        return
                    M0 = Mcur[h]
                    W_, U_ = Wb[h], Ub[h]
                    slot = k.mif.next()
                    k.pe(lambda e: e.matmul(slot.a[0:64, 0:64], lhsT=q4.a[:, 2, 0:64], rhs=M0.a[:], start=True, stop=False), [q4, M0], [slot])
                    k.pe(lambda e: e.matmul(slot.a[0:64, 0:64], lhsT=GK[h].a[0:64, 0:64], rhs=vv.a[0:64, hc], start=False, stop=True),
                         [GK[h], vv], [slot])
                    k.evac(W_.a[0:64, :], slot.a[0:64, 0:64], [slot], [W_])
                    yield
                    slot = k.mif.next()
                    k.pe(lambda e: e.matmul(slot.a[0:64, 0:64], lhsT=TT.a[0:64, 0:64], rhs=W_.a[0:64, :], start=True, stop=True), [TT, W_], [slot])
                    k.evac(U_.a[0:64, :], slot.a[0:64, 0:64], [slot], [U_])
                    yield
                    slot = k.mif.next()
                    k.pe(lambda e: e.matmul(slot.a[0:64, 0:64], lhsT=qa_.a[0:64, hc], rhs=U_.a[0:64, :], start=True, stop=False), [qa_, U_], [slot])
                    k.pe(lambda e: e.matmul(slot.a[0:64, 0:64], lhsT=qk_.a[0:64, hc], rhs=vv.a[0:64, hc], start=False, stop=True), [qk_, vv], [slot])
                    M1 = Mst[h].next()
                    k.dve(lambda e: e.tensor_add(out=mtmp[h].a[:], in0=slot.a[0:64, 0:64], in1=M0.a[:]), [slot, M0], [mtmp[h]])
                    k.dve(lambda e: e.tensor_scalar(out=M1.a[:], in0=mtmp[h].a[:], scalar1=pC[h].a[:, 0:1], scalar2=None, op0=ALU.mult),
                          [mtmp[h], pC[h]], [M1])
                    yield
                    slot = k.mif.next()
                    k.pe(lambda e: e.matmul(slot.a[:, 0:64], lhsT=q4.a[:, 2, :], rhs=M1.a[:], start=True, stop=False), [q4, M1], [slot])
                    k.pe(lambda e: e.matmul(slot.a[:, 0:64], lhsT=GK[h].a[:, 0:128], rhs=vv.a[:, hc], start=False, stop=True),
                         [GK[h], vv], [slot])
                    k.evac(W_.a[64:128, :], slot.a[64:128, 0:64], [slot], [W_])
                    yield
                    slot = k.mif.next()
                    k.pe(lambda e: e.matmul(slot.a[:, 0:64], lhsT=TT.a[:, 0:128], rhs=W_.a[:], start=True, stop=True), [TT, W_], [slot])
                    k.evac(U_.a[64:128, :], slot.a[64:128, 0:64], [slot], [U_])
                    yield
                    slot = k.mif.next()
                    k.pe(lambda e: e.matmul(slot.a[0:64, 0:64], lhsT=qa_hi.a[:, hc], rhs=U_.a[:], start=True, stop=False), [qa_hi, U_], [slot])
                    k.pe(lambda e: e.matmul(slot.a[0:64, 0:64], lhsT=qk_hi.a[:, hc], rhs=vv.a[:, hc], start=False, stop=True), [qk_hi, vv], [slot])
                    M2 = Mst[h].next()
                    k.dve(lambda e: e.tensor_add(out=mtmp[h].a[:], in0=slot.a[0:64, 0:64], in1=M1.a[:]), [slot, M1], [mtmp[h]])
                    k.dve(lambda e: e.tensor_scalar(out=M2.a[:], in0=mtmp[h].a[:], scalar1=pC[h].a[:, 1:2], scalar2=None, op0=ALU.mult),
                          [mtmp[h], pC[h]], [M2])
                    yield
                    slot = k.mif.next()
                    k.pe(lambda e: e.matmul(slot.a[:, 0:64], lhsT=rlo[h].a[:], rhs=M0.a[:], start=True, stop=False), [rlo[h], M0], [slot])
                    k.pe(lambda e: e.matmul(slot.a[:, 0:64], lhsT=rhi[h].a[:], rhs=M1.a[:], start=False, stop=False), [rhi[h], M1], [slot])
                    k.pe(lambda e: e.matmul(slot.a[:, 0:64], lhsT=GA[h].a[:, 128:256], rhs=U_.a[:], start=False, stop=False), [GA[h], U_], [slot])
                    k.pe(lambda e: e.matmul(slot.a[:, 0:64], lhsT=GK[h].a[:, 128:256], rhs=vv.a[:, hc], start=False, stop=True), [GK[h], vv], [slot])
                    k.evac(yv.a[:, h, :], slot.a[:, 0:64], [slot], [yv])
                    yield
                    Mcur[h] = M2
                gens = [head_gen(0), head_gen(1)]
                alive = [True, True]
                while any(alive):
                    for gi in range(2):
                        if alive[gi]:
                            try:
                                next(gens[gi])
                            except StopIteration:
                                alive[gi] = False
                if ti == k.dbg_tile:
                    k.dump(13, yv, yv.a[:].rearrange("p h d -> p (h d)"), 128)
                    k.dump(18, GA[0], GA[0].a[:], 256)
                    k.dump(19, GK[0], GK[0].a[:], 256)
                if k.stage <= 9:
                    continue
                k.dve(lambda e: e.tensor_reduce(out=yst.a[:, 0:2], in_=yv.a[:], axis=AX.X, op=ALU.add), [yv], [yst])
                k.dve(lambda e: e.tensor_scalar_mul(out=yst.a[:, 0:2], in0=yst.a[:, 0:2], scalar1=1.0 / 64), [yst], [yst])
                for h in range(2):
                    k.dve(lambda e: e.tensor_scalar(out=yv.a[:, h, :], in0=yv.a[:, h, :], scalar1=yst.a[:, h:h + 1], scalar2=None, op0=ALU.subtract),
                          [yv, yst], [yv])
                k.dve(lambda e: e.tensor_mul(out=ysq.a[:], in0=yv.a[:], in1=yv.a[:]), [yv], [ysq])
                k.dve(lambda e: e.tensor_reduce(out=yst.a[:, 2:4], in_=ysq.a[:], axis=AX.X, op=ALU.add), [ysq], [yst])
                k.dve(lambda e: e.tensor_scalar(out=yst.a[:, 2:4], in0=yst.a[:, 2:4], scalar1=1.0 / 64, scalar2=GN_EPS, op0=ALU.mult, op1=ALU.add), [yst], [yst])
                k.act(lambda e: e.sqrt(out=yst.a[:, 2:4], in_=yst.a[:, 2:4]), [yst], [yst])
                k.dve(lambda e: e.reciprocal(out=yst.a[:, 2:4], in_=yst.a[:, 2:4]), [yst], [yst])
                for h in range(2):
                    hc = slice(h * 64, (h + 1) * 64)
                    k.dve(lambda e: e.scalar_tensor_tensor(out=yv.a[:, h, :], in0=yv.a[:, h, :], scalar=yst.a[:, 2 + h:3 + h],
                                                           in1=vec.a[:, V_LG + h * 64:V_LG + (h + 1) * 64], op0=ALU.mult, op1=ALU.mult),
                          [yv, yst, vec], [yv])
                    k.dve(lambda e: e.tensor_add(out=yv.a[:, h, :], in0=yv.a[:, h, :], in1=vec.a[:, V_LB + h * 64:V_LB + (h + 1) * 64]), [yv, vec], [yv])
                    k.dve(lambda e: e.scalar_tensor_tensor(out=yv.a[:, h, :], in0=vv.a[:, hc], scalar=bon.a[:, h:h + 1], in1=yv.a[:, h, :],
                                                           op0=ALU.mult, op1=ALU.add), [vv, bon, yv], [yv])
                if ti == k.dbg_tile:
                    k.dump(14, yv, yv.a[:].rearrange("p h d -> p (h d)"), 128)
                k.dve(lambda e: e.tensor_mul(out=ocb.a[:], in0=yv.a[:].rearrange("p h d -> p (h d)"), in1=sg_c.a[:]), [yv, sg_c], [ocb])
                oc = oTc.next()
                k.transpose(oc.a[:], ocb.a[:], 128, 128, BF16, [ocb], [oc])
                fw.dma("pool", k.oT_scr[2, :, cols], oc.a[:], [oc], [k.BoT], oc)

            for mixer in range(2 if k.stage > 10 else 0):
                og = ogm if mixer == 0 else ogf
                gate = sga if mixer == 0 else sgb
                Vr = Vm if mixer == 0 else Vf
                for h in range(2):
                    nkt = 4 * (ch + 1)
                    oa = k.oa
                    k.dve(lambda e: e.memset(oa.a[:], 0.0), [], [oa])
                    for kt in range(nkt):
                        dj = kt - 4 * ch
                        j0 = max(0, dj)
                        c0 = j0 * 128
                        kc = slice(kt * 128, (kt + 1) * 128)
                        s = k.sc.next()
                        if mixer == 0:
                            k.pe(lambda e: e.matmul(s.a[:, c0:512], lhsT=kTm[h].a[:, kc], rhs=qTm[h].a[:, c0:512], start=True, stop=True),
                                 [kTm[h], qTm[h]], [s])
                        else:
                            hp = slice(h * 64, (h + 1) * 64)
                            k.pe(lambda e: e.matmul(s.a[:, c0:512], lhsT=kTf.a[hp, kc], rhs=qTf.a[hp, c0:512], start=True, stop=True),
                                 [kTf, qTf], [s])
                        pT = pTr.next()
                        if mixer == 0:
                            k.act(lambda e: e.activation(out=pT.a[:, c0:512], in_=s.a[:, c0:512], func=AF.Exp), [s], [pT])
                        else:
                            ft = ftmp.next()
                            k.dve(lambda e: e.tensor_add(out=ft.a[:, c0:512], in0=s.a[:, c0:512], in1=cumbc.a[:, h, c0:512]), [s, cumbc], [ft])
                            k.act(lambda e: e.activation(out=pT.a[:, c0:512], in_=ft.a[:, c0:512], func=AF.Exp, bias=ncum.a[:, kt, h:h + 1]),
                                  [ft, ncum], [pT])
                        if dj >= 0:
                            k.pool(lambda e: e.affine_select(out=pT.a[:, c0:c0 + 128], in_=pT.a[:, c0:c0 + 128], pattern=[[1, 128]],
                                                             compare_op=ALU.is_ge, fill=0.0, base=0, channel_multiplier=-1), [pT], [pT])
                        for j in range(j0, 4):
                            k.pe(lambda e: e.matmul(oa.a[:, j, 0:65], lhsT=pT.a[:, j * 128:(j + 1) * 128], rhs=Vr.a[:, kt, h, :],
                                                    start=False, stop=(kt == 4 * ch + j), skip_group_check=True), [pT, Vr], [oa])
                    k.dve(lambda e: e.reciprocal(out=rinv.a[:], in_=oa.a[:, :, 64:65].rearrange("p j o -> p (j o)")), [oa], [rinv])
                    for j in range(4):
                        k.dve(lambda e: e.scalar_tensor_tensor(out=og.a[:, j, h * 64:(h + 1) * 64], in0=oa.a[:, j, 0:64], scalar=rinv.a[:, j:j + 1],
                                                               in1=gate.a[:, j, h * 64:(h + 1) * 64], op0=ALU.mult, op1=ALU.mult),
                              [oa, rinv, gate], [og])
                oT = oTr.next()
                for j in range(4):
                    k.transpose(oT.a[:, j * 128:(j + 1) * 128], og.a[:, j, :], 128, 128, BF16, [og], [oT])
                fw.dma("pool", k.oT_scr[mixer, :, ch * 512:(ch + 1) * 512], oT.a[:], [oT], [k.BoT], oT)

    def passB(self, es, l):
        k = self
        fw = k.fw
        dr = k.dr
        NT = k.NT
        gT = k.small_load(es, "gTb", [128, 8], dr["gT%d" % l])
        wG = k.sb(es, "wG", [128, 8, 3072], BF16)
        wp = k.sb(es, "wp", [128, 3, D], BF16)
        wo = k.sb(es, "wo", [128, 8, D], BF16)
        with contextlib.ExitStack() as est:
            stg_rot = Rot([k.sb(est, "stgb", [128, 8, 256], F32) for _ in range(2)])
            k.load_cast(wG, 0, dr["wG%d" % l], 3072, gT, stg_rot)
            k.load_cast(wo, 0, dr["wo%d" % l], D, None, stg_rot)
            for br in range(3):
                for c0 in range(0, D, 256):
                    stg = stg_rot.next()
                    fw.dma("sp", stg.a[:, 0, :], dr["wp%d" % l][br, :, c0:c0 + 256], [], [stg], stg)
                    k.dve(lambda e: e.tensor_copy(out=wp.a[:, br, c0:c0 + 256], in_=stg.a[:, 0, :]), [stg], [wp])
        fw.barrier()
        k.xrot = Rot([k.sb(es, "xtb", [128, D], F32) for _ in range(2)])
        k.rrot = Rot([k.sb(es, "rtb", [128, 512], F32) for _ in range(2)])
        k.xstat = Rot([k.sb(es, "xstb", [128, 4], F32) for _ in range(2)])
        k.hbrot = Rot([k.sb(es, "hbb", [128, D], BF16) for _ in range(2)])
        k.hTrot = Rot([k.sb(es, "hTb", [128, 8, 129], BF16) for _ in range(2)])
        oTl = Rot([k.sb(es, "oTl", [128, 3, 128], BF16) for _ in range(2)])
        gs = Rot([k.sb(es, "gs", [128, 512], F32) for _ in range(2)])
        mg = k.sb(es, "mg", [128, D], F32)
        mgt = Rot([k.sb(es, "mgt", [128, 512], F32) for _ in range(2)])
        mb = k.sb(es, "mb", [128, D], BF16)
        mT = k.sb(es, "mT", [128, 8, 128], BF16)
        po = Rot([k.sb(es, "po", [128, D], F32) for _ in range(2)])
        for ti in range(NT):
            rows = slice(ti * 128, (ti + 1) * 128)
            hT = k.load_h(l, ti, True)
            ot = oTl.next()
            for br in range(3):
                fw.dma("sp", ot.a[:, br, :], k.oT_scr[br, :, rows], [k.BoT], [ot], ot)
            for br in range(3):
                for half in range(2):
                    hc = slice(half * 512, (half + 1) * 512)
                    pg = k.pj.next()
                    for kk in range(8):
                        k.pe(lambda e: e.matmul(pg.a[:], lhsT=hT.a[:, kk, 1:129], rhs=wG.a[:, kk, br * D + half * 512:br * D + (half + 1) * 512],
                                                start=(kk == 0), stop=(kk == 7)), [hT, wG], [pg])
                    g = gs.next()
                    k.act(lambda e: e.activation(out=g.a[:], in_=pg.a[:], func=AF.Sigmoid), [pg], [g])
                    pb = k.sc.next()
                    k.pe(lambda e: e.matmul(pb.a[:], lhsT=ot.a[:, br, :], rhs=wp.a[:, br, hc], start=True, stop=True), [ot, wp], [pb])
                    if br == 0:
                        k.dve(lambda e: e.tensor_mul(out=mg.a[:, hc], in0=pb.a[:], in1=g.a[:]), [pb, g], [mg])
                    else:
                        t_ = mgt.next()
                        k.dve(lambda e: e.tensor_mul(out=t_.a[:], in0=pb.a[:], in1=g.a[:]), [pb, g], [t_])
                        k.pool(lambda e: e.tensor_add(out=mg.a[:, hc], in0=mg.a[:, hc], in1=t_.a[:]), [mg, t_], [mg])
            k.act(lambda e: e.activation(out=mb.a[:], in_=mg.a[:], func=AF.Copy), [mg], [mb])
            for half in range(2):
                slot = k.mib.next()
                for q in range(4):
                    kk = half * 4 + q
                    k.pe(lambda e: e.transpose(out=slot.a[:, q * 128:(q + 1) * 128], in_=mb.a[:, kk * 128:(kk + 1) * 128],
                                               identity=k.ident_b.a[:]), [mb, k.ident_b], [slot])
                k.evac(mT.a[:, half * 4:half * 4 + 4, :], slot.a[:, 0:512].rearrange("p (q c) -> p q c", q=4), [slot], [mT])
            pout = po.next()
            for half in range(2):
                hc = slice(half * 512, (half + 1) * 512)
                pp = k.pj.next()
                for kk in range(8):
                    k.pe(lambda e: e.matmul(pp.a[:], lhsT=mT.a[:, kk, :], rhs=wo.a[:, kk, hc], start=(kk == 0), stop=(kk == 7)), [mT, wo], [pp])
                k.evac(pout.a[:, hc], pp.a[:], [pp], [pout])
            if ti == k.dbg_tile and k.debug:
                for br in range(3):
                    fw.dma("sp", k.dbgb[br, :, 0:128], ot.a[:, br, :], [ot], [k.Bdbg], k.Bdbg)
                fw.dma("sp", k.dbgb[3, :, 0:512], wp.a[:, 0, 0:512], [wp], [k.Bdbg], k.Bdbg)
            if ti == k.dbg_tile:
                k.dump(22, pout, pout.a[:, 0:512], 512)
                k.dump(23, mg, mg.a[:, 0:512], 512)
            fw.dma("pool", k.part[l][rows, :], pout.a[:], [pout], [k.Bpart[l]], pout)

    def final(self, es, out):
        k = self
        fw = k.fw
        L = k.n_layers
        Sq = k.S // 4
        xr = Rot([k.sb(es, "xf", [128, D], F32) for _ in range(2)])
        rr = Rot([k.sb(es, "rf", [128, D], F32) for _ in range(2)])
        for i in range(Sq // 128):
            xt = xr.next()
            fw.dma("sp", xt.a[:], k.dr["xq"][i * 128:(i + 1) * 128, :], [], [xt], xt)
            for l in k.layers:
                rt = rr.next()
                fw.dma("sp", rt.a[:], k.rs[l][i * 128:(i + 1) * 128, :], [k.Brs[l]], [rt], rt)
                k.dve(lambda e: e.tensor_add(out=xt.a[:], in0=xt.a[:], in1=rt.a[:]), [xt, rt], [xt])
            fw.dma("pool", out[i * 128:(i + 1) * 128, :], xt.a[:], [xt], [k.Bout], xt)


IN_SIZES = (256, 128, 32, 512, 512, 512, 512, 8, 512, 1664, 512, 3072)
OFF = np.concatenate([[0], np.cumsum(IN_SIZES)]).tolist()
(O_CQ, O_CKV, O_KR, O_GA, O_FQ, O_FK, O_FV, O_FF, O_GB, O_SH, O_GC, O_MG) = OFF[:12]


def rope_cs(S):
    inv = (np.float32(10000.0) ** (-np.arange(0, 32, 2, dtype=np.float32) / np.float32(32))).astype(np.float32)
    ang = (np.arange(S, dtype=np.float32)[:, None] * inv[None, :]).astype(np.float32)
    c, s_ = np.cos(ang).astype(np.float32), np.sin(ang).astype(np.float32)
    return np.ascontiguousarray(np.concatenate([c, c, s_, s_], axis=1))


def core_inputs(inp, c, S, L, layers=None):
    f = lambda a: np.ascontiguousarray(a, dtype=np.float32)
    b, j = c // 4, c % 4
    hs = slice(128 * j, 128 * (j + 1))
    m = {"x": f(inp["x"][b, :S]), "xq": f(inp["x"][b, j * (S // 4):(j + 1) * (S // 4)]), "cs": rope_cs(S)}
    layers = list(range(L)) if layers is None else layers
    for l in layers:
        w = inp["w_in"][l]
        colsA = np.concatenate([
            np.arange(O_CQ, O_CQ + 416),
            np.arange(O_FF + 2 * j, O_FF + 2 * j + 2),
            np.arange(O_FQ + 128 * j, O_FQ + 128 * j + 128),
            np.arange(O_FK + 128 * j, O_FK + 128 * j + 128),
            np.arange(O_FV + 128 * j, O_FV + 128 * j + 128),
            np.arange(O_GB + 128 * j, O_GB + 128 * j + 128),
            np.arange(O_GA + 128 * j, O_GA + 128 * j + 128),
            np.arange(O_GC + 128 * j, O_GC + 128 * j + 128)])
        shl = np.concatenate([np.arange(128 * j, 128 * j + 128), np.arange(512 + 128 * j, 512 + 128 * j + 128),
                              np.arange(1024 + 128 * j, 1024 + 128 * j + 128), np.arange(1536, 1664)])
        m["wA%d" % l] = f(w[:, colsA])
        m["wS%d" % l] = f(w[:, O_SH + shl])
        m["wG%d" % l] = f(w[:, O_MG:O_MG + 3072])
        uq = inp["mla_w_uq"][l].reshape(256, 8, 96)[:, 2 * j:2 * j + 2].reshape(256, 192)
        m["wuq%d" % l] = f(uq)
        ukv = inp["mla_w_ukv"][l].reshape(128, 8, 128)[:, 2 * j:2 * j + 2]
        m["wukv%d" % l] = f(np.concatenate([ukv[:, 0, :64], ukv[:, 1, :64], ukv[:, 0, 64:], ukv[:, 1, 64:]], axis=1))
        m["wup%d" % l] = f(np.concatenate([inp["rwkv_w_up"][l][:, hs], inp["rwkv_w0"][l][None, hs]], axis=0))
        m["aup%d" % l] = f(np.concatenate([inp["rwkv_a_up"][l][:, hs], inp["rwkv_a0"][l][None, hs]], axis=0))
        qg = inp["mla_q_g"][l]
        vec = np.concatenate([
            qg, qg, inp["mla_knope_g"][l], inp["mla_knope_g"][l], inp["mla_krope_g"][l],
            inp["fox_q_g"][l], inp["fox_q_g"][l], inp["fox_k_g"][l], inp["fox_k_g"][l],
            inp["fox_b_f"][l][2 * j:2 * j + 2], inp["rwkv_k_k"][l][hs], inp["rwkv_k_a"][l][hs],
            inp["rwkv_r_k"][l].reshape(-1)[hs], inp["rwkv_lnx_g"][l][hs], inp["rwkv_lnx_b"][l][hs],
            inp["rwkv_mu"][l][shl]])
        assert vec.shape[0] == NV
        m["vec%d" % l] = f(vec[None, :])
        m["gT%d" % l] = f(inp["norm_g"][l].reshape(8, 128).T)
        m["qagT%d" % l] = f(inp["mla_qa_g"][l].reshape(2, 128).T)
        m["kvagT%d" % l] = f(inp["mla_kva_g"][l].reshape(1, 128).T)
        m["wp%d" % l] = f(np.stack([inp["w_pa"][l][hs], inp["w_pb"][l][hs], inp["w_pc"][l][hs]]))
        m["wo%d" % l] = f(inp["w_out"][l])
    if 1 in layers:
        w = inp["w_in"][1]
        m["wvT"] = f(w[:, O_SH + 1024:O_SH + 1536].T)
        m["muVT"] = f(inp["rwkv_mu"][1][1024:1536].reshape(4, 128).T)
        m["vdown"] = f(inp["rwkv_v_down"][0])
        m["vup"] = f(np.concatenate([inp["rwkv_v_up"][0][:, hs], inp["rwkv_v0"][0][None, hs]], axis=0))
    return m


_CACHE = {}


def run(inp, S, L):
    key = (S, L)
    if key not in _CACHE:
        _CACHE[key] = Kern(S, L).build()
    nc = _CACHE[key]
    in_maps = [core_inputs(inp, c, S, L) for c in range(8)]
    res = run_bass_kernel_spmd(nc, in_maps, core_ids=list(range(8)))
    global LAST
    LAST = res
    out = np.zeros((2, S, D), np.float32)
    q = S // 4
    for c in range(8):
        b, j = c // 4, c % 4
        out[b, j * q:(j + 1) * q] = res.results[c]["out"]
    return out


def run_split(inp, S):
    q = S // 4
    key = (S, "l0")
    if key not in _CACHE:
        _CACHE[key] = Kern(S, 1, layers=[0]).build()
    res0 = run_bass_kernel_spmd(_CACHE[key], [core_inputs(inp, c, S, 2, layers=[0]) for c in range(8)], core_ids=list(range(8)))
    x1 = np.zeros((2, S, D), np.float32)
    for c in range(8):
        x1[c // 4, (c % 4) * q:(c % 4 + 1) * q] = res0.results[c]["out"]
    key = (S, "l1")
    if key not in _CACHE:
        _CACHE[key] = Kern(S, 2, layers=[1]).build()
    inp1 = dict(inp)
    inp1["x"] = x1
    maps = []
    for c in range(8):
        m = core_inputs(inp1, c, S, 2, layers=[1])
        m["vf"] = np.ascontiguousarray(res0.results[c]["vf"])
        maps.append(m)
    res1 = run_bass_kernel_spmd(_CACHE[key], maps, core_ids=list(range(8)))
    out = np.zeros((2, S, D), np.float32)
    for c in range(8):
        out[c // 4, (c % 4) * q:(c % 4 + 1) * q] = res1.results[c]["out"]
    return out


def kernel(**inputs):
    inp = {k: np.asarray(v) for k, v in inputs.items()}
    return run(inp, inp["x"].shape[1], 2)
```

```python
import contextlib
import numpy as np
import concourse.bass as bass
import concourse.mybir as mybir
from concourse.bass_utils import run_bass_kernel_spmd

F32 = mybir.dt.float32
BF16 = mybir.dt.bfloat16
AF = mybir.ActivationFunctionType
ALU = mybir.AluOpType
AX = mybir.AxisListType
SEM_LIMIT = 30000

D = 1024
EPS = 1e-6
GN_EPS = 64e-5
DECAY = 0.606531
NA = 1186
V_QG2, V_KNG2, V_KRG, V_FQG2, V_FKG2, V_BF, V_KK, V_KA, V_RK, V_LG, V_LB, V_MU = (
    0, 192, 320, 352, 480, 608, 610, 738, 866, 994, 1122, 1250)
NV = 1762


class T:
    __slots__ = ("name", "w", "r", "nowaw", "stream")

    def __init__(self, name, nowaw=False):
        self.name = name
        self.w = {}
        self.r = {}
        self.nowaw = nowaw
        self.stream = None


class B:
    def __init__(self, a, name, nowaw=False, psum=False):
        self.a = a
        self.T = T(name, nowaw)
        self.psum = psum


class Rot:
    def __init__(self, items):
        self.items = items
        self.i = 0

    def next(self):
        x = self.items[self.i % len(self.items)]
        self.i += 1
        return x


class FW:
    ENG = ("pe", "dve", "act", "pool", "sp")

    def __init__(self, nc, es):
        self.nc = nc
        self.es = es
        self.e = {"pe": nc.tensor, "dve": nc.vector, "act": nc.scalar, "pool": nc.gpsimd, "sp": nc.sync}
        self.sem = {}
        self.cnt = {}
        self.cur = {}
        self.nsem = 0
        self.seen = {k: {} for k in self.ENG}
        self.free_dma = []
        self.used_dma = []
        for k in self.ENG:
            self._new_key(k)

    def _new_key(self, stream):
        key = "%s#%d" % (stream, self.nsem)
        self.sem[key] = self.es.enter_context(self.nc.semaphore("s%d" % self.nsem))
        self.nsem += 1
        self.cnt[key] = 0
        self.cur[stream] = key
        return key

    def _wait(self, eng, key, seq):
        if self.seen[eng].get(key, 0) >= seq:
            return
        self.seen[eng][key] = seq
        self.e[eng].wait_ge(self.sem[key], seq)

    def deps(self, eng, reads, writes):
        own = eng + "#"
        for b in reads:
            for k, s in b.T.w.items():
                if eng == "pe" and k.startswith(own):
                    continue
                self._wait(eng, k, s)
        for b in writes:
            t = b.T
            if not t.nowaw:
                for k, s in t.w.items():
                    if eng == "pe" and k.startswith(own):
                        continue
                    self._wait(eng, k, s)
            for k, s in t.r.items():
                if k.startswith(own):
                    continue
                self._wait(eng, k, s)

    def done(self, ins, stream, inc, reads, writes):
        key = self.cur[stream]
        if self.cnt[key] + inc > SEM_LIMIT:
            key = self._new_key(stream)
        self.cnt[key] += inc
        seq = self.cnt[key]
        ins.then_inc(self.sem[key], inc)
        for b in reads:
            b.T.r[key] = seq
        for b in writes:
            t = b.T
            if t.nowaw:
                t.w[key] = seq
            else:
                t.w = {key: seq}
                t.r = {}

    def op(self, eng, fn, R, W):
        pr = [b for b in R if b.psum]
        if pr:
            R = [b for b in R if not b.psum]
            W = list(W) + [b for b in pr if b not in W]
        self.deps(eng, R, W)
        ins = fn(self.e[eng])
        self.done(ins, eng, 1, R, W)

    def dma(self, issuer, out, in_, R, W, slot):
        t = slot.T
        if t.stream is None:
            self.nstream = getattr(self, "nstream", 0) + 1
            t.stream = "dma%d" % self.nstream
            if self.free_dma:
                self.cur[t.stream] = self.free_dma.pop()
            else:
                self._new_key(t.stream)
            self.used_dma.append(t.stream)
        self.deps(issuer, R, W)
        ins = self.e[issuer].dma_start(out=out, in_=in_)
        self.done(ins, t.stream, 16, R, W)

    def barrier(self):
        snap = {k: c for k, c in self.cnt.items() if c > 0}
        for eng in self.ENG:
            for k, c in snap.items():
                if k.startswith(eng + "#"):
                    continue
                self._wait(eng, k, c)
        keep = getattr(self, "keep", set())
        for st in self.used_dma:
            if st not in keep:
                self.free_dma.append(self.cur[st])
        self.used_dma = [st for st in self.used_dma if st in keep]


class StopBuild(Exception):
    pass


class Kern:
    stage = 99
    dve_only = False
    debug = False
    dbg_tile = 0

    def dump(self, idx, src_b, ap, n):
        if not self.debug:
            return
        self.fw.dma("sp", self.dbg[idx, :, 0:n], ap, [src_b], [self.Bdbg], self.Bdbg)
        self.fw.keep = {self.Bdbg.T.stream}

    def chk(self, st):
        if self.stage <= st:
            raise StopBuild()

    def __init__(self, S, n_layers, layers=None):
        self.S = S
        self.NT = S // 128
        self.NCH = S // 512
        self.layers = list(range(n_layers)) if layers is None else list(layers)
        self.n_layers = max(self.layers) + 1
        self.uid = 0

    def sb(self, es, name, shape, dt, nowaw=False):
        self.uid += 1
        a = es.enter_context(self.nc.sbuf_tensor("%s_%d" % (name, self.uid), shape, dt))
        return B(a, name, nowaw)

    def ps(self, es, name, shape, dt):
        self.uid += 1
        a = es.enter_context(self.nc.psum_tensor("%s_%d" % (name, self.uid), shape, dt))
        return B(a, name, psum=True)

    def dve(self, fn, R, W):
        self.fw.op("dve", fn, R, W)

    def act(self, fn, R, W):
        self.fw.op("act", fn, R, W)

    def pool(self, fn, R, W):
        self.fw.op("pool", fn, R, W)

    def pe(self, fn, R, W):
        self.fw.op("pe", fn, R, W)

    def evac(self, out, in_, R, W):
        self._ev = getattr(self, "_ev", 0) + 1
        if self._ev % 2 or self.dve_only:
            self.dve(lambda e: e.tensor_copy(out=out, in_=in_), R, W)
        else:
            self.act(lambda e: e.activation(out=out, in_=in_, func=AF.Copy), R, W)

    def transpose(self, out_ap, in_ap, np_in, nf_in, dt, R, W, evac_eng=None, scale=None):
        if dt == BF16:
            slot = self.mib.next()
            idn = self.ident_b
        else:
            slot = self.mif.next()
            idn = self.ident_f
        pv = slot.a[0:nf_in, 0:np_in]
        self.pe(lambda e: e.transpose(out=pv, in_=in_ap, identity=idn.a[0:np_in, 0:np_in]), R + [idn], [slot])
        self.evac(out_ap, pv, [slot], W)

    def build(self):
        S, NT = self.S, self.NT
        nc = bass.Bass("TRN2", target_bir_lowering=False)
        self.nc = nc
        L = self.n_layers
        dr = {}

        def din(name, shape):
            dr[name] = nc.dram_tensor(name, shape, F32, kind="ExternalInput").ap()

        din("x", [S, D])
        din("xq", [S // 4, D])
        din("cs", [S, 64])
        for l in self.layers:
            din("wA%d" % l, [D, NA])
            din("wS%d" % l, [D, 512])
            din("wG%d" % l, [D, 3072])
            din("wuq%d" % l, [256, 192])
            din("wukv%d" % l, [128, 256])
            din("wup%d" % l, [65, 128])
            din("aup%d" % l, [65, 128])
            din("vec%d" % l, [1, NV])
            din("gT%d" % l, [128, 8])
            din("qagT%d" % l, [128, 2])
            din("kvagT%d" % l, [128, 1])
            din("wp%d" % l, [3, 128, D])
            din("wo%d" % l, [D, D])
        if 1 in self.layers:
            din("wvT", [512, D])
            din("muVT", [128, 4])
            din("vdown", [512, 32])
            din("vup", [33, 128])
        out = nc.dram_tensor("out", [S // 4, D], F32, kind="ExternalOutput").ap()
        if self.debug:
            self.dbg = nc.dram_tensor("dbg", [24, 128, 512], F32, kind="ExternalOutput").ap()
            self.Bdbg = B(None, "dbg", nowaw=True)
            self.dbgb = nc.dram_tensor("dbgb", [4, 128, 512], BF16, kind="ExternalOutput").ap()
        self.dr = dr
        part = [nc.dram_tensor("part%d" % l, [S, D], F32).ap() for l in range(L)]
        red = [nc.dram_tensor("red%d" % l, [S, D], F32).ap() for l in range(L)]
        rs = [nc.dram_tensor("rs%d" % l, [S // 4, D], F32).ap() for l in range(L)]
        self.rs = rs
        self.Brs = [B(None, "rs%d" % l) for l in range(L)]
        oT_scr = nc.dram_tensor("oT_scr", [3, 128, S], BF16).ap()
        if 1 in self.layers and 0 not in self.layers:
            vf_scr = nc.dram_tensor("vf", [S, 128], F32, kind="ExternalInput").ap()
        elif self.layers == [0]:
            vf_scr = nc.dram_tensor("vf", [S, 128], F32, kind="ExternalOutput").ap()
        else:
            vf_scr = nc.dram_tensor("vf_scr", [S, 128], F32).ap()
        self.part, self.red, self.oT_scr, self.vf_scr = part, red, oT_scr, vf_scr
        self.Bpart = [B(None, "part%d" % l, nowaw=True) for l in range(L)]
        self.Bred = [B(None, "red%d" % l) for l in range(L)]
        self.BoT = B(None, "oTscr", nowaw=True)
        self.Bvf = B(None, "vfscr", nowaw=True)
        self.Bout = B(None, "out", nowaw=True)

        with contextlib.ExitStack() as es:
            self.fw = FW(nc, es)
            self.consts(es)
            self.pj = Rot([self.ps(es, "pj", [128, 512], F32) for _ in range(2)])
            self.sc = Rot([self.ps(es, "sc", [128, 512], F32) for _ in range(2)])
            self.oa = self.ps(es, "oa", [128, 4, 128], F32)
            self.mib = Rot([self.ps(es, "mib", [128, 1024], BF16)])
            self.mif = Rot([self.ps(es, "mif", [128, 512], F32) for _ in range(2)])
            for l in self.layers:
                with contextlib.ExitStack() as esA:
                    self.passA(esA, l)
                if self.stage <= 50:
                    self.fw.barrier()
                    self.layers = []
                    break
                self.fw.barrier()
                with contextlib.ExitStack() as esB:
                    self.passB(esB, l)
                self.fw.barrier()
                fw = self.fw
                groups = [[0, 1, 2, 3], [4, 5, 6, 7]]
                fw.deps("pool", [self.Bpart[l]], [self.Brs[l]])
                ins = nc.gpsimd.collective_compute("ReduceScatter", ALU.add, replica_groups=groups,
                                                   ins=[part[l]], outs=[rs[l]])
                fw.done(ins, "pool", 1, [self.Bpart[l]], [self.Brs[l]])
                if l < self.layers[-1]:
                    nchk = 8
                    rows = S // nchk
                    for c in range(nchk):
                        fw.deps("pool", [self.Bpart[l]], [self.Bred[l]])
                        ins = nc.gpsimd.collective_compute("AllReduce", ALU.add, replica_groups=groups,
                                                           ins=[part[l][c * rows:(c + 1) * rows, :]],
                                                           outs=[red[l][c * rows:(c + 1) * rows, :]])
                        fw.done(ins, "pool", 1, [self.Bpart[l]], [self.Bred[l]])
                self.fw.barrier()
            with contextlib.ExitStack() as esF:
                self.final(esF, out)
            self.fw.barrier()
        return nc

    def consts(self, es):
        k = self
        self.ident_f = k.sb(es, "identf", [128, 128], F32)
        self.ident_b = k.sb(es, "identb", [128, 128], BF16)
        self.triU = k.sb(es, "triU", [128, 128], F32)
        self.triBD = k.sb(es, "triBD", [128, 128], F32)
        self.sel127 = k.sb(es, "sel127", [128, 128], F32)
        self.ones_f = k.sb(es, "onesf", [128, 128], F32)
        self.mask2 = k.sb(es, "mask2", [128, 256], F32)
        self.masksl = k.sb(es, "masksl", [128, 128], F32)
        idf, idb = self.ident_f, self.ident_b
        k.pool(lambda e: e.memset(idf.a[:], 1.0), [], [idf])
        k.pool(lambda e: e.affine_select(out=idf.a[:], in_=idf.a[:], pattern=[[-1, 128]], compare_op=ALU.is_equal,
                                         fill=0.0, base=0, channel_multiplier=1), [idf], [idf])
        k.pool(lambda e: e.tensor_copy(out=idb.a[:], in_=idf.a[:]), [idf], [idb])
        tu = self.triU
        k.pool(lambda e: e.memset(tu.a[:], 1.0), [], [tu])
        k.pool(lambda e: e.affine_select(out=tu.a[:], in_=tu.a[:], pattern=[[1, 128]], compare_op=ALU.is_ge,
                                         fill=0.0, base=0, channel_multiplier=-1), [tu], [tu])
        tb = self.triBD
        k.pool(lambda e: e.tensor_copy(out=tb.a[:], in_=tu.a[:]), [tu], [tb])
        k.pool(lambda e: e.memset(tb.a[0:64, 64:128], 0.0), [tb], [tb])
        s1 = self.sel127
        k.pool(lambda e: e.memset(s1.a[:], 1.0), [], [s1])
        k.pool(lambda e: e.affine_select(out=s1.a[:], in_=s1.a[:], pattern=[[0, 128]], compare_op=ALU.is_ge,
                                         fill=0.0, base=-127, channel_multiplier=1), [s1], [s1])
        k.pool(lambda e: e.memset(self.ones_f.a[:], 1.0), [], [self.ones_f])
        m2 = self.mask2
        k.pool(lambda e: e.memset(m2.a[:, 0:128], 1.0), [], [m2])
        k.pool(lambda e: e.affine_select(out=m2.a[:, 0:128], in_=m2.a[:, 0:128], pattern=[[1, 128]], compare_op=ALU.is_gt,
                                         fill=0.0, base=0, channel_multiplier=-1), [m2], [m2])
        k.pool(lambda e: e.memset(m2.a[0:64, 64:128], 0.0), [m2], [m2])
        k.pool(lambda e: e.tensor_copy(out=m2.a[:, 128:256], in_=tb.a[:]), [tb, m2], [m2])
        ml = self.masksl
        k.pool(lambda e: e.memset(ml.a[:], 1.0), [], [ml])
        k.pool(lambda e: e.affine_select(out=ml.a[:], in_=ml.a[:], pattern=[[-1, 128]], compare_op=ALU.is_gt,
                                         fill=0.0, base=0, channel_multiplier=1), [ml], [ml])
        k.pool(lambda e: e.memset(ml.a[64:128, 0:64], 0.0), [ml], [ml])

    def load_cast(self, dst, dcol0, src_ap, ncols, gT, stg_rot, nk=8):
        k = self
        c0 = 0
        while c0 < ncols:
            cw = min(256, ncols - c0)
            stg = stg_rot.next()
            for kk in range(nk):
                k.fw.dma("sp", stg.a[:, kk, 0:cw], src_ap[kk * 128:(kk + 1) * 128, c0:c0 + cw], [], [stg], stg)
            for kk in range(nk):
                o = dst.a[:, kk, dcol0 + c0:dcol0 + c0 + cw]
                i = stg.a[:, kk, 0:cw]
                if gT is None:
                    if kk % 2:
                        k.pool(lambda e: e.tensor_copy(out=o, in_=i), [stg], [dst])
                    else:
                        k.dve(lambda e: e.tensor_copy(out=o, in_=i), [stg], [dst])
                else:
                    g = gT.a[:, kk:kk + 1]
                    if kk % 2:
                        k.pool(lambda e: e.tensor_scalar(out=o, in0=i, scalar1=g, scalar2=None, op0=ALU.mult), [stg, gT], [dst])
                    else:
                        k.dve(lambda e: e.tensor_scalar(out=o, in0=i, scalar1=g, scalar2=None, op0=ALU.mult), [stg, gT], [dst])
            c0 += cw

    def small_load(self, es, name, shape, src_ap):
        b = self.sb(es, name, shape, F32)
        self.fw.dma("sp", b.a[:], src_ap, [], [b], b)
        return b

    def load_h(self, l, ti, first, want_hb=False):
        k = self
        fw = k.fw
        xt = k.xrot.next()
        rows = slice(ti * 128, (ti + 1) * 128)
        fw.dma("sp", xt.a[:], k.dr["x"][rows, :], [], [xt], xt)
        for ll in [q for q in self.layers if q < l]:
            for half in range(2):
                rt = k.rrot.next()
                hc = slice(half * 512, (half + 1) * 512)
                fw.dma("sp", rt.a[:, 0:512], k.red[ll][rows, hc], [k.Bred[ll]], [rt], rt)
                k.dve(lambda e: e.tensor_add(out=xt.a[:, hc], in0=xt.a[:, hc], in1=rt.a[:, 0:512]), [xt, rt], [xt])
        if k.stage <= 1.1:
            return None
        hb = k.hbrot.next()
        junk, st = hb, k.xstat.next()
        k.act(lambda e: e.activation(out=junk.a[:], in_=xt.a[:], func=AF.Square, scale=float(D ** -0.5),
                                     accum_out=st.a[:, 0:1]), [xt], [junk, st])
        if k.stage <= 1.2:
            return None
        k.dve(lambda e: e.tensor_scalar_add(out=st.a[:, 1:2], in0=st.a[:, 0:1], scalar1=EPS), [st], [st])
        k.act(lambda e: e.sqrt(out=st.a[:, 1:2], in_=st.a[:, 1:2]), [st], [st])
        k.dve(lambda e: e.reciprocal(out=st.a[:, 2:3], in_=st.a[:, 1:2]), [st], [st])
        if k.stage <= 1.3:
            return None
        k.act(lambda e: e.activation(out=hb.a[:], in_=xt.a[:], func=AF.Copy, scale=st.a[:, 2:3]), [xt, st], [hb])
        if k.stage <= 1.4:
            return None
        hT = k.hTrot.next()
        for half in range(2):
            slot = k.mib.next()
            for q in range(4):
                kk = half * 4 + q
                k.pe(lambda e: e.transpose(out=slot.a[:, q * 128:(q + 1) * 128], in_=hb.a[:, kk * 128:(kk + 1) * 128],
                                           identity=k.ident_b.a[:]), [hb, k.ident_b], [slot])
            if k.stage <= 1.45:
                continue
            for q in range(4):
                k.evac(hT.a[:, half * 4 + q, 1:129], slot.a[:, q * 128:(q + 1) * 128], [slot], [hT])
        if k.stage <= 1.5:
            return None
        if first:
            k.dve(lambda e: e.memset(hT.a[:, :, 0:1], 0.0), [], [hT])
        else:
            prev = k.hT_prev
            k.dve(lambda e: e.tensor_copy(out=hT.a[:, :, 0:1], in_=prev.a[:, :, 128:129]), [prev], [hT])
        k.hT_prev = hT
        return hT

    def rstd_cols(self, st, n, kk_cols=()):
        k = self
        k.act(lambda e: e.sqrt(out=st.a[:, 0:n], in_=st.a[:, 0:n]), [st], [st])
        for c in kk_cols:
            k.dve(lambda e: e.tensor_scalar_max(out=st.a[:, c:c + 1], in0=st.a[:, c:c + 1], scalar1=1e-12), [st], [st])
        k.dve(lambda e: e.reciprocal(out=st.a[:, 0:n], in_=st.a[:, 0:n]), [st], [st])

    def passA(self, es, l):
        k = self
        fw = k.fw
        S, NT, NCH = k.S, k.NT, k.NCH
        dr = k.dr
        L1 = l > 0
        NS = 544 if L1 else 512
        vec = k.sb(es, "vec", [128, V_MU], F32)
        fw.dma("sp", vec.a[:], dr["vec%d" % l][:, 0:V_MU].partition_broadcast(128), [], [vec], vec)
        gT = k.small_load(es, "gT", [128, 8], dr["gT%d" % l])
        qagT = k.small_load(es, "qagT", [128, 2], dr["qagT%d" % l])
        kvagT = k.small_load(es, "kvagT", [128, 1], dr["kvagT%d" % l])
        wup = k.small_load(es, "wup", [65, 128], dr["wup%d" % l])
        aup = k.small_load(es, "aup", [65, 128], dr["aup%d" % l])
        wA = k.sb(es, "wA", [128, 8, NA], BF16)
        wS1 = k.sb(es, "wS1", [128, 8, NS], BF16)
        wS2 = k.sb(es, "wS2", [128, 8, NS], BF16)
        wuq = k.sb(es, "wuq", [128, 2, 192], BF16)
        wukv = k.sb(es, "wukv", [128, 1, 256], BF16)
        with contextlib.ExitStack() as est:
            stg_rot = Rot([k.sb(est, "stg", [128, 8, 256], F32) for _ in range(2)])
            k.load_cast(wA, 0, dr["wA%d" % l], NA, gT, stg_rot)
            k.load_cast(wuq, 0, dr["wuq%d" % l], 192, qagT, stg_rot, nk=2)
            k.load_cast(wukv, 0, dr["wukv%d" % l], 256, kvagT, stg_rot, nk=1)
            muv = k.sb(est, "muv", [128, 512], F32)
            fw.dma("sp", muv.a[:], dr["vec%d" % l][:, V_MU:V_MU + 512].partition_broadcast(128), [], [muv], muv)
            tmp = k.sb(est, "wtmp", [128, 256], F32)
            tmp2 = k.sb(est, "wtmp2", [128, 256], F32)
            for c0 in (0, 256):
                stg = stg_rot.next()
                for kk in range(8):
                    fw.dma("sp", stg.a[:, kk, :], dr["wS%d" % l][kk * 128:(kk + 1) * 128, c0:c0 + 256], [], [stg], stg)
                mu = muv.a[:, c0:c0 + 256]
                for kk in range(8):
                    g = gT.a[:, kk:kk + 1]
                    k.dve(lambda e: e.tensor_mul(out=tmp.a[:], in0=stg.a[:, kk, :], in1=mu), [stg, muv], [tmp])
                    k.dve(lambda e: e.tensor_scalar(out=wS2.a[:, kk, c0:c0 + 256], in0=tmp.a[:], scalar1=g, scalar2=None,
                                                    op0=ALU.mult), [tmp, gT], [wS2])
                    k.dve(lambda e: e.tensor_sub(out=tmp2.a[:], in0=stg.a[:, kk, :], in1=tmp.a[:]), [stg, tmp], [tmp2])
                    k.dve(lambda e: e.tensor_scalar(out=wS1.a[:, kk, c0:c0 + 256], in0=tmp2.a[:], scalar1=g, scalar2=None,
                                                    op0=ALU.mult), [tmp2, gT], [wS1])
            if L1:
                muVT = k.small_load(est, "muVT", [128, 4], dr["muVT"])
                vdn = k.sb(est, "vdn", [128, 4, 32], F32)
                for kc in range(4):
                    fw.dma("sp", vdn.a[:, kc, :], dr["vdown"][kc * 128:(kc + 1) * 128, :], [], [vdn], vdn)
                wv = k.sb(est, "wv", [128, 4, 128], F32)
                wv1 = k.sb(est, "wv1", [128, 4, 128], F32)
                wv2 = k.sb(est, "wv2", [128, 4, 128], F32)
                for dc in range(8):
                    for kc in range(4):
                        fw.dma("sp", wv.a[:, kc, :], dr["wvT"][kc * 128:(kc + 1) * 128, dc * 128:(dc + 1) * 128], [], [wv], wv)
                    for kc in range(4):
                        k.dve(lambda e: e.tensor_scalar(out=wv2.a[:, kc, :], in0=wv.a[:, kc, :], scalar1=muVT.a[:, kc:kc + 1],
                                                        scalar2=None, op0=ALU.mult), [wv, muVT], [wv2])
                    k.dve(lambda e: e.tensor_sub(out=wv1.a[:], in0=wv.a[:], in1=wv2.a[:]), [wv, wv2], [wv1])
                    for (wsrc, wdst) in ((wv1, wS1), (wv2, wS2)):
                        slot = k.mif.next()
                        for kc in range(4):
                            k.pe(lambda e: e.matmul(slot.a[:, 0:32], lhsT=wsrc.a[:, kc, :], rhs=vdn.a[:, kc, :],
                                                    start=(kc == 0), stop=(kc == 3)), [wsrc, vdn], [slot])
                        k.dve(lambda e: e.tensor_scalar(out=wdst.a[:, dc, 512:544], in0=slot.a[:, 0:32], scalar1=gT.a[:, dc:dc + 1],
                                                        scalar2=None, op0=ALU.mult), [slot, gT], [wdst])
        fw.barrier()
        if k.stage <= 1:
            return
        if L1:
            vup = k.small_load(es, "vup", [33, 128], dr["vup"])
        kTm = [k.sb(es, "kTm%d" % h, [96, S], BF16) for h in range(2)]
        kTf = k.sb(es, "kTf", [128, S], BF16)
        Vm = k.sb(es, "Vm", [128, NT, 2, 65], BF16)
        Vf = k.sb(es, "Vf", [128, NT, 2, 65], BF16)
        ncum = k.sb(es, "ncum", [128, NT, 2], F32)
        k.xrot = Rot([k.sb(es, "xt", [128, D], F32) for _ in range(2)])
        k.xstat = Rot([k.sb(es, "xst", [128, 4], F32) for _ in range(2)])
        k.hbrot = Rot([k.sb(es, "hb", [128, D], BF16) for _ in range(1)])
        k.hTrot = Rot([k.sb(es, "hT", [128, 8, 129], BF16) for _ in range(2)])
        csr = Rot([k.sb(es, "cs", [128, 64], F32) for _ in range(2)])
        s1r = Rot([k.sb(es, "s1", [128, 418], F32) for _ in range(1)])
        s2r = Rot([k.sb(es, "s2", [128, 384], F32) for _ in range(1)])
        s4r = Rot([k.sb(es, "s4", [128, 512], F32) for _ in range(1)])
        s5r = Rot([k.sb(es, "s5", [128, 32], F32) for _ in range(2)])
        sga = k.sb(es, "sga", [128, 4, 128], BF16)
        sgb = k.sb(es, "sgb", [128, 4, 128], BF16)
        sgc = Rot([k.sb(es, "sgc", [128, 128], F32) for _ in range(1)])
        sq = k.sb(es, "sq", [128, 512], F32)
        k.rrot = Rot([sq])
        stA = Rot([k.sb(es, "stA", [128, 16], F32) for _ in range(2)])
        stB = Rot([k.sb(es, "stB", [128, 8], F32) for _ in range(2)])
        cqn = k.sb(es, "cqn", [128, 384], BF16)
        cT = k.sb(es, "cT", [128, 3, 128], BF16)
        qk = k.sb(es, "qk", [128, 448], F32)
        qn = k.sb(es, "qn", [128, 2, 96], F32)
        kn = k.sb(es, "kn", [128, 2, 96], F32)
        rp = k.sb(es, "rp", [128, 4, 32], F32)
        qb = k.sb(es, "qb", [128, 2, 96], BF16)
        kb = k.sb(es, "kb", [128, 2, 96], BF16)
        fqb = k.sb(es, "fqb", [128, 128], BF16)
        fkb = k.sb(es, "fkb", [128, 128], BF16)
        qTm = [k.sb(es, "qTm%d" % h, [96, 512], BF16) for h in range(2)]
        qTf = k.sb(es, "qTf", [128, 512], BF16)
        lf = Rot([k.sb(es, "lf", [128, 4], F32) for _ in range(2)])
        cumr = Rot([k.sb(es, "cum", [128, 2], F32) for _ in range(2)])
        dcum = k.sb(es, "dcum", [128, 2, 128], F32)
        cumbc = k.sb(es, "cumbc", [128, 2, 512], F32)
        pTr = Rot([k.sb(es, "pT", [128, 512], BF16) for _ in range(2)])
        ftmp = Rot([k.sb(es, "ftmp", [128, 512], F32) for _ in range(1)])
        rinv = k.sb(es, "rinv", [128, 4], F32)
        ogm = k.sb(es, "ogm", [128, 4, 128], BF16)
        ogf = k.sb(es, "ogf", [128, 4, 128], BF16)
        oTr = Rot([k.sb(es, "oT", [128, 512], BF16) for _ in range(1)])
        ocb = k.sb(es, "ocb", [128, 128], BF16)
        oTc = Rot([k.sb(es, "oTc", [128, 128], BF16) for _ in range(2)])
        twT = k.sb(es, "twT", [65, 128], F32)
        alT = k.sb(es, "alT", [65, 128], F32)
        k.pool(lambda e: e.memset(twT.a[64:65, :], 1.0), [], [twT])
        k.pool(lambda e: e.memset(alT.a[64:65, :], 1.0), [], [alT])
        tw = k.sb(es, "tw", [128, 64], F32)
        lw = k.sb(es, "lw", [128, 128], F32)
        av = k.sb(es, "av", [128, 128], F32)
        kkn = k.sb(es, "kkn", [128, 128], F32)
        kmod = k.sb(es, "kmod", [128, 128], F32)
        vv = k.sb(es, "vv", [128, 128], F32)
        rt1 = k.sb(es, "rt1", [128, 128], F32)
        rt2 = k.sb(es, "rt2", [128, 128], F32)
        bon = k.sb(es, "bon", [128, 2], F32)
        E1 = k.sb(es, "E1", [128, 128], F32)
        E2 = k.sb(es, "E2", [128, 128], F32)
        E3 = k.sb(es, "E3", [128, 128], F32)
        qa_ = k.sb(es, "qalpha", [128, 128], F32)
        qk_ = k.sb(es, "qkt", [128, 128], F32)
        qb_ = k.sb(es, "qbeta", [128, 128], F32)
        qr_ = k.sb(es, "qr", [128, 128], F32)
        qa_hi = k.sb(es, "qahi", [128, 128], F32)
        qk_hi = k.sb(es, "qkhi", [128, 128], F32)
        k.pool(lambda e: e.memset(qa_hi.a[:], 0.0), [], [qa_hi])
        k.pool(lambda e: e.memset(qk_hi.a[:], 0.0), [], [qk_hi])
        qT4 = [k.sb(es, "qT4_%d" % h, [64, 4, 128], F32) for h in range(2)]
        rlo = [k.sb(es, "rlo%d" % h, [64, 128], F32) for h in range(2)]
        rhi = [k.sb(es, "rhi%d" % h, [64, 128], F32) for h in range(2)]
        for h in range(2):
            k.pool(lambda e: e.memset(rlo[h].a[:], 0.0), [], [rlo[h]])
            k.pool(lambda e: e.memset(rhi[h].a[:], 0.0), [], [rhi[h]])
        GA = [k.sb(es, "GA%d" % h, [128, 256], F32) for h in range(2)]
        GK = [k.sb(es, "GK%d" % h, [128, 256], F32) for h in range(2)]
        XT = [Rot([k.sb(es, "XT%d_%d" % (h, i), [128, 128], F32) for i in range(2)]) for h in range(2)]
        XX = [Rot([k.sb(es, "XX%d_%d" % (h, i), [128, 128], F32) for i in range(2)]) for h in range(2)]
        PP = [Rot([k.sb(es, "PP%d_%d" % (h, i), [128, 128], F32) for i in range(2)]) for h in range(2)]
        Wb = [k.sb(es, "Wb%d" % h, [128, 64], F32) for h in range(2)]
        Ub = [k.sb(es, "Ub%d" % h, [128, 64], F32) for h in range(2)]
        Mst = [Rot([k.sb(es, "M%d_%d" % (h, i), [64, 64], F32) for i in range(3)]) for h in range(2)]
        pC = [k.sb(es, "pC%d" % h, [64, 2], F32) for h in range(2)]
        mtmp = [k.sb(es, "mtmp%d" % h, [64, 64], F32) for h in range(2)]
        yv = k.sb(es, "yv", [128, 2, 64], F32)
        ysq = k.sb(es, "ysq", [128, 2, 64], F32)
        yst = k.sb(es, "yst", [128, 8], F32)
        vdT = k.sb(es, "vdT", [33, 128], F32)
        k.pool(lambda e: e.memset(vdT.a[32:33, :], 1.0), [], [vdT])
        vfl = k.sb(es, "vfl", [128, 128], F32)
        nu = rt1
        Mcur = []
        for h in range(2):
            m0 = Mst[h].next()
            k.pool(lambda e: e.memset(m0.a[:], 0.0), [], [m0])
            Mcur.append(m0)
        cum_prev = None
        qscale_m = float(96 ** -0.5)
        qscale_f = float(64 ** -0.5)

        for ch in range(NCH):
            for jq in range(4):
                ti = ch * 4 + jq
                rows = slice(ti * 128, (ti + 1) * 128)
                cols = slice(ti * 128, (ti + 1) * 128)
                ccols = slice(jq * 128, (jq + 1) * 128)
                if ti == 0:
                    hT_nextA = k.load_h(l, 0, True)
                hT = hT_nextA
                if k.stage <= 2:
                    continue
                cs = csr.next()
                fw.dma("sp", cs.a[:], dr["cs"][rows, :], [], [cs], cs)
                s1, s2, s4, s5 = s1r.next(), s2r.next(), s4r.next(), s5r.next()
                sg_c = sgc.next()
                p = k.pj.next()
                for kk in range(8):
                    k.pe(lambda e: e.matmul(p.a[:, 0:418], lhsT=hT.a[:, kk, 1:129], rhs=wA.a[:, kk, 0:418],
                                            start=(kk == 0), stop=(kk == 7)), [hT, wA], [p])
                k.evac(s1.a[:], p.a[:, 0:418], [p], [s1])
                p = k.pj.next()
                for kk in range(8):
                    k.pe(lambda e: e.matmul(p.a[:, 0:512], lhsT=hT.a[:, kk, 1:129], rhs=wA.a[:, kk, 418:930],
                                            start=(kk == 0), stop=(kk == 7)), [hT, wA], [p])
                k.dve(lambda e: e.tensor_copy(out=s2.a[:], in_=p.a[:, 0:384]), [p], [s2])
                k.act(lambda e: e.activation(out=sgb.a[:, jq, :], in_=p.a[:, 384:512], func=AF.Silu), [p], [sgb])
                p = k.pj.next()
                for kk in range(8):
                    k.pe(lambda e: e.matmul(p.a[:, 0:256], lhsT=hT.a[:, kk, 1:129], rhs=wA.a[:, kk, 930:1186],
                                            start=(kk == 0), stop=(kk == 7)), [hT, wA], [p])
                k.act(lambda e: e.activation(out=sga.a[:, jq, :], in_=p.a[:, 0:128], func=AF.Silu), [p], [sga])
                k.act(lambda e: e.activation(out=sg_c.a[:], in_=p.a[:, 128:256], func=AF.Silu), [p], [sg_c])
                p = k.pj.next()
                for kk in range(8):
                    k.pe(lambda e: e.matmul(p.a[:, 0:512], lhsT=hT.a[:, kk, 1:129], rhs=wS1.a[:, kk, 0:512],
                                            start=(kk == 0), stop=False), [hT, wS1], [p])
                for kk in range(8):
                    k.pe(lambda e: e.matmul(p.a[:, 0:512], lhsT=hT.a[:, kk, 0:128], rhs=wS2.a[:, kk, 0:512],
                                            start=False, stop=(kk == 7)), [hT, wS2], [p])
                k.evac(s4.a[:], p.a[:, 0:512], [p], [s4])
                if L1:
                    p = k.mif.next()
                    for kk in range(8):
                        k.pe(lambda e: e.matmul(p.a[:, 0:32], lhsT=hT.a[:, kk, 1:129], rhs=wS1.a[:, kk, 512:544],
                                                start=(kk == 0), stop=False), [hT, wS1], [p])
                    for kk in range(8):
                        k.pe(lambda e: e.matmul(p.a[:, 0:32], lhsT=hT.a[:, kk, 0:128], rhs=wS2.a[:, kk, 512:544],
                                                start=False, stop=(kk == 7)), [hT, wS2], [p])
                    k.evac(s5.a[:], p.a[:, 0:32], [p], [s5])
                if ti + 1 < NT:
                    hT_nextA = k.load_h(l, ti + 1, False)
                if ti == k.dbg_tile:
                    k.dump(0, s1, s1.a[:], 418)
                    k.dump(1, s2, s2.a[:], 384)
                    k.dump(2, s4, s4.a[:], 512)
                    k.dump(15, sga, sga.a[:, jq, :], 128)
                if k.stage <= 3:
                    continue
                st = stA.next()
                k.dve(lambda e: e.tensor_mul(out=sq.a[:, 0:416], in0=s1.a[:, 0:416], in1=s1.a[:, 0:416]), [s1], [sq])
                k.dve(lambda e: e.tensor_reduce(out=st.a[:, 0:1], in_=sq.a[:, 0:256], axis=AX.X, op=ALU.add), [sq], [st])
                k.dve(lambda e: e.tensor_reduce(out=st.a[:, 1:2], in_=sq.a[:, 256:384], axis=AX.X, op=ALU.add), [sq], [st])
                k.dve(lambda e: e.tensor_reduce(out=st.a[:, 2:3], in_=sq.a[:, 384:416], axis=AX.X, op=ALU.add), [sq], [st])
                k.dve(lambda e: e.tensor_mul(out=sq.a[:, 0:256], in0=s2.a[:, 0:256], in1=s2.a[:, 0:256]), [s2, st], [sq])
                k.dve(lambda e: e.tensor_reduce(out=st.a[:, 3:7], in_=sq.a[:, 0:256].rearrange("p (h d) -> p h d", h=4),
                                                axis=AX.X, op=ALU.add), [sq], [st])
                k.dve(lambda e: e.tensor_mul(out=kkn.a[:], in0=s4.a[:, 128:256], in1=vec.a[:, V_KK:V_KK + 128]), [s4, vec], [kkn])
                k.dve(lambda e: e.tensor_mul(out=sq.a[:, 256:384], in0=kkn.a[:], in1=kkn.a[:]), [kkn], [sq])
                k.dve(lambda e: e.tensor_reduce(out=st.a[:, 7:9], in_=sq.a[:, 256:384].rearrange("p (h d) -> p h d", h=2),
                                                axis=AX.X, op=ALU.add), [sq], [st])
                k.dve(lambda e: e.tensor_scalar(out=st.a[:, 0:1], in0=st.a[:, 0:1], scalar1=1.0 / 256, scalar2=EPS, op0=ALU.mult, op1=ALU.add), [st], [st])
                k.dve(lambda e: e.tensor_scalar(out=st.a[:, 1:2], in0=st.a[:, 1:2], scalar1=1.0 / 128, scalar2=EPS, op0=ALU.mult, op1=ALU.add), [st], [st])
                k.dve(lambda e: e.tensor_scalar(out=st.a[:, 2:3], in0=st.a[:, 2:3], scalar1=1.0 / 32, scalar2=EPS, op0=ALU.mult, op1=ALU.add), [st], [st])
                k.dve(lambda e: e.tensor_scalar(out=st.a[:, 3:7], in0=st.a[:, 3:7], scalar1=1.0 / 64, scalar2=EPS, op0=ALU.mult, op1=ALU.add), [st], [st])
                k.rstd_cols(st, 9, kk_cols=(7, 8))
                k.act(lambda e: e.activation(out=cqn.a[:, 0:256], in_=s1.a[:, 0:256], func=AF.Copy, scale=st.a[:, 0:1]), [s1, st], [cqn])
                k.act(lambda e: e.activation(out=cqn.a[:, 256:384], in_=s1.a[:, 256:384], func=AF.Copy, scale=st.a[:, 1:2]), [s1, st], [cqn])
                for c in range(3):
                    k.transpose(cT.a[:, c, :], cqn.a[:, c * 128:(c + 1) * 128], 128, 128, BF16, [cqn], [cT])
                p = k.pj.next()
                for c in range(2):
                    k.pe(lambda e: e.matmul(p.a[:, 0:192], lhsT=cT.a[:, c, :], rhs=wuq.a[:, c, :], start=(c == 0), stop=(c == 1)),
                         [cT, wuq], [p])
                k.pe(lambda e: e.matmul(p.a[:, 192:448], lhsT=cT.a[:, 2, :], rhs=wukv.a[:, 0, :], start=True, stop=True),
                     [cT, wukv], [p])
                k.evac(qk.a[:], p.a[:, 0:448], [p], [qk])
                sb_ = stB.next()
                k.dve(lambda e: e.tensor_mul(out=sq.a[:, 0:320], in0=qk.a[:, 0:320], in1=qk.a[:, 0:320]), [qk], [sq])
                k.dve(lambda e: e.tensor_reduce(out=sb_.a[:, 0:2], in_=sq.a[:, 0:192].rearrange("p (h d) -> p h d", h=2),
                                                axis=AX.X, op=ALU.add), [sq], [sb_])
                k.dve(lambda e: e.tensor_reduce(out=sb_.a[:, 2:4], in_=sq.a[:, 192:320].rearrange("p (h d) -> p h d", h=2),
                                                axis=AX.X, op=ALU.add), [sq], [sb_])
                k.dve(lambda e: e.tensor_scalar(out=sb_.a[:, 0:2], in0=sb_.a[:, 0:2], scalar1=1.0 / 96, scalar2=EPS, op0=ALU.mult, op1=ALU.add), [sb_], [sb_])
                k.dve(lambda e: e.tensor_scalar(out=sb_.a[:, 2:4], in0=sb_.a[:, 2:4], scalar1=1.0 / 64, scalar2=EPS, op0=ALU.mult, op1=ALU.add), [sb_], [sb_])
                k.rstd_cols(sb_, 4)
                for h in range(2):
                    k.dve(lambda e: e.scalar_tensor_tensor(out=qn.a[:, h, :], in0=qk.a[:, h * 96:(h + 1) * 96], scalar=sb_.a[:, h:h + 1],
                                                           in1=vec.a[:, V_QG2 + h * 96:V_QG2 + (h + 1) * 96], op0=ALU.mult, op1=ALU.mult),
                          [qk, sb_, vec], [qn])
                    k.dve(lambda e: e.scalar_tensor_tensor(out=kn.a[:, h, 0:64], in0=qk.a[:, 192 + h * 64:192 + (h + 1) * 64],
                                                           scalar=sb_.a[:, 2 + h:3 + h],
                                                           in1=vec.a[:, V_KNG2 + h * 64:V_KNG2 + (h + 1) * 64], op0=ALU.mult, op1=ALU.mult),
                          [qk, sb_, vec], [kn])
                    k.dve(lambda e: e.scalar_tensor_tensor(out=kn.a[:, h, 64:96], in0=s1.a[:, 384:416], scalar=st.a[:, 2:3],
                                                           in1=vec.a[:, V_KRG:V_KRG + 32], op0=ALU.mult, op1=ALU.mult),
                          [s1, st, vec], [kn])
                cosv = cs.a[:, 0:32].rearrange("p (h d) -> p h d", h=2)
                sinv = cs.a[:, 32:64].rearrange("p (h d) -> p h d", h=2)
                for (src, dstb, scl) in ((qn, qb, qscale_m), (kn, kb, 1.0)):
                    x1 = src.a[:, :, 64:80]
                    x2 = src.a[:, :, 80:96]
                    r1 = rp.a[:, 0:1, :].rearrange("p a (h d) -> p (a h) d", h=2)
                    r2 = rp.a[:, 1:2, :].rearrange("p a (h d) -> p (a h) d", h=2)
                    r3 = rp.a[:, 2:3, :].rearrange("p a (h d) -> p (a h) d", h=2)
                    r4 = rp.a[:, 3:4, :].rearrange("p a (h d) -> p (a h) d", h=2)
                    k.dve(lambda e: e.tensor_mul(out=r1, in0=x1, in1=cosv), [src, cs], [rp])
                    k.dve(lambda e: e.tensor_mul(out=r2, in0=x2, in1=sinv), [src, cs, rp], [rp])
                    k.dve(lambda e: e.tensor_mul(out=r3, in0=x2, in1=cosv), [src, cs, rp], [rp])
                    k.dve(lambda e: e.tensor_mul(out=r4, in0=x1, in1=sinv), [src, cs, rp], [rp])
                    k.dve(lambda e: e.tensor_sub(out=src.a[:, :, 64:80], in0=r1, in1=r2), [rp, src], [src])
                    k.dve(lambda e: e.tensor_add(out=src.a[:, :, 80:96], in0=r3, in1=r4), [rp, src], [src])
                    k.act(lambda e: e.activation(out=dstb.a[:], in_=src.a[:], func=AF.Copy, scale=scl), [src], [dstb])
                for h in range(2):
                    k.transpose(qTm[h].a[:, ccols], qb.a[:, h, :], 128, 96, BF16, [qb], [qTm[h]])
                    k.transpose(kTm[h].a[:, cols], kb.a[:, h, :], 128, 96, BF16, [kb], [kTm[h]])
                k.dve(lambda e: e.tensor_copy(out=Vm.a[:, ti, :, 0:64], in_=qk.a[:, 320:448].rearrange("p (h d) -> p h d", h=2)), [qk], [Vm])
                k.dve(lambda e: e.tensor_copy(out=Vm.a[:, ti, :, 64:65], in_=k.ones_f.a[:, 0:2].rearrange("p (h o) -> p h o", o=1)), [k.ones_f], [Vm])
                if ti == k.dbg_tile:
                    k.dump(3, qk, qk.a[:], 448)
                    k.dump(4, qn, qn.a[:].rearrange("p h d -> p (h d)"), 192)
                    k.dump(5, kn, kn.a[:].rearrange("p h d -> p (h d)"), 192)
                    k.dump(16, st, st.a[:], 16)
                if k.stage <= 4:
                    continue
                for h in range(2):
                    k.dve(lambda e: e.scalar_tensor_tensor(out=sq.a[:, h * 64:(h + 1) * 64], in0=s2.a[:, h * 64:(h + 1) * 64],
                                                           scalar=st.a[:, 3 + h:4 + h], in1=vec.a[:, V_FQG2 + h * 64:V_FQG2 + (h + 1) * 64],
                                                           op0=ALU.mult, op1=ALU.mult), [s2, st, vec, sq], [sq])
                    k.dve(lambda e: e.scalar_tensor_tensor(out=sq.a[:, 128 + h * 64:128 + (h + 1) * 64], in0=s2.a[:, 128 + h * 64:128 + (h + 1) * 64],
                                                           scalar=st.a[:, 5 + h:6 + h], in1=vec.a[:, V_FKG2 + h * 64:V_FKG2 + (h + 1) * 64],
                                                           op0=ALU.mult, op1=ALU.mult), [s2, st, vec, sq], [sq])
                k.act(lambda e: e.activation(out=fqb.a[:], in_=sq.a[:, 0:128], func=AF.Copy, scale=qscale_f), [sq], [fqb])
                k.act(lambda e: e.activation(out=fkb.a[:], in_=sq.a[:, 128:256], func=AF.Copy), [sq], [fkb])
                k.transpose(qTf.a[:, ccols], fqb.a[:], 128, 128, BF16, [fqb], [qTf])
                k.transpose(kTf.a[:, cols], fkb.a[:], 128, 128, BF16, [fkb], [kTf])
                k.dve(lambda e: e.tensor_copy(out=Vf.a[:, ti, :, 0:64], in_=s2.a[:, 256:384].rearrange("p (h d) -> p h d", h=2)), [s2], [Vf])
                k.dve(lambda e: e.tensor_copy(out=Vf.a[:, ti, :, 64:65], in_=k.ones_f.a[:, 0:2].rearrange("p (h o) -> p h o", o=1)), [k.ones_f], [Vf])
                lf_ = lf.next()
                k.dve(lambda e: e.tensor_add(out=lf_.a[:, 0:2], in0=s1.a[:, 416:418], in1=vec.a[:, V_BF:V_BF + 2]), [s1, vec], [lf_])
                k.act(lambda e: e.activation(out=lf_.a[:, 0:2], in_=lf_.a[:, 0:2], func=AF.Exp, scale=-1.0), [lf_], [lf_])
                k.act(lambda e: e.activation(out=lf_.a[:, 0:2], in_=lf_.a[:, 0:2], func=AF.Ln, bias=1.0), [lf_], [lf_])
                k.dve(lambda e: e.tensor_scalar_mul(out=lf_.a[:, 2:4], in0=lf_.a[:, 0:2], scalar1=-1.0), [lf_], [lf_])
                cum = cumr.next()
                p = k.mif.next()
                k.pe(lambda e: e.matmul(p.a[:, 0:2], lhsT=k.triU.a[:], rhs=lf_.a[:, 2:4], start=True, stop=(cum_prev is None)),
                     [k.triU, lf_], [p])
                if cum_prev is not None:
                    cp = cum_prev
                    k.pe(lambda e: e.matmul(p.a[:, 0:2], lhsT=k.sel127.a[:], rhs=cp.a[:], start=False, stop=True),
                         [k.sel127, cp], [p])
                k.dve(lambda e: e.tensor_copy(out=cum.a[:], in_=p.a[:, 0:2]), [p], [cum])
                cum_prev = cum
                k.dve(lambda e: e.tensor_scalar_mul(out=ncum.a[:, ti, :], in0=cum.a[:], scalar1=-1.0), [cum], [ncum])
                for h in range(2):
                    k.dve(lambda e: e.tensor_scalar(out=dcum.a[:, h, :], in0=k.ident_f.a[:], scalar1=cum.a[:, h:h + 1], scalar2=None,
                                                    op0=ALU.mult), [k.ident_f, cum], [dcum])
                p = k.mif.next()
                for h in range(2):
                    k.pe(lambda e: e.matmul(p.a[:, h * 128:(h + 1) * 128], lhsT=k.ones_f.a[:], rhs=dcum.a[:, h, :], start=True, stop=True),
                         [k.ones_f, dcum], [p])
                k.evac(cumbc.a[:, :, ccols], p.a[:, 0:256].rearrange("p (h c) -> p h c", h=2), [p], [cumbc])

                if ti == k.dbg_tile:
                    k.dump(6, cum, cum.a[:], 2)
                    k.dump(7, cumbc, cumbc.a[:, 0, ccols], 128)
                    k.dump(17, sq, sq.a[:, 0:256], 256)
                if k.stage <= 5:
                    continue
                k.act(lambda e: e.activation(out=tw.a[:], in_=s4.a[:, 384:448], func=AF.Tanh), [s4], [tw])
                k.transpose(twT.a[0:64, :], tw.a[:], 128, 64, F32, [tw], [twT])
                k.transpose(alT.a[0:64, :], s4.a[:, 448:512], 128, 64, F32, [s4], [alT])
                p = k.mif.next()
                k.pe(lambda e: e.matmul(p.a[:, 0:128], lhsT=twT.a[:], rhs=wup.a[:], start=True, stop=True), [twT, wup], [p])
                k.pe(lambda e: e.matmul(p.a[:, 128:256], lhsT=alT.a[:], rhs=aup.a[:], start=True, stop=True), [alT, aup], [p])
                k.act(lambda e: e.activation(out=lw.a[:], in_=p.a[:, 0:128], func=AF.Sigmoid), [p], [lw])
                k.act(lambda e: e.activation(out=av.a[:], in_=p.a[:, 128:256], func=AF.Sigmoid), [p], [av])
                k.dve(lambda e: e.tensor_scalar_mul(out=lw.a[:], in0=lw.a[:], scalar1=-DECAY), [lw], [lw])
                for h in range(2):
                    k.dve(lambda e: e.tensor_scalar(out=kkn.a[:, h * 64:(h + 1) * 64], in0=kkn.a[:, h * 64:(h + 1) * 64],
                                                    scalar1=st.a[:, 7 + h:8 + h], scalar2=None, op0=ALU.mult), [kkn, st], [kkn])
                k.dve(lambda e: e.scalar_tensor_tensor(out=rt1.a[:], in0=av.a[:], scalar=-1.0, in1=vec.a[:, V_KA:V_KA + 128],
                                                       op0=ALU.add, op1=ALU.mult), [av, vec], [rt1])
                k.dve(lambda e: e.scalar_tensor_tensor(out=kmod.a[:], in0=rt1.a[:], scalar=1.0, in1=s4.a[:, 128:256],
                                                       op0=ALU.add, op1=ALU.mult), [rt1, s4], [kmod])
                if not L1:
                    k.dve(lambda e: e.tensor_copy(out=vv.a[:], in_=s4.a[:, 256:384]), [s4], [vv])
                    fw.dma("pool", k.vf_scr[rows, :], vv.a[:], [vv], [k.Bvf], vv)
                else:
                    fw.dma("sp", vfl.a[:], k.vf_scr[rows, :], [k.Bvf], [vfl], vfl)
                    k.transpose(vdT.a[0:32, :], s5.a[:], 128, 32, F32, [s5], [vdT])
                    p = k.mif.next()
                    k.pe(lambda e: e.matmul(p.a[:, 0:128], lhsT=vdT.a[:], rhs=vup.a[:], start=True, stop=True), [vdT, vup], [p])
                    k.act(lambda e: e.activation(out=nu.a[:], in_=p.a[:, 0:128], func=AF.Sigmoid), [p], [nu])
                    k.dve(lambda e: e.tensor_sub(out=rt2.a[:], in0=vfl.a[:], in1=s4.a[:, 256:384]), [vfl, s4], [rt2])
                    k.dve(lambda e: e.tensor_mul(out=rt2.a[:], in0=rt2.a[:], in1=nu.a[:]), [rt2, nu], [rt2])
                    k.dve(lambda e: e.tensor_add(out=vv.a[:], in0=rt2.a[:], in1=s4.a[:, 256:384]), [rt2, s4], [vv])
                k.dve(lambda e: e.tensor_mul(out=rt1.a[:], in0=s4.a[:, 0:128], in1=kmod.a[:]), [s4, kmod], [rt1])
                k.dve(lambda e: e.tensor_mul(out=rt1.a[:], in0=rt1.a[:], in1=vec.a[:, V_RK:V_RK + 128]), [rt1, vec], [rt1])
                k.dve(lambda e: e.tensor_reduce(out=bon.a[:], in_=rt1.a[:].rearrange("p (h d) -> p h d", h=2), axis=AX.X, op=ALU.add),
                      [rt1], [bon])
                if ti == k.dbg_tile:
                    k.dump(8, lw, lw.a[:], 128)
                    k.dump(9, av, av.a[:], 128)
                    k.dump(10, kkn, kkn.a[:], 128)
                    k.dump(11, kmod, kmod.a[:], 128)
                    k.dump(12, vv, vv.a[:], 128)
                if k.stage <= 6:
                    continue
                p = k.mif.next()
                k.pe(lambda e: e.matmul(p.a[:, 0:128], lhsT=k.triBD.a[:], rhs=lw.a[:], start=True, stop=True), [k.triBD, lw], [p])
                k.act(lambda e: e.activation(out=E1.a[:], in_=p.a[:, 0:128], func=AF.Exp), [p], [E1])
                k.act(lambda e: e.activation(out=E2.a[:], in_=p.a[:, 0:128], func=AF.Exp, scale=-1.0), [p], [E2])
                k.dve(lambda e: e.tensor_sub(out=E3.a[:], in0=p.a[:, 0:128], in1=lw.a[:]), [p, lw], [E3])
                k.act(lambda e: e.activation(out=E3.a[:], in_=E3.a[:], func=AF.Exp), [E3], [E3])
                for h in range(2):
                    pp = k.mif.next()
                    k.pe(lambda e: e.matmul(pp.a[0:64, 0:1], lhsT=lw.a[:, h * 64:(h + 1) * 64], rhs=k.triBD.a[:, 63:64], start=True, stop=True),
                         [lw, k.triBD], [pp])
                    k.pe(lambda e: e.matmul(pp.a[0:64, 1:2], lhsT=lw.a[:, h * 64:(h + 1) * 64], rhs=k.triBD.a[:, 127:128], start=True, stop=True),
                         [lw, k.triBD], [pp])
                    k.act(lambda e: e.activation(out=pC[h].a[:], in_=pp.a[0:64, 0:2], func=AF.Exp), [pp], [pC[h]])
                k.dve(lambda e: e.tensor_mul(out=qr_.a[:], in0=s4.a[:, 0:128], in1=E1.a[:]), [s4, E1], [qr_])
                k.dve(lambda e: e.tensor_mul(out=qk_.a[:], in0=kmod.a[:], in1=E2.a[:]), [kmod, E2], [qk_])
                k.dve(lambda e: e.tensor_mul(out=qa_.a[:], in0=kkn.a[:], in1=av.a[:]), [kkn, av], [qa_])
                k.dve(lambda e: e.tensor_mul(out=qa_.a[:], in0=qa_.a[:], in1=E2.a[:]), [qa_, E2], [qa_])
                k.dve(lambda e: e.scalar_tensor_tensor(out=qb_.a[:], in0=kkn.a[:], scalar=-1.0, in1=E3.a[:], op0=ALU.mult, op1=ALU.mult),
                      [kkn, E3], [qb_])
                k.dve(lambda e: e.tensor_copy(out=qa_hi.a[64:128, :], in_=qa_.a[64:128, :]), [qa_], [qa_hi])
                k.dve(lambda e: e.tensor_copy(out=qk_hi.a[64:128, :], in_=qk_.a[64:128, :]), [qk_], [qk_hi])
                def head_gen(h):
                    hc = slice(h * 64, (h + 1) * 64)
                    q4 = qT4[h]
                    for pair in range(2):
                        slot = k.mif.next()
                        for qi in range(2):
                            src = (qa_, qk_, qb_, qr_)[pair * 2 + qi]
                            k.pe(lambda e: e.transpose(out=slot.a[0:64, qi * 128:(qi + 1) * 128], in_=src.a[:, hc], identity=k.ident_f.a[:]),
                                 [src, k.ident_f], [slot])
                        k.evac(q4.a[:, pair * 2:pair * 2 + 2, :], slot.a[0:64, 0:256].rearrange("p (q c) -> p q c", q=2), [slot], [q4])
                        yield
                    k.dve(lambda e: e.tensor_copy(out=rlo[h].a[:, 0:64], in_=q4.a[:, 3, 0:64]), [q4], [rlo[h]])
                    k.dve(lambda e: e.tensor_copy(out=rhi[h].a[:, 64:128], in_=q4.a[:, 3, 64:128]), [q4], [rhi[h]])
                    aT, kT_, bT, rT = q4.a[:, 0, :], q4.a[:, 1, :], q4.a[:, 2, :], q4.a[:, 3, :]
                    brT = q4.a[:, 2:4, :].rearrange("p q c -> p (q c)")
                    slot = k.mif.next()
                    k.pe(lambda e: e.matmul(slot.a[:, 0:256], lhsT=aT, rhs=brT, start=True, stop=True), [q4], [slot])
                    k.dve(lambda e: e.tensor_mul(out=GA[h].a[:], in0=slot.a[:, 0:256], in1=k.mask2.a[:]), [slot, k.mask2], [GA[h]])
                    yield
                    slot = k.mif.next()
                    k.pe(lambda e: e.matmul(slot.a[:, 0:256], lhsT=kT_, rhs=brT, start=True, stop=True), [q4], [slot])
                    k.dve(lambda e: e.tensor_mul(out=GK[h].a[:], in0=slot.a[:, 0:256], in1=k.mask2.a[:]), [slot, k.mask2], [GK[h]])
                    yield
                    slot = k.mif.next()
                    k.pe(lambda e: e.matmul(slot.a[:, 0:128], lhsT=bT, rhs=aT, start=True, stop=True), [q4], [slot])
                    xt_ = XT[h].next()
                    k.dve(lambda e: e.tensor_mul(out=xt_.a[:], in0=slot.a[:, 0:128], in1=k.masksl.a[:]), [slot, k.masksl], [xt_])
                    yield
                    if k.stage <= 7:
                        return
                    Pc = PP[h].next()
                    k.dve(lambda e: e.tensor_add(out=Pc.a[:], in0=GA[h].a[:, 0:128], in1=k.ident_f.a[:]), [GA[h], k.ident_f], [Pc])
                    Xc_ap, Xc_b = GA[h].a[:, 0:128], GA[h]
                    XTc = xt_
                    for lev in range(5):
                        last = lev == 4
                        slot = k.mif.next()
                        k.pe(lambda e: e.matmul(slot.a[:, 0:128], lhsT=Xc_ap, rhs=XTc.a[:], start=True, stop=True), [Xc_b, XTc], [slot])
                        XTn = XT[h].next()
                        if not last:
                            slot2 = k.mif.next()
                            k.pe(lambda e: e.matmul(slot2.a[:, 0:128], lhsT=XTc.a[:], rhs=Xc_ap, start=True, stop=True), [Xc_b, XTc], [slot2])
                        k.evac(XTn.a[:], slot.a[:, 0:128], [slot], [XTn])
                        yield
                        if not last:
                            Xn = XX[h].next()
                            k.evac(Xn.a[:], slot2.a[:, 0:128], [slot2], [Xn])
                            yield
                        slot3 = k.mif.next()
                        k.pe(lambda e: e.matmul(slot3.a[:, 0:128], lhsT=XTn.a[:], rhs=Pc.a[:], start=True, stop=True), [XTn, Pc], [slot3])
                        Pn = PP[h].next()
                        k.dve(lambda e: e.tensor_add(out=Pn.a[:], in0=slot3.a[:, 0:128], in1=Pc.a[:]), [slot3, Pc], [Pn])
                        yield
                        Pc = Pn
                        XTc = XTn
                        if not last:
                            Xc_ap, Xc_b = Xn.a[:], Xn
                    TT = Pc
                    if k.stage <= 8:
                        return
                    M0 = Mcur[h]
                    W_, U_ = Wb[h], Ub[h]
                    slot = k.mif.next()
                    k.pe(lambda e: e.matmul(slot.a[0:64, 0:64], lhsT=q4.a[:, 2, 0:64], rhs=M0.a[:], start=True, stop=False), [q4, M0], [slot])
                    k.pe(lambda e: e.matmul(slot.a[0:64, 0:64], lhsT=GK[h].a[0:64, 0:64], rhs=vv.a[0:64, hc], start=False, stop=True),
                         [GK[h], vv], [slot])
                    k.evac(W_.a[0:64, :], slot.a[0:64, 0:64], [slot], [W_])
                    yield
                    slot = k.mif.next()
                    k.pe(lambda e: e.matmul(slot.a[0:64, 0:64], lhsT=TT.a[0:64, 0:64], rhs=W_.a[0:64, :], start=True, stop=True), [TT, W_], [slot])
                    k.evac(U_.a[0:64, :], slot.a[0:64, 0:64], [slot], [U_])
                    yield
                    slot = k.mif.next()
                    k.pe(lambda e: e.matmul(slot.a[0:64, 0:64], lhsT=qa_.a[0:64, hc], rhs=U_.a[0:64, :], start=True, stop=False), [qa_, U_], [slot])
                    k.pe(lambda e: e.matmul(slot.a[0:64, 0:64], lhsT=qk_.a[0:64, hc], rhs=vv.a[0:64, hc], start=False, stop=True), [qk_, vv], [slot])
                    M1 = Mst[h].next()
                    k.dve(lambda e: e.tensor_add(out=mtmp[h].a[:], in0=slot.a[0:64, 0:64], in1=M0.a[:]), [slot, M0], [mtmp[h]])
                    k.dve(lambda e: e.tensor_scalar(out=M1.a[:], in0=mtmp[h].a[:], scalar1=pC[h].a[:, 0:1], scalar2=None, op0=ALU.mult),
                          [mtmp[h], pC[h]], [M1])
                    yield
                    slot = k.mif.next()
                    k.pe(lambda e: e.matmul(slot.a[:, 0:64], lhsT=q4.a[:, 2, :], rhs=M1.a[:], start=True, stop=False), [q4, M1], [slot])
                    k.pe(lambda e: e.matmul(slot.a[:, 0:64], lhsT=GK[h].a[:, 0:128], rhs=vv.a[:, hc], start=False, stop=True),
                         [GK[h], vv], [slot])
                    k.evac(W_.a[64:128, :], slot.a[64:128, 0:64], [slot], [W_])
                    yield
                    slot = k.mif.next()
                    k.pe(lambda e: e.matmul(slot.a[:, 0:64], lhsT=TT.a[:, 0:128], rhs=W_.a[:], start=True, stop=True), [TT, W_], [slot])
                    k.evac(U_.a[64:128, :], slot.a[64:128, 0:64], [slot], [U_])
                    yield
                    slot = k.mif.next()
                    k.pe(lambda e: e.matmul(slot.a[0:64, 0:64], lhsT=qa_hi.a[:, hc], rhs=U_.a[:], start=True, stop=False), [qa_hi, U_], [slot])
                    k.pe(lambda e: e.matmul(slot.a[0:64, 0:64], lhsT=qk_hi.a[:, hc], rhs=vv.a[:, hc], start=False, stop=True), [qk_hi, vv], [slot])
                    M2 = Mst[h].next()
                    k.dve(lambda e: e.tensor_add(out=mtmp[h].a[:], in0=slot.a[0:64, 0:64], in1=M1.a[:]), [slot, M1], [mtmp[h]])
                    k.dve(lambda e: e.tensor_scalar(out=M2.a[:], in0=mtmp[h].a[:], scalar1=pC[h].a[:, 1:2], scalar2=None, op0=ALU.mult),
                          [mtmp[h], pC[h]], [M2])
                    yield
                    slot = k.mif.next()
                    k.pe(lambda e: e.matmul(slot.a[:, 0:64], lhsT=rlo[h].a[:], rhs=M0.a[:], start=True, stop=False), [rlo[h], M0], [slot])
                    k.pe(lambda e: e.matmul(slot.a[:, 0:64], lhsT=rhi[h].a[:], rhs=M1.a[:], start=False, stop=False), [rhi[h], M1], [slot])
                    k.pe(lambda e: e.matmul(slot.a[:, 0:64], lhsT=GA[h].a[:, 128:256], rhs=U_.a[:], start=False, stop=False), [GA[h], U_], [slot])
                    k.pe(lambda e: e.matmul(slot.a[:, 0:64], lhsT=GK[h].a[:, 128:256], rhs=vv.a[:, hc], start=False, stop=True), [GK[h], vv], [slot])
                    k.evac(yv.a[:, h, :], slot.a[:, 0:64], [slot], [yv])
                    yield
                    Mcur[h] = M2
                gens = [head_gen(0), head_gen(1)]
                alive = [True, True]
                while any(alive):
                    for gi in range(2):
                        if alive[gi]:
                            try:
                                next(gens[gi])
                            except StopIteration:
                                alive[gi] = False
                if ti == k.dbg_tile:
                    k.dump(13, yv, yv.a[:].rearrange("p h d -> p (h d)"), 128)
                    k.dump(18, GA[0], GA[0].a[:], 256)
                    k.dump(19, GK[0], GK[0].a[:], 256)
                if k.stage <= 9:
                    continue
                k.dve(lambda e: e.tensor_reduce(out=yst.a[:, 0:2], in_=yv.a[:], axis=AX.X, op=ALU.add), [yv], [yst])
                k.dve(lambda e: e.tensor_scalar_mul(out=yst.a[:, 0:2], in0=yst.a[:, 0:2], scalar1=1.0 / 64), [yst], [yst])
                for h in range(2):
                    k.dve(lambda e: e.tensor_scalar(out=yv.a[:, h, :], in0=yv.a[:, h, :], scalar1=yst.a[:, h:h + 1], scalar2=None, op0=ALU.subtract),
                          [yv, yst], [yv])
                k.dve(lambda e: e.tensor_mul(out=ysq.a[:], in0=yv.a[:], in1=yv.a[:]), [yv], [ysq])
                k.dve(lambda e: e.tensor_reduce(out=yst.a[:, 2:4], in_=ysq.a[:], axis=AX.X, op=ALU.add), [ysq], [yst])
                k.dve(lambda e: e.tensor_scalar(out=yst.a[:, 2:4], in0=yst.a[:, 2:4], scalar1=1.0 / 64, scalar2=GN_EPS, op0=ALU.mult, op1=ALU.add), [yst], [yst])
                k.act(lambda e: e.sqrt(out=yst.a[:, 2:4], in_=yst.a[:, 2:4]), [yst], [yst])
                k.dve(lambda e: e.reciprocal(out=yst.a[:, 2:4], in_=yst.a[:, 2:4]), [yst], [yst])
                for h in range(2):
                    hc = slice(h * 64, (h + 1) * 64)
                    k.dve(lambda e: e.scalar_tensor_tensor(out=yv.a[:, h, :], in0=yv.a[:, h, :], scalar=yst.a[:, 2 + h:3 + h],
                                                           in1=vec.a[:, V_LG + h * 64:V_LG + (h + 1) * 64], op0=ALU.mult, op1=ALU.mult),
                          [yv, yst, vec], [yv])
                    k.dve(lambda e: e.tensor_add(out=yv.a[:, h, :], in0=yv.a[:, h, :], in1=vec.a[:, V_LB + h * 64:V_LB + (h + 1) * 64]), [yv, vec], [yv])
                    k.dve(lambda e: e.scalar_tensor_tensor(out=yv.a[:, h, :], in0=vv.a[:, hc], scalar=bon.a[:, h:h + 1], in1=yv.a[:, h, :],
                                                           op0=ALU.mult, op1=ALU.add), [vv, bon, yv], [yv])
                if ti == k.dbg_tile:
                    k.dump(14, yv, yv.a[:].rearrange("p h d -> p (h d)"), 128)
                k.dve(lambda e: e.tensor_mul(out=ocb.a[:], in0=yv.a[:].rearrange("p h d -> p (h d)"), in1=sg_c.a[:]), [yv, sg_c], [ocb])
                oc = oTc.next()
                k.transpose(oc.a[:], ocb.a[:], 128, 128, BF16, [ocb], [oc])
                fw.dma("pool", k.oT_scr[2, :, cols], oc.a[:], [oc], [k.BoT], oc)

            for mixer in range(2 if k.stage > 10 else 0):
                og = ogm if mixer == 0 else ogf
                gate = sga if mixer == 0 else sgb
                Vr = Vm if mixer == 0 else Vf
                for h in range(2):
                    nkt = 4 * (ch + 1)
                    oa = k.oa
                    k.dve(lambda e: e.memset(oa.a[:], 0.0), [], [oa])
                    def emit_pv(kt, j0, pT):
                        for j in range(j0, 4):
                            k.pe(lambda e: e.matmul(oa.a[:, j, 0:65], lhsT=pT.a[:, j * 128:(j + 1) * 128], rhs=Vr.a[:, kt, h, :],
                                                    start=False, stop=(kt == 4 * ch + j), skip_group_check=True), [pT, Vr], [oa])
                    pend = None
                    for kt in range(nkt):
                        dj = kt - 4 * ch
                        j0 = max(0, dj)
                        c0 = j0 * 128
                        kc = slice(kt * 128, (kt + 1) * 128)
                        s = k.sc.next()
                        if mixer == 0:
                            k.pe(lambda e: e.matmul(s.a[:, c0:512], lhsT=kTm[h].a[:, kc], rhs=qTm[h].a[:, c0:512], start=True, stop=True),
                                 [kTm[h], qTm[h]], [s])
                        else:
                            hp = slice(h * 64, (h + 1) * 64)
                            k.pe(lambda e: e.matmul(s.a[:, c0:512], lhsT=kTf.a[hp, kc], rhs=qTf.a[hp, c0:512], start=True, stop=True),
                                 [kTf, qTf], [s])
                        pT = pTr.next()
                        if mixer == 0:
                            k.act(lambda e: e.activation(out=pT.a[:, c0:512], in_=s.a[:, c0:512], func=AF.Exp), [s], [pT])
                        else:
                            ft = ftmp.next()
                            k.dve(lambda e: e.tensor_add(out=ft.a[:, c0:512], in0=s.a[:, c0:512], in1=cumbc.a[:, h, c0:512]), [s, cumbc], [ft])
                            k.act(lambda e: e.activation(out=pT.a[:, c0:512], in_=ft.a[:, c0:512], func=AF.Exp, bias=ncum.a[:, kt, h:h + 1]),
                                  [ft, ncum], [pT])
                        if dj >= 0:
                            k.pool(lambda e: e.affine_select(out=pT.a[:, c0:c0 + 128], in_=pT.a[:, c0:c0 + 128], pattern=[[1, 128]],
                                                             compare_op=ALU.is_ge, fill=0.0, base=0, channel_multiplier=-1), [pT], [pT])
                        if pend is not None:
                            emit_pv(*pend)
                        pend = (kt, j0, pT)
                    emit_pv(*pend)
                    k.dve(lambda e: e.reciprocal(out=rinv.a[:], in_=oa.a[:, :, 64:65].rearrange("p j o -> p (j o)")), [oa], [rinv])
                    for j in range(4):
                        k.dve(lambda e: e.scalar_tensor_tensor(out=og.a[:, j, h * 64:(h + 1) * 64], in0=oa.a[:, j, 0:64], scalar=rinv.a[:, j:j + 1],
                                                               in1=gate.a[:, j, h * 64:(h + 1) * 64], op0=ALU.mult, op1=ALU.mult),
                              [oa, rinv, gate], [og])
                oT = oTr.next()
                for j in range(4):
                    k.transpose(oT.a[:, j * 128:(j + 1) * 128], og.a[:, j, :], 128, 128, BF16, [og], [oT])
                fw.dma("pool", k.oT_scr[mixer, :, ch * 512:(ch + 1) * 512], oT.a[:], [oT], [k.BoT], oT)

    def passB(self, es, l):
        k = self
        fw = k.fw
        dr = k.dr
        NT = k.NT
        gT = k.small_load(es, "gTb", [128, 8], dr["gT%d" % l])
        wG = k.sb(es, "wG", [128, 8, 3072], BF16)
        wp = k.sb(es, "wp", [128, 3, D], BF16)
        wo = k.sb(es, "wo", [128, 8, D], BF16)
        with contextlib.ExitStack() as est:
            stg_rot = Rot([k.sb(est, "stgb", [128, 8, 256], F32) for _ in range(2)])
            k.load_cast(wG, 0, dr["wG%d" % l], 3072, gT, stg_rot)
            k.load_cast(wo, 0, dr["wo%d" % l], D, None, stg_rot)
            for br in range(3):
                for c0 in range(0, D, 256):
                    stg = stg_rot.next()
                    fw.dma("sp", stg.a[:, 0, :], dr["wp%d" % l][br, :, c0:c0 + 256], [], [stg], stg)
                    k.dve(lambda e: e.tensor_copy(out=wp.a[:, br, c0:c0 + 256], in_=stg.a[:, 0, :]), [stg], [wp])
        fw.barrier()
        k.xrot = Rot([k.sb(es, "xtb", [128, D], F32) for _ in range(2)])
        k.rrot = Rot([k.sb(es, "rtb", [128, 512], F32) for _ in range(2)])
        k.xstat = Rot([k.sb(es, "xstb", [128, 4], F32) for _ in range(2)])
        k.hbrot = Rot([k.sb(es, "hbb", [128, D], BF16) for _ in range(2)])
        k.hTrot = Rot([k.sb(es, "hTb", [128, 8, 129], BF16) for _ in range(2)])
        oTl = Rot([k.sb(es, "oTl", [128, 3, 128], BF16) for _ in range(2)])
        gs = Rot([k.sb(es, "gs", [128, 512], F32) for _ in range(2)])
        mg = k.sb(es, "mg", [128, D], F32)
        mgt = Rot([k.sb(es, "mgt", [128, 512], F32) for _ in range(2)])
        mb = k.sb(es, "mb", [128, D], BF16)
        mT = k.sb(es, "mT", [128, 8, 128], BF16)
        po = Rot([k.sb(es, "po", [128, D], F32) for _ in range(2)])
        hT_next = k.load_h(l, 0, True)
        for ti in range(NT):
            rows = slice(ti * 128, (ti + 1) * 128)
            hT = hT_next
            if ti + 1 < NT:
                hT_next = k.load_h(l, ti + 1, True)
            ot = oTl.next()
            for br in range(3):
                fw.dma("sp", ot.a[:, br, :], k.oT_scr[br, :, rows], [k.BoT], [ot], ot)
            for br in range(3):
                for half in range(2):
                    hc = slice(half * 512, (half + 1) * 512)
                    pg = k.pj.next()
                    for kk in range(8):
                        k.pe(lambda e: e.matmul(pg.a[:], lhsT=hT.a[:, kk, 1:129], rhs=wG.a[:, kk, br * D + half * 512:br * D + (half + 1) * 512],
                                                start=(kk == 0), stop=(kk == 7)), [hT, wG], [pg])
                    g = gs.next()
                    k.act(lambda e: e.activation(out=g.a[:], in_=pg.a[:], func=AF.Sigmoid), [pg], [g])
                    pb = k.sc.next()
                    k.pe(lambda e: e.matmul(pb.a[:], lhsT=ot.a[:, br, :], rhs=wp.a[:, br, hc], start=True, stop=True), [ot, wp], [pb])
                    if br == 0:
                        k.dve(lambda e: e.tensor_mul(out=mg.a[:, hc], in0=pb.a[:], in1=g.a[:]), [pb, g], [mg])
                    else:
                        t_ = mgt.next()
                        k.dve(lambda e: e.tensor_mul(out=t_.a[:], in0=pb.a[:], in1=g.a[:]), [pb, g], [t_])
                        k.pool(lambda e: e.tensor_add(out=mg.a[:, hc], in0=mg.a[:, hc], in1=t_.a[:]), [mg, t_], [mg])
            k.act(lambda e: e.activation(out=mb.a[:], in_=mg.a[:], func=AF.Copy), [mg], [mb])
            for half in range(2):
                slot = k.mib.next()
                for q in range(4):
                    kk = half * 4 + q
                    k.pe(lambda e: e.transpose(out=slot.a[:, q * 128:(q + 1) * 128], in_=mb.a[:, kk * 128:(kk + 1) * 128],
                                               identity=k.ident_b.a[:]), [mb, k.ident_b], [slot])
                k.evac(mT.a[:, half * 4:half * 4 + 4, :], slot.a[:, 0:512].rearrange("p (q c) -> p q c", q=4), [slot], [mT])
            pout = po.next()
            for half in range(2):
                hc = slice(half * 512, (half + 1) * 512)
                pp = k.pj.next()
                for kk in range(8):
                    k.pe(lambda e: e.matmul(pp.a[:], lhsT=mT.a[:, kk, :], rhs=wo.a[:, kk, hc], start=(kk == 0), stop=(kk == 7)), [mT, wo], [pp])
                k.evac(pout.a[:, hc], pp.a[:], [pp], [pout])
            if ti == k.dbg_tile and k.debug:
                for br in range(3):
                    fw.dma("sp", k.dbgb[br, :, 0:128], ot.a[:, br, :], [ot], [k.Bdbg], k.Bdbg)
                fw.dma("sp", k.dbgb[3, :, 0:512], wp.a[:, 0, 0:512], [wp], [k.Bdbg], k.Bdbg)
            if ti == k.dbg_tile:
                k.dump(22, pout, pout.a[:, 0:512], 512)
                k.dump(23, mg, mg.a[:, 0:512], 512)
            fw.dma("pool", k.part[l][rows, :], pout.a[:], [pout], [k.Bpart[l]], pout)

    def final(self, es, out):
        k = self
        fw = k.fw
        L = k.n_layers
        Sq = k.S // 4
        xr = Rot([k.sb(es, "xf", [128, D], F32) for _ in range(2)])
        rr = Rot([k.sb(es, "rf", [128, D], F32) for _ in range(2)])
        for i in range(Sq // 128):
            xt = xr.next()
            fw.dma("sp", xt.a[:], k.dr["xq"][i * 128:(i + 1) * 128, :], [], [xt], xt)
            for l in k.layers:
                rt = rr.next()
                fw.dma("sp", rt.a[:], k.rs[l][i * 128:(i + 1) * 128, :], [k.Brs[l]], [rt], rt)
                k.dve(lambda e: e.tensor_add(out=xt.a[:], in0=xt.a[:], in1=rt.a[:]), [xt, rt], [xt])
            fw.dma("pool", out[i * 128:(i + 1) * 128, :], xt.a[:], [xt], [k.Bout], xt)


IN_SIZES = (256, 128, 32, 512, 512, 512, 512, 8, 512, 1664, 512, 3072)
OFF = np.concatenate([[0], np.cumsum(IN_SIZES)]).tolist()
(O_CQ, O_CKV, O_KR, O_GA, O_FQ, O_FK, O_FV, O_FF, O_GB, O_SH, O_GC, O_MG) = OFF[:12]


def rope_cs(S):
    inv = (np.float32(10000.0) ** (-np.arange(0, 32, 2, dtype=np.float32) / np.float32(32))).astype(np.float32)
    ang = (np.arange(S, dtype=np.float32)[:, None] * inv[None, :]).astype(np.float32)
    c, s_ = np.cos(ang).astype(np.float32), np.sin(ang).astype(np.float32)
    return np.ascontiguousarray(np.concatenate([c, c, s_, s_], axis=1))


def core_inputs(inp, c, S, L, layers=None):
    f = lambda a: np.ascontiguousarray(a, dtype=np.float32)
    b, j = c // 4, c % 4
    hs = slice(128 * j, 128 * (j + 1))
    m = {"x": f(inp["x"][b, :S]), "xq": f(inp["x"][b, j * (S // 4):(j + 1) * (S // 4)]), "cs": rope_cs(S)}
    layers = list(range(L)) if layers is None else layers
    for l in layers:
        w = inp["w_in"][l]
        colsA = np.concatenate([
            np.arange(O_CQ, O_CQ + 416),
            np.arange(O_FF + 2 * j, O_FF + 2 * j + 2),
            np.arange(O_FQ + 128 * j, O_FQ + 128 * j + 128),
            np.arange(O_FK + 128 * j, O_FK + 128 * j + 128),
            np.arange(O_FV + 128 * j, O_FV + 128 * j + 128),
            np.arange(O_GB + 128 * j, O_GB + 128 * j + 128),
            np.arange(O_GA + 128 * j, O_GA + 128 * j + 128),
            np.arange(O_GC + 128 * j, O_GC + 128 * j + 128)])
        shl = np.concatenate([np.arange(128 * j, 128 * j + 128), np.arange(512 + 128 * j, 512 + 128 * j + 128),
                              np.arange(1024 + 128 * j, 1024 + 128 * j + 128), np.arange(1536, 1664)])
        m["wA%d" % l] = f(w[:, colsA])
        m["wS%d" % l] = f(w[:, O_SH + shl])
        m["wG%d" % l] = f(w[:, O_MG:O_MG + 3072])
        uq = inp["mla_w_uq"][l].reshape(256, 8, 96)[:, 2 * j:2 * j + 2].reshape(256, 192)
        m["wuq%d" % l] = f(uq)
        ukv = inp["mla_w_ukv"][l].reshape(128, 8, 128)[:, 2 * j:2 * j + 2]
        m["wukv%d" % l] = f(np.concatenate([ukv[:, 0, :64], ukv[:, 1, :64], ukv[:, 0, 64:], ukv[:, 1, 64:]], axis=1))
        m["wup%d" % l] = f(np.concatenate([inp["rwkv_w_up"][l][:, hs], inp["rwkv_w0"][l][None, hs]], axis=0))
        m["aup%d" % l] = f(np.concatenate([inp["rwkv_a_up"][l][:, hs], inp["rwkv_a0"][l][None, hs]], axis=0))
        qg = inp["mla_q_g"][l]
        vec = np.concatenate([
            qg, qg, inp["mla_knope_g"][l], inp["mla_knope_g"][l], inp["mla_krope_g"][l],
            inp["fox_q_g"][l], inp["fox_q_g"][l], inp["fox_k_g"][l], inp["fox_k_g"][l],
            inp["fox_b_f"][l][2 * j:2 * j + 2], inp["rwkv_k_k"][l][hs], inp["rwkv_k_a"][l][hs],
            inp["rwkv_r_k"][l].reshape(-1)[hs], inp["rwkv_lnx_g"][l][hs], inp["rwkv_lnx_b"][l][hs],
            inp["rwkv_mu"][l][shl]])
        assert vec.shape[0] == NV
        m["vec%d" % l] = f(vec[None, :])
        m["gT%d" % l] = f(inp["norm_g"][l].reshape(8, 128).T)
        m["qagT%d" % l] = f(inp["mla_qa_g"][l].reshape(2, 128).T)
        m["kvagT%d" % l] = f(inp["mla_kva_g"][l].reshape(1, 128).T)
        m["wp%d" % l] = f(np.stack([inp["w_pa"][l][hs], inp["w_pb"][l][hs], inp["w_pc"][l][hs]]))
        m["wo%d" % l] = f(inp["w_out"][l])
    if 1 in layers:
        w = inp["w_in"][1]
        m["wvT"] = f(w[:, O_SH + 1024:O_SH + 1536].T)
        m["muVT"] = f(inp["rwkv_mu"][1][1024:1536].reshape(4, 128).T)
        m["vdown"] = f(inp["rwkv_v_down"][0])
        m["vup"] = f(np.concatenate([inp["rwkv_v_up"][0][:, hs], inp["rwkv_v0"][0][None, hs]], axis=0))
    return m


_CACHE = {}


def run(inp, S, L):
    key = (S, L)
    if key not in _CACHE:
        _CACHE[key] = Kern(S, L).build()
    nc = _CACHE[key]
    in_maps = [core_inputs(inp, c, S, L) for c in range(8)]
    res = run_bass_kernel_spmd(nc, in_maps, core_ids=list(range(8)))
    global LAST
    LAST = res
    out = np.zeros((2, S, D), np.float32)
    q = S // 4
    for c in range(8):
        b, j = c // 4, c % 4
        out[b, j * q:(j + 1) * q] = res.results[c]["out"]
    return out


def run_split(inp, S):
    q = S // 4
    key = (S, "l0")
    if key not in _CACHE:
        _CACHE[key] = Kern(S, 1, layers=[0]).build()
    res0 = run_bass_kernel_spmd(_CACHE[key], [core_inputs(inp, c, S, 2, layers=[0]) for c in range(8)], core_ids=list(range(8)))
    x1 = np.zeros((2, S, D), np.float32)
    for c in range(8):
        x1[c // 4, (c % 4) * q:(c % 4 + 1) * q] = res0.results[c]["out"]
    key = (S, "l1")
    if key not in _CACHE:
        _CACHE[key] = Kern(S, 2, layers=[1]).build()
    inp1 = dict(inp)
    inp1["x"] = x1
    maps = []
    for c in range(8):
        m = core_inputs(inp1, c, S, 2, layers=[1])
        m["vf"] = np.ascontiguousarray(res0.results[c]["vf"])
        maps.append(m)
    res1 = run_bass_kernel_spmd(_CACHE[key], maps, core_ids=list(range(8)))
    out = np.zeros((2, S, D), np.float32)
    for c in range(8):
        out[c // 4, (c % 4) * q:(c % 4 + 1) * q] = res1.results[c]["out"]
    return out


def kernel(**inputs):
    inp = {k: np.asarray(v) for k, v in inputs.items()}
    return run(inp, inp["x"].shape[1], 2)
```

```python
import contextlib
import numpy as np
import concourse.bass as bass
import concourse.mybir as mybir
from concourse.bass_utils import run_bass_kernel_spmd

F32 = mybir.dt.float32
BF16 = mybir.dt.bfloat16
AF = mybir.ActivationFunctionType
ALU = mybir.AluOpType
AX = mybir.AxisListType
SEM_LIMIT = 30000

D = 1024
EPS = 1e-6
GN_EPS = 64e-5
DECAY = 0.606531
NA = 1186
V_QG2, V_KNG2, V_KRG, V_FQG2, V_FKG2, V_BF, V_KK, V_KA, V_RK, V_LG, V_LB, V_MU = (
    0, 192, 320, 352, 480, 608, 610, 738, 866, 994, 1122, 1250)
NV = 1762


class T:
    __slots__ = ("name", "w", "r", "nowaw", "stream")

    def __init__(self, name, nowaw=False):
        self.name = name
        self.w = {}
        self.r = {}
        self.nowaw = nowaw
        self.stream = None


class B:
    def __init__(self, a, name, nowaw=False, psum=False):
        self.a = a
        self.T = T(name, nowaw)
        self.psum = psum


class Rot:
    def __init__(self, items):
        self.items = items
        self.i = 0

    def next(self):
        x = self.items[self.i % len(self.items)]
        self.i += 1
        return x


class FW:
    ENG = ("pe", "dve", "act", "pool", "sp")

    def __init__(self, nc, es):
        self.nc = nc
        self.es = es
        self.e = {"pe": nc.tensor, "dve": nc.vector, "act": nc.scalar, "pool": nc.gpsimd, "sp": nc.sync}
        self.sem = {}
        self.cnt = {}
        self.cur = {}
        self.nsem = 0
        self.seen = {k: {} for k in self.ENG}
        self.free_dma = []
        self.used_dma = []
        for k in self.ENG:
            self._new_key(k)

    def _new_key(self, stream):
        key = "%s#%d" % (stream, self.nsem)
        self.sem[key] = self.es.enter_context(self.nc.semaphore("s%d" % self.nsem))
        self.nsem += 1
        self.cnt[key] = 0
        self.cur[stream] = key
        return key

    def _wait(self, eng, key, seq):
        if self.seen[eng].get(key, 0) >= seq:
            return
        self.seen[eng][key] = seq
        self.e[eng].wait_ge(self.sem[key], seq)

    def deps(self, eng, reads, writes):
        own = eng + "#"
        for b in reads:
            for k, s in b.T.w.items():
                if eng == "pe" and k.startswith(own):
                    continue
                self._wait(eng, k, s)
        for b in writes:
            t = b.T
            if not t.nowaw:
                for k, s in t.w.items():
                    if eng == "pe" and k.startswith(own):
                        continue
                    self._wait(eng, k, s)
            for k, s in t.r.items():
                if k.startswith(own):
                    continue
                self._wait(eng, k, s)

    def done(self, ins, stream, inc, reads, writes):
        key = self.cur[stream]
        if self.cnt[key] + inc > SEM_LIMIT:
            key = self._new_key(stream)
        self.cnt[key] += inc
        seq = self.cnt[key]
        ins.then_inc(self.sem[key], inc)
        for b in reads:
            b.T.r[key] = seq
        for b in writes:
            t = b.T
            if t.nowaw:
                t.w[key] = seq
            else:
                t.w = {key: seq}
                t.r = {}

    def op(self, eng, fn, R, W):
        pr = [b for b in R if b.psum]
        if pr:
            R = [b for b in R if not b.psum]
            W = list(W) + [b for b in pr if b not in W]
        self.deps(eng, R, W)
        ins = fn(self.e[eng])
        self.done(ins, eng, 1, R, W)

    def dma(self, issuer, out, in_, R, W, slot):
        t = slot.T
        if t.stream is None:
            self.nstream = getattr(self, "nstream", 0) + 1
            t.stream = "dma%d" % self.nstream
            if self.free_dma:
                self.cur[t.stream] = self.free_dma.pop()
            else:
                self._new_key(t.stream)
            self.used_dma.append(t.stream)
        self.deps(issuer, R, W)
        ins = self.e[issuer].dma_start(out=out, in_=in_)
        self.done(ins, t.stream, 16, R, W)

    def barrier(self):
        snap = {k: c for k, c in self.cnt.items() if c > 0}
        for eng in self.ENG:
            for k, c in snap.items():
                if k.startswith(eng + "#"):
                    continue
                self._wait(eng, k, c)
        keep = getattr(self, "keep", set())
        for st in self.used_dma:
            if st not in keep:
                self.free_dma.append(self.cur[st])
        self.used_dma = [st for st in self.used_dma if st in keep]


class StopBuild(Exception):
    pass


class Kern:
    stage = 99
    dve_only = False
    debug = False
    dbg_tile = 0

    def dump(self, idx, src_b, ap, n):
        if not self.debug:
            return
        self.fw.dma("sp", self.dbg[idx, :, 0:n], ap, [src_b], [self.Bdbg], self.Bdbg)
        self.fw.keep = {self.Bdbg.T.stream}

    def chk(self, st):
        if self.stage <= st:
            raise StopBuild()

    def __init__(self, S, n_layers, layers=None):
        self.S = S
        self.NT = S // 128
        self.NCH = S // 512
        self.layers = list(range(n_layers)) if layers is None else list(layers)
        self.n_layers = max(self.layers) + 1
        self.uid = 0

    def sb(self, es, name, shape, dt, nowaw=False):
        self.uid += 1
        a = es.enter_context(self.nc.sbuf_tensor("%s_%d" % (name, self.uid), shape, dt))
        return B(a, name, nowaw)

    def ps(self, es, name, shape, dt):
        self.uid += 1
        a = es.enter_context(self.nc.psum_tensor("%s_%d" % (name, self.uid), shape, dt))
        return B(a, name, psum=True)

    def dve(self, fn, R, W):
        self.fw.op("dve", fn, R, W)

    def act(self, fn, R, W):
        self.fw.op("act", fn, R, W)

    def pool(self, fn, R, W):
        self.fw.op("pool", fn, R, W)

    def pe(self, fn, R, W):
        self.fw.op("pe", fn, R, W)

    def evac(self, out, in_, R, W):
        self._ev = getattr(self, "_ev", 0) + 1
        if self._ev % 2 or self.dve_only:
            self.dve(lambda e: e.tensor_copy(out=out, in_=in_), R, W)
        else:
            self.act(lambda e: e.activation(out=out, in_=in_, func=AF.Copy), R, W)

    def transpose(self, out_ap, in_ap, np_in, nf_in, dt, R, W, evac_eng=None, scale=None):
        if dt == BF16:
            slot = self.mib.next()
            idn = self.ident_b
        else:
            slot = self.mif.next()
            idn = self.ident_f
        pv = slot.a[0:nf_in, 0:np_in]
        self.pe(lambda e: e.transpose(out=pv, in_=in_ap, identity=idn.a[0:np_in, 0:np_in]), R + [idn], [slot])
        self.evac(out_ap, pv, [slot], W)

    def build(self):
        S, NT = self.S, self.NT
        nc = bass.Bass("TRN2", target_bir_lowering=False)
        self.nc = nc
        L = self.n_layers
        dr = {}

        def din(name, shape):
            dr[name] = nc.dram_tensor(name, shape, F32, kind="ExternalInput").ap()

        din("x", [S, D])
        din("xq", [S // 4, D])
        din("cs", [S, 64])
        for l in self.layers:
            din("wA%d" % l, [D, NA])
            din("wS%d" % l, [D, 512])
            din("wG%d" % l, [D, 3072])
            din("wuq%d" % l, [256, 192])
            din("wukv%d" % l, [128, 256])
            din("wup%d" % l, [65, 128])
            din("aup%d" % l, [65, 128])
            din("vec%d" % l, [1, NV])
            din("gT%d" % l, [128, 8])
            din("qagT%d" % l, [128, 2])
            din("kvagT%d" % l, [128, 1])
            din("wp%d" % l, [3, 128, D])
            din("wo%d" % l, [D, D])
        if 1 in self.layers:
            din("wvT", [512, D])
            din("muVT", [128, 4])
            din("vdown", [512, 32])
            din("vup", [33, 128])
        out = nc.dram_tensor("out", [S // 4, D], F32, kind="ExternalOutput").ap()
        if self.debug:
            self.dbg = nc.dram_tensor("dbg", [24, 128, 512], F32, kind="ExternalOutput").ap()
            self.Bdbg = B(None, "dbg", nowaw=True)
            self.dbgb = nc.dram_tensor("dbgb", [4, 128, 512], BF16, kind="ExternalOutput").ap()
        self.dr = dr
        part = [nc.dram_tensor("part%d" % l, [S, D], F32).ap() for l in range(L)]
        red = [nc.dram_tensor("red%d" % l, [S, D], F32).ap() for l in range(L)]
        rs = [nc.dram_tensor("rs%d" % l, [S // 4, D], F32).ap() for l in range(L)]
        self.rs = rs
        self.Brs = [B(None, "rs%d" % l) for l in range(L)]
        oT_scr = nc.dram_tensor("oT_scr", [3, 128, S], BF16).ap()
        if 1 in self.layers and 0 not in self.layers:
            vf_scr = nc.dram_tensor("vf", [S, 128], F32, kind="ExternalInput").ap()
        elif self.layers == [0]:
            vf_scr = nc.dram_tensor("vf", [S, 128], F32, kind="ExternalOutput").ap()
        else:
            vf_scr = nc.dram_tensor("vf_scr", [S, 128], F32).ap()
        self.part, self.red, self.oT_scr, self.vf_scr = part, red, oT_scr, vf_scr
        self.Bpart = [B(None, "part%d" % l, nowaw=True) for l in range(L)]
        self.Bred = [B(None, "red%d" % l) for l in range(L)]
        self.BoT = B(None, "oTscr", nowaw=True)
        self.Bvf = B(None, "vfscr", nowaw=True)
        self.Bout = B(None, "out", nowaw=True)

        with contextlib.ExitStack() as es:
            self.fw = FW(nc, es)
            self.consts(es)
            self.pj = Rot([self.ps(es, "pj", [128, 512], F32) for _ in range(2)])
            self.sc = Rot([self.ps(es, "sc", [128, 512], F32) for _ in range(2)])
            self.oa = self.ps(es, "oa", [128, 4, 128], F32)
            self.mib = Rot([self.ps(es, "mib", [128, 1024], BF16)])
            self.mif = Rot([self.ps(es, "mif", [128, 512], F32) for _ in range(2)])
            for l in self.layers:
                with contextlib.ExitStack() as esA:
                    self.passA(esA, l)
                if self.stage <= 50:
                    self.fw.barrier()
                    self.layers = []
                    break
                self.fw.barrier()
                with contextlib.ExitStack() as esB:
                    self.passB(esB, l)
                self.fw.barrier()
                fw = self.fw
                groups = [[0, 1, 2, 3], [4, 5, 6, 7]]
                fw.deps("pool", [self.Bpart[l]], [self.Brs[l]])
                ins = nc.gpsimd.collective_compute("ReduceScatter", ALU.add, replica_groups=groups,
                                                   ins=[part[l]], outs=[rs[l]])
                fw.done(ins, "pool", 1, [self.Bpart[l]], [self.Brs[l]])
                if l < self.layers[-1]:
                    nchk = 8
                    rows = S // nchk
                    for c in range(nchk):
                        fw.deps("pool", [self.Bpart[l]], [self.Bred[l]])
                        ins = nc.gpsimd.collective_compute("AllReduce", ALU.add, replica_groups=groups,
                                                           ins=[part[l][c * rows:(c + 1) * rows, :]],
                                                           outs=[red[l][c * rows:(c + 1) * rows, :]])
                        fw.done(ins, "pool", 1, [self.Bpart[l]], [self.Bred[l]])
                self.fw.barrier()
            with contextlib.ExitStack() as esF:
                self.final(esF, out)
            self.fw.barrier()
        return nc

    def consts(self, es):
        k = self
        self.ident_f = k.sb(es, "identf", [128, 128], F32)
        self.ident_b = k.sb(es, "identb", [128, 128], BF16)
        self.triU = k.sb(es, "triU", [128, 128], F32)
        self.triBD = k.sb(es, "triBD", [128, 128], F32)
        self.sel127 = k.sb(es, "sel127", [128, 128], F32)
        self.ones_f = k.sb(es, "onesf", [128, 128], F32)
        self.mask2 = k.sb(es, "mask2", [128, 256], F32)
        self.masksl = k.sb(es, "masksl", [128, 128], F32)
        idf, idb = self.ident_f, self.ident_b
        k.pool(lambda e: e.memset(idf.a[:], 1.0), [], [idf])
        k.pool(lambda e: e.affine_select(out=idf.a[:], in_=idf.a[:], pattern=[[-1, 128]], compare_op=ALU.is_equal,
                                         fill=0.0, base=0, channel_multiplier=1), [idf], [idf])
        k.pool(lambda e: e.tensor_copy(out=idb.a[:], in_=idf.a[:]), [idf], [idb])
        tu = self.triU
        k.pool(lambda e: e.memset(tu.a[:], 1.0), [], [tu])
        k.pool(lambda e: e.affine_select(out=tu.a[:], in_=tu.a[:], pattern=[[1, 128]], compare_op=ALU.is_ge,
                                         fill=0.0, base=0, channel_multiplier=-1), [tu], [tu])
        tb = self.triBD
        k.pool(lambda e: e.tensor_copy(out=tb.a[:], in_=tu.a[:]), [tu], [tb])
        k.pool(lambda e: e.memset(tb.a[0:64, 64:128], 0.0), [tb], [tb])
        s1 = self.sel127
        k.pool(lambda e: e.memset(s1.a[:], 1.0), [], [s1])
        k.pool(lambda e: e.affine_select(out=s1.a[:], in_=s1.a[:], pattern=[[0, 128]], compare_op=ALU.is_ge,
                                         fill=0.0, base=-127, channel_multiplier=1), [s1], [s1])
        k.pool(lambda e: e.memset(self.ones_f.a[:], 1.0), [], [self.ones_f])
        m2 = self.mask2
        k.pool(lambda e: e.memset(m2.a[:, 0:128], 1.0), [], [m2])
        k.pool(lambda e: e.affine_select(out=m2.a[:, 0:128], in_=m2.a[:, 0:128], pattern=[[1, 128]], compare_op=ALU.is_gt,
                                         fill=0.0, base=0, channel_multiplier=-1), [m2], [m2])
        k.pool(lambda e: e.memset(m2.a[0:64, 64:128], 0.0), [m2], [m2])
        k.pool(lambda e: e.tensor_copy(out=m2.a[:, 128:256], in_=tb.a[:]), [tb, m2], [m2])
        ml = self.masksl
        k.pool(lambda e: e.memset(ml.a[:], 1.0), [], [ml])
        k.pool(lambda e: e.affine_select(out=ml.a[:], in_=ml.a[:], pattern=[[-1, 128]], compare_op=ALU.is_gt,
                                         fill=0.0, base=0, channel_multiplier=1), [ml], [ml])
        k.pool(lambda e: e.memset(ml.a[64:128, 0:64], 0.0), [ml], [ml])

    def load_cast(self, dst, dcol0, src_ap, ncols, gT, stg_rot, nk=8):
        k = self
        c0 = 0
        while c0 < ncols:
            cw = min(256, ncols - c0)
            stg = stg_rot.next()
            for kk in range(nk):
                k.fw.dma("sp", stg.a[:, kk, 0:cw], src_ap[kk * 128:(kk + 1) * 128, c0:c0 + cw], [], [stg], stg)
            for kk in range(nk):
                o = dst.a[:, kk, dcol0 + c0:dcol0 + c0 + cw]
                i = stg.a[:, kk, 0:cw]
                if gT is None:
                    if kk % 2:
                        k.pool(lambda e: e.tensor_copy(out=o, in_=i), [stg], [dst])
                    else:
                        k.dve(lambda e: e.tensor_copy(out=o, in_=i), [stg], [dst])
                else:
                    g = gT.a[:, kk:kk + 1]
                    if kk % 2:
                        k.pool(lambda e: e.tensor_scalar(out=o, in0=i, scalar1=g, scalar2=None, op0=ALU.mult), [stg, gT], [dst])
                    else:
                        k.dve(lambda e: e.tensor_scalar(out=o, in0=i, scalar1=g, scalar2=None, op0=ALU.mult), [stg, gT], [dst])
            c0 += cw

    def small_load(self, es, name, shape, src_ap):
        b = self.sb(es, name, shape, F32)
        self.fw.dma("sp", b.a[:], src_ap, [], [b], b)
        return b

    def load_h(self, l, ti, first, want_hb=False):
        k = self
        fw = k.fw
        xt = k.xrot.next()
        rows = slice(ti * 128, (ti + 1) * 128)
        fw.dma("sp", xt.a[:], k.dr["x"][rows, :], [], [xt], xt)
        for ll in [q for q in self.layers if q < l]:
            for half in range(2):
                rt = k.rrot.next()
                hc = slice(half * 512, (half + 1) * 512)
                fw.dma("sp", rt.a[:, 0:512], k.red[ll][rows, hc], [k.Bred[ll]], [rt], rt)
                k.dve(lambda e: e.tensor_add(out=xt.a[:, hc], in0=xt.a[:, hc], in1=rt.a[:, 0:512]), [xt, rt], [xt])
        if k.stage <= 1.1:
            return None
        hb = k.hbrot.next()
        junk, st = hb, k.xstat.next()
        k.act(lambda e: e.activation(out=junk.a[:], in_=xt.a[:], func=AF.Square, scale=float(D ** -0.5),
                                     accum_out=st.a[:, 0:1]), [xt], [junk, st])
        if k.stage <= 1.2:
            return None
        k.dve(lambda e: e.tensor_scalar_add(out=st.a[:, 1:2], in0=st.a[:, 0:1], scalar1=EPS), [st], [st])
        k.act(lambda e: e.sqrt(out=st.a[:, 1:2], in_=st.a[:, 1:2]), [st], [st])
        k.dve(lambda e: e.reciprocal(out=st.a[:, 2:3], in_=st.a[:, 1:2]), [st], [st])
        if k.stage <= 1.3:
            return None
        k.act(lambda e: e.activation(out=hb.a[:], in_=xt.a[:], func=AF.Copy, scale=st.a[:, 2:3]), [xt, st], [hb])
        if k.stage <= 1.4:
            return None
        hT = k.hTrot.next()
        for half in range(2):
            slot = k.mib.next()
            for q in range(4):
                kk = half * 4 + q
                k.pe(lambda e: e.transpose(out=slot.a[:, q * 128:(q + 1) * 128], in_=hb.a[:, kk * 128:(kk + 1) * 128],
                                           identity=k.ident_b.a[:]), [hb, k.ident_b], [slot])
            if k.stage <= 1.45:
                continue
            for q in range(4):
                k.evac(hT.a[:, half * 4 + q, 1:129], slot.a[:, q * 128:(q + 1) * 128], [slot], [hT])
        if k.stage <= 1.5:
            return None
        if first:
            k.dve(lambda e: e.memset(hT.a[:, :, 0:1], 0.0), [], [hT])
        else:
            prev = k.hT_prev
            k.dve(lambda e: e.tensor_copy(out=hT.a[:, :, 0:1], in_=prev.a[:, :, 128:129]), [prev], [hT])
        k.hT_prev = hT
        return hT

    def rstd_cols(self, st, n, kk_cols=()):
        k = self
        k.act(lambda e: e.sqrt(out=st.a[:, 0:n], in_=st.a[:, 0:n]), [st], [st])
        for c in kk_cols:
            k.dve(lambda e: e.tensor_scalar_max(out=st.a[:, c:c + 1], in0=st.a[:, c:c + 1], scalar1=1e-12), [st], [st])
        k.dve(lambda e: e.reciprocal(out=st.a[:, 0:n], in_=st.a[:, 0:n]), [st], [st])

    def passA(self, es, l):
        k = self
        fw = k.fw
        S, NT, NCH = k.S, k.NT, k.NCH
        dr = k.dr
        L1 = l > 0
        NS = 544 if L1 else 512
        vec = k.sb(es, "vec", [128, V_MU], F32)
        fw.dma("sp", vec.a[:], dr["vec%d" % l][:, 0:V_MU].partition_broadcast(128), [], [vec], vec)
        gT = k.small_load(es, "gT", [128, 8], dr["gT%d" % l])
        qagT = k.small_load(es, "qagT", [128, 2], dr["qagT%d" % l])
        kvagT = k.small_load(es, "kvagT", [128, 1], dr["kvagT%d" % l])
        wup = k.small_load(es, "wup", [65, 128], dr["wup%d" % l])
        aup = k.small_load(es, "aup", [65, 128], dr["aup%d" % l])
        wA = k.sb(es, "wA", [128, 8, NA], BF16)
        wS1 = k.sb(es, "wS1", [128, 8, NS], BF16)
        wS2 = k.sb(es, "wS2", [128, 8, NS], BF16)
        wuq = k.sb(es, "wuq", [128, 2, 192], BF16)
        wukv = k.sb(es, "wukv", [128, 1, 256], BF16)
        with contextlib.ExitStack() as est:
            stg_rot = Rot([k.sb(est, "stg", [128, 8, 256], F32) for _ in range(2)])
            k.load_cast(wA, 0, dr["wA%d" % l], NA, gT, stg_rot)
            k.load_cast(wuq, 0, dr["wuq%d" % l], 192, qagT, stg_rot, nk=2)
            k.load_cast(wukv, 0, dr["wukv%d" % l], 256, kvagT, stg_rot, nk=1)
            muv = k.sb(est, "muv", [128, 512], F32)
            fw.dma("sp", muv.a[:], dr["vec%d" % l][:, V_MU:V_MU + 512].partition_broadcast(128), [], [muv], muv)
            tmp = k.sb(est, "wtmp", [128, 256], F32)
            tmp2 = k.sb(est, "wtmp2", [128, 256], F32)
            for c0 in (0, 256):
                stg = stg_rot.next()
                for kk in range(8):
                    fw.dma("sp", stg.a[:, kk, :], dr["wS%d" % l][kk * 128:(kk + 1) * 128, c0:c0 + 256], [], [stg], stg)
                mu = muv.a[:, c0:c0 + 256]
                for kk in range(8):
                    g = gT.a[:, kk:kk + 1]
                    k.dve(lambda e: e.tensor_mul(out=tmp.a[:], in0=stg.a[:, kk, :], in1=mu), [stg, muv], [tmp])
                    k.dve(lambda e: e.tensor_scalar(out=wS2.a[:, kk, c0:c0 + 256], in0=tmp.a[:], scalar1=g, scalar2=None,
                                                    op0=ALU.mult), [tmp, gT], [wS2])
                    k.dve(lambda e: e.tensor_sub(out=tmp2.a[:], in0=stg.a[:, kk, :], in1=tmp.a[:]), [stg, tmp], [tmp2])
                    k.dve(lambda e: e.tensor_scalar(out=wS1.a[:, kk, c0:c0 + 256], in0=tmp2.a[:], scalar1=g, scalar2=None,
                                                    op0=ALU.mult), [tmp2, gT], [wS1])
            if L1:
                muVT = k.small_load(est, "muVT", [128, 4], dr["muVT"])
                vdn = k.sb(est, "vdn", [128, 4, 32], F32)
                for kc in range(4):
                    fw.dma("sp", vdn.a[:, kc, :], dr["vdown"][kc * 128:(kc + 1) * 128, :], [], [vdn], vdn)
                wv = k.sb(est, "wv", [128, 4, 128], F32)
                wv1 = k.sb(est, "wv1", [128, 4, 128], F32)
                wv2 = k.sb(est, "wv2", [128, 4, 128], F32)
                for dc in range(8):
                    for kc in range(4):
                        fw.dma("sp", wv.a[:, kc, :], dr["wvT"][kc * 128:(kc + 1) * 128, dc * 128:(dc + 1) * 128], [], [wv], wv)
                    for kc in range(4):
                        k.dve(lambda e: e.tensor_scalar(out=wv2.a[:, kc, :], in0=wv.a[:, kc, :], scalar1=muVT.a[:, kc:kc + 1],
                                                        scalar2=None, op0=ALU.mult), [wv, muVT], [wv2])
                    k.dve(lambda e: e.tensor_sub(out=wv1.a[:], in0=wv.a[:], in1=wv2.a[:]), [wv, wv2], [wv1])
                    for (wsrc, wdst) in ((wv1, wS1), (wv2, wS2)):
                        slot = k.mif.next()
                        for kc in range(4):
                            k.pe(lambda e: e.matmul(slot.a[:, 0:32], lhsT=wsrc.a[:, kc, :], rhs=vdn.a[:, kc, :],
                                                    start=(kc == 0), stop=(kc == 3)), [wsrc, vdn], [slot])
                        k.dve(lambda e: e.tensor_scalar(out=wdst.a[:, dc, 512:544], in0=slot.a[:, 0:32], scalar1=gT.a[:, dc:dc + 1],
                                                        scalar2=None, op0=ALU.mult), [slot, gT], [wdst])
        fw.barrier()
        if k.stage <= 1:
            return
        if L1:
            vup = k.small_load(es, "vup", [33, 128], dr["vup"])
        kTm = [k.sb(es, "kTm%d" % h, [96, S], BF16) for h in range(2)]
        kTf = k.sb(es, "kTf", [128, S], BF16)
        Vm = k.sb(es, "Vm", [128, NT, 2, 65], BF16)
        Vf = k.sb(es, "Vf", [128, NT, 2, 65], BF16)
        ncum = k.sb(es, "ncum", [128, NT, 2], F32)
        k.xrot = Rot([k.sb(es, "xt", [128, D], F32) for _ in range(2)])
        k.xstat = Rot([k.sb(es, "xst", [128, 4], F32) for _ in range(2)])
        k.hbrot = Rot([k.sb(es, "hb", [128, D], BF16) for _ in range(1)])
        k.hTrot = Rot([k.sb(es, "hT", [128, 8, 129], BF16) for _ in range(2)])
        csr = Rot([k.sb(es, "cs", [128, 64], F32) for _ in range(2)])
        s1r = Rot([k.sb(es, "s1", [128, 418], F32) for _ in range(1)])
        s2r = Rot([k.sb(es, "s2", [128, 384], F32) for _ in range(1)])
        s4r = Rot([k.sb(es, "s4", [128, 512], F32) for _ in range(1)])
        s5r = Rot([k.sb(es, "s5", [128, 32], F32) for _ in range(2)])
        sga = k.sb(es, "sga", [128, 4, 128], BF16)
        sgb = k.sb(es, "sgb", [128, 4, 128], BF16)
        sgc = Rot([k.sb(es, "sgc", [128, 128], F32) for _ in range(1)])
        sq = k.sb(es, "sq", [128, 512], F32)
        k.rrot = Rot([sq])
        stA = Rot([k.sb(es, "stA", [128, 16], F32) for _ in range(2)])
        stB = Rot([k.sb(es, "stB", [128, 8], F32) for _ in range(2)])
        cqn = k.sb(es, "cqn", [128, 384], BF16)
        cT = k.sb(es, "cT", [128, 3, 128], BF16)
        qk = k.sb(es, "qk", [128, 448], F32)
        qn = k.sb(es, "qn", [128, 2, 96], F32)
        kn = k.sb(es, "kn", [128, 2, 96], F32)
        rp = k.sb(es, "rp", [128, 4, 32], F32)
        qb = k.sb(es, "qb", [128, 2, 96], BF16)
        kb = k.sb(es, "kb", [128, 2, 96], BF16)
        fqb = k.sb(es, "fqb", [128, 128], BF16)
        fkb = k.sb(es, "fkb", [128, 128], BF16)
        qTm = [k.sb(es, "qTm%d" % h, [96, 512], BF16) for h in range(2)]
        qTf = k.sb(es, "qTf", [128, 512], BF16)
        lf = Rot([k.sb(es, "lf", [128, 4], F32) for _ in range(2)])
        cumr = Rot([k.sb(es, "cum", [128, 2], F32) for _ in range(2)])
        dcum = k.sb(es, "dcum", [128, 2, 128], F32)
        cumbc = k.sb(es, "cumbc", [128, 2, 512], F32)
        pTr = Rot([k.sb(es, "pT", [128, 512], BF16) for _ in range(2)])
        ftmp = Rot([k.sb(es, "ftmp", [128, 512], F32) for _ in range(1)])
        rinv = k.sb(es, "rinv", [128, 4], F32)
        ogm = k.sb(es, "ogm", [128, 4, 128], BF16)
        ogf = k.sb(es, "ogf", [128, 4, 128], BF16)
        oTr = Rot([k.sb(es, "oT", [128, 512], BF16) for _ in range(1)])
        ocb = k.sb(es, "ocb", [128, 128], BF16)
        oTc = Rot([k.sb(es, "oTc", [128, 128], BF16) for _ in range(2)])
        twT = k.sb(es, "twT", [65, 128], F32)
        alT = k.sb(es, "alT", [65, 128], F32)
        k.pool(lambda e: e.memset(twT.a[64:65, :], 1.0), [], [twT])
        k.pool(lambda e: e.memset(alT.a[64:65, :], 1.0), [], [alT])
        tw = k.sb(es, "tw", [128, 64], F32)
        lw = k.sb(es, "lw", [128, 128], F32)
        av = k.sb(es, "av", [128, 128], F32)
        kkn = k.sb(es, "kkn", [128, 128], F32)
        kmod = k.sb(es, "kmod", [128, 128], F32)
        vv = k.sb(es, "vv", [128, 128], F32)
        rt1 = k.sb(es, "rt1", [128, 128], F32)
        rt2 = k.sb(es, "rt2", [128, 128], F32)
        bon = k.sb(es, "bon", [128, 2], F32)
        E1 = k.sb(es, "E1", [128, 128], F32)
        E2 = k.sb(es, "E2", [128, 128], F32)
        E3 = k.sb(es, "E3", [128, 128], F32)
        qa_ = k.sb(es, "qalpha", [128, 128], F32)
        qk_ = k.sb(es, "qkt", [128, 128], F32)
        qb_ = k.sb(es, "qbeta", [128, 128], F32)
        qr_ = k.sb(es, "qr", [128, 128], F32)
        qa_hi = k.sb(es, "qahi", [128, 128], F32)
        qk_hi = k.sb(es, "qkhi", [128, 128], F32)
        k.pool(lambda e: e.memset(qa_hi.a[:], 0.0), [], [qa_hi])
        k.pool(lambda e: e.memset(qk_hi.a[:], 0.0), [], [qk_hi])
        qT4 = [k.sb(es, "qT4_%d" % h, [64, 4, 128], F32) for h in range(2)]
        rlo = [k.sb(es, "rlo%d" % h, [64, 128], F32) for h in range(2)]
        rhi = [k.sb(es, "rhi%d" % h, [64, 128], F32) for h in range(2)]
        for h in range(2):
            k.pool(lambda e: e.memset(rlo[h].a[:], 0.0), [], [rlo[h]])
            k.pool(lambda e: e.memset(rhi[h].a[:], 0.0), [], [rhi[h]])
        GA = [k.sb(es, "GA%d" % h, [128, 256], F32) for h in range(2)]
        GK = [k.sb(es, "GK%d" % h, [128, 256], F32) for h in range(2)]
        XT = [Rot([k.sb(es, "XT%d_%d" % (h, i), [128, 128], F32) for i in range(2)]) for h in range(2)]
        XX = [Rot([k.sb(es, "XX%d_%d" % (h, i), [128, 128], F32) for i in range(2)]) for h in range(2)]
        PP = [Rot([k.sb(es, "PP%d_%d" % (h, i), [128, 128], F32) for i in range(2)]) for h in range(2)]
        Wb = [k.sb(es, "Wb%d" % h, [128, 64], F32) for h in range(2)]
        Ub = [k.sb(es, "Ub%d" % h, [128, 64], F32) for h in range(2)]
        Mst = [Rot([k.sb(es, "M%d_%d" % (h, i), [64, 64], F32) for i in range(3)]) for h in range(2)]
        pC = [k.sb(es, "pC%d" % h, [64, 2], F32) for h in range(2)]
        mtmp = [k.sb(es, "mtmp%d" % h, [64, 64], F32) for h in range(2)]
        yv = k.sb(es, "yv", [128, 2, 64], F32)
        ysq = k.sb(es, "ysq", [128, 2, 64], F32)
        yst = k.sb(es, "yst", [128, 8], F32)
        vdT = k.sb(es, "vdT", [33, 128], F32)
        k.pool(lambda e: e.memset(vdT.a[32:33, :], 1.0), [], [vdT])
        vfl = k.sb(es, "vfl", [128, 128], F32)
        nu = rt1
        Mcur = []
        for h in range(2):
            m0 = Mst[h].next()
            k.pool(lambda e: e.memset(m0.a[:], 0.0), [], [m0])
            Mcur.append(m0)
        cum_prev = None
        qscale_m = float(96 ** -0.5)
        qscale_f = float(64 ** -0.5)

        for ch in range(NCH):
            for jq in range(4):
                ti = ch * 4 + jq
                rows = slice(ti * 128, (ti + 1) * 128)
                cols = slice(ti * 128, (ti + 1) * 128)
                ccols = slice(jq * 128, (jq + 1) * 128)
                if ti == 0:
                    hT_nextA = k.load_h(l, 0, True)
                hT = hT_nextA
                if k.stage <= 2:
                    continue
                cs = csr.next()
                fw.dma("sp", cs.a[:], dr["cs"][rows, :], [], [cs], cs)
                s1, s2, s4, s5 = s1r.next(), s2r.next(), s4r.next(), s5r.next()
                sg_c = sgc.next()
                p = k.pj.next()
                for kk in range(8):
                    k.pe(lambda e: e.matmul(p.a[:, 0:418], lhsT=hT.a[:, kk, 1:129], rhs=wA.a[:, kk, 0:418],
                                            start=(kk == 0), stop=(kk == 7)), [hT, wA], [p])
                k.evac(s1.a[:], p.a[:, 0:418], [p], [s1])
                p = k.pj.next()
                for kk in range(8):
                    k.pe(lambda e: e.matmul(p.a[:, 0:512], lhsT=hT.a[:, kk, 1:129], rhs=wA.a[:, kk, 418:930],
                                            start=(kk == 0), stop=(kk == 7)), [hT, wA], [p])
                k.dve(lambda e: e.tensor_copy(out=s2.a[:], in_=p.a[:, 0:384]), [p], [s2])
                k.act(lambda e: e.activation(out=sgb.a[:, jq, :], in_=p.a[:, 384:512], func=AF.Silu), [p], [sgb])
                p = k.pj.next()
                for kk in range(8):
                    k.pe(lambda e: e.matmul(p.a[:, 0:256], lhsT=hT.a[:, kk, 1:129], rhs=wA.a[:, kk, 930:1186],
                                            start=(kk == 0), stop=(kk == 7)), [hT, wA], [p])
                k.act(lambda e: e.activation(out=sga.a[:, jq, :], in_=p.a[:, 0:128], func=AF.Silu), [p], [sga])
                k.act(lambda e: e.activation(out=sg_c.a[:], in_=p.a[:, 128:256], func=AF.Silu), [p], [sg_c])
                p = k.pj.next()
                for kk in range(8):
                    k.pe(lambda e: e.matmul(p.a[:, 0:512], lhsT=hT.a[:, kk, 1:129], rhs=wS1.a[:, kk, 0:512],
                                            start=(kk == 0), stop=False), [hT, wS1], [p])
                for kk in range(8):
                    k.pe(lambda e: e.matmul(p.a[:, 0:512], lhsT=hT.a[:, kk, 0:128], rhs=wS2.a[:, kk, 0:512],
                                            start=False, stop=(kk == 7)), [hT, wS2], [p])
                k.evac(s4.a[:], p.a[:, 0:512], [p], [s4])
                if L1:
                    p = k.mif.next()
                    for kk in range(8):
                        k.pe(lambda e: e.matmul(p.a[:, 0:32], lhsT=hT.a[:, kk, 1:129], rhs=wS1.a[:, kk, 512:544],
                                                start=(kk == 0), stop=False), [hT, wS1], [p])
                    for kk in range(8):
                        k.pe(lambda e: e.matmul(p.a[:, 0:32], lhsT=hT.a[:, kk, 0:128], rhs=wS2.a[:, kk, 512:544],
                                                start=False, stop=(kk == 7)), [hT, wS2], [p])
                    k.evac(s5.a[:], p.a[:, 0:32], [p], [s5])
                if ti + 1 < NT:
                    hT_nextA = k.load_h(l, ti + 1, False)
                if ti == k.dbg_tile:
                    k.dump(0, s1, s1.a[:], 418)
                    k.dump(1, s2, s2.a[:], 384)
                    k.dump(2, s4, s4.a[:], 512)
                    k.dump(15, sga, sga.a[:, jq, :], 128)
                if k.stage <= 3:
                    continue
                st = stA.next()
                k.dve(lambda e: e.tensor_mul(out=sq.a[:, 0:416], in0=s1.a[:, 0:416], in1=s1.a[:, 0:416]), [s1], [sq])
                k.dve(lambda e: e.tensor_reduce(out=st.a[:, 0:1], in_=sq.a[:, 0:256], axis=AX.X, op=ALU.add), [sq], [st])
                k.dve(lambda e: e.tensor_reduce(out=st.a[:, 1:2], in_=sq.a[:, 256:384], axis=AX.X, op=ALU.add), [sq], [st])
                k.dve(lambda e: e.tensor_reduce(out=st.a[:, 2:3], in_=sq.a[:, 384:416], axis=AX.X, op=ALU.add), [sq], [st])
                k.dve(lambda e: e.tensor_mul(out=sq.a[:, 0:256], in0=s2.a[:, 0:256], in1=s2.a[:, 0:256]), [s2, st], [sq])
                k.dve(lambda e: e.tensor_reduce(out=st.a[:, 3:7], in_=sq.a[:, 0:256].rearrange("p (h d) -> p h d", h=4),
                                                axis=AX.X, op=ALU.add), [sq], [st])
                k.dve(lambda e: e.tensor_mul(out=kkn.a[:], in0=s4.a[:, 128:256], in1=vec.a[:, V_KK:V_KK + 128]), [s4, vec], [kkn])
                k.dve(lambda e: e.tensor_mul(out=sq.a[:, 256:384], in0=kkn.a[:], in1=kkn.a[:]), [kkn], [sq])
                k.dve(lambda e: e.tensor_reduce(out=st.a[:, 7:9], in_=sq.a[:, 256:384].rearrange("p (h d) -> p h d", h=2),
                                                axis=AX.X, op=ALU.add), [sq], [st])
                k.dve(lambda e: e.tensor_scalar(out=st.a[:, 0:1], in0=st.a[:, 0:1], scalar1=1.0 / 256, scalar2=EPS, op0=ALU.mult, op1=ALU.add), [st], [st])
                k.dve(lambda e: e.tensor_scalar(out=st.a[:, 1:2], in0=st.a[:, 1:2], scalar1=1.0 / 128, scalar2=EPS, op0=ALU.mult, op1=ALU.add), [st], [st])
                k.dve(lambda e: e.tensor_scalar(out=st.a[:, 2:3], in0=st.a[:, 2:3], scalar1=1.0 / 32, scalar2=EPS, op0=ALU.mult, op1=ALU.add), [st], [st])
                k.dve(lambda e: e.tensor_scalar(out=st.a[:, 3:7], in0=st.a[:, 3:7], scalar1=1.0 / 64, scalar2=EPS, op0=ALU.mult, op1=ALU.add), [st], [st])
                k.rstd_cols(st, 9, kk_cols=(7, 8))
                k.act(lambda e: e.activation(out=cqn.a[:, 0:256], in_=s1.a[:, 0:256], func=AF.Copy, scale=st.a[:, 0:1]), [s1, st], [cqn])
                k.act(lambda e: e.activation(out=cqn.a[:, 256:384], in_=s1.a[:, 256:384], func=AF.Copy, scale=st.a[:, 1:2]), [s1, st], [cqn])
                for c in range(3):
                    k.transpose(cT.a[:, c, :], cqn.a[:, c * 128:(c + 1) * 128], 128, 128, BF16, [cqn], [cT])
                p = k.pj.next()
                for c in range(2):
                    k.pe(lambda e: e.matmul(p.a[:, 0:192], lhsT=cT.a[:, c, :], rhs=wuq.a[:, c, :], start=(c == 0), stop=(c == 1)),
                         [cT, wuq], [p])
                k.pe(lambda e: e.matmul(p.a[:, 192:448], lhsT=cT.a[:, 2, :], rhs=wukv.a[:, 0, :], start=True, stop=True),
                     [cT, wukv], [p])
                k.evac(qk.a[:], p.a[:, 0:448], [p], [qk])
                sb_ = stB.next()
                k.dve(lambda e: e.tensor_mul(out=sq.a[:, 0:320], in0=qk.a[:, 0:320], in1=qk.a[:, 0:320]), [qk], [sq])
                k.dve(lambda e: e.tensor_reduce(out=sb_.a[:, 0:2], in_=sq.a[:, 0:192].rearrange("p (h d) -> p h d", h=2),
                                                axis=AX.X, op=ALU.add), [sq], [sb_])
                k.dve(lambda e: e.tensor_reduce(out=sb_.a[:, 2:4], in_=sq.a[:, 192:320].rearrange("p (h d) -> p h d", h=2),
                                                axis=AX.X, op=ALU.add), [sq], [sb_])
                k.dve(lambda e: e.tensor_scalar(out=sb_.a[:, 0:2], in0=sb_.a[:, 0:2], scalar1=1.0 / 96, scalar2=EPS, op0=ALU.mult, op1=ALU.add), [sb_], [sb_])
                k.dve(lambda e: e.tensor_scalar(out=sb_.a[:, 2:4], in0=sb_.a[:, 2:4], scalar1=1.0 / 64, scalar2=EPS, op0=ALU.mult, op1=ALU.add), [sb_], [sb_])
                k.rstd_cols(sb_, 4)
                for h in range(2):
                    k.dve(lambda e: e.scalar_tensor_tensor(out=qn.a[:, h, :], in0=qk.a[:, h * 96:(h + 1) * 96], scalar=sb_.a[:, h:h + 1],
                                                           in1=vec.a[:, V_QG2 + h * 96:V_QG2 + (h + 1) * 96], op0=ALU.mult, op1=ALU.mult),
                          [qk, sb_, vec], [qn])
                    k.dve(lambda e: e.scalar_tensor_tensor(out=kn.a[:, h, 0:64], in0=qk.a[:, 192 + h * 64:192 + (h + 1) * 64],
                                                           scalar=sb_.a[:, 2 + h:3 + h],
                                                           in1=vec.a[:, V_KNG2 + h * 64:V_KNG2 + (h + 1) * 64], op0=ALU.mult, op1=ALU.mult),
                          [qk, sb_, vec], [kn])
                    k.dve(lambda e: e.scalar_tensor_tensor(out=kn.a[:, h, 64:96], in0=s1.a[:, 384:416], scalar=st.a[:, 2:3],
                                                           in1=vec.a[:, V_KRG:V_KRG + 32], op0=ALU.mult, op1=ALU.mult),
                          [s1, st, vec], [kn])
                cosv = cs.a[:, 0:32].rearrange("p (h d) -> p h d", h=2)
                sinv = cs.a[:, 32:64].rearrange("p (h d) -> p h d", h=2)
                for (src, dstb, scl) in ((qn, qb, qscale_m), (kn, kb, 1.0)):
                    x1 = src.a[:, :, 64:80]
                    x2 = src.a[:, :, 80:96]
                    r1 = rp.a[:, 0:1, :].rearrange("p a (h d) -> p (a h) d", h=2)
                    r2 = rp.a[:, 1:2, :].rearrange("p a (h d) -> p (a h) d", h=2)
                    r3 = rp.a[:, 2:3, :].rearrange("p a (h d) -> p (a h) d", h=2)
                    r4 = rp.a[:, 3:4, :].rearrange("p a (h d) -> p (a h) d", h=2)
                    k.dve(lambda e: e.tensor_mul(out=r1, in0=x1, in1=cosv), [src, cs], [rp])
                    k.dve(lambda e: e.tensor_mul(out=r2, in0=x2, in1=sinv), [src, cs, rp], [rp])
                    k.dve(lambda e: e.tensor_mul(out=r3, in0=x2, in1=cosv), [src, cs, rp], [rp])
                    k.dve(lambda e: e.tensor_mul(out=r4, in0=x1, in1=sinv), [src, cs, rp], [rp])
                    k.dve(lambda e: e.tensor_sub(out=src.a[:, :, 64:80], in0=r1, in1=r2), [rp, src], [src])
                    k.dve(lambda e: e.tensor_add(out=src.a[:, :, 80:96], in0=r3, in1=r4), [rp, src], [src])
                    k.act(lambda e: e.activation(out=dstb.a[:], in_=src.a[:], func=AF.Copy, scale=scl), [src], [dstb])
                for h in range(2):
                    k.transpose(qTm[h].a[:, ccols], qb.a[:, h, :], 128, 96, BF16, [qb], [qTm[h]])
                    k.transpose(kTm[h].a[:, cols], kb.a[:, h, :], 128, 96, BF16, [kb], [kTm[h]])
                k.dve(lambda e: e.tensor_copy(out=Vm.a[:, ti, :, 0:64], in_=qk.a[:, 320:448].rearrange("p (h d) -> p h d", h=2)), [qk], [Vm])
                k.dve(lambda e: e.tensor_copy(out=Vm.a[:, ti, :, 64:65], in_=k.ones_f.a[:, 0:2].rearrange("p (h o) -> p h o", o=1)), [k.ones_f], [Vm])
                if ti == k.dbg_tile:
                    k.dump(3, qk, qk.a[:], 448)
                    k.dump(4, qn, qn.a[:].rearrange("p h d -> p (h d)"), 192)
                    k.dump(5, kn, kn.a[:].rearrange("p h d -> p (h d)"), 192)
                    k.dump(16, st, st.a[:], 16)
                if k.stage <= 4:
                    continue
                for h in range(2):
                    k.dve(lambda e: e.scalar_tensor_tensor(out=sq.a[:, h * 64:(h + 1) * 64], in0=s2.a[:, h * 64:(h + 1) * 64],
                                                           scalar=st.a[:, 3 + h:4 + h], in1=vec.a[:, V_FQG2 + h * 64:V_FQG2 + (h + 1) * 64],
                                                           op0=ALU.mult, op1=ALU.mult), [s2, st, vec, sq], [sq])
                    k.dve(lambda e: e.scalar_tensor_tensor(out=sq.a[:, 128 + h * 64:128 + (h + 1) * 64], in0=s2.a[:, 128 + h * 64:128 + (h + 1) * 64],
                                                           scalar=st.a[:, 5 + h:6 + h], in1=vec.a[:, V_FKG2 + h * 64:V_FKG2 + (h + 1) * 64],
                                                           op0=ALU.mult, op1=ALU.mult), [s2, st, vec, sq], [sq])
                k.act(lambda e: e.activation(out=fqb.a[:], in_=sq.a[:, 0:128], func=AF.Copy, scale=qscale_f), [sq], [fqb])
                k.act(lambda e: e.activation(out=fkb.a[:], in_=sq.a[:, 128:256], func=AF.Copy), [sq], [fkb])
                k.transpose(qTf.a[:, ccols], fqb.a[:], 128, 128, BF16, [fqb], [qTf])
                k.transpose(kTf.a[:, cols], fkb.a[:], 128, 128, BF16, [fkb], [kTf])
                k.dve(lambda e: e.tensor_copy(out=Vf.a[:, ti, :, 0:64], in_=s2.a[:, 256:384].rearrange("p (h d) -> p h d", h=2)), [s2], [Vf])
                k.dve(lambda e: e.tensor_copy(out=Vf.a[:, ti, :, 64:65], in_=k.ones_f.a[:, 0:2].rearrange("p (h o) -> p h o", o=1)), [k.ones_f], [Vf])
                lf_ = lf.next()
                k.dve(lambda e: e.tensor_add(out=lf_.a[:, 0:2], in0=s1.a[:, 416:418], in1=vec.a[:, V_BF:V_BF + 2]), [s1, vec], [lf_])
                k.act(lambda e: e.activation(out=lf_.a[:, 0:2], in_=lf_.a[:, 0:2], func=AF.Exp, scale=-1.0), [lf_], [lf_])
                k.act(lambda e: e.activation(out=lf_.a[:, 0:2], in_=lf_.a[:, 0:2], func=AF.Ln, bias=1.0), [lf_], [lf_])
                k.dve(lambda e: e.tensor_scalar_mul(out=lf_.a[:, 2:4], in0=lf_.a[:, 0:2], scalar1=-1.0), [lf_], [lf_])
                cum = cumr.next()
                p = k.mif.next()
                k.pe(lambda e: e.matmul(p.a[:, 0:2], lhsT=k.triU.a[:], rhs=lf_.a[:, 2:4], start=True, stop=(cum_prev is None)),
                     [k.triU, lf_], [p])
                if cum_prev is not None:
                    cp = cum_prev
                    k.pe(lambda e: e.matmul(p.a[:, 0:2], lhsT=k.sel127.a[:], rhs=cp.a[:], start=False, stop=True),
                         [k.sel127, cp], [p])
                k.dve(lambda e: e.tensor_copy(out=cum.a[:], in_=p.a[:, 0:2]), [p], [cum])
                cum_prev = cum
                k.dve(lambda e: e.tensor_scalar_mul(out=ncum.a[:, ti, :], in0=cum.a[:], scalar1=-1.0), [cum], [ncum])
                for h in range(2):
                    k.dve(lambda e: e.tensor_scalar(out=dcum.a[:, h, :], in0=k.ident_f.a[:], scalar1=cum.a[:, h:h + 1], scalar2=None,
                                                    op0=ALU.mult), [k.ident_f, cum], [dcum])
                p = k.mif.next()
                for h in range(2):
                    k.pe(lambda e: e.matmul(p.a[:, h * 128:(h + 1) * 128], lhsT=k.ones_f.a[:], rhs=dcum.a[:, h, :], start=True, stop=True),
                         [k.ones_f, dcum], [p])
                k.evac(cumbc.a[:, :, ccols], p.a[:, 0:256].rearrange("p (h c) -> p h c", h=2), [p], [cumbc])

                if ti == k.dbg_tile:
                    k.dump(6, cum, cum.a[:], 2)
                    k.dump(7, cumbc, cumbc.a[:, 0, ccols], 128)
                    k.dump(17, sq, sq.a[:, 0:256], 256)
                if k.stage <= 5:
                    continue
                k.act(lambda e: e.activation(out=tw.a[:], in_=s4.a[:, 384:448], func=AF.Tanh), [s4], [tw])
                k.transpose(twT.a[0:64, :], tw.a[:], 128, 64, F32, [tw], [twT])
                k.transpose(alT.a[0:64, :], s4.a[:, 448:512], 128, 64, F32, [s4], [alT])
                p = k.mif.next()
                k.pe(lambda e: e.matmul(p.a[:, 0:128], lhsT=twT.a[:], rhs=wup.a[:], start=True, stop=True), [twT, wup], [p])
                k.pe(lambda e: e.matmul(p.a[:, 128:256], lhsT=alT.a[:], rhs=aup.a[:], start=True, stop=True), [alT, aup], [p])
                k.act(lambda e: e.activation(out=lw.a[:], in_=p.a[:, 0:128], func=AF.Sigmoid), [p], [lw])
                k.act(lambda e: e.activation(out=av.a[:], in_=p.a[:, 128:256], func=AF.Sigmoid), [p], [av])
                k.dve(lambda e: e.tensor_scalar_mul(out=lw.a[:], in0=lw.a[:], scalar1=-DECAY), [lw], [lw])
                for h in range(2):
                    k.dve(lambda e: e.tensor_scalar(out=kkn.a[:, h * 64:(h + 1) * 64], in0=kkn.a[:, h * 64:(h + 1) * 64],
                                                    scalar1=st.a[:, 7 + h:8 + h], scalar2=None, op0=ALU.mult), [kkn, st], [kkn])
                k.dve(lambda e: e.scalar_tensor_tensor(out=rt1.a[:], in0=av.a[:], scalar=-1.0, in1=vec.a[:, V_KA:V_KA + 128],
                                                       op0=ALU.add, op1=ALU.mult), [av, vec], [rt1])
                k.dve(lambda e: e.scalar_tensor_tensor(out=kmod.a[:], in0=rt1.a[:], scalar=1.0, in1=s4.a[:, 128:256],
                                                       op0=ALU.add, op1=ALU.mult), [rt1, s4], [kmod])
                if not L1:
                    k.dve(lambda e: e.tensor_copy(out=vv.a[:], in_=s4.a[:, 256:384]), [s4], [vv])
                    fw.dma("pool", k.vf_scr[rows, :], vv.a[:], [vv], [k.Bvf], vv)
                else:
                    fw.dma("sp", vfl.a[:], k.vf_scr[rows, :], [k.Bvf], [vfl], vfl)
                    k.transpose(vdT.a[0:32, :], s5.a[:], 128, 32, F32, [s5], [vdT])
                    p = k.mif.next()
                    k.pe(lambda e: e.matmul(p.a[:, 0:128], lhsT=vdT.a[:], rhs=vup.a[:], start=True, stop=True), [vdT, vup], [p])
                    k.act(lambda e: e.activation(out=nu.a[:], in_=p.a[:, 0:128], func=AF.Sigmoid), [p], [nu])
                    k.dve(lambda e: e.tensor_sub(out=rt2.a[:], in0=vfl.a[:], in1=s4.a[:, 256:384]), [vfl, s4], [rt2])
                    k.dve(lambda e: e.tensor_mul(out=rt2.a[:], in0=rt2.a[:], in1=nu.a[:]), [rt2, nu], [rt2])
                    k.dve(lambda e: e.tensor_add(out=vv.a[:], in0=rt2.a[:], in1=s4.a[:, 256:384]), [rt2, s4], [vv])
                k.dve(lambda e: e.tensor_mul(out=rt1.a[:], in0=s4.a[:, 0:128], in1=kmod.a[:]), [s4, kmod], [rt1])
                k.dve(lambda e: e.tensor_mul(out=rt1.a[:], in0=rt1.a[:], in1=vec.a[:, V_RK:V_RK + 128]), [rt1, vec], [rt1])
                k.dve(lambda e: e.tensor_reduce(out=bon.a[:], in_=rt1.a[:].rearrange("p (h d) -> p h d", h=2), axis=AX.X, op=ALU.add),
                      [rt1], [bon])
                if ti == k.dbg_tile:
                    k.dump(8, lw, lw.a[:], 128)
                    k.dump(9, av, av.a[:], 128)
                    k.dump(10, kkn, kkn.a[:], 128)
                    k.dump(11, kmod, kmod.a[:], 128)
                    k.dump(12, vv, vv.a[:], 128)
                if k.stage <= 6:
                    continue
                p = k.mif.next()
                k.pe(lambda e: e.matmul(p.a[:, 0:128], lhsT=k.triBD.a[:], rhs=lw.a[:], start=True, stop=True), [k.triBD, lw], [p])
                k.act(lambda e: e.activation(out=E1.a[:], in_=p.a[:, 0:128], func=AF.Exp), [p], [E1])
                k.act(lambda e: e.activation(out=E2.a[:], in_=p.a[:, 0:128], func=AF.Exp, scale=-1.0), [p], [E2])
                k.dve(lambda e: e.tensor_sub(out=E3.a[:], in0=p.a[:, 0:128], in1=lw.a[:]), [p, lw], [E3])
                k.act(lambda e: e.activation(out=E3.a[:], in_=E3.a[:], func=AF.Exp), [E3], [E3])
                for h in range(2):
                    pp = k.mif.next()
                    k.pe(lambda e: e.matmul(pp.a[0:64, 0:1], lhsT=lw.a[:, h * 64:(h + 1) * 64], rhs=k.triBD.a[:, 63:64], start=True, stop=True),
                         [lw, k.triBD], [pp])
                    k.pe(lambda e: e.matmul(pp.a[0:64, 1:2], lhsT=lw.a[:, h * 64:(h + 1) * 64], rhs=k.triBD.a[:, 127:128], start=True, stop=True),
                         [lw, k.triBD], [pp])
                    k.act(lambda e: e.activation(out=pC[h].a[:], in_=pp.a[0:64, 0:2], func=AF.Exp), [pp], [pC[h]])
                k.dve(lambda e: e.tensor_mul(out=qr_.a[:], in0=s4.a[:, 0:128], in1=E1.a[:]), [s4, E1], [qr_])
                k.dve(lambda e: e.tensor_mul(out=qk_.a[:], in0=kmod.a[:], in1=E2.a[:]), [kmod, E2], [qk_])
                k.dve(lambda e: e.tensor_mul(out=qa_.a[:], in0=kkn.a[:], in1=av.a[:]), [kkn, av], [qa_])
                k.dve(lambda e: e.tensor_mul(out=qa_.a[:], in0=qa_.a[:], in1=E2.a[:]), [qa_, E2], [qa_])
                k.dve(lambda e: e.scalar_tensor_tensor(out=qb_.a[:], in0=kkn.a[:], scalar=-1.0, in1=E3.a[:], op0=ALU.mult, op1=ALU.mult),
                      [kkn, E3], [qb_])
                k.dve(lambda e: e.tensor_copy(out=qa_hi.a[64:128, :], in_=qa_.a[64:128, :]), [qa_], [qa_hi])
                k.dve(lambda e: e.tensor_copy(out=qk_hi.a[64:128, :], in_=qk_.a[64:128, :]), [qk_], [qk_hi])
                def head_gen(h):
                    hc = slice(h * 64, (h + 1) * 64)
                    q4 = qT4[h]
                    for pair in range(2):
                        slot = k.mif.next()
                        for qi in range(2):
                            src = (qa_, qk_, qb_, qr_)[pair * 2 + qi]
                            k.pe(lambda e: e.transpose(out=slot.a[0:64, qi * 128:(qi + 1) * 128], in_=src.a[:, hc], identity=k.ident_f.a[:]),
                                 [src, k.ident_f], [slot])
                        k.evac(q4.a[:, pair * 2:pair * 2 + 2, :], slot.a[0:64, 0:256].rearrange("p (q c) -> p q c", q=2), [slot], [q4])
                        yield
                    k.dve(lambda e: e.tensor_copy(out=rlo[h].a[:, 0:64], in_=q4.a[:, 3, 0:64]), [q4], [rlo[h]])
                    k.dve(lambda e: e.tensor_copy(out=rhi[h].a[:, 64:128], in_=q4.a[:, 3, 64:128]), [q4], [rhi[h]])
                    aT, kT_, bT, rT = q4.a[:, 0, :], q4.a[:, 1, :], q4.a[:, 2, :], q4.a[:, 3, :]
                    brT = q4.a[:, 2:4, :].rearrange("p q c -> p (q c)")
                    slot = k.mif.next()
                    k.pe(lambda e: e.matmul(slot.a[:, 0:256], lhsT=aT, rhs=brT, start=True, stop=True), [q4], [slot])
                    k.dve(lambda e: e.tensor_mul(out=GA[h].a[:], in0=slot.a[:, 0:256], in1=k.mask2.a[:]), [slot, k.mask2], [GA[h]])
                    yield
                    slot = k.mif.next()
                    k.pe(lambda e: e.matmul(slot.a[:, 0:256], lhsT=kT_, rhs=brT, start=True, stop=True), [q4], [slot])
                    k.dve(lambda e: e.tensor_mul(out=GK[h].a[:], in0=slot.a[:, 0:256], in1=k.mask2.a[:]), [slot, k.mask2], [GK[h]])
                    yield
                    slot = k.mif.next()
                    k.pe(lambda e: e.matmul(slot.a[:, 0:128], lhsT=bT, rhs=aT, start=True, stop=True), [q4], [slot])
                    xt_ = XT[h].next()
                    k.dve(lambda e: e.tensor_mul(out=xt_.a[:], in0=slot.a[:, 0:128], in1=k.masksl.a[:]), [slot, k.masksl], [xt_])
                    yield
                    if k.stage <= 7:
                        return
                    Pc = PP[h].next()
                    k.dve(lambda e: e.tensor_add(out=Pc.a[:], in0=GA[h].a[:, 0:128], in1=k.ident_f.a[:]), [GA[h], k.ident_f], [Pc])
                    Xc_ap, Xc_b = GA[h].a[:, 0:128], GA[h]
                    XTc = xt_
                    for lev in range(5):
                        last = lev == 4
                        slot = k.mif.next()
                        k.pe(lambda e: e.matmul(slot.a[:, 0:128], lhsT=Xc_ap, rhs=XTc.a[:], start=True, stop=True), [Xc_b, XTc], [slot])
                        XTn = XT[h].next()
                        if not last:
                            slot2 = k.mif.next()
                            k.pe(lambda e: e.matmul(slot2.a[:, 0:128], lhsT=XTc.a[:], rhs=Xc_ap, start=True, stop=True), [Xc_b, XTc], [slot2])
                        k.evac(XTn.a[:], slot.a[:, 0:128], [slot], [XTn])
                        yield
                        if not last:
                            Xn = XX[h].next()
                            k.evac(Xn.a[:], slot2.a[:, 0:128], [slot2], [Xn])
                            yield
                        slot3 = k.mif.next()
                        k.pe(lambda e: e.matmul(slot3.a[:, 0:128], lhsT=XTn.a[:], rhs=Pc.a[:], start=True, stop=True), [XTn, Pc], [slot3])
                        Pn = PP[h].next()
                        k.dve(lambda e: e.tensor_add(out=Pn.a[:], in0=slot3.a[:, 0:128], in1=Pc.a[:]), [slot3, Pc], [Pn])
                        yield
                        Pc = Pn
                        XTc = XTn
                        if not last:
                            Xc_ap, Xc_b = Xn.a[:], Xn
                    TT = Pc
                    if k.stage <= 8:
                        return
                    M0 = Mcur[h]
                    W_, U_ = Wb[h], Ub[h]
                    slot = k.mif.next()
                    k.pe(lambda e: e.matmul(slot.a[0:64, 0:64], lhsT=q4.a[:, 2, 0:64], rhs=M0.a[:], start=True, stop=False), [q4, M0], [slot])
                    k.pe(lambda e: e.matmul(slot.a[0:64, 0:64], lhsT=GK[h].a[0:64, 0:64], rhs=vv.a[0:64, hc], start=False, stop=True),
                         [GK[h], vv], [slot])
                    k.evac(W_.a[0:64, :], slot.a[0:64, 0:64], [slot], [W_])
                    yield
                    slot = k.mif.next()
                    k.pe(lambda e: e.matmul(slot.a[0:64, 0:64], lhsT=TT.a[0:64, 0:64], rhs=W_.a[0:64, :], start=True, stop=True), [TT, W_], [slot])
                    k.evac(U_.a[0:64, :], slot.a[0:64, 0:64], [slot], [U_])
                    yield
                    slot = k.mif.next()
                    k.pe(lambda e: e.matmul(slot.a[0:64, 0:64], lhsT=qa_.a[0:64, hc], rhs=U_.a[0:64, :], start=True, stop=False), [qa_, U_], [slot])
                    k.pe(lambda e: e.matmul(slot.a[0:64, 0:64], lhsT=qk_.a[0:64, hc], rhs=vv.a[0:64, hc], start=False, stop=True), [qk_, vv], [slot])
                    M1 = Mst[h].next()
                    k.dve(lambda e: e.tensor_add(out=mtmp[h].a[:], in0=slot.a[0:64, 0:64], in1=M0.a[:]), [slot, M0], [mtmp[h]])
                    k.dve(lambda e: e.tensor_scalar(out=M1.a[:], in0=mtmp[h].a[:], scalar1=pC[h].a[:, 0:1], scalar2=None, op0=ALU.mult),
                          [mtmp[h], pC[h]], [M1])
                    yield
                    slot = k.mif.next()
                    k.pe(lambda e: e.matmul(slot.a[:, 0:64], lhsT=q4.a[:, 2, :], rhs=M1.a[:], start=True, stop=False), [q4, M1], [slot])
                    k.pe(lambda e: e.matmul(slot.a[:, 0:64], lhsT=GK[h].a[:, 0:128], rhs=vv.a[:, hc], start=False, stop=True),
                         [GK[h], vv], [slot])
                    k.evac(W_.a[64:128, :], slot.a[64:128, 0:64], [slot], [W_])
                    yield
                    slot = k.mif.next()
                    k.pe(lambda e: e.matmul(slot.a[:, 0:64], lhsT=TT.a[:, 0:128], rhs=W_.a[:], start=True, stop=True), [TT, W_], [slot])
                    k.evac(U_.a[64:128, :], slot.a[64:128, 0:64], [slot], [U_])
                    yield
                    slot = k.mif.next()
                    k.pe(lambda e: e.matmul(slot.a[0:64, 0:64], lhsT=qa_hi.a[:, hc], rhs=U_.a[:], start=True, stop=False), [qa_hi, U_], [slot])
                    k.pe(lambda e: e.matmul(slot.a[0:64, 0:64], lhsT=qk_hi.a[:, hc], rhs=vv.a[:, hc], start=False, stop=True), [qk_hi, vv], [slot])
                    M2 = Mst[h].next()
                    k.dve(lambda e: e.tensor_add(out=mtmp[h].a[:], in0=slot.a[0:64, 0:64], in1=M1.a[:]), [slot, M1], [mtmp[h]])
                    k.dve(lambda e: e.tensor_scalar(out=M2.a[:], in0=mtmp[h].a[:], scalar1=pC[h].a[:, 1:2], scalar2=None, op0=ALU.mult),
                          [mtmp[h], pC[h]], [M2])
                    yield
                    slot = k.mif.next()
                    k.pe(lambda e: e.matmul(slot.a[:, 0:64], lhsT=rlo[h].a[:], rhs=M0.a[:], start=True, stop=False), [rlo[h], M0], [slot])
                    k.pe(lambda e: e.matmul(slot.a[:, 0:64], lhsT=rhi[h].a[:], rhs=M1.a[:], start=False, stop=False), [rhi[h], M1], [slot])
                    k.pe(lambda e: e.matmul(slot.a[:, 0:64], lhsT=GA[h].a[:, 128:256], rhs=U_.a[:], start=False, stop=False), [GA[h], U_], [slot])
                    k.pe(lambda e: e.matmul(slot.a[:, 0:64], lhsT=GK[h].a[:, 128:256], rhs=vv.a[:, hc], start=False, stop=True), [GK[h], vv], [slot])
                    k.evac(yv.a[:, h, :], slot.a[:, 0:64], [slot], [yv])
                    yield
                    Mcur[h] = M2
                gens = [head_gen(0), head_gen(1)]
                alive = [True, True]
                while any(alive):
                    for gi in range(2):
                        if alive[gi]:
                            try:
                                next(gens[gi])
                            except StopIteration:
                                alive[gi] = False
                if ti == k.dbg_tile:
                    k.dump(13, yv, yv.a[:].rearrange("p h d -> p (h d)"), 128)
                    k.dump(18, GA[0], GA[0].a[:], 256)
                    k.dump(19, GK[0], GK[0].a[:], 256)
                if k.stage <= 9:
                    continue
                k.dve(lambda e: e.tensor_reduce(out=yst.a[:, 0:2], in_=yv.a[:], axis=AX.X, op=ALU.add), [yv], [yst])
                k.dve(lambda e: e.tensor_scalar_mul(out=yst.a[:, 0:2], in0=yst.a[:, 0:2], scalar1=1.0 / 64), [yst], [yst])
                for h in range(2):
                    k.dve(lambda e: e.tensor_scalar(out=yv.a[:, h, :], in0=yv.a[:, h, :], scalar1=yst.a[:, h:h + 1], scalar2=None, op0=ALU.subtract),
                          [yv, yst], [yv])
                k.dve(lambda e: e.tensor_mul(out=ysq.a[:], in0=yv.a[:], in1=yv.a[:]), [yv], [ysq])
                k.dve(lambda e: e.tensor_reduce(out=yst.a[:, 2:4], in_=ysq.a[:], axis=AX.X, op=ALU.add), [ysq], [yst])
                k.dve(lambda e: e.tensor_scalar(out=yst.a[:, 2:4], in0=yst.a[:, 2:4], scalar1=1.0 / 64, scalar2=GN_EPS, op0=ALU.mult, op1=ALU.add), [yst], [yst])
                k.act(lambda e: e.sqrt(out=yst.a[:, 2:4], in_=yst.a[:, 2:4]), [yst], [yst])
                k.dve(lambda e: e.reciprocal(out=yst.a[:, 2:4], in_=yst.a[:, 2:4]), [yst], [yst])
                for h in range(2):
                    hc = slice(h * 64, (h + 1) * 64)
                    k.dve(lambda e: e.scalar_tensor_tensor(out=yv.a[:, h, :], in0=yv.a[:, h, :], scalar=yst.a[:, 2 + h:3 + h],
                                                           in1=vec.a[:, V_LG + h * 64:V_LG + (h + 1) * 64], op0=ALU.mult, op1=ALU.mult),
                          [yv, yst, vec], [yv])
                    k.dve(lambda e: e.tensor_add(out=yv.a[:, h, :], in0=yv.a[:, h, :], in1=vec.a[:, V_LB + h * 64:V_LB + (h + 1) * 64]), [yv, vec], [yv])
                    k.dve(lambda e: e.scalar_tensor_tensor(out=yv.a[:, h, :], in0=vv.a[:, hc], scalar=bon.a[:, h:h + 1], in1=yv.a[:, h, :],
                                                           op0=ALU.mult, op1=ALU.add), [vv, bon, yv], [yv])
                if ti == k.dbg_tile:
                    k.dump(14, yv, yv.a[:].rearrange("p h d -> p (h d)"), 128)
                k.dve(lambda e: e.tensor_mul(out=ocb.a[:], in0=yv.a[:].rearrange("p h d -> p (h d)"), in1=sg_c.a[:]), [yv, sg_c], [ocb])
                oc = oTc.next()
                k.transpose(oc.a[:], ocb.a[:], 128, 128, BF16, [ocb], [oc])
                fw.dma("pool", k.oT_scr[2, :, cols], oc.a[:], [oc], [k.BoT], oc)

            for mixer in range(2 if k.stage > 10 else 0):
                og = ogm if mixer == 0 else ogf
                gate = sga if mixer == 0 else sgb
                Vr = Vm if mixer == 0 else Vf
                for h in range(2):
                    nkt = 4 * (ch + 1)
                    oa = k.oa
                    k.dve(lambda e: e.memset(oa.a[:], 0.0), [], [oa])
                    def emit_pv(kt, j0, pT):
                        for j in range(j0, 4):
                            k.pe(lambda e: e.matmul(oa.a[:, j, 0:65], lhsT=pT.a[:, j * 128:(j + 1) * 128], rhs=Vr.a[:, kt, h, :],
                                                    start=False, stop=(kt == 4 * ch + j), skip_group_check=True), [pT, Vr], [oa])
                    pend = None
                    for kt in range(nkt):
                        dj = kt - 4 * ch
                        j0 = max(0, dj)
                        c0 = j0 * 128
                        kc = slice(kt * 128, (kt + 1) * 128)
                        s = k.sc.next()
                        if mixer == 0:
                            k.pe(lambda e: e.matmul(s.a[:, c0:512], lhsT=kTm[h].a[:, kc], rhs=qTm[h].a[:, c0:512], start=True, stop=True),
                                 [kTm[h], qTm[h]], [s])
                        else:
                            hp = slice(h * 64, (h + 1) * 64)
                            k.pe(lambda e: e.matmul(s.a[:, c0:512], lhsT=kTf.a[hp, kc], rhs=qTf.a[hp, c0:512], start=True, stop=True),
                                 [kTf, qTf], [s])
                        pT = pTr.next()
                        if mixer == 0:
                            k.act(lambda e: e.activation(out=pT.a[:, c0:512], in_=s.a[:, c0:512], func=AF.Exp), [s], [pT])
                        else:
                            ft = ftmp.next()
                            k.dve(lambda e: e.tensor_add(out=ft.a[:, c0:512], in0=s.a[:, c0:512], in1=cumbc.a[:, h, c0:512]), [s, cumbc], [ft])
                            k.act(lambda e: e.activation(out=pT.a[:, c0:512], in_=ft.a[:, c0:512], func=AF.Exp, bias=ncum.a[:, kt, h:h + 1]),
                                  [ft, ncum], [pT])
                        if dj >= 0:
                            k.pool(lambda e: e.affine_select(out=pT.a[:, c0:c0 + 128], in_=pT.a[:, c0:c0 + 128], pattern=[[1, 128]],
                                                             compare_op=ALU.is_ge, fill=0.0, base=0, channel_multiplier=-1), [pT], [pT])
                        if pend is not None:
                            emit_pv(*pend)
                        pend = (kt, j0, pT)
                    emit_pv(*pend)
                    k.dve(lambda e: e.reciprocal(out=rinv.a[:], in_=oa.a[:, :, 64:65].rearrange("p j o -> p (j o)")), [oa], [rinv])
                    for j in range(4):
                        k.dve(lambda e: e.scalar_tensor_tensor(out=og.a[:, j, h * 64:(h + 1) * 64], in0=oa.a[:, j, 0:64], scalar=rinv.a[:, j:j + 1],
                                                               in1=gate.a[:, j, h * 64:(h + 1) * 64], op0=ALU.mult, op1=ALU.mult),
                              [oa, rinv, gate], [og])
                oT = oTr.next()
                for j in range(4):
                    k.transpose(oT.a[:, j * 128:(j + 1) * 128], og.a[:, j, :], 128, 128, BF16, [og], [oT])
                fw.dma("pool", k.oT_scr[mixer, :, ch * 512:(ch + 1) * 512], oT.a[:], [oT], [k.BoT], oT)

    def passB(self, es, l):
        k = self
        fw = k.fw
        dr = k.dr
        NT = k.NT
        gT = k.small_load(es, "gTb", [128, 8], dr["gT%d" % l])
        wG = k.sb(es, "wG", [128, 8, 3072], BF16)
        wp = k.sb(es, "wp", [128, 3, D], BF16)
        wo = k.sb(es, "wo", [128, 8, D], BF16)
        with contextlib.ExitStack() as est:
            stg_rot = Rot([k.sb(est, "stgb", [128, 8, 256], F32) for _ in range(2)])
            k.load_cast(wG, 0, dr["wG%d" % l], 3072, gT, stg_rot)
            k.load_cast(wo, 0, dr["wo%d" % l], D, None, stg_rot)
            for br in range(3):
                for c0 in range(0, D, 256):
                    stg = stg_rot.next()
                    fw.dma("sp", stg.a[:, 0, :], dr["wp%d" % l][br, :, c0:c0 + 256], [], [stg], stg)
                    k.dve(lambda e: e.tensor_copy(out=wp.a[:, br, c0:c0 + 256], in_=stg.a[:, 0, :]), [stg], [wp])
        fw.barrier()
        k.xrot = Rot([k.sb(es, "xtb", [128, D], F32) for _ in range(3)])
        k.rrot = Rot([k.sb(es, "rtb", [128, 512], F32) for _ in range(2)])
        k.xstat = Rot([k.sb(es, "xstb", [128, 4], F32) for _ in range(2)])
        k.hbrot = Rot([k.sb(es, "hbb", [128, D], BF16) for _ in range(2)])
        k.hTrot = Rot([k.sb(es, "hTb", [128, 8, 129], BF16) for _ in range(3)])
        oTl = Rot([k.sb(es, "oTl", [128, 3, 128], BF16) for _ in range(2)])
        gs = Rot([k.sb(es, "gs", [128, 512], F32) for _ in range(2)])
        mg = k.sb(es, "mg", [128, D], F32)
        mgt = Rot([k.sb(es, "mgt", [128, 512], F32) for _ in range(2)])
        mb = k.sb(es, "mb", [128, D], BF16)
        mT = k.sb(es, "mT", [128, 8, 128], BF16)
        po = Rot([k.sb(es, "po", [128, D], F32) for _ in range(2)])
        mgr = Rot([mg, k.sb(es, "mg2", [128, D], F32)])

        def stage_g(ti, hT):
            rows = slice(ti * 128, (ti + 1) * 128)
            mgc = mgr.next()
            ot = oTl.next()
            for br in range(3):
                fw.dma("sp", ot.a[:, br, :], k.oT_scr[br, :, rows], [k.BoT], [ot], ot)
            for br in range(3):
                for half in range(2):
                    hc = slice(half * 512, (half + 1) * 512)
                    pg = k.pj.next()
                    for kk in range(8):
                        k.pe(lambda e: e.matmul(pg.a[:], lhsT=hT.a[:, kk, 1:129], rhs=wG.a[:, kk, br * D + half * 512:br * D + (half + 1) * 512],
                                                start=(kk == 0), stop=(kk == 7)), [hT, wG], [pg])
                    g = gs.next()
                    k.act(lambda e: e.activation(out=g.a[:], in_=pg.a[:], func=AF.Sigmoid), [pg], [g])
                    pb = k.sc.next()
                    k.pe(lambda e: e.matmul(pb.a[:], lhsT=ot.a[:, br, :], rhs=wp.a[:, br, hc], start=True, stop=True), [ot, wp], [pb])
                    if br == 0:
                        k.dve(lambda e: e.tensor_mul(out=mgc.a[:, hc], in0=pb.a[:], in1=g.a[:]), [pb, g], [mgc])
                    else:
                        t_ = mgt.next()
                        k.dve(lambda e: e.tensor_mul(out=t_.a[:], in0=pb.a[:], in1=g.a[:]), [pb, g], [t_])
                        k.pool(lambda e: e.tensor_add(out=mgc.a[:, hc], in0=mgc.a[:, hc], in1=t_.a[:]), [mgc, t_], [mgc])
            return mgc

        def stage_o(ti, mgc):
            rows = slice(ti * 128, (ti + 1) * 128)
            k.act(lambda e: e.activation(out=mb.a[:], in_=mgc.a[:], func=AF.Copy), [mgc], [mb])
            for half in range(2):
                slot = k.mib.next()
                for q in range(4):
                    kk = half * 4 + q
                    k.pe(lambda e: e.transpose(out=slot.a[:, q * 128:(q + 1) * 128], in_=mb.a[:, kk * 128:(kk + 1) * 128],
                                               identity=k.ident_b.a[:]), [mb, k.ident_b], [slot])
                k.evac(mT.a[:, half * 4:half * 4 + 4, :], slot.a[:, 0:512].rearrange("p (q c) -> p q c", q=4), [slot], [mT])
            pout = po.next()
            for half in range(2):
                hc = slice(half * 512, (half + 1) * 512)
                pp = k.pj.next()
                for kk in range(8):
                    k.pe(lambda e: e.matmul(pp.a[:], lhsT=mT.a[:, kk, :], rhs=wo.a[:, kk, hc], start=(kk == 0), stop=(kk == 7)), [mT, wo], [pp])
                k.evac(pout.a[:, hc], pp.a[:], [pp], [pout])
            fw.dma("pool", k.part[l][rows, :], pout.a[:], [pout], [k.Bpart[l]], pout)

        hT_q = [k.load_h(l, 0, True)]
        if NT > 1:
            hT_q.append(k.load_h(l, 1, True))
        pend = None
        for ti in range(NT):
            hT = hT_q.pop(0)
            mgc = stage_g(ti, hT)
            if ti + 2 < NT:
                hT_q.append(k.load_h(l, ti + 2, True))
            if pend is not None:
                stage_o(*pend)
            pend = (ti, mgc)
        stage_o(*pend)

    def final(self, es, out):
        k = self
        fw = k.fw
        L = k.n_layers
        Sq = k.S // 4
        xr = Rot([k.sb(es, "xf", [128, D], F32) for _ in range(2)])
        rr = Rot([k.sb(es, "rf", [128, D], F32) for _ in range(2)])
        for i in range(Sq // 128):
            xt = xr.next()
            fw.dma("sp", xt.a[:], k.dr["xq"][i * 128:(i + 1) * 128, :], [], [xt], xt)
            for l in k.layers:
                rt = rr.next()
                fw.dma("sp", rt.a[:], k.rs[l][i * 128:(i + 1) * 128, :], [k.Brs[l]], [rt], rt)
                k.dve(lambda e: e.tensor_add(out=xt.a[:], in0=xt.a[:], in1=rt.a[:]), [xt, rt], [xt])
            fw.dma("pool", out[i * 128:(i + 1) * 128, :], xt.a[:], [xt], [k.Bout], xt)


IN_SIZES = (256, 128, 32, 512, 512, 512, 512, 8, 512, 1664, 512, 3072)
OFF = np.concatenate([[0], np.cumsum(IN_SIZES)]).tolist()
(O_CQ, O_CKV, O_KR, O_GA, O_FQ, O_FK, O_FV, O_FF, O_GB, O_SH, O_GC, O_MG) = OFF[:12]


def rope_cs(S):
    inv = (np.float32(10000.0) ** (-np.arange(0, 32, 2, dtype=np.float32) / np.float32(32))).astype(np.float32)
    ang = (np.arange(S, dtype=np.float32)[:, None] * inv[None, :]).astype(np.float32)
    c, s_ = np.cos(ang).astype(np.float32), np.sin(ang).astype(np.float32)
    return np.ascontiguousarray(np.concatenate([c, c, s_, s_], axis=1))


def core_inputs(inp, c, S, L, layers=None):
    f = lambda a: np.ascontiguousarray(a, dtype=np.float32)
    b, j = c // 4, c % 4
    hs = slice(128 * j, 128 * (j + 1))
    m = {"x": f(inp["x"][b, :S]), "xq": f(inp["x"][b, j * (S // 4):(j + 1) * (S // 4)]), "cs": rope_cs(S)}
    layers = list(range(L)) if layers is None else layers
    for l in layers:
        w = inp["w_in"][l]
        colsA = np.concatenate([
            np.arange(O_CQ, O_CQ + 416),
            np.arange(O_FF + 2 * j, O_FF + 2 * j + 2),
            np.arange(O_FQ + 128 * j, O_FQ + 128 * j + 128),
            np.arange(O_FK + 128 * j, O_FK + 128 * j + 128),
            np.arange(O_FV + 128 * j, O_FV + 128 * j + 128),
            np.arange(O_GB + 128 * j, O_GB + 128 * j + 128),
            np.arange(O_GA + 128 * j, O_GA + 128 * j + 128),
            np.arange(O_GC + 128 * j, O_GC + 128 * j + 128)])
        shl = np.concatenate([np.arange(128 * j, 128 * j + 128), np.arange(512 + 128 * j, 512 + 128 * j + 128),
                              np.arange(1024 + 128 * j, 1024 + 128 * j + 128), np.arange(1536, 1664)])
        m["wA%d" % l] = f(w[:, colsA])
        m["wS%d" % l] = f(w[:, O_SH + shl])
        m["wG%d" % l] = f(w[:, O_MG:O_MG + 3072])
        uq = inp["mla_w_uq"][l].reshape(256, 8, 96)[:, 2 * j:2 * j + 2].reshape(256, 192)
        m["wuq%d" % l] = f(uq)
        ukv = inp["mla_w_ukv"][l].reshape(128, 8, 128)[:, 2 * j:2 * j + 2]
        m["wukv%d" % l] = f(np.concatenate([ukv[:, 0, :64], ukv[:, 1, :64], ukv[:, 0, 64:], ukv[:, 1, 64:]], axis=1))
        m["wup%d" % l] = f(np.concatenate([inp["rwkv_w_up"][l][:, hs], inp["rwkv_w0"][l][None, hs]], axis=0))
        m["aup%d" % l] = f(np.concatenate([inp["rwkv_a_up"][l][:, hs], inp["rwkv_a0"][l][None, hs]], axis=0))
        qg = inp["mla_q_g"][l]
        vec = np.concatenate([
            qg, qg, inp["mla_knope_g"][l], inp["mla_knope_g"][l], inp["mla_krope_g"][l],
            inp["fox_q_g"][l], inp["fox_q_g"][l], inp["fox_k_g"][l], inp["fox_k_g"][l],
            inp["fox_b_f"][l][2 * j:2 * j + 2], inp["rwkv_k_k"][l][hs], inp["rwkv_k_a"][l][hs],
            inp["rwkv_r_k"][l].reshape(-1)[hs], inp["rwkv_lnx_g"][l][hs], inp["rwkv_lnx_b"][l][hs],
            inp["rwkv_mu"][l][shl]])
        assert vec.shape[0] == NV
        m["vec%d" % l] = f(vec[None, :])
        m["gT%d" % l] = f(inp["norm_g"][l].reshape(8, 128).T)
        m["qagT%d" % l] = f(inp["mla_qa_g"][l].reshape(2, 128).T)
        m["kvagT%d" % l] = f(inp["mla_kva_g"][l].reshape(1, 128).T)
        m["wp%d" % l] = f(np.stack([inp["w_pa"][l][hs], inp["w_pb"][l][hs], inp["w_pc"][l][hs]]))
        m["wo%d" % l] = f(inp["w_out"][l])
    if 1 in layers:
        w = inp["w_in"][1]
        m["wvT"] = f(w[:, O_SH + 1024:O_SH + 1536].T)
        m["muVT"] = f(inp["rwkv_mu"][1][1024:1536].reshape(4, 128).T)
        m["vdown"] = f(inp["rwkv_v_down"][0])
        m["vup"] = f(np.concatenate([inp["rwkv_v_up"][0][:, hs], inp["rwkv_v0"][0][None, hs]], axis=0))
    return m


_CACHE = {}


def run(inp, S, L):
    key = (S, L)
    if key not in _CACHE:
        _CACHE[key] = Kern(S, L).build()
    nc = _CACHE[key]
    in_maps = [core_inputs(inp, c, S, L) for c in range(8)]
    res = run_bass_kernel_spmd(nc, in_maps, core_ids=list(range(8)))
    global LAST
    LAST = res
    out = np.zeros((2, S, D), np.float32)
    q = S // 4
    for c in range(8):
        b, j = c // 4, c % 4
        out[b, j * q:(j + 1) * q] = res.results[c]["out"]
    return out


def run_split(inp, S):
    q = S // 4
    key = (S, "l0")
    if key not in _CACHE:
        _CACHE[key] = Kern(S, 1, layers=[0]).build()
    res0 = run_bass_kernel_spmd(_CACHE[key], [core_inputs(inp, c, S, 2, layers=[0]) for c in range(8)], core_ids=list(range(8)))
    x1 = np.zeros((2, S, D), np.float32)
    for c in range(8):
        x1[c // 4, (c % 4) * q:(c % 4 + 1) * q] = res0.results[c]["out"]
    key = (S, "l1")
    if key not in _CACHE:
        _CACHE[key] = Kern(S, 2, layers=[1]).build()
    inp1 = dict(inp)
    inp1["x"] = x1
    maps = []
    for c in range(8):
        m = core_inputs(inp1, c, S, 2, layers=[1])
        m["vf"] = np.ascontiguousarray(res0.results[c]["vf"])
        maps.append(m)
    res1 = run_bass_kernel_spmd(_CACHE[key], maps, core_ids=list(range(8)))
    out = np.zeros((2, S, D), np.float32)
    for c in range(8):
        out[c // 4, (c % 4) * q:(c % 4 + 1) * q] = res1.results[c]["out"]
    return out


def kernel(**inputs):
    inp = {k: np.asarray(v) for k, v in inputs.items()}
    return run(inp, inp["x"].shape[1], 2)
```

```python
import contextlib
import numpy as np
import concourse.bass as bass
import concourse.mybir as mybir
from concourse.bass_utils import run_bass_kernel_spmd

F32 = mybir.dt.float32
BF16 = mybir.dt.bfloat16
AF = mybir.ActivationFunctionType
ALU = mybir.AluOpType
AX = mybir.AxisListType
SEM_LIMIT = 30000

D = 1024
EPS = 1e-6
GN_EPS = 64e-5
DECAY = 0.606531
NA = 1186
V_QG2, V_KNG2, V_KRG, V_FQG2, V_FKG2, V_BF, V_KK, V_KA, V_RK, V_LG, V_LB, V_MU = (
    0, 192, 320, 352, 480, 608, 610, 738, 866, 994, 1122, 1250)
NV = 1762


class T:
    __slots__ = ("name", "w", "r", "nowaw", "stream")

    def __init__(self, name, nowaw=False):
        self.name = name
        self.w = {}
        self.r = {}
        self.nowaw = nowaw
        self.stream = None


class B:
    def __init__(self, a, name, nowaw=False, psum=False):
        self.a = a
        self.T = T(name, nowaw)
        self.psum = psum


class Rot:
    def __init__(self, items):
        self.items = items
        self.i = 0

    def next(self):
        x = self.items[self.i % len(self.items)]
        self.i += 1
        return x


class FW:
    ENG = ("pe", "dve", "act", "pool", "sp")

    def __init__(self, nc, es):
        self.nc = nc
        self.es = es
        self.e = {"pe": nc.tensor, "dve": nc.vector, "act": nc.scalar, "pool": nc.gpsimd, "sp": nc.sync}
        self.sem = {}
        self.cnt = {}
        self.cur = {}
        self.nsem = 0
        self.seen = {k: {} for k in self.ENG}
        self.free_dma = []
        self.used_dma = []
        for k in self.ENG:
            self._new_key(k)

    def _new_key(self, stream):
        key = "%s#%d" % (stream, self.nsem)
        self.sem[key] = self.es.enter_context(self.nc.semaphore("s%d" % self.nsem))
        self.nsem += 1
        self.cnt[key] = 0
        self.cur[stream] = key
        return key

    def _wait(self, eng, key, seq):
        if self.seen[eng].get(key, 0) >= seq:
            return
        self.seen[eng][key] = seq
        self.e[eng].wait_ge(self.sem[key], seq)

    def deps(self, eng, reads, writes):
        own = eng + "#"
        for b in reads:
            for k, s in b.T.w.items():
                if eng == "pe" and k.startswith(own):
                    continue
                self._wait(eng, k, s)
        for b in writes:
            t = b.T
            if not t.nowaw:
                for k, s in t.w.items():
                    if eng == "pe" and k.startswith(own):
                        continue
                    self._wait(eng, k, s)
            for k, s in t.r.items():
                if k.startswith(own):
                    continue
                self._wait(eng, k, s)

    def done(self, ins, stream, inc, reads, writes):
        key = self.cur[stream]
        if self.cnt[key] + inc > SEM_LIMIT:
            key = self._new_key(stream)
        self.cnt[key] += inc
        seq = self.cnt[key]
        ins.then_inc(self.sem[key], inc)
        for b in reads:
            b.T.r[key] = seq
        for b in writes:
            t = b.T
            if t.nowaw:
                t.w[key] = seq
            else:
                t.w = {key: seq}
                t.r = {}

    def op(self, eng, fn, R, W):
        pr = [b for b in R if b.psum]
        if pr:
            R = [b for b in R if not b.psum]
            W = list(W) + [b for b in pr if b not in W]
        self.deps(eng, R, W)
        ins = fn(self.e[eng])
        self.done(ins, eng, 1, R, W)

    def dma(self, issuer, out, in_, R, W, slot):
        t = slot.T
        if t.stream is None:
            self.nstream = getattr(self, "nstream", 0) + 1
            t.stream = "dma%d" % self.nstream
            if self.free_dma:
                self.cur[t.stream] = self.free_dma.pop()
            else:
                self._new_key(t.stream)
            self.used_dma.append(t.stream)
        self.deps(issuer, R, W)
        ins = self.e[issuer].dma_start(out=out, in_=in_)
        self.done(ins, t.stream, 16, R, W)

    def barrier(self):
        snap = {k: c for k, c in self.cnt.items() if c > 0}
        for eng in self.ENG:
            for k, c in snap.items():
                if k.startswith(eng + "#"):
                    continue
                self._wait(eng, k, c)
        keep = getattr(self, "keep", set())
        for st in self.used_dma:
            if st not in keep:
                self.free_dma.append(self.cur[st])
        self.used_dma = [st for st in self.used_dma if st in keep]


class StopBuild(Exception):
    pass


class Kern:
    stage = 99
    dve_only = False
    debug = False
    dbg_tile = 0

    def dump(self, idx, src_b, ap, n):
        if not self.debug:
            return
        self.fw.dma("sp", self.dbg[idx, :, 0:n], ap, [src_b], [self.Bdbg], self.Bdbg)
        self.fw.keep = {self.Bdbg.T.stream}

    def chk(self, st):
        if self.stage <= st:
            raise StopBuild()

    def __init__(self, S, n_layers, layers=None):
        self.S = S
        self.NT = S // 128
        self.NCH = S // 512
        self.layers = list(range(n_layers)) if layers is None else list(layers)
        self.n_layers = max(self.layers) + 1
        self.uid = 0

    def sb(self, es, name, shape, dt, nowaw=False):
        self.uid += 1
        a = es.enter_context(self.nc.sbuf_tensor("%s_%d" % (name, self.uid), shape, dt))
        return B(a, name, nowaw)

    def ps(self, es, name, shape, dt):
        self.uid += 1
        a = es.enter_context(self.nc.psum_tensor("%s_%d" % (name, self.uid), shape, dt))
        return B(a, name, psum=True)

    def dve(self, fn, R, W):
        self.fw.op("dve", fn, R, W)

    def act(self, fn, R, W):
        self.fw.op("act", fn, R, W)

    def pool(self, fn, R, W):
        self.fw.op("pool", fn, R, W)

    def pe(self, fn, R, W):
        self.fw.op("pe", fn, R, W)

    def evac(self, out, in_, R, W):
        self._ev = getattr(self, "_ev", 0) + 1
        if self._ev % 4 == 0 or self.dve_only:
            self.dve(lambda e: e.tensor_copy(out=out, in_=in_), R, W)
        else:
            self.act(lambda e: e.activation(out=out, in_=in_, func=AF.Copy), R, W)

    def transpose(self, out_ap, in_ap, np_in, nf_in, dt, R, W, evac_eng=None, scale=None):
        if dt == BF16:
            slot = self.mib.next()
            idn = self.ident_b
        else:
            slot = self.mif.next()
            idn = self.ident_f
        pv = slot.a[0:nf_in, 0:np_in]
        self.pe(lambda e: e.transpose(out=pv, in_=in_ap, identity=idn.a[0:np_in, 0:np_in]), R + [idn], [slot])
        self.evac(out_ap, pv, [slot], W)

    def build(self):
        S, NT = self.S, self.NT
        nc = bass.Bass("TRN2", target_bir_lowering=False)
        self.nc = nc
        L = self.n_layers
        dr = {}

        def din(name, shape):
            dr[name] = nc.dram_tensor(name, shape, F32, kind="ExternalInput").ap()

        din("x", [S, D])
        din("xq", [S // 4, D])
        din("cs", [S, 64])
        for l in self.layers:
            din("wA%d" % l, [D, NA])
            din("wS%d" % l, [D, 512])
            din("wG%d" % l, [D, 3072])
            din("wuq%d" % l, [256, 192])
            din("wukv%d" % l, [128, 256])
            din("wup%d" % l, [65, 128])
            din("aup%d" % l, [65, 128])
            din("vec%d" % l, [1, NV])
            din("gT%d" % l, [128, 8])
            din("qagT%d" % l, [128, 2])
            din("kvagT%d" % l, [128, 1])
            din("wp%d" % l, [3, 128, D])
            din("wo%d" % l, [D, D])
        if 1 in self.layers:
            din("wvT", [512, D])
            din("muVT", [128, 4])
            din("vdown", [512, 32])
            din("vup", [33, 128])
        out = nc.dram_tensor("out", [S // 4, D], F32, kind="ExternalOutput").ap()
        if self.debug:
            self.dbg = nc.dram_tensor("dbg", [24, 128, 512], F32, kind="ExternalOutput").ap()
            self.Bdbg = B(None, "dbg", nowaw=True)
            self.dbgb = nc.dram_tensor("dbgb", [4, 128, 512], BF16, kind="ExternalOutput").ap()
        self.dr = dr
        part = [nc.dram_tensor("part%d" % l, [S, D], F32).ap() for l in range(L)]
        red = [nc.dram_tensor("red%d" % l, [S, D], F32).ap() for l in range(L)]
        rs = [nc.dram_tensor("rs%d" % l, [S // 4, D], F32).ap() for l in range(L)]
        self.rs = rs
        self.Brs = [B(None, "rs%d" % l) for l in range(L)]
        oT_scr = nc.dram_tensor("oT_scr", [3, 128, S], BF16).ap()
        if 1 in self.layers and 0 not in self.layers:
            vf_scr = nc.dram_tensor("vf", [S, 128], F32, kind="ExternalInput").ap()
        elif self.layers == [0]:
            vf_scr = nc.dram_tensor("vf", [S, 128], F32, kind="ExternalOutput").ap()
        else:
            vf_scr = nc.dram_tensor("vf_scr", [S, 128], F32).ap()
        self.part, self.red, self.oT_scr, self.vf_scr = part, red, oT_scr, vf_scr
        self.Bpart = [B(None, "part%d" % l, nowaw=True) for l in range(L)]
        self.Bred = [B(None, "red%d" % l) for l in range(L)]
        self.BoT = B(None, "oTscr", nowaw=True)
        self.Bvf = B(None, "vfscr", nowaw=True)
        self.Bout = B(None, "out", nowaw=True)

        with contextlib.ExitStack() as es:
            self.fw = FW(nc, es)
            self.consts(es)
            self.pj = Rot([self.ps(es, "pj", [128, 512], F32) for _ in range(2)])
            self.sc = Rot([self.ps(es, "sc", [128, 512], F32) for _ in range(2)])
            self.oa = self.ps(es, "oa", [128, 4, 128], F32)
            self.mib = Rot([self.ps(es, "mib", [128, 1024], BF16)])
            self.mif = Rot([self.ps(es, "mif", [128, 512], F32) for _ in range(2)])
            for l in self.layers:
                with contextlib.ExitStack() as esA:
                    self.passA(esA, l)
                if self.stage <= 50:
                    self.fw.barrier()
                    self.layers = []
                    break
                self.fw.barrier()
                with contextlib.ExitStack() as esB:
                    self.passB(esB, l)
                self.fw.barrier()
                fw = self.fw
                groups = [[0, 1, 2, 3], [4, 5, 6, 7]]
                fw.deps("pool", [self.Bpart[l]], [self.Brs[l]])
                ins = nc.gpsimd.collective_compute("ReduceScatter", ALU.add, replica_groups=groups,
                                                   ins=[part[l]], outs=[rs[l]])
                fw.done(ins, "pool", 1, [self.Bpart[l]], [self.Brs[l]])
                if l < self.layers[-1]:
                    nchk = 8
                    rows = S // nchk
                    for c in range(nchk):
                        fw.deps("pool", [self.Bpart[l]], [self.Bred[l]])
                        ins = nc.gpsimd.collective_compute("AllReduce", ALU.add, replica_groups=groups,
                                                           ins=[part[l][c * rows:(c + 1) * rows, :]],
                                                           outs=[red[l][c * rows:(c + 1) * rows, :]])
                        fw.done(ins, "pool", 1, [self.Bpart[l]], [self.Bred[l]])
                self.fw.barrier()
            with contextlib.ExitStack() as esF:
                self.final(esF, out)
            self.fw.barrier()
        return nc

    def consts(self, es):
        k = self
        self.ident_f = k.sb(es, "identf", [128, 128], F32)
        self.ident_b = k.sb(es, "identb", [128, 128], BF16)
        self.triU = k.sb(es, "triU", [128, 128], F32)
        self.triBD = k.sb(es, "triBD", [128, 128], F32)
        self.sel127 = k.sb(es, "sel127", [128, 128], F32)
        self.ones_f = k.sb(es, "onesf", [128, 128], F32)
        self.mask2 = k.sb(es, "mask2", [128, 256], F32)
        self.masksl = k.sb(es, "masksl", [128, 128], F32)
        idf, idb = self.ident_f, self.ident_b
        k.pool(lambda e: e.memset(idf.a[:], 1.0), [], [idf])
        k.pool(lambda e: e.affine_select(out=idf.a[:], in_=idf.a[:], pattern=[[-1, 128]], compare_op=ALU.is_equal,
                                         fill=0.0, base=0, channel_multiplier=1), [idf], [idf])
        k.pool(lambda e: e.tensor_copy(out=idb.a[:], in_=idf.a[:]), [idf], [idb])
        tu = self.triU
        k.pool(lambda e: e.memset(tu.a[:], 1.0), [], [tu])
        k.pool(lambda e: e.affine_select(out=tu.a[:], in_=tu.a[:], pattern=[[1, 128]], compare_op=ALU.is_ge,
                                         fill=0.0, base=0, channel_multiplier=-1), [tu], [tu])
        tb = self.triBD
        k.pool(lambda e: e.tensor_copy(out=tb.a[:], in_=tu.a[:]), [tu], [tb])
        k.pool(lambda e: e.memset(tb.a[0:64, 64:128], 0.0), [tb], [tb])
        s1 = self.sel127
        k.pool(lambda e: e.memset(s1.a[:], 1.0), [], [s1])
        k.pool(lambda e: e.affine_select(out=s1.a[:], in_=s1.a[:], pattern=[[0, 128]], compare_op=ALU.is_ge,
                                         fill=0.0, base=-127, channel_multiplier=1), [s1], [s1])
        k.pool(lambda e: e.memset(self.ones_f.a[:], 1.0), [], [self.ones_f])
        m2 = self.mask2
        k.pool(lambda e: e.memset(m2.a[:, 0:128], 1.0), [], [m2])
        k.pool(lambda e: e.affine_select(out=m2.a[:, 0:128], in_=m2.a[:, 0:128], pattern=[[1, 128]], compare_op=ALU.is_gt,
                                         fill=0.0, base=0, channel_multiplier=-1), [m2], [m2])
        k.pool(lambda e: e.memset(m2.a[0:64, 64:128], 0.0), [m2], [m2])
        k.pool(lambda e: e.tensor_copy(out=m2.a[:, 128:256], in_=tb.a[:]), [tb, m2], [m2])
        ml = self.masksl
        k.pool(lambda e: e.memset(ml.a[:], 1.0), [], [ml])
        k.pool(lambda e: e.affine_select(out=ml.a[:], in_=ml.a[:], pattern=[[-1, 128]], compare_op=ALU.is_gt,
                                         fill=0.0, base=0, channel_multiplier=1), [ml], [ml])
        k.pool(lambda e: e.memset(ml.a[64:128, 0:64], 0.0), [ml], [ml])

    def load_cast(self, dst, dcol0, src_ap, ncols, gT, stg_rot, nk=8):
        k = self
        c0 = 0
        while c0 < ncols:
            cw = min(256, ncols - c0)
            stg = stg_rot.next()
            for kk in range(nk):
                k.fw.dma("sp", stg.a[:, kk, 0:cw], src_ap[kk * 128:(kk + 1) * 128, c0:c0 + cw], [], [stg], stg)
            for kk in range(nk):
                o = dst.a[:, kk, dcol0 + c0:dcol0 + c0 + cw]
                i = stg.a[:, kk, 0:cw]
                if gT is None:
                    if kk % 2:
                        k.pool(lambda e: e.tensor_copy(out=o, in_=i), [stg], [dst])
                    else:
                        k.dve(lambda e: e.tensor_copy(out=o, in_=i), [stg], [dst])
                else:
                    g = gT.a[:, kk:kk + 1]
                    if kk % 2:
                        k.pool(lambda e: e.tensor_scalar(out=o, in0=i, scalar1=g, scalar2=None, op0=ALU.mult), [stg, gT], [dst])
                    else:
                        k.dve(lambda e: e.tensor_scalar(out=o, in0=i, scalar1=g, scalar2=None, op0=ALU.mult), [stg, gT], [dst])
            c0 += cw

    def small_load(self, es, name, shape, src_ap):
        b = self.sb(es, name, shape, F32)
        self.fw.dma("sp", b.a[:], src_ap, [], [b], b)
        return b

    def load_h(self, l, ti, first, want_hb=False):
        k = self
        fw = k.fw
        xt = k.xrot.next()
        rows = slice(ti * 128, (ti + 1) * 128)
        fw.dma("sp", xt.a[:], k.dr["x"][rows, :], [], [xt], xt)
        for ll in [q for q in self.layers if q < l]:
            for half in range(2):
                rt = k.rrot.next()
                hc = slice(half * 512, (half + 1) * 512)
                fw.dma("sp", rt.a[:, 0:512], k.red[ll][rows, hc], [k.Bred[ll]], [rt], rt)
                k.dve(lambda e: e.tensor_add(out=xt.a[:, hc], in0=xt.a[:, hc], in1=rt.a[:, 0:512]), [xt, rt], [xt])
        if k.stage <= 1.1:
            return None
        hb = k.hbrot.next()
        junk, st = hb, k.xstat.next()
        k.act(lambda e: e.activation(out=junk.a[:], in_=xt.a[:], func=AF.Square, scale=float(D ** -0.5),
                                     accum_out=st.a[:, 0:1]), [xt], [junk, st])
        if k.stage <= 1.2:
            return None
        k.dve(lambda e: e.tensor_scalar_add(out=st.a[:, 1:2], in0=st.a[:, 0:1], scalar1=EPS), [st], [st])
        k.act(lambda e: e.sqrt(out=st.a[:, 1:2], in_=st.a[:, 1:2]), [st], [st])
        k.dve(lambda e: e.reciprocal(out=st.a[:, 2:3], in_=st.a[:, 1:2]), [st], [st])
        if k.stage <= 1.3:
            return None
        k.act(lambda e: e.activation(out=hb.a[:], in_=xt.a[:], func=AF.Copy, scale=st.a[:, 2:3]), [xt, st], [hb])
        if k.stage <= 1.4:
            return None
        hT = k.hTrot.next()
        for half in range(2):
            slot = k.mib.next()
            for q in range(4):
                kk = half * 4 + q
                k.pe(lambda e: e.transpose(out=slot.a[:, q * 128:(q + 1) * 128], in_=hb.a[:, kk * 128:(kk + 1) * 128],
                                           identity=k.ident_b.a[:]), [hb, k.ident_b], [slot])
            if k.stage <= 1.45:
                continue
            for q in range(4):
                k.evac(hT.a[:, half * 4 + q, 1:129], slot.a[:, q * 128:(q + 1) * 128], [slot], [hT])
        if k.stage <= 1.5:
            return None
        if first:
            k.dve(lambda e: e.memset(hT.a[:, :, 0:1], 0.0), [], [hT])
        else:
            prev = k.hT_prev
            k.dve(lambda e: e.tensor_copy(out=hT.a[:, :, 0:1], in_=prev.a[:, :, 128:129]), [prev], [hT])
        k.hT_prev = hT
        return hT

    def rstd_cols(self, st, n, kk_cols=()):
        k = self
        k.act(lambda e: e.sqrt(out=st.a[:, 0:n], in_=st.a[:, 0:n]), [st], [st])
        for c in kk_cols:
            k.dve(lambda e: e.tensor_scalar_max(out=st.a[:, c:c + 1], in0=st.a[:, c:c + 1], scalar1=1e-12), [st], [st])
        k.dve(lambda e: e.reciprocal(out=st.a[:, 0:n], in_=st.a[:, 0:n]), [st], [st])

    def passA(self, es, l):
        k = self
        fw = k.fw
        S, NT, NCH = k.S, k.NT, k.NCH
        dr = k.dr
        L1 = l > 0
        NS = 544 if L1 else 512
        vec = k.sb(es, "vec", [128, V_MU], F32)
        fw.dma("sp", vec.a[:], dr["vec%d" % l][:, 0:V_MU].partition_broadcast(128), [], [vec], vec)
        gT = k.small_load(es, "gT", [128, 8], dr["gT%d" % l])
        qagT = k.small_load(es, "qagT", [128, 2], dr["qagT%d" % l])
        kvagT = k.small_load(es, "kvagT", [128, 1], dr["kvagT%d" % l])
        wup = k.small_load(es, "wup", [65, 128], dr["wup%d" % l])
        aup = k.small_load(es, "aup", [65, 128], dr["aup%d" % l])
        wA = k.sb(es, "wA", [128, 8, NA], BF16)
        wS1 = k.sb(es, "wS1", [128, 8, NS], BF16)
        wS2 = k.sb(es, "wS2", [128, 8, NS], BF16)
        wuq = k.sb(es, "wuq", [128, 2, 192], BF16)
        wukv = k.sb(es, "wukv", [128, 1, 256], BF16)
        with contextlib.ExitStack() as est:
            stg_rot = Rot([k.sb(est, "stg", [128, 8, 256], F32) for _ in range(2)])
            k.load_cast(wA, 0, dr["wA%d" % l], NA, gT, stg_rot)
            k.load_cast(wuq, 0, dr["wuq%d" % l], 192, qagT, stg_rot, nk=2)
            k.load_cast(wukv, 0, dr["wukv%d" % l], 256, kvagT, stg_rot, nk=1)
            muv = k.sb(est, "muv", [128, 512], F32)
            fw.dma("sp", muv.a[:], dr["vec%d" % l][:, V_MU:V_MU + 512].partition_broadcast(128), [], [muv], muv)
            tmp = k.sb(est, "wtmp", [128, 256], F32)
            tmp2 = k.sb(est, "wtmp2", [128, 256], F32)
            for c0 in (0, 256):
                stg = stg_rot.next()
                for kk in range(8):
                    fw.dma("sp", stg.a[:, kk, :], dr["wS%d" % l][kk * 128:(kk + 1) * 128, c0:c0 + 256], [], [stg], stg)
                mu = muv.a[:, c0:c0 + 256]
                for kk in range(8):
                    g = gT.a[:, kk:kk + 1]
                    k.dve(lambda e: e.tensor_mul(out=tmp.a[:], in0=stg.a[:, kk, :], in1=mu), [stg, muv], [tmp])
                    k.dve(lambda e: e.tensor_scalar(out=wS2.a[:, kk, c0:c0 + 256], in0=tmp.a[:], scalar1=g, scalar2=None,
                                                    op0=ALU.mult), [tmp, gT], [wS2])
                    k.dve(lambda e: e.tensor_sub(out=tmp2.a[:], in0=stg.a[:, kk, :], in1=tmp.a[:]), [stg, tmp], [tmp2])
                    k.dve(lambda e: e.tensor_scalar(out=wS1.a[:, kk, c0:c0 + 256], in0=tmp2.a[:], scalar1=g, scalar2=None,
                                                    op0=ALU.mult), [tmp2, gT], [wS1])
            if L1:
                muVT = k.small_load(est, "muVT", [128, 4], dr["muVT"])
                vdn = k.sb(est, "vdn", [128, 4, 32], F32)
                for kc in range(4):
                    fw.dma("sp", vdn.a[:, kc, :], dr["vdown"][kc * 128:(kc + 1) * 128, :], [], [vdn], vdn)
                wv = k.sb(est, "wv", [128, 4, 128], F32)
                wv1 = k.sb(est, "wv1", [128, 4, 128], F32)
                wv2 = k.sb(est, "wv2", [128, 4, 128], F32)
                for dc in range(8):
                    for kc in range(4):
                        fw.dma("sp", wv.a[:, kc, :], dr["wvT"][kc * 128:(kc + 1) * 128, dc * 128:(dc + 1) * 128], [], [wv], wv)
                    for kc in range(4):
                        k.dve(lambda e: e.tensor_scalar(out=wv2.a[:, kc, :], in0=wv.a[:, kc, :], scalar1=muVT.a[:, kc:kc + 1],
                                                        scalar2=None, op0=ALU.mult), [wv, muVT], [wv2])
                    k.dve(lambda e: e.tensor_sub(out=wv1.a[:], in0=wv.a[:], in1=wv2.a[:]), [wv, wv2], [wv1])
                    for (wsrc, wdst) in ((wv1, wS1), (wv2, wS2)):
                        slot = k.mif.next()
                        for kc in range(4):
                            k.pe(lambda e: e.matmul(slot.a[:, 0:32], lhsT=wsrc.a[:, kc, :], rhs=vdn.a[:, kc, :],
                                                    start=(kc == 0), stop=(kc == 3)), [wsrc, vdn], [slot])
                        k.dve(lambda e: e.tensor_scalar(out=wdst.a[:, dc, 512:544], in0=slot.a[:, 0:32], scalar1=gT.a[:, dc:dc + 1],
                                                        scalar2=None, op0=ALU.mult), [slot, gT], [wdst])
        fw.barrier()
        if k.stage <= 1:
            return
        if L1:
            vup = k.small_load(es, "vup", [33, 128], dr["vup"])
        kTm = [k.sb(es, "kTm%d" % h, [96, S], BF16) for h in range(2)]
        kTf = k.sb(es, "kTf", [128, S], BF16)
        Vm = k.sb(es, "Vm", [128, NT, 2, 65], BF16)
        Vf = k.sb(es, "Vf", [128, NT, 2, 65], BF16)
        ncum = k.sb(es, "ncum", [128, NT, 2], F32)
        k.xrot = Rot([k.sb(es, "xt", [128, D], F32) for _ in range(2)])
        k.xstat = Rot([k.sb(es, "xst", [128, 4], F32) for _ in range(2)])
        k.hbrot = Rot([k.sb(es, "hb", [128, D], BF16) for _ in range(1)])
        k.hTrot = Rot([k.sb(es, "hT", [128, 8, 129], BF16) for _ in range(2)])
        csr = Rot([k.sb(es, "cs", [128, 64], F32) for _ in range(2)])
        s1r = Rot([k.sb(es, "s1", [128, 418], F32) for _ in range(1)])
        s2r = Rot([k.sb(es, "s2", [128, 384], F32) for _ in range(1)])
        s4r = Rot([k.sb(es, "s4", [128, 512], F32) for _ in range(1)])
        s5r = Rot([k.sb(es, "s5", [128, 32], F32) for _ in range(2)])
        sga = k.sb(es, "sga", [128, 4, 128], BF16)
        sgb = k.sb(es, "sgb", [128, 4, 128], BF16)
        sgc = Rot([k.sb(es, "sgc", [128, 128], F32) for _ in range(1)])
        sq = k.sb(es, "sq", [128, 512], F32)
        k.rrot = Rot([sq])
        stA = Rot([k.sb(es, "stA", [128, 16], F32) for _ in range(2)])
        stB = Rot([k.sb(es, "stB", [128, 8], F32) for _ in range(2)])
        cqn = k.sb(es, "cqn", [128, 384], BF16)
        cT = k.sb(es, "cT", [128, 3, 128], BF16)
        qk = k.sb(es, "qk", [128, 448], F32)
        qn = k.sb(es, "qn", [128, 2, 96], F32)
        kn = k.sb(es, "kn", [128, 2, 96], F32)
        rp = k.sb(es, "rp", [128, 4, 32], F32)
        qb = k.sb(es, "qb", [128, 2, 96], BF16)
        kb = k.sb(es, "kb", [128, 2, 96], BF16)
        fqb = k.sb(es, "fqb", [128, 128], BF16)
        fkb = k.sb(es, "fkb", [128, 128], BF16)
        qTm = [k.sb(es, "qTm%d" % h, [96, 512], BF16) for h in range(2)]
        qTf = k.sb(es, "qTf", [128, 512], BF16)
        lf = Rot([k.sb(es, "lf", [128, 4], F32) for _ in range(2)])
        cumr = Rot([k.sb(es, "cum", [128, 2], F32) for _ in range(2)])
        dcum = k.sb(es, "dcum", [128, 2, 128], F32)
        cumbc = k.sb(es, "cumbc", [128, 2, 512], F32)
        pTr = Rot([k.sb(es, "pT", [128, 512], BF16) for _ in range(2)])
        ftmp = Rot([k.sb(es, "ftmp", [128, 512], F32) for _ in range(1)])
        rinv = k.sb(es, "rinv", [128, 4], F32)
        ogm = k.sb(es, "ogm", [128, 4, 128], BF16)
        ogf = k.sb(es, "ogf", [128, 4, 128], BF16)
        oTr = Rot([k.sb(es, "oT", [128, 512], BF16) for _ in range(1)])
        ocb = k.sb(es, "ocb", [128, 128], BF16)
        oTc = Rot([k.sb(es, "oTc", [128, 128], BF16) for _ in range(2)])
        twT = k.sb(es, "twT", [65, 128], F32)
        alT = k.sb(es, "alT", [65, 128], F32)
        k.pool(lambda e: e.memset(twT.a[64:65, :], 1.0), [], [twT])
        k.pool(lambda e: e.memset(alT.a[64:65, :], 1.0), [], [alT])
        tw = k.sb(es, "tw", [128, 64], F32)
        lw = k.sb(es, "lw", [128, 128], F32)
        av = k.sb(es, "av", [128, 128], F32)
        kkn = k.sb(es, "kkn", [128, 128], F32)
        kmod = k.sb(es, "kmod", [128, 128], F32)
        vv = k.sb(es, "vv", [128, 128], F32)
        rt1 = k.sb(es, "rt1", [128, 128], F32)
        rt2 = k.sb(es, "rt2", [128, 128], F32)
        bon = k.sb(es, "bon", [128, 2], F32)
        E1 = k.sb(es, "E1", [128, 128], F32)
        E2 = k.sb(es, "E2", [128, 128], F32)
        E3 = k.sb(es, "E3", [128, 128], F32)
        qa_ = k.sb(es, "qalpha", [128, 128], F32)
        qk_ = k.sb(es, "qkt", [128, 128], F32)
        qb_ = k.sb(es, "qbeta", [128, 128], F32)
        qr_ = k.sb(es, "qr", [128, 128], F32)
        qa_hi = k.sb(es, "qahi", [128, 128], F32)
        qk_hi = k.sb(es, "qkhi", [128, 128], F32)
        k.pool(lambda e: e.memset(qa_hi.a[:], 0.0), [], [qa_hi])
        k.pool(lambda e: e.memset(qk_hi.a[:], 0.0), [], [qk_hi])
        qT4 = [k.sb(es, "qT4_%d" % h, [64, 4, 128], F32) for h in range(2)]
        rlo = [k.sb(es, "rlo%d" % h, [64, 128], F32) for h in range(2)]
        rhi = [k.sb(es, "rhi%d" % h, [64, 128], F32) for h in range(2)]
        for h in range(2):
            k.pool(lambda e: e.memset(rlo[h].a[:], 0.0), [], [rlo[h]])
            k.pool(lambda e: e.memset(rhi[h].a[:], 0.0), [], [rhi[h]])
        GA = [k.sb(es, "GA%d" % h, [128, 256], F32) for h in range(2)]
        GK = [k.sb(es, "GK%d" % h, [128, 256], F32) for h in range(2)]
        XT = [Rot([k.sb(es, "XT%d_%d" % (h, i), [128, 128], F32) for i in range(2)]) for h in range(2)]
        XX = [Rot([k.sb(es, "XX%d_%d" % (h, i), [128, 128], F32) for i in range(2)]) for h in range(2)]
        PP = [Rot([k.sb(es, "PP%d_%d" % (h, i), [128, 128], F32) for i in range(2)]) for h in range(2)]
        Wb = [k.sb(es, "Wb%d" % h, [128, 64], F32) for h in range(2)]
        Ub = [k.sb(es, "Ub%d" % h, [128, 64], F32) for h in range(2)]
        Mst = [Rot([k.sb(es, "M%d_%d" % (h, i), [64, 64], F32) for i in range(3)]) for h in range(2)]
        pC = [k.sb(es, "pC%d" % h, [64, 2], F32) for h in range(2)]
        mtmp = [k.sb(es, "mtmp%d" % h, [64, 64], F32) for h in range(2)]
        yv = k.sb(es, "yv", [128, 2, 64], F32)
        ysq = k.sb(es, "ysq", [128, 2, 64], F32)
        yst = k.sb(es, "yst", [128, 8], F32)
        vdT = k.sb(es, "vdT", [33, 128], F32)
        k.pool(lambda e: e.memset(vdT.a[32:33, :], 1.0), [], [vdT])
        vfl = k.sb(es, "vfl", [128, 128], F32)
        nu = rt1
        Mcur = []
        for h in range(2):
            m0 = Mst[h].next()
            k.pool(lambda e: e.memset(m0.a[:], 0.0), [], [m0])
            Mcur.append(m0)
        cum_prev = None
        qscale_m = float(96 ** -0.5)
        qscale_f = float(64 ** -0.5)

        for ch in range(NCH):
            for jq in range(4):
                ti = ch * 4 + jq
                rows = slice(ti * 128, (ti + 1) * 128)
                cols = slice(ti * 128, (ti + 1) * 128)
                ccols = slice(jq * 128, (jq + 1) * 128)
                if ti == 0:
                    hT_nextA = k.load_h(l, 0, True)
                hT = hT_nextA
                if k.stage <= 2:
                    continue
                cs = csr.next()
                fw.dma("sp", cs.a[:], dr["cs"][rows, :], [], [cs], cs)
                s1, s2, s4, s5 = s1r.next(), s2r.next(), s4r.next(), s5r.next()
                sg_c = sgc.next()
                p = k.pj.next()
                for kk in range(8):
                    k.pe(lambda e: e.matmul(p.a[:, 0:418], lhsT=hT.a[:, kk, 1:129], rhs=wA.a[:, kk, 0:418],
                                            start=(kk == 0), stop=(kk == 7)), [hT, wA], [p])
                k.evac(s1.a[:], p.a[:, 0:418], [p], [s1])
                p = k.pj.next()
                for kk in range(8):
                    k.pe(lambda e: e.matmul(p.a[:, 0:512], lhsT=hT.a[:, kk, 1:129], rhs=wA.a[:, kk, 418:930],
                                            start=(kk == 0), stop=(kk == 7)), [hT, wA], [p])
                k.dve(lambda e: e.tensor_copy(out=s2.a[:], in_=p.a[:, 0:384]), [p], [s2])
                k.act(lambda e: e.activation(out=sgb.a[:, jq, :], in_=p.a[:, 384:512], func=AF.Silu), [p], [sgb])
                p = k.pj.next()
                for kk in range(8):
                    k.pe(lambda e: e.matmul(p.a[:, 0:256], lhsT=hT.a[:, kk, 1:129], rhs=wA.a[:, kk, 930:1186],
                                            start=(kk == 0), stop=(kk == 7)), [hT, wA], [p])
                k.act(lambda e: e.activation(out=sga.a[:, jq, :], in_=p.a[:, 0:128], func=AF.Silu), [p], [sga])
                k.act(lambda e: e.activation(out=sg_c.a[:], in_=p.a[:, 128:256], func=AF.Silu), [p], [sg_c])
                p = k.pj.next()
                for kk in range(8):
                    k.pe(lambda e: e.matmul(p.a[:, 0:512], lhsT=hT.a[:, kk, 1:129], rhs=wS1.a[:, kk, 0:512],
                                            start=(kk == 0), stop=False), [hT, wS1], [p])
                for kk in range(8):
                    k.pe(lambda e: e.matmul(p.a[:, 0:512], lhsT=hT.a[:, kk, 0:128], rhs=wS2.a[:, kk, 0:512],
                                            start=False, stop=(kk == 7)), [hT, wS2], [p])
                k.evac(s4.a[:], p.a[:, 0:512], [p], [s4])
                if L1:
                    p = k.mif.next()
                    for kk in range(8):
                        k.pe(lambda e: e.matmul(p.a[:, 0:32], lhsT=hT.a[:, kk, 1:129], rhs=wS1.a[:, kk, 512:544],
                                                start=(kk == 0), stop=False), [hT, wS1], [p])
                    for kk in range(8):
                        k.pe(lambda e: e.matmul(p.a[:, 0:32], lhsT=hT.a[:, kk, 0:128], rhs=wS2.a[:, kk, 512:544],
                                                start=False, stop=(kk == 7)), [hT, wS2], [p])
                    k.evac(s5.a[:], p.a[:, 0:32], [p], [s5])
                if ti + 1 < NT:
                    hT_nextA = k.load_h(l, ti + 1, False)
                if ti == k.dbg_tile:
                    k.dump(0, s1, s1.a[:], 418)
                    k.dump(1, s2, s2.a[:], 384)
                    k.dump(2, s4, s4.a[:], 512)
                    k.dump(15, sga, sga.a[:, jq, :], 128)
                if k.stage <= 3:
                    continue
                st = stA.next()
                k.dve(lambda e: e.tensor_mul(out=sq.a[:, 0:416], in0=s1.a[:, 0:416], in1=s1.a[:, 0:416]), [s1], [sq])
                k.dve(lambda e: e.tensor_reduce(out=st.a[:, 0:1], in_=sq.a[:, 0:256], axis=AX.X, op=ALU.add), [sq], [st])
                k.dve(lambda e: e.tensor_reduce(out=st.a[:, 1:2], in_=sq.a[:, 256:384], axis=AX.X, op=ALU.add), [sq], [st])
                k.dve(lambda e: e.tensor_reduce(out=st.a[:, 2:3], in_=sq.a[:, 384:416], axis=AX.X, op=ALU.add), [sq], [st])
                k.dve(lambda e: e.tensor_mul(out=sq.a[:, 0:256], in0=s2.a[:, 0:256], in1=s2.a[:, 0:256]), [s2, st], [sq])
                k.dve(lambda e: e.tensor_reduce(out=st.a[:, 3:7], in_=sq.a[:, 0:256].rearrange("p (h d) -> p h d", h=4),
                                                axis=AX.X, op=ALU.add), [sq], [st])
                k.dve(lambda e: e.tensor_mul(out=kkn.a[:], in0=s4.a[:, 128:256], in1=vec.a[:, V_KK:V_KK + 128]), [s4, vec], [kkn])
                k.dve(lambda e: e.tensor_mul(out=sq.a[:, 256:384], in0=kkn.a[:], in1=kkn.a[:]), [kkn], [sq])
                k.dve(lambda e: e.tensor_reduce(out=st.a[:, 7:9], in_=sq.a[:, 256:384].rearrange("p (h d) -> p h d", h=2),
                                                axis=AX.X, op=ALU.add), [sq], [st])
                k.dve(lambda e: e.tensor_scalar(out=st.a[:, 0:1], in0=st.a[:, 0:1], scalar1=1.0 / 256, scalar2=EPS, op0=ALU.mult, op1=ALU.add), [st], [st])
                k.dve(lambda e: e.tensor_scalar(out=st.a[:, 1:2], in0=st.a[:, 1:2], scalar1=1.0 / 128, scalar2=EPS, op0=ALU.mult, op1=ALU.add), [st], [st])
                k.dve(lambda e: e.tensor_scalar(out=st.a[:, 2:3], in0=st.a[:, 2:3], scalar1=1.0 / 32, scalar2=EPS, op0=ALU.mult, op1=ALU.add), [st], [st])
                k.dve(lambda e: e.tensor_scalar(out=st.a[:, 3:7], in0=st.a[:, 3:7], scalar1=1.0 / 64, scalar2=EPS, op0=ALU.mult, op1=ALU.add), [st], [st])
                k.rstd_cols(st, 9, kk_cols=(7, 8))
                k.act(lambda e: e.activation(out=cqn.a[:, 0:256], in_=s1.a[:, 0:256], func=AF.Copy, scale=st.a[:, 0:1]), [s1, st], [cqn])
                k.act(lambda e: e.activation(out=cqn.a[:, 256:384], in_=s1.a[:, 256:384], func=AF.Copy, scale=st.a[:, 1:2]), [s1, st], [cqn])
                for c in range(3):
                    k.transpose(cT.a[:, c, :], cqn.a[:, c * 128:(c + 1) * 128], 128, 128, BF16, [cqn], [cT])
                p = k.pj.next()
                for c in range(2):
                    k.pe(lambda e: e.matmul(p.a[:, 0:192], lhsT=cT.a[:, c, :], rhs=wuq.a[:, c, :], start=(c == 0), stop=(c == 1)),
                         [cT, wuq], [p])
                k.pe(lambda e: e.matmul(p.a[:, 192:448], lhsT=cT.a[:, 2, :], rhs=wukv.a[:, 0, :], start=True, stop=True),
                     [cT, wukv], [p])
                k.evac(qk.a[:], p.a[:, 0:448], [p], [qk])
                sb_ = stB.next()
                k.dve(lambda e: e.tensor_mul(out=sq.a[:, 0:320], in0=qk.a[:, 0:320], in1=qk.a[:, 0:320]), [qk], [sq])
                k.dve(lambda e: e.tensor_reduce(out=sb_.a[:, 0:2], in_=sq.a[:, 0:192].rearrange("p (h d) -> p h d", h=2),
                                                axis=AX.X, op=ALU.add), [sq], [sb_])
                k.dve(lambda e: e.tensor_reduce(out=sb_.a[:, 2:4], in_=sq.a[:, 192:320].rearrange("p (h d) -> p h d", h=2),
                                                axis=AX.X, op=ALU.add), [sq], [sb_])
                k.dve(lambda e: e.tensor_scalar(out=sb_.a[:, 0:2], in0=sb_.a[:, 0:2], scalar1=1.0 / 96, scalar2=EPS, op0=ALU.mult, op1=ALU.add), [sb_], [sb_])
                k.dve(lambda e: e.tensor_scalar(out=sb_.a[:, 2:4], in0=sb_.a[:, 2:4], scalar1=1.0 / 64, scalar2=EPS, op0=ALU.mult, op1=ALU.add), [sb_], [sb_])
                k.rstd_cols(sb_, 4)
                for h in range(2):
                    k.dve(lambda e: e.scalar_tensor_tensor(out=qn.a[:, h, :], in0=qk.a[:, h * 96:(h + 1) * 96], scalar=sb_.a[:, h:h + 1],
                                                           in1=vec.a[:, V_QG2 + h * 96:V_QG2 + (h + 1) * 96], op0=ALU.mult, op1=ALU.mult),
                          [qk, sb_, vec], [qn])
                    k.dve(lambda e: e.scalar_tensor_tensor(out=kn.a[:, h, 0:64], in0=qk.a[:, 192 + h * 64:192 + (h + 1) * 64],
                                                           scalar=sb_.a[:, 2 + h:3 + h],
                                                           in1=vec.a[:, V_KNG2 + h * 64:V_KNG2 + (h + 1) * 64], op0=ALU.mult, op1=ALU.mult),
                          [qk, sb_, vec], [kn])
                    k.dve(lambda e: e.scalar_tensor_tensor(out=kn.a[:, h, 64:96], in0=s1.a[:, 384:416], scalar=st.a[:, 2:3],
                                                           in1=vec.a[:, V_KRG:V_KRG + 32], op0=ALU.mult, op1=ALU.mult),
                          [s1, st, vec], [kn])
                cosv = cs.a[:, 0:32].rearrange("p (h d) -> p h d", h=2)
                sinv = cs.a[:, 32:64].rearrange("p (h d) -> p h d", h=2)
                for (src, dstb, scl) in ((qn, qb, qscale_m), (kn, kb, 1.0)):
                    x1 = src.a[:, :, 64:80]
                    x2 = src.a[:, :, 80:96]
                    r1 = rp.a[:, 0:1, :].rearrange("p a (h d) -> p (a h) d", h=2)
                    r2 = rp.a[:, 1:2, :].rearrange("p a (h d) -> p (a h) d", h=2)
                    r3 = rp.a[:, 2:3, :].rearrange("p a (h d) -> p (a h) d", h=2)
                    r4 = rp.a[:, 3:4, :].rearrange("p a (h d) -> p (a h) d", h=2)
                    k.dve(lambda e: e.tensor_mul(out=r1, in0=x1, in1=cosv), [src, cs], [rp])
                    k.dve(lambda e: e.tensor_mul(out=r2, in0=x2, in1=sinv), [src, cs, rp], [rp])
                    k.dve(lambda e: e.tensor_mul(out=r3, in0=x2, in1=cosv), [src, cs, rp], [rp])
                    k.dve(lambda e: e.tensor_mul(out=r4, in0=x1, in1=sinv), [src, cs, rp], [rp])
                    k.dve(lambda e: e.tensor_sub(out=src.a[:, :, 64:80], in0=r1, in1=r2), [rp, src], [src])
                    k.dve(lambda e: e.tensor_add(out=src.a[:, :, 80:96], in0=r3, in1=r4), [rp, src], [src])
                    k.act(lambda e: e.activation(out=dstb.a[:], in_=src.a[:], func=AF.Copy, scale=scl), [src], [dstb])
                for h in range(2):
                    k.transpose(qTm[h].a[:, ccols], qb.a[:, h, :], 128, 96, BF16, [qb], [qTm[h]])
                    k.transpose(kTm[h].a[:, cols], kb.a[:, h, :], 128, 96, BF16, [kb], [kTm[h]])
                k.dve(lambda e: e.tensor_copy(out=Vm.a[:, ti, :, 0:64], in_=qk.a[:, 320:448].rearrange("p (h d) -> p h d", h=2)), [qk], [Vm])
                k.dve(lambda e: e.tensor_copy(out=Vm.a[:, ti, :, 64:65], in_=k.ones_f.a[:, 0:2].rearrange("p (h o) -> p h o", o=1)), [k.ones_f], [Vm])
                if ti == k.dbg_tile:
                    k.dump(3, qk, qk.a[:], 448)
                    k.dump(4, qn, qn.a[:].rearrange("p h d -> p (h d)"), 192)
                    k.dump(5, kn, kn.a[:].rearrange("p h d -> p (h d)"), 192)
                    k.dump(16, st, st.a[:], 16)
                if k.stage <= 4:
                    continue
                for h in range(2):
                    k.dve(lambda e: e.scalar_tensor_tensor(out=sq.a[:, h * 64:(h + 1) * 64], in0=s2.a[:, h * 64:(h + 1) * 64],
                                                           scalar=st.a[:, 3 + h:4 + h], in1=vec.a[:, V_FQG2 + h * 64:V_FQG2 + (h + 1) * 64],
                                                           op0=ALU.mult, op1=ALU.mult), [s2, st, vec, sq], [sq])
                    k.dve(lambda e: e.scalar_tensor_tensor(out=sq.a[:, 128 + h * 64:128 + (h + 1) * 64], in0=s2.a[:, 128 + h * 64:128 + (h + 1) * 64],
                                                           scalar=st.a[:, 5 + h:6 + h], in1=vec.a[:, V_FKG2 + h * 64:V_FKG2 + (h + 1) * 64],
                                                           op0=ALU.mult, op1=ALU.mult), [s2, st, vec, sq], [sq])
                k.act(lambda e: e.activation(out=fqb.a[:], in_=sq.a[:, 0:128], func=AF.Copy, scale=qscale_f), [sq], [fqb])
                k.act(lambda e: e.activation(out=fkb.a[:], in_=sq.a[:, 128:256], func=AF.Copy), [sq], [fkb])
                k.transpose(qTf.a[:, ccols], fqb.a[:], 128, 128, BF16, [fqb], [qTf])
                k.transpose(kTf.a[:, cols], fkb.a[:], 128, 128, BF16, [fkb], [kTf])
                k.dve(lambda e: e.tensor_copy(out=Vf.a[:, ti, :, 0:64], in_=s2.a[:, 256:384].rearrange("p (h d) -> p h d", h=2)), [s2], [Vf])
                k.dve(lambda e: e.tensor_copy(out=Vf.a[:, ti, :, 64:65], in_=k.ones_f.a[:, 0:2].rearrange("p (h o) -> p h o", o=1)), [k.ones_f], [Vf])
                lf_ = lf.next()
                k.dve(lambda e: e.tensor_add(out=lf_.a[:, 0:2], in0=s1.a[:, 416:418], in1=vec.a[:, V_BF:V_BF + 2]), [s1, vec], [lf_])
                k.act(lambda e: e.activation(out=lf_.a[:, 0:2], in_=lf_.a[:, 0:2], func=AF.Exp, scale=-1.0), [lf_], [lf_])
                k.act(lambda e: e.activation(out=lf_.a[:, 0:2], in_=lf_.a[:, 0:2], func=AF.Ln, bias=1.0), [lf_], [lf_])
                k.dve(lambda e: e.tensor_scalar_mul(out=lf_.a[:, 2:4], in0=lf_.a[:, 0:2], scalar1=-1.0), [lf_], [lf_])
                cum = cumr.next()
                p = k.mif.next()
                k.pe(lambda e: e.matmul(p.a[:, 0:2], lhsT=k.triU.a[:], rhs=lf_.a[:, 2:4], start=True, stop=(cum_prev is None)),
                     [k.triU, lf_], [p])
                if cum_prev is not None:
                    cp = cum_prev
                    k.pe(lambda e: e.matmul(p.a[:, 0:2], lhsT=k.sel127.a[:], rhs=cp.a[:], start=False, stop=True),
                         [k.sel127, cp], [p])
                k.dve(lambda e: e.tensor_copy(out=cum.a[:], in_=p.a[:, 0:2]), [p], [cum])
                cum_prev = cum
                k.dve(lambda e: e.tensor_scalar_mul(out=ncum.a[:, ti, :], in0=cum.a[:], scalar1=-1.0), [cum], [ncum])
                for h in range(2):
                    k.dve(lambda e: e.tensor_scalar(out=dcum.a[:, h, :], in0=k.ident_f.a[:], scalar1=cum.a[:, h:h + 1], scalar2=None,
                                                    op0=ALU.mult), [k.ident_f, cum], [dcum])
                p = k.mif.next()
                for h in range(2):
                    k.pe(lambda e: e.matmul(p.a[:, h * 128:(h + 1) * 128], lhsT=k.ones_f.a[:], rhs=dcum.a[:, h, :], start=True, stop=True),
                         [k.ones_f, dcum], [p])
                k.evac(cumbc.a[:, :, ccols], p.a[:, 0:256].rearrange("p (h c) -> p h c", h=2), [p], [cumbc])

                if ti == k.dbg_tile:
                    k.dump(6, cum, cum.a[:], 2)
                    k.dump(7, cumbc, cumbc.a[:, 0, ccols], 128)
                    k.dump(17, sq, sq.a[:, 0:256], 256)
                if k.stage <= 5:
                    continue
                k.act(lambda e: e.activation(out=tw.a[:], in_=s4.a[:, 384:448], func=AF.Tanh), [s4], [tw])
                k.transpose(twT.a[0:64, :], tw.a[:], 128, 64, F32, [tw], [twT])
                k.transpose(alT.a[0:64, :], s4.a[:, 448:512], 128, 64, F32, [s4], [alT])
                p = k.mif.next()
                k.pe(lambda e: e.matmul(p.a[:, 0:128], lhsT=twT.a[:], rhs=wup.a[:], start=True, stop=True), [twT, wup], [p])
                k.pe(lambda e: e.matmul(p.a[:, 128:256], lhsT=alT.a[:], rhs=aup.a[:], start=True, stop=True), [alT, aup], [p])
                k.act(lambda e: e.activation(out=lw.a[:], in_=p.a[:, 0:128], func=AF.Sigmoid), [p], [lw])
                k.act(lambda e: e.activation(out=av.a[:], in_=p.a[:, 128:256], func=AF.Sigmoid), [p], [av])
                k.dve(lambda e: e.tensor_scalar_mul(out=lw.a[:], in0=lw.a[:], scalar1=-DECAY), [lw], [lw])
                for h in range(2):
                    k.dve(lambda e: e.tensor_scalar(out=kkn.a[:, h * 64:(h + 1) * 64], in0=kkn.a[:, h * 64:(h + 1) * 64],
                                                    scalar1=st.a[:, 7 + h:8 + h], scalar2=None, op0=ALU.mult), [kkn, st], [kkn])
                k.dve(lambda e: e.scalar_tensor_tensor(out=rt1.a[:], in0=av.a[:], scalar=-1.0, in1=vec.a[:, V_KA:V_KA + 128],
                                                       op0=ALU.add, op1=ALU.mult), [av, vec], [rt1])
                k.dve(lambda e: e.scalar_tensor_tensor(out=kmod.a[:], in0=rt1.a[:], scalar=1.0, in1=s4.a[:, 128:256],
                                                       op0=ALU.add, op1=ALU.mult), [rt1, s4], [kmod])
                if not L1:
                    k.dve(lambda e: e.tensor_copy(out=vv.a[:], in_=s4.a[:, 256:384]), [s4], [vv])
                    fw.dma("pool", k.vf_scr[rows, :], vv.a[:], [vv], [k.Bvf], vv)
                else:
                    fw.dma("sp", vfl.a[:], k.vf_scr[rows, :], [k.Bvf], [vfl], vfl)
                    k.transpose(vdT.a[0:32, :], s5.a[:], 128, 32, F32, [s5], [vdT])
                    p = k.mif.next()
                    k.pe(lambda e: e.matmul(p.a[:, 0:128], lhsT=vdT.a[:], rhs=vup.a[:], start=True, stop=True), [vdT, vup], [p])
                    k.act(lambda e: e.activation(out=nu.a[:], in_=p.a[:, 0:128], func=AF.Sigmoid), [p], [nu])
                    k.dve(lambda e: e.tensor_sub(out=rt2.a[:], in0=vfl.a[:], in1=s4.a[:, 256:384]), [vfl, s4], [rt2])
                    k.dve(lambda e: e.tensor_mul(out=rt2.a[:], in0=rt2.a[:], in1=nu.a[:]), [rt2, nu], [rt2])
                    k.dve(lambda e: e.tensor_add(out=vv.a[:], in0=rt2.a[:], in1=s4.a[:, 256:384]), [rt2, s4], [vv])
                k.dve(lambda e: e.tensor_mul(out=rt1.a[:], in0=s4.a[:, 0:128], in1=kmod.a[:]), [s4, kmod], [rt1])
                k.dve(lambda e: e.tensor_mul(out=rt1.a[:], in0=rt1.a[:], in1=vec.a[:, V_RK:V_RK + 128]), [rt1, vec], [rt1])
                k.dve(lambda e: e.tensor_reduce(out=bon.a[:], in_=rt1.a[:].rearrange("p (h d) -> p h d", h=2), axis=AX.X, op=ALU.add),
                      [rt1], [bon])
                if ti == k.dbg_tile:
                    k.dump(8, lw, lw.a[:], 128)
                    k.dump(9, av, av.a[:], 128)
                    k.dump(10, kkn, kkn.a[:], 128)
                    k.dump(11, kmod, kmod.a[:], 128)
                    k.dump(12, vv, vv.a[:], 128)
                if k.stage <= 6:
                    continue
                p = k.mif.next()
                k.pe(lambda e: e.matmul(p.a[:, 0:128], lhsT=k.triBD.a[:], rhs=lw.a[:], start=True, stop=True), [k.triBD, lw], [p])
                k.act(lambda e: e.activation(out=E1.a[:], in_=p.a[:, 0:128], func=AF.Exp), [p], [E1])
                k.act(lambda e: e.activation(out=E2.a[:], in_=p.a[:, 0:128], func=AF.Exp, scale=-1.0), [p], [E2])
                k.dve(lambda e: e.tensor_sub(out=E3.a[:], in0=p.a[:, 0:128], in1=lw.a[:]), [p, lw], [E3])
                k.act(lambda e: e.activation(out=E3.a[:], in_=E3.a[:], func=AF.Exp), [E3], [E3])
                for h in range(2):
                    pp = k.mif.next()
                    k.pe(lambda e: e.matmul(pp.a[0:64, 0:1], lhsT=lw.a[:, h * 64:(h + 1) * 64], rhs=k.triBD.a[:, 63:64], start=True, stop=True),
                         [lw, k.triBD], [pp])
                    k.pe(lambda e: e.matmul(pp.a[0:64, 1:2], lhsT=lw.a[:, h * 64:(h + 1) * 64], rhs=k.triBD.a[:, 127:128], start=True, stop=True),
                         [lw, k.triBD], [pp])
                    k.act(lambda e: e.activation(out=pC[h].a[:], in_=pp.a[0:64, 0:2], func=AF.Exp), [pp], [pC[h]])
                k.dve(lambda e: e.tensor_mul(out=qr_.a[:], in0=s4.a[:, 0:128], in1=E1.a[:]), [s4, E1], [qr_])
                k.dve(lambda e: e.tensor_mul(out=qk_.a[:], in0=kmod.a[:], in1=E2.a[:]), [kmod, E2], [qk_])
                k.dve(lambda e: e.tensor_mul(out=qa_.a[:], in0=kkn.a[:], in1=av.a[:]), [kkn, av], [qa_])
                k.dve(lambda e: e.tensor_mul(out=qa_.a[:], in0=qa_.a[:], in1=E2.a[:]), [qa_, E2], [qa_])
                k.dve(lambda e: e.scalar_tensor_tensor(out=qb_.a[:], in0=kkn.a[:], scalar=-1.0, in1=E3.a[:], op0=ALU.mult, op1=ALU.mult),
                      [kkn, E3], [qb_])
                k.dve(lambda e: e.tensor_copy(out=qa_hi.a[64:128, :], in_=qa_.a[64:128, :]), [qa_], [qa_hi])
                k.dve(lambda e: e.tensor_copy(out=qk_hi.a[64:128, :], in_=qk_.a[64:128, :]), [qk_], [qk_hi])
                def head_gen(h):
                    hc = slice(h * 64, (h + 1) * 64)
                    q4 = qT4[h]
                    for pair in range(2):
                        slot = k.mif.next()
                        for qi in range(2):
                            src = (qa_, qk_, qb_, qr_)[pair * 2 + qi]
                            k.pe(lambda e: e.transpose(out=slot.a[0:64, qi * 128:(qi + 1) * 128], in_=src.a[:, hc], identity=k.ident_f.a[:]),
                                 [src, k.ident_f], [slot])
                        k.evac(q4.a[:, pair * 2:pair * 2 + 2, :], slot.a[0:64, 0:256].rearrange("p (q c) -> p q c", q=2), [slot], [q4])
                        yield
                    k.dve(lambda e: e.tensor_copy(out=rlo[h].a[:, 0:64], in_=q4.a[:, 3, 0:64]), [q4], [rlo[h]])
                    k.dve(lambda e: e.tensor_copy(out=rhi[h].a[:, 64:128], in_=q4.a[:, 3, 64:128]), [q4], [rhi[h]])
                    aT, kT_, bT, rT = q4.a[:, 0, :], q4.a[:, 1, :], q4.a[:, 2, :], q4.a[:, 3, :]
                    brT = q4.a[:, 2:4, :].rearrange("p q c -> p (q c)")
                    slot = k.mif.next()
                    k.pe(lambda e: e.matmul(slot.a[:, 0:256], lhsT=aT, rhs=brT, start=True, stop=True), [q4], [slot])
                    k.dve(lambda e: e.tensor_mul(out=GA[h].a[:], in0=slot.a[:, 0:256], in1=k.mask2.a[:]), [slot, k.mask2], [GA[h]])
                    yield
                    slot = k.mif.next()
                    k.pe(lambda e: e.matmul(slot.a[:, 0:256], lhsT=kT_, rhs=brT, start=True, stop=True), [q4], [slot])
                    k.dve(lambda e: e.tensor_mul(out=GK[h].a[:], in0=slot.a[:, 0:256], in1=k.mask2.a[:]), [slot, k.mask2], [GK[h]])
                    yield
                    slot = k.mif.next()
                    k.pe(lambda e: e.matmul(slot.a[:, 0:128], lhsT=bT, rhs=aT, start=True, stop=True), [q4], [slot])
                    xt_ = XT[h].next()
                    k.dve(lambda e: e.tensor_mul(out=xt_.a[:], in0=slot.a[:, 0:128], in1=k.masksl.a[:]), [slot, k.masksl], [xt_])
                    yield
                    if k.stage <= 7:
                        return
                    Pc = PP[h].next()
                    k.dve(lambda e: e.tensor_add(out=Pc.a[:], in0=GA[h].a[:, 0:128], in1=k.ident_f.a[:]), [GA[h], k.ident_f], [Pc])
                    Xc_ap, Xc_b = GA[h].a[:, 0:128], GA[h]
                    XTc = xt_
                    for lev in range(5):
                        last = lev == 4
                        slot = k.mif.next()
                        k.pe(lambda e: e.matmul(slot.a[:, 0:128], lhsT=Xc_ap, rhs=XTc.a[:], start=True, stop=True), [Xc_b, XTc], [slot])
                        XTn = XT[h].next()
                        if not last:
                            slot2 = k.mif.next()
                            k.pe(lambda e: e.matmul(slot2.a[:, 0:128], lhsT=XTc.a[:], rhs=Xc_ap, start=True, stop=True), [Xc_b, XTc], [slot2])
                        k.evac(XTn.a[:], slot.a[:, 0:128], [slot], [XTn])
                        yield
                        if not last:
                            Xn = XX[h].next()
                            k.evac(Xn.a[:], slot2.a[:, 0:128], [slot2], [Xn])
                            yield
                        slot3 = k.mif.next()
                        k.pe(lambda e: e.matmul(slot3.a[:, 0:128], lhsT=XTn.a[:], rhs=Pc.a[:], start=True, stop=True), [XTn, Pc], [slot3])
                        Pn = PP[h].next()
                        k.dve(lambda e: e.tensor_add(out=Pn.a[:], in0=slot3.a[:, 0:128], in1=Pc.a[:]), [slot3, Pc], [Pn])
                        yield
                        Pc = Pn
                        XTc = XTn
                        if not last:
                            Xc_ap, Xc_b = Xn.a[:], Xn
                    TT = Pc
                    if k.stage <= 8:
                        return
                    M0 = Mcur[h]
                    W_, U_ = Wb[h], Ub[h]
                    slot = k.mif.next()
                    k.pe(lambda e: e.matmul(slot.a[0:64, 0:64], lhsT=q4.a[:, 2, 0:64], rhs=M0.a[:], start=True, stop=False), [q4, M0], [slot])
                    k.pe(lambda e: e.matmul(slot.a[0:64, 0:64], lhsT=GK[h].a[0:64, 0:64], rhs=vv.a[0:64, hc], start=False, stop=True),
                         [GK[h], vv], [slot])
                    k.evac(W_.a[0:64, :], slot.a[0:64, 0:64], [slot], [W_])
                    yield
                    slot = k.mif.next()
                    k.pe(lambda e: e.matmul(slot.a[0:64, 0:64], lhsT=TT.a[0:64, 0:64], rhs=W_.a[0:64, :], start=True, stop=True), [TT, W_], [slot])
                    k.evac(U_.a[0:64, :], slot.a[0:64, 0:64], [slot], [U_])
                    yield
                    slot = k.mif.next()
                    k.pe(lambda e: e.matmul(slot.a[0:64, 0:64], lhsT=qa_.a[0:64, hc], rhs=U_.a[0:64, :], start=True, stop=False), [qa_, U_], [slot])
                    k.pe(lambda e: e.matmul(slot.a[0:64, 0:64], lhsT=qk_.a[0:64, hc], rhs=vv.a[0:64, hc], start=False, stop=True), [qk_, vv], [slot])
                    M1 = Mst[h].next()
                    k.dve(lambda e: e.tensor_add(out=mtmp[h].a[:], in0=slot.a[0:64, 0:64], in1=M0.a[:]), [slot, M0], [mtmp[h]])
                    k.dve(lambda e: e.tensor_scalar(out=M1.a[:], in0=mtmp[h].a[:], scalar1=pC[h].a[:, 0:1], scalar2=None, op0=ALU.mult),
                          [mtmp[h], pC[h]], [M1])
                    yield
                    slot = k.mif.next()
                    k.pe(lambda e: e.matmul(slot.a[:, 0:64], lhsT=q4.a[:, 2, :], rhs=M1.a[:], start=True, stop=False), [q4, M1], [slot])
                    k.pe(lambda e: e.matmul(slot.a[:, 0:64], lhsT=GK[h].a[:, 0:128], rhs=vv.a[:, hc], start=False, stop=True),
                         [GK[h], vv], [slot])
                    k.evac(W_.a[64:128, :], slot.a[64:128, 0:64], [slot], [W_])
                    yield
                    slot = k.mif.next()
                    k.pe(lambda e: e.matmul(slot.a[:, 0:64], lhsT=TT.a[:, 0:128], rhs=W_.a[:], start=True, stop=True), [TT, W_], [slot])
                    k.evac(U_.a[64:128, :], slot.a[64:128, 0:64], [slot], [U_])
                    yield
                    slot = k.mif.next()
                    k.pe(lambda e: e.matmul(slot.a[0:64, 0:64], lhsT=qa_hi.a[:, hc], rhs=U_.a[:], start=True, stop=False), [qa_hi, U_], [slot])
                    k.pe(lambda e: e.matmul(slot.a[0:64, 0:64], lhsT=qk_hi.a[:, hc], rhs=vv.a[:, hc], start=False, stop=True), [qk_hi, vv], [slot])
                    M2 = Mst[h].next()
                    k.dve(lambda e: e.tensor_add(out=mtmp[h].a[:], in0=slot.a[0:64, 0:64], in1=M1.a[:]), [slot, M1], [mtmp[h]])
                    k.dve(lambda e: e.tensor_scalar(out=M2.a[:], in0=mtmp[h].a[:], scalar1=pC[h].a[:, 1:2], scalar2=None, op0=ALU.mult),
                          [mtmp[h], pC[h]], [M2])
                    yield
                    slot = k.mif.next()
                    k.pe(lambda e: e.matmul(slot.a[:, 0:64], lhsT=rlo[h].a[:], rhs=M0.a[:], start=True, stop=False), [rlo[h], M0], [slot])
                    k.pe(lambda e: e.matmul(slot.a[:, 0:64], lhsT=rhi[h].a[:], rhs=M1.a[:], start=False, stop=False), [rhi[h], M1], [slot])
                    k.pe(lambda e: e.matmul(slot.a[:, 0:64], lhsT=GA[h].a[:, 128:256], rhs=U_.a[:], start=False, stop=False), [GA[h], U_], [slot])
                    k.pe(lambda e: e.matmul(slot.a[:, 0:64], lhsT=GK[h].a[:, 128:256], rhs=vv.a[:, hc], start=False, stop=True), [GK[h], vv], [slot])
                    k.evac(yv.a[:, h, :], slot.a[:, 0:64], [slot], [yv])
                    yield
                    Mcur[h] = M2
                gens = [head_gen(0), head_gen(1)]
                alive = [True, True]
                while any(alive):
                    for gi in range(2):
                        if alive[gi]:
                            try:
                                next(gens[gi])
                            except StopIteration:
                                alive[gi] = False
                if ti == k.dbg_tile:
                    k.dump(13, yv, yv.a[:].rearrange("p h d -> p (h d)"), 128)
                    k.dump(18, GA[0], GA[0].a[:], 256)
                    k.dump(19, GK[0], GK[0].a[:], 256)
                if k.stage <= 9:
                    continue
                k.dve(lambda e: e.tensor_reduce(out=yst.a[:, 0:2], in_=yv.a[:], axis=AX.X, op=ALU.add), [yv], [yst])
                k.dve(lambda e: e.tensor_scalar_mul(out=yst.a[:, 0:2], in0=yst.a[:, 0:2], scalar1=1.0 / 64), [yst], [yst])
                for h in range(2):
                    k.dve(lambda e: e.tensor_scalar(out=yv.a[:, h, :], in0=yv.a[:, h, :], scalar1=yst.a[:, h:h + 1], scalar2=None, op0=ALU.subtract),
                          [yv, yst], [yv])
                k.dve(lambda e: e.tensor_mul(out=ysq.a[:], in0=yv.a[:], in1=yv.a[:]), [yv], [ysq])
                k.dve(lambda e: e.tensor_reduce(out=yst.a[:, 2:4], in_=ysq.a[:], axis=AX.X, op=ALU.add), [ysq], [yst])
                k.dve(lambda e: e.tensor_scalar(out=yst.a[:, 2:4], in0=yst.a[:, 2:4], scalar1=1.0 / 64, scalar2=GN_EPS, op0=ALU.mult, op1=ALU.add), [yst], [yst])
                k.act(lambda e: e.sqrt(out=yst.a[:, 2:4], in_=yst.a[:, 2:4]), [yst], [yst])
                k.dve(lambda e: e.reciprocal(out=yst.a[:, 2:4], in_=yst.a[:, 2:4]), [yst], [yst])
                for h in range(2):
                    hc = slice(h * 64, (h + 1) * 64)
                    k.dve(lambda e: e.scalar_tensor_tensor(out=yv.a[:, h, :], in0=yv.a[:, h, :], scalar=yst.a[:, 2 + h:3 + h],
                                                           in1=vec.a[:, V_LG + h * 64:V_LG + (h + 1) * 64], op0=ALU.mult, op1=ALU.mult),
                          [yv, yst, vec], [yv])
                    k.dve(lambda e: e.tensor_add(out=yv.a[:, h, :], in0=yv.a[:, h, :], in1=vec.a[:, V_LB + h * 64:V_LB + (h + 1) * 64]), [yv, vec], [yv])
                    k.dve(lambda e: e.scalar_tensor_tensor(out=yv.a[:, h, :], in0=vv.a[:, hc], scalar=bon.a[:, h:h + 1], in1=yv.a[:, h, :],
                                                           op0=ALU.mult, op1=ALU.add), [vv, bon, yv], [yv])
                if ti == k.dbg_tile:
                    k.dump(14, yv, yv.a[:].rearrange("p h d -> p (h d)"), 128)
                k.dve(lambda e: e.tensor_mul(out=ocb.a[:], in0=yv.a[:].rearrange("p h d -> p (h d)"), in1=sg_c.a[:]), [yv, sg_c], [ocb])
                oc = oTc.next()
                k.transpose(oc.a[:], ocb.a[:], 128, 128, BF16, [ocb], [oc])
                fw.dma("pool", k.oT_scr[2, :, cols], oc.a[:], [oc], [k.BoT], oc)

            for mixer in range(2 if k.stage > 10 else 0):
                og = ogm if mixer == 0 else ogf
                gate = sga if mixer == 0 else sgb
                Vr = Vm if mixer == 0 else Vf
                for h in range(2):
                    nkt = 4 * (ch + 1)
                    oa = k.oa
                    k.dve(lambda e: e.memset(oa.a[:], 0.0), [], [oa])
                    def emit_pv(kt, j0, pT):
                        for j in range(j0, 4):
                            k.pe(lambda e: e.matmul(oa.a[:, j, 0:65], lhsT=pT.a[:, j * 128:(j + 1) * 128], rhs=Vr.a[:, kt, h, :],
                                                    start=False, stop=(kt == 4 * ch + j), skip_group_check=True), [pT, Vr], [oa])
                    pend = None
                    for kt in range(nkt):
                        dj = kt - 4 * ch
                        j0 = max(0, dj)
                        c0 = j0 * 128
                        kc = slice(kt * 128, (kt + 1) * 128)
                        s = k.sc.next()
                        if mixer == 0:
                            k.pe(lambda e: e.matmul(s.a[:, c0:512], lhsT=kTm[h].a[:, kc], rhs=qTm[h].a[:, c0:512], start=True, stop=True),
                                 [kTm[h], qTm[h]], [s])
                        else:
                            hp = slice(h * 64, (h + 1) * 64)
                            k.pe(lambda e: e.matmul(s.a[:, c0:512], lhsT=kTf.a[hp, kc], rhs=qTf.a[hp, c0:512], start=True, stop=True),
                                 [kTf, qTf], [s])
                        pT = pTr.next()
                        if mixer == 0:
                            k.act(lambda e: e.activation(out=pT.a[:, c0:512], in_=s.a[:, c0:512], func=AF.Exp), [s], [pT])
                        else:
                            ft = ftmp.next()
                            k.dve(lambda e: e.tensor_add(out=ft.a[:, c0:512], in0=s.a[:, c0:512], in1=cumbc.a[:, h, c0:512]), [s, cumbc], [ft])
                            k.act(lambda e: e.activation(out=pT.a[:, c0:512], in_=ft.a[:, c0:512], func=AF.Exp, bias=ncum.a[:, kt, h:h + 1]),
                                  [ft, ncum], [pT])
                        if dj >= 0:
                            k.pool(lambda e: e.affine_select(out=pT.a[:, c0:c0 + 128], in_=pT.a[:, c0:c0 + 128], pattern=[[1, 128]],
                                                             compare_op=ALU.is_ge, fill=0.0, base=0, channel_multiplier=-1), [pT], [pT])
                        if pend is not None:
                            emit_pv(*pend)
                        pend = (kt, j0, pT)
                    emit_pv(*pend)
                    k.dve(lambda e: e.reciprocal(out=rinv.a[:], in_=oa.a[:, :, 64:65].rearrange("p j o -> p (j o)")), [oa], [rinv])
                    for j in range(4):
                        k.dve(lambda e: e.scalar_tensor_tensor(out=og.a[:, j, h * 64:(h + 1) * 64], in0=oa.a[:, j, 0:64], scalar=rinv.a[:, j:j + 1],
                                                               in1=gate.a[:, j, h * 64:(h + 1) * 64], op0=ALU.mult, op1=ALU.mult),
                              [oa, rinv, gate], [og])
                oT = oTr.next()
                for j in range(4):
                    k.transpose(oT.a[:, j * 128:(j + 1) * 128], og.a[:, j, :], 128, 128, BF16, [og], [oT])
                fw.dma("pool", k.oT_scr[mixer, :, ch * 512:(ch + 1) * 512], oT.a[:], [oT], [k.BoT], oT)

    def passB(self, es, l):
        k = self
        fw = k.fw
        dr = k.dr
        NT = k.NT
        gT = k.small_load(es, "gTb", [128, 8], dr["gT%d" % l])
        wG = k.sb(es, "wG", [128, 8, 3072], BF16)
        wp = k.sb(es, "wp", [128, 3, D], BF16)
        wo = k.sb(es, "wo", [128, 8, D], BF16)
        with contextlib.ExitStack() as est:
            stg_rot = Rot([k.sb(est, "stgb", [128, 8, 256], F32) for _ in range(2)])
            k.load_cast(wG, 0, dr["wG%d" % l], 3072, gT, stg_rot)
            k.load_cast(wo, 0, dr["wo%d" % l], D, None, stg_rot)
            for br in range(3):
                for c0 in range(0, D, 256):
                    stg = stg_rot.next()
                    fw.dma("sp", stg.a[:, 0, :], dr["wp%d" % l][br, :, c0:c0 + 256], [], [stg], stg)
                    k.dve(lambda e: e.tensor_copy(out=wp.a[:, br, c0:c0 + 256], in_=stg.a[:, 0, :]), [stg], [wp])
        fw.barrier()
        k.xrot = Rot([k.sb(es, "xtb", [128, D], F32) for _ in range(3)])
        k.rrot = Rot([k.sb(es, "rtb", [128, 512], F32) for _ in range(2)])
        k.xstat = Rot([k.sb(es, "xstb", [128, 4], F32) for _ in range(2)])
        k.hbrot = Rot([k.sb(es, "hbb", [128, D], BF16) for _ in range(2)])
        k.hTrot = Rot([k.sb(es, "hTb", [128, 8, 129], BF16) for _ in range(3)])
        oTl = Rot([k.sb(es, "oTl", [128, 3, 128], BF16) for _ in range(2)])
        gs = Rot([k.sb(es, "gs", [128, 512], F32) for _ in range(2)])
        mg = k.sb(es, "mg", [128, D], F32)
        mgt = Rot([k.sb(es, "mgt", [128, 512], F32) for _ in range(2)])
        mb = k.sb(es, "mb", [128, D], BF16)
        mT = k.sb(es, "mT", [128, 8, 128], BF16)
        po = Rot([k.sb(es, "po", [128, D], F32) for _ in range(2)])
        mgr = Rot([mg, k.sb(es, "mg2", [128, D], F32)])

        def stage_g(ti, hT):
            rows = slice(ti * 128, (ti + 1) * 128)
            mgc = mgr.next()
            ot = oTl.next()
            for br in range(3):
                fw.dma("sp", ot.a[:, br, :], k.oT_scr[br, :, rows], [k.BoT], [ot], ot)
            for br in range(3):
                for half in range(2):
                    hc = slice(half * 512, (half + 1) * 512)
                    pg = k.pj.next()
                    for kk in range(8):
                        k.pe(lambda e: e.matmul(pg.a[:], lhsT=hT.a[:, kk, 1:129], rhs=wG.a[:, kk, br * D + half * 512:br * D + (half + 1) * 512],
                                                start=(kk == 0), stop=(kk == 7)), [hT, wG], [pg])
                    g = gs.next()
                    k.act(lambda e: e.activation(out=g.a[:], in_=pg.a[:], func=AF.Sigmoid), [pg], [g])
                    pb = k.sc.next()
                    k.pe(lambda e: e.matmul(pb.a[:], lhsT=ot.a[:, br, :], rhs=wp.a[:, br, hc], start=True, stop=True), [ot, wp], [pb])
                    if br == 0:
                        k.dve(lambda e: e.tensor_mul(out=mgc.a[:, hc], in0=pb.a[:], in1=g.a[:]), [pb, g], [mgc])
                    else:
                        t_ = mgt.next()
                        k.dve(lambda e: e.tensor_mul(out=t_.a[:], in0=pb.a[:], in1=g.a[:]), [pb, g], [t_])
                        k.pool(lambda e: e.tensor_add(out=mgc.a[:, hc], in0=mgc.a[:, hc], in1=t_.a[:]), [mgc, t_], [mgc])
            return mgc

        def stage_o(ti, mgc):
            rows = slice(ti * 128, (ti + 1) * 128)
            k.act(lambda e: e.activation(out=mb.a[:], in_=mgc.a[:], func=AF.Copy), [mgc], [mb])
            for half in range(2):
                slot = k.mib.next()
                for q in range(4):
                    kk = half * 4 + q
                    k.pe(lambda e: e.transpose(out=slot.a[:, q * 128:(q + 1) * 128], in_=mb.a[:, kk * 128:(kk + 1) * 128],
                                               identity=k.ident_b.a[:]), [mb, k.ident_b], [slot])
                k.evac(mT.a[:, half * 4:half * 4 + 4, :], slot.a[:, 0:512].rearrange("p (q c) -> p q c", q=4), [slot], [mT])
            pout = po.next()
            for half in range(2):
                hc = slice(half * 512, (half + 1) * 512)
                pp = k.pj.next()
                for kk in range(8):
                    k.pe(lambda e: e.matmul(pp.a[:], lhsT=mT.a[:, kk, :], rhs=wo.a[:, kk, hc], start=(kk == 0), stop=(kk == 7)), [mT, wo], [pp])
                k.evac(pout.a[:, hc], pp.a[:], [pp], [pout])
            fw.dma("pool", k.part[l][rows, :], pout.a[:], [pout], [k.Bpart[l]], pout)

        hT_q = [k.load_h(l, 0, True)]
        if NT > 1:
            hT_q.append(k.load_h(l, 1, True))
        pend = None
        for ti in range(NT):
            hT = hT_q.pop(0)
            mgc = stage_g(ti, hT)
            if ti + 2 < NT:
                hT_q.append(k.load_h(l, ti + 2, True))
            if pend is not None:
                stage_o(*pend)
            pend = (ti, mgc)
        stage_o(*pend)

    def final(self, es, out):
        k = self
        fw = k.fw
        L = k.n_layers
        Sq = k.S // 4
        xr = Rot([k.sb(es, "xf", [128, D], F32) for _ in range(2)])
        rr = Rot([k.sb(es, "rf", [128, D], F32) for _ in range(2)])
        for i in range(Sq // 128):
            xt = xr.next()
            fw.dma("sp", xt.a[:], k.dr["xq"][i * 128:(i + 1) * 128, :], [], [xt], xt)
            for l in k.layers:
                rt = rr.next()
                fw.dma("sp", rt.a[:], k.rs[l][i * 128:(i + 1) * 128, :], [k.Brs[l]], [rt], rt)
                k.dve(lambda e: e.tensor_add(out=xt.a[:], in0=xt.a[:], in1=rt.a[:]), [xt, rt], [xt])
            fw.dma("pool", out[i * 128:(i + 1) * 128, :], xt.a[:], [xt], [k.Bout], xt)


IN_SIZES = (256, 128, 32, 512, 512, 512, 512, 8, 512, 1664, 512, 3072)
OFF = np.concatenate([[0], np.cumsum(IN_SIZES)]).tolist()
(O_CQ, O_CKV, O_KR, O_GA, O_FQ, O_FK, O_FV, O_FF, O_GB, O_SH, O_GC, O_MG) = OFF[:12]


def rope_cs(S):
    inv = (np.float32(10000.0) ** (-np.arange(0, 32, 2, dtype=np.float32) / np.float32(32))).astype(np.float32)
    ang = (np.arange(S, dtype=np.float32)[:, None] * inv[None, :]).astype(np.float32)
    c, s_ = np.cos(ang).astype(np.float32), np.sin(ang).astype(np.float32)
    return np.ascontiguousarray(np.concatenate([c, c, s_, s_], axis=1))


def core_inputs(inp, c, S, L, layers=None):
    f = lambda a: np.ascontiguousarray(a, dtype=np.float32)
    b, j = c // 4, c % 4
    hs = slice(128 * j, 128 * (j + 1))
    m = {"x": f(inp["x"][b, :S]), "xq": f(inp["x"][b, j * (S // 4):(j + 1) * (S // 4)]), "cs": rope_cs(S)}
    layers = list(range(L)) if layers is None else layers
    for l in layers:
        w = inp["w_in"][l]
        colsA = np.concatenate([
            np.arange(O_CQ, O_CQ + 416),
            np.arange(O_FF + 2 * j, O_FF + 2 * j + 2),
            np.arange(O_FQ + 128 * j, O_FQ + 128 * j + 128),
            np.arange(O_FK + 128 * j, O_FK + 128 * j + 128),
            np.arange(O_FV + 128 * j, O_FV + 128 * j + 128),
            np.arange(O_GB + 128 * j, O_GB + 128 * j + 128),
            np.arange(O_GA + 128 * j, O_GA + 128 * j + 128),
            np.arange(O_GC + 128 * j, O_GC + 128 * j + 128)])
        shl = np.concatenate([np.arange(128 * j, 128 * j + 128), np.arange(512 + 128 * j, 512 + 128 * j + 128),
                              np.arange(1024 + 128 * j, 1024 + 128 * j + 128), np.arange(1536, 1664)])
        m["wA%d" % l] = f(w[:, colsA])
        m["wS%d" % l] = f(w[:, O_SH + shl])
        m["wG%d" % l] = f(w[:, O_MG:O_MG + 3072])
        uq = inp["mla_w_uq"][l].reshape(256, 8, 96)[:, 2 * j:2 * j + 2].reshape(256, 192)
        m["wuq%d" % l] = f(uq)
        ukv = inp["mla_w_ukv"][l].reshape(128, 8, 128)[:, 2 * j:2 * j + 2]
        m["wukv%d" % l] = f(np.concatenate([ukv[:, 0, :64], ukv[:, 1, :64], ukv[:, 0, 64:], ukv[:, 1, 64:]], axis=1))
        m["wup%d" % l] = f(np.concatenate([inp["rwkv_w_up"][l][:, hs], inp["rwkv_w0"][l][None, hs]], axis=0))
        m["aup%d" % l] = f(np.concatenate([inp["rwkv_a_up"][l][:, hs], inp["rwkv_a0"][l][None, hs]], axis=0))
        qg = inp["mla_q_g"][l]
        vec = np.concatenate([
            qg, qg, inp["mla_knope_g"][l], inp["mla_knope_g"][l], inp["mla_krope_g"][l],
            inp["fox_q_g"][l], inp["fox_q_g"][l], inp["fox_k_g"][l], inp["fox_k_g"][l],
            inp["fox_b_f"][l][2 * j:2 * j + 2], inp["rwkv_k_k"][l][hs], inp["rwkv_k_a"][l][hs],
            inp["rwkv_r_k"][l].reshape(-1)[hs], inp["rwkv_lnx_g"][l][hs], inp["rwkv_lnx_b"][l][hs],
            inp["rwkv_mu"][l][shl]])
        assert vec.shape[0] == NV
        m["vec%d" % l] = f(vec[None, :])
        m["gT%d" % l] = f(inp["norm_g"][l].reshape(8, 128).T)
        m["qagT%d" % l] = f(inp["mla_qa_g"][l].reshape(2, 128).T)
        m["kvagT%d" % l] = f(inp["mla_kva_g"][l].reshape(1, 128).T)
        m["wp%d" % l] = f(np.stack([inp["w_pa"][l][hs], inp["w_pb"][l][hs], inp["w_pc"][l][hs]]))
        m["wo%d" % l] = f(inp["w_out"][l])
    if 1 in layers:
        w = inp["w_in"][1]
        m["wvT"] = f(w[:, O_SH + 1024:O_SH + 1536].T)
        m["muVT"] = f(inp["rwkv_mu"][1][1024:1536].reshape(4, 128).T)
        m["vdown"] = f(inp["rwkv_v_down"][0])
        m["vup"] = f(np.concatenate([inp["rwkv_v_up"][0][:, hs], inp["rwkv_v0"][0][None, hs]], axis=0))
    return m


_CACHE = {}


def run(inp, S, L):
    key = (S, L)
    if key not in _CACHE:
        _CACHE[key] = Kern(S, L).build()
    nc = _CACHE[key]
    in_maps = [core_inputs(inp, c, S, L) for c in range(8)]
    res = run_bass_kernel_spmd(nc, in_maps, core_ids=list(range(8)))
    global LAST
    LAST = res
    out = np.zeros((2, S, D), np.float32)
    q = S // 4
    for c in range(8):
        b, j = c // 4, c % 4
        out[b, j * q:(j + 1) * q] = res.results[c]["out"]
    return out


def run_split(inp, S):
    q = S // 4
    key = (S, "l0")
    if key not in _CACHE:
        _CACHE[key] = Kern(S, 1, layers=[0]).build()
    res0 = run_bass_kernel_spmd(_CACHE[key], [core_inputs(inp, c, S, 2, layers=[0]) for c in range(8)], core_ids=list(range(8)))
    x1 = np.zeros((2, S, D), np.float32)
    for c in range(8):
        x1[c // 4, (c % 4) * q:(c % 4 + 1) * q] = res0.results[c]["out"]
    key = (S, "l1")
    if key not in _CACHE:
        _CACHE[key] = Kern(S, 2, layers=[1]).build()
    inp1 = dict(inp)
    inp1["x"] = x1
    maps = []
    for c in range(8):
        m = core_inputs(inp1, c, S, 2, layers=[1])
        m["vf"] = np.ascontiguousarray(res0.results[c]["vf"])
        maps.append(m)
    res1 = run_bass_kernel_spmd(_CACHE[key], maps, core_ids=list(range(8)))
    out = np.zeros((2, S, D), np.float32)
    for c in range(8):
        out[c // 4, (c % 4) * q:(c % 4 + 1) * q] = res1.results[c]["out"]
    return out


def kernel(**inputs):
    inp = {k: np.asarray(v) for k, v in inputs.items()}
    return run(inp, inp["x"].shape[1], 2)
```
